# Optimizing a Trainium2 kernel written in Bass

```python
import math
import jax, jax.numpy as jnp
from jax import lax
import numpy as np

D_MODEL = 1024
BATCH = 2
SEQ = 8192
DEPTH = 1
DEC_BATCH = 32
DEC_SEQ = 1
PAST_LEN = 8192
PAGE_SIZE = 128

HEAD_DIM = 64
N_NSA_HEADS = 8
N_KV = 2
HPG = N_NSA_HEADS // N_KV
N_RET_HEADS = 8
RET_DK = 64
RET_DV = 64
NSA_WIDTH = N_NSA_HEADS * HEAD_DIM
RET_QK = N_RET_HEADS * RET_DK
RET_WIDTH = N_RET_HEADS * RET_DV
MIX_WIDTH = NSA_WIDTH + RET_WIDTH
KV_WIDTH = N_KV * HEAD_DIM
N_IN = NSA_WIDTH + 6 * KV_WIDTH + 3 * N_NSA_HEADS + 2 * RET_QK + 2 * RET_WIDTH
D_FF = 4 * D_MODEL
CMP_STRIDE = 16
CMP_BLOCK = 2 * CMP_STRIDE
SEL_BLOCK = 64
TOP_N = 16
WINDOW = 512
Q_BLOCK = 128
RET_CHUNK = 128
ROPE_THETA = 500000.0
ROPE_DIMS = HEAD_DIM // 4
RET_THETA = 10000.0
ALPHA = (2 * DEPTH) ** 0.25
BETA = (8 * DEPTH) ** -0.25
LN_EPS = 1e-5
FORCE_BONUS = 1e4
NEG = -1e30

kernel_name = 'nsa_retention_parallel_heads_step'


def layer_norm(x, g, b):
    xf = x.astype(jnp.float32)
    mu = xf.mean(-1, keepdims=True)
    var = jnp.square(xf - mu).mean(-1, keepdims=True)
    return ((xf - mu) * lax.rsqrt(var + LN_EPS) * g + b).astype(x.dtype)


def rope(x, pos, theta, n_rot):
    half = n_rot // 2
    inv = theta ** (-jnp.arange(half, dtype=jnp.float32) / half)
    ang = pos.astype(jnp.float32)[:, None] * inv[None, :]
    cos = jnp.cos(ang)[:, None, :]
    sin = jnp.sin(ang)[:, None, :]
    xr = x[..., :n_rot].astype(jnp.float32)
    x1, x2 = xr[..., :half], xr[..., half:]
    rot = jnp.concatenate([x1 * cos - x2 * sin, x1 * sin + x2 * cos], -1).astype(x.dtype)
    return jnp.concatenate([rot, x[..., n_rot:]], -1)


def masked_softmax(s, mask):
    s = jnp.where(mask, s, NEG)
    m = jnp.max(s, -1, keepdims=True)
    p = jnp.exp(s - m) * mask
    return p / jnp.maximum(p.sum(-1, keepdims=True), 1e-30)


def compress(rows, pos_emb, w):
    B, L, G, d = rows.shape
    ch = rows.reshape(B, L // CMP_STRIDE, CMP_STRIDE, G, d)
    blocks = jnp.concatenate([ch[:, :-1], ch[:, 1:]], axis=2)
    blocks = blocks + pos_emb[None, None, :, None, :]
    return jnp.einsum('bnrgd,rde->bnge', blocks, w)


def nsa_query_block(q, gates, qpos, kc, vc, ks_blk, vs_blk, kw, vw, wpos):
    B, Q, H, d = q.shape
    scale = d ** -0.5
    qg = q.reshape(B, Q, N_KV, HPG, d)
    t = qpos[:, None]
    nc = kc.shape[1]
    cend = jnp.arange(nc) * CMP_STRIDE + CMP_BLOCK - 1
    s_c = jnp.einsum('bqghd,bngd->bghqn', qg, kc).astype(jnp.float32) * scale
    p_c = masked_softmax(s_c, cend[None, :] <= t)
    o_c = jnp.einsum('bghqn,bngd->bqghd', p_c.astype(vc.dtype), vc)
    ns = ks_blk.shape[2]
    imp = jnp.pad(p_c.sum(2), ((0, 0), (0, 0), (0, 0), (1, 1)))
    chunk = imp[..., 1:] + imp[..., :-1]
    imp_s = chunk.reshape(B, N_KV, Q, ns, SEL_BLOCK // CMP_STRIDE).sum(-1)
    blk = jnp.arange(ns)[None, :]
    cur = (qpos // SEL_BLOCK)[:, None]
    forced = (blk == 0) | (blk == cur) | (blk == cur - 1)
    score = jnp.where(blk <= cur, imp_s + FORCE_BONUS * forced, NEG)
    k_top = min(TOP_N, ns)
    _, idx = lax.top_k(score, k_top)
    flat = idx.reshape(B, N_KV, Q * k_top)
    b_i = jnp.arange(B)[:, None, None]
    g_i = jnp.arange(N_KV)[None, :, None]
    ksel = ks_blk[b_i, g_i, flat].reshape(B, N_KV, Q, k_top * SEL_BLOCK, d)
    vsel = vs_blk[b_i, g_i, flat].reshape(B, N_KV, Q, k_top * SEL_BLOCK, d)
    kpos = (idx[..., None] * SEL_BLOCK + jnp.arange(SEL_BLOCK)).reshape(B, N_KV, Q, k_top * SEL_BLOCK)
    s_s = jnp.einsum('bqghd,bgqnd->bghqn', qg, ksel).astype(jnp.float32) * scale
    p_s = masked_softmax(s_s, (kpos <= t)[:, :, None])
    o_s = jnp.einsum('bghqn,bgqnd->bqghd', p_s.astype(vsel.dtype), vsel)
    dist = t - wpos[None, :]
    wmask = (dist >= 0) & (dist < WINDOW) & (wpos[None, :] >= 0)
    s_w = jnp.einsum('bqghd,bwgd->bghqw', qg, kw).astype(jnp.float32) * scale
    p_w = masked_softmax(s_w, wmask)
    o_w = jnp.einsum('bghqw,bwgd->bqghd', p_w.astype(vw.dtype), vw)
    g = gates.reshape(B, Q, N_KV, HPG, 3)
    o = g[..., 0:1] * o_c + g[..., 1:2] * o_s + g[..., 2:3] * o_w
    return o.reshape(B, Q, H * d)


def project(h, pos, w_in):
    B, T, _ = h.shape
    z = h @ w_in
    p0 = NSA_WIDTH
    p1 = p0 + 6 * KV_WIDTH
    p2 = p1 + 3 * N_NSA_HEADS
    p3 = p2 + RET_QK
    p4 = p3 + RET_QK
    p5 = p4 + RET_WIDTH
    q, kv, gt, rq, rk, rv, rg = jnp.split(z, [p0, p1, p2, p3, p4, p5], axis=-1)
    q = rope(q.reshape(B, T, N_NSA_HEADS, HEAD_DIM), pos, ROPE_THETA, ROPE_DIMS)
    kv = kv.reshape(B, T, 6, N_KV, HEAD_DIM)
    k_slc = rope(kv[:, :, 2], pos, ROPE_THETA, ROPE_DIMS)
    k_win = rope(kv[:, :, 4], pos, ROPE_THETA, ROPE_DIMS)
    kv4 = jnp.stack([kv[:, :, 0], kv[:, :, 1], k_slc, kv[:, :, 3]], axis=2)
    win = jnp.stack([k_win, kv[:, :, 5]], axis=2)
    gates = jax.nn.sigmoid(gt.astype(jnp.float32)).astype(h.dtype).reshape(B, T, N_NSA_HEADS, 3)
    rq = rope(rq.reshape(B, T, N_RET_HEADS, RET_DK), pos, RET_THETA, RET_DK)
    rk = rope(rk.reshape(B, T, N_RET_HEADS, RET_DK), pos, RET_THETA, RET_DK) * (RET_DK ** -0.5)
    rv = rv.reshape(B, T, N_RET_HEADS, RET_DV)
    return q, gates, kv4, win, rq, rk, rv, rg


def nsa_prompt(q, gates, kv4, win, w_cmp_k, w_cmp_v, pos_k, pos_v):
    B, T, H, d = q.shape
    kc = compress(kv4[:, :, 0], pos_k, w_cmp_k)
    vc = compress(kv4[:, :, 1], pos_v, w_cmp_v)
    ns = T // SEL_BLOCK
    ks_blk = kv4[:, :, 2].reshape(B, ns, SEL_BLOCK, N_KV, d).transpose(0, 3, 1, 2, 4)
    vs_blk = kv4[:, :, 3].reshape(B, ns, SEL_BLOCK, N_KV, d).transpose(0, 3, 1, 2, 4)
    kw_pad = jnp.pad(win[:, :, 0], ((0, 0), (WINDOW, 0), (0, 0), (0, 0)))
    vw_pad = jnp.pad(win[:, :, 1], ((0, 0), (WINDOW, 0), (0, 0), (0, 0)))
    nb = T // Q_BLOCK
    qb = q.reshape(B, nb, Q_BLOCK, H, d).swapaxes(0, 1)
    gb = gates.reshape(B, nb, Q_BLOCK, H, 3).swapaxes(0, 1)

    def one(args):
        qi, gi, bi = args
        start = bi * Q_BLOCK
        qpos = start + jnp.arange(Q_BLOCK)
        kw = lax.dynamic_slice_in_dim(kw_pad, start, WINDOW + Q_BLOCK, axis=1)
        vw = lax.dynamic_slice_in_dim(vw_pad, start, WINDOW + Q_BLOCK, axis=1)
        wpos = start - WINDOW + jnp.arange(WINDOW + Q_BLOCK)
        return nsa_query_block(qi, gi, qpos, kc, vc, ks_blk, vs_blk, kw, vw, wpos)

    o = lax.map(one, (qb, gb, jnp.arange(nb)))
    return o.swapaxes(0, 1).reshape(B, T, NSA_WIDTH)


def nsa_sample(q, gates, kv4_new, win_new, cache_kv_l, page_table, cache_win_l, w_cmp_k, w_cmp_v, pos_k, pos_v):
    Bd, S, H, d = q.shape
    past = page_table.shape[1] * PAGE_SIZE
    rows = cache_kv_l[page_table].reshape(Bd, past, 4, N_KV, d)
    full = jnp.concatenate([rows, kv4_new.astype(rows.dtype)], axis=1)
    L = past + S
    Lp = -(-L // SEL_BLOCK) * SEL_BLOCK
    full = jnp.pad(full, ((0, 0), (0, Lp - L), (0, 0), (0, 0), (0, 0)))
    kc = compress(full[:, :, 0], pos_k, w_cmp_k)
    vc = compress(full[:, :, 1], pos_v, w_cmp_v)
    ns = Lp // SEL_BLOCK
    ks_blk = full[:, :, 2].reshape(Bd, ns, SEL_BLOCK, N_KV, d).transpose(0, 3, 1, 2, 4)
    vs_blk = full[:, :, 3].reshape(Bd, ns, SEL_BLOCK, N_KV, d).transpose(0, 3, 1, 2, 4)
    wb = cache_win_l.shape[1]
    wall = jnp.concatenate([cache_win_l, win_new.astype(cache_win_l.dtype)], axis=1)
    wpos = past - wb + jnp.arange(wb + S)
    qpos = past + jnp.arange(S)
    o = nsa_query_block(q, gates, qpos, kc, vc, ks_blk, vs_blk, wall[:, :, 0], wall[:, :, 1], wpos)
    return o, wall[:, S:]


def retention_chunk(q, k, v, state, log_gamma):
    C = q.shape[1]
    qf, kf, vf = q.astype(jnp.float32), k.astype(jnp.float32), v.astype(jnp.float32)
    i = jnp.arange(C, dtype=jnp.float32)
    diff = i[:, None] - i[None, :]
    decay = jnp.where(diff >= 0, jnp.exp(log_gamma[:, None, None] * jnp.maximum(diff, 0.0)), 0.0)
    inner = jnp.einsum('bihd,bjhd->bhij', qf, kf) * decay
    o = jnp.einsum('bhij,bjhe->bihe', inner, vf)
    q_dec = jnp.exp((i + 1.0)[:, None] * log_gamma[None, :])[None, :, :, None]
    o = o + jnp.einsum('bihd,bhde->bihe', qf * q_dec, state)
    k_dec = jnp.exp((C - 1.0 - i)[:, None] * log_gamma[None, :])[None, :, :, None]
    new_state = jnp.exp(C * log_gamma)[None, :, None, None] * state + jnp.einsum('bjhd,bjhe->bhde', kf * k_dec, vf)
    return o, new_state


def retention_prompt(q, k, v, log_gamma):
    B, T, H, dk = q.shape
    n = T // RET_CHUNK

    def split(a):
        return a.reshape(B, n, RET_CHUNK, H, a.shape[-1]).swapaxes(0, 1)

    def step(st, xs):
        qc, kc, vc = xs
        o, st = retention_chunk(qc, kc, vc, st, log_gamma)
        return st, o

    s0 = jnp.zeros((B, H, dk, RET_DV), jnp.float32)
    st, o = lax.scan(step, s0, (split(q), split(k), split(v)))
    return o.swapaxes(0, 1).reshape(B, T, H, RET_DV), st


def layer_tail(x, o_nsa, o_ret, rg, ret_norm_g, w_o, ln1_g, ln1_b, w_up, w_down, ln2_g, ln2_b):
    B, T = x.shape[:2]
    mu = o_ret.mean(-1, keepdims=True)
    var = jnp.square(o_ret - mu).mean(-1, keepdims=True)
    gn = ((o_ret - mu) * lax.rsqrt(var + LN_EPS)).reshape(B, T, RET_WIDTH) * ret_norm_g
    ret = jax.nn.silu(rg.astype(jnp.float32)) * gn
    mix = jnp.concatenate([o_nsa.astype(x.dtype), ret.astype(x.dtype)], axis=-1) @ w_o
    x1 = layer_norm(ALPHA * x + mix, ln1_g, ln1_b)
    f = jnp.square(jax.nn.relu(x1 @ w_up)) @ w_down
    return layer_norm(ALPHA * x1 + f, ln2_g, ln2_b)


def setup_inputs(seed: int = 0) -> dict:
    key = jax.random.key(seed)
    ks = jax.random.split(key, 24)
    n_pages = PAST_LEN // PAGE_SIZE
    n_used = DEC_BATCH * n_pages
    n_phys = n_used + (n_used + 3) // 4
    perm = jax.random.permutation(ks[0], n_phys)
    page_table = perm[:n_used].reshape(DEC_BATCH, n_pages).astype(jnp.int32)
    wb = min(WINDOW, PAST_LEN)
    nrm = jax.random.normal
    f32 = jnp.float32
    return {
        'x_prompt': nrm(ks[1], (BATCH, SEQ, D_MODEL), f32),
        'x_sample': nrm(ks[2], (DEC_BATCH, DEC_SEQ, D_MODEL), f32),
        'cache_kv': nrm(ks[3], (DEPTH, n_phys, PAGE_SIZE, 4, N_KV, HEAD_DIM), f32),
        'cache_win': nrm(ks[4], (DEPTH, DEC_BATCH, wb, 2, N_KV, HEAD_DIM), f32),
        'state_ret': 0.5 * nrm(ks[5], (DEPTH, DEC_BATCH, N_RET_HEADS, RET_DK, RET_DV), f32),
        'page_table': page_table,
        'w_in': nrm(ks[6], (DEPTH, D_MODEL, N_IN), f32) * D_MODEL ** -0.5,
        'w_cmp_k': nrm(ks[7], (DEPTH, CMP_BLOCK, HEAD_DIM, HEAD_DIM), f32) * (CMP_BLOCK * HEAD_DIM) ** -0.5,
        'w_cmp_v': nrm(ks[8], (DEPTH, CMP_BLOCK, HEAD_DIM, HEAD_DIM), f32) * (CMP_BLOCK * HEAD_DIM) ** -0.5,
        'pos_cmp_k': 0.5 * nrm(ks[9], (DEPTH, CMP_BLOCK, HEAD_DIM), f32),
        'pos_cmp_v': 0.5 * nrm(ks[10], (DEPTH, CMP_BLOCK, HEAD_DIM), f32),
        'ret_norm_g': 1.0 + 0.1 * nrm(ks[11], (DEPTH, RET_WIDTH), f32),
        'w_o': nrm(ks[12], (DEPTH, MIX_WIDTH, D_MODEL), f32) * MIX_WIDTH ** -0.5 * BETA,
        'ln1_g': 1.0 + 0.1 * nrm(ks[13], (DEPTH, D_MODEL), f32),
        'ln1_b': 0.02 * nrm(ks[14], (DEPTH, D_MODEL), f32),
        'w_up': nrm(ks[15], (DEPTH, D_MODEL, D_FF), f32) * D_MODEL ** -0.5,
        'w_down': nrm(ks[16], (DEPTH, D_FF, D_MODEL), f32) * D_FF ** -0.5 * BETA,
        'ln2_g': 1.0 + 0.1 * nrm(ks[17], (DEPTH, D_MODEL), f32),
        'ln2_b': 0.02 * nrm(ks[18], (DEPTH, D_MODEL), f32),
    }


def reference(x_prompt, x_sample, cache_kv, cache_win, state_ret, page_table, w_in, w_cmp_k, w_cmp_v, pos_cmp_k, pos_cmp_v, ret_norm_g, w_o, ln1_g, ln1_b, w_up, w_down, ln2_g, ln2_b):
    T = x_prompt.shape[1]
    S = x_sample.shape[1]
    past = page_table.shape[1] * PAGE_SIZE
    pos_p = jnp.arange(T, dtype=jnp.int32)
    pos_s = past + jnp.arange(S, dtype=jnp.int32)
    log_gamma = jnp.log1p(-jnp.exp2(-5.0 - jnp.arange(N_RET_HEADS, dtype=jnp.float32)))
    hp, hs = x_prompt, x_sample
    kv_p, kv_s, win_p, win_s, ret_p, ret_s = [], [], [], [], [], []
    for l in range(DEPTH):
        q, g, kv4, win, rq, rk, rv, rg = project(hp, pos_p, w_in[l])
        o_nsa = nsa_prompt(q, g, kv4, win, w_cmp_k[l], w_cmp_v[l], pos_cmp_k[l], pos_cmp_v[l])
        o_ret, st = retention_prompt(rq, rk, rv, log_gamma)
        kv_p.append(kv4)
        win_p.append(win[:, T - min(WINDOW, T):])
        ret_p.append(st)
        hp = layer_tail(hp, o_nsa, o_ret, rg, ret_norm_g[l], w_o[l], ln1_g[l], ln1_b[l], w_up[l], w_down[l], ln2_g[l], ln2_b[l])
        q, g, kv4, win, rq, rk, rv, rg = project(hs, pos_s, w_in[l])
        o_nsa, new_win = nsa_sample(q, g, kv4, win, cache_kv[l], page_table, cache_win[l], w_cmp_k[l], w_cmp_v[l], pos_cmp_k[l], pos_cmp_v[l])
        o_ret, st = retention_chunk(rq, rk, rv, state_ret[l].astype(jnp.float32), log_gamma)
        kv_s.append(kv4)
        win_s.append(new_win)
        ret_s.append(st)
        hs = layer_tail(hs, o_nsa, o_ret.reshape(hs.shape[0], S, N_RET_HEADS, RET_DV), rg, ret_norm_g[l], w_o[l], ln1_g[l], ln1_b[l], w_up[l], w_down[l], ln2_g[l], ln2_b[l])
    y_prompt = hp
    y_sample = hs
    kv_prompt = jnp.stack(kv_p)
    kv_sample = jnp.stack(kv_s)
    win_prompt = jnp.stack(win_p)
    win_sample = jnp.stack(win_s)
    ret_prompt = jnp.stack(ret_p)
    ret_sample = jnp.stack(ret_s)
    return (y_prompt, y_sample, kv_prompt, kv_sample, win_prompt, win_sample, ret_prompt, ret_sample)
```

```python
import contextlib
import os
import numpy as np
import concourse.bass as bass
import concourse.mybir as mybir
from concourse.bass_utils import run_bass_kernel_spmd

F32 = mybir.dt.float32
BF16 = mybir.dt.bfloat16
I32 = mybir.dt.int32
U32 = mybir.dt.uint32
AF = mybir.ActivationFunctionType
ALU = mybir.AluOpType
AX = mybir.AxisListType

D = 1024
SEQ = 8192
NT = 16
ALPHA = 2.0 ** 0.25
BETA = 8.0 ** -0.25
LN_EPS = 1e-5
NEGB = -30000.0
PAST = 8192
DO_SAMPLE = True


class _Stop(Exception):
    pass


_CUR = [None]
_CUR_S = [-1]


def _stop(n):
    if int(os.environ.get('K_STOP', 999)) == n and int(os.environ.get('K_STOP_S', _CUR_S[0])) == _CUR_S[0]:
        _CUR[0].finish('sp')
        raise _Stop()


class Sched:
    EPOCH = 30000
    NDMA = 16

    def __init__(self, nc, stack):
        self.nc = nc
        self.stack = stack
        self.eng = {'pe': nc.tensor, 'act': nc.scalar, 'dve': nc.vector,
                    'pool': nc.gpsimd, 'sp': nc.sync}
        self.cur_sem = {}
        self.cnt = {}
        self.nsem = 0
        for e in ('pe', 'act', 'dve', 'pool'):
            self._new_epoch(e)
        self.dma_sems = [self._alloc_sem('dma%d' % i) for i in range(2 * self.NDMA)]
        self.dma_cnt = [0] * (2 * self.NDMA)
        self.dma_rr = {'sp': 0, 'pool': 0, 'act': 0}
        self.known = {e: {} for e in self.eng}
        self.last_w = {}
        self.readers = {}
        self.ninstr = 0

    def _alloc_sem(self, name):
        self.nsem += 1
        return self.stack.enter_context(self.nc.semaphore('%s_%d' % (name, self.nsem)))

    def _new_epoch(self, e):
        self.cur_sem[e] = self._alloc_sem('e_' + e)
        self.cnt[e] = 0

    def _wait(self, e, tok):
        if tok is None:
            return
        sem, val, src = tok
        if src == e and e == 'pe':
            return
        k = self.known[e]
        if k.get(id(sem), 0) >= val:
            return
        self.eng[e].wait_ge(sem, val)
        k[id(sem)] = val

    def _deps(self, e, reads, writes):
        for b in reads:
            self._wait(e, self.last_w.get(b))
            if b[0] == 'P':
                for t in self.readers.get(b, ()):
                    if t[2] != e:
                        self._wait(e, t)
        for b in writes:
            self._wait(e, self.last_w.get(b))
            for t in self.readers.get(b, ()):
                self._wait(e, t)

    def _commit(self, tok, reads, writes):
        for b in reads:
            self.readers.setdefault(b, []).append(tok)
        for b in writes:
            self.last_w[b] = tok
            self.readers[b] = []

    def op(self, e, fn, reads=(), writes=()):
        self._deps(e, reads, writes)
        if self.cnt[e] >= self.EPOCH:
            self._new_epoch(e)
        ins = fn(self.eng[e])
        self.cnt[e] += 1
        sem = self.cur_sem[e]
        ins.then_inc(sem, 1)
        tok = (sem, self.cnt[e], e)
        self._commit(tok, reads, writes)
        self.ninstr += 1
        return tok

    def dma(self, q, fn, reads=(), writes=()):
        self._deps(q, reads, writes)
        i = self.dma_rr[q] + (self.NDMA if q == 'pool' else 0)
        self.dma_rr[q] = (self.dma_rr[q] + 1) % self.NDMA
        sem = self.dma_sems[i]
        if self.dma_cnt[i] > 0:
            self._wait(q, (sem, 16 * self.dma_cnt[i], 'dma'))
        ins = fn(self.eng[q])
        self.dma_cnt[i] += 1
        ins.then_inc(sem, 16)
        tok = (sem, 16 * self.dma_cnt[i], 'dma')
        self._commit(tok, reads, writes)
        self.ninstr += 1
        return tok

    def barrier(self):
        for e in ('sp', 'pool', 'act', 'dve', 'pe'):
            self.finish(e)

    def finish(self, e='sp'):
        for i, sem in enumerate(self.dma_sems):
            if self.dma_cnt[i]:
                self._wait(e, (sem, 16 * self.dma_cnt[i], 'dma'))
        for x in ('pe', 'act', 'dve', 'pool'):
            if self.cnt[x]:
                self._wait(e, (self.cur_sem[x], self.cnt[x], x))


C_KVA, C_RK, C_RV, C_WIN, C_Q, C_GT, C_RQ, C_RG = 0, 512, 1024, 1536, 1792, 2304, 2328, 2840


def build_program():
    nc = bass.Bass("TRN2", target_bir_lowering=False)

    def din(name, shape, dt=F32):
        return nc.dram_tensor(name, list(shape), dt, kind="ExternalInput").ap()

    def dout(name, shape, dt=F32):
        return nc.dram_tensor(name, list(shape), dt, kind="ExternalOutput").ap()

    xT_d = din("xT", [8, 128, SEQ])
    xown_d = din("xown", [16, 128, D])
    tabs_d = din("tabs", [64, 128, 80])
    dec_d = din("dec", [128, 16])
    gC_d = din("gC", [128, 256])
    win_d = din("w_in", [8, 128, 3352])
    wo_d = din("w_o", [8, 128, D])
    wup_d = din("w_up", [8, 128, 4096])
    wdn_d = din("w_down", [32, 128, D])
    wck_d = din("wck", [64, 32, 64])
    wcv_d = din("wcv", [64, 32, 64])
    posk_d = din("posk", [64, 32])
    posv_d = din("posv", [64, 32])
    retg_d = din("retg", [128, 512])
    ln_d = din("lnp", [4, 128, D])
    kbias_d = din("kbias", [128, 64])
    kbc_d = din("kbias_c", [128, 4])
    bonus_d = din("bonus", [16, 128, 128])
    masks_d = din("masks", [128, 2304])
    tri_d = din("tri", [128, 128])
    E_d = din("Eoh", [64, 4096])
    mimp_d = din("mimp", [128, 4, 128])

    y_d = dout("y_own", [16, 128, D])
    kv_d = dout("kv_own", [16, 128, 512])
    wout_d = dout("win_out", [4, 128, 256])
    ret_d = dout("ret_out", [128, 256])
    x1s_d = nc.dram_tensor("x1_scratch", [17, 128, D], F32, kind="Internal").ap()

    oh_d = din("ohs", [128, 4]); kbself_d = din("kbself", [128, 4]); kbws_d = din("kbws", [128, 4]); kbcs_d = din("kbcs", [128, 4])
    bons_d = din("bons", [1, 128]); decs_d = din("decs", [128, 16]); gC1_d = din("gC1", [128, 256])
    pt_d = din("pt_rep", [128, 256], I32)
    xsT_d = din("xsT", [8, 128, 128]); tabs_s_d = din("tabs_s", [128, 80]); xs_own_d = din("xs_own", [128, D])
    ckv_d = din("cache_kv", [2560 * 128, 512]); cwin_d = din("cache_win", [4, 512, 256]); stin_d = din("state_in", [4, 8, 64, 64])
    kvs_d = dout("kv_s", [128, 512]); wins_d = dout("win_s", [4, 512, 256]); rets_d = dout("ret_s", [4, 128, 256]); ys_d = dout("y_s", [128, D])
    ons_d = nc.dram_tensor("ons_scratch", [4, 8, 64], F32, kind="Internal").ap()

    with contextlib.ExitStack() as st:
        S = Sched(nc, st)
        _CUR[0] = S
        op, dma = S.op, S.dma

        def T(name, shape, dt=BF16):
            return st.enter_context(nc.sbuf_tensor("s_" + name, list(shape), dt))

        P = [st.enter_context(nc.psum_tensor("P%d" % i, [128, 512], F32)) for i in range(6)]
        PTb = [st.enter_context(nc.psum_tensor("PTr%d" % i, [128, 8, 128], BF16)) for i in range(2)]
        tr_rr = [0]

        def mm(out, lhsT, rhs, start, stop, reads, writes, **kw):
            return op('pe', lambda e: e.matmul(out, lhsT=lhsT, rhs=rhs, start=start, stop=stop, **kw),
                      reads=reads, writes=writes)

        def bc(ap, shape):
            return ap.to_broadcast(list(shape))

        ident = T("ident", [128, 128])
        op('pool', lambda e: e.memset(ident[:], 1.0), writes=['ident'])
        op('pool', lambda e: e.affine_select(out=ident[:], in_=ident[:], pattern=[[-1, 128]],
                                             compare_op=ALU.is_equal, fill=0.0, base=0,
                                             channel_multiplier=1), reads=['ident'], writes=['ident'])
        identf = T("identf", [128, 128], F32)
        op('act', lambda e: e.copy(out=identf[:], in_=ident[:]), reads=['ident'], writes=['identf'])
        eps_t = T("eps_t", [128, 1], F32)
        op('pool', lambda e: e.memset(eps_t[:], LN_EPS), writes=['eps'])

        def transpose_to(dst, src, rows, cols, reads, writes, evac='act'):
            slot = tr_rr[0]
            tr_rr[0] = (slot + 1) % 2
            pst = PTb[slot][0:cols, 0, 0:rows]
            nm = 'PTr%d' % slot
            op('pe', lambda e: e.transpose(pst, src, ident[0:rows, 0:rows]),
               reads=list(reads) + ['ident'], writes=[nm])
            if evac == 'act':
                op('act', lambda e: e.copy(out=dst, in_=pst), reads=[nm], writes=writes)
            else:
                op('dve', lambda e: e.tensor_copy(out=dst, in_=pst), reads=[nm], writes=writes)

        def tbatch(srcs, reads):
            bank = tr_rr[0]
            tr_rr[0] = (bank + 1) % 2
            nm = 'PTr%d' % bank
            for i, src in enumerate(srcs):
                op('pe', lambda e, i=i, src=src: e.transpose(PTb[bank][0:64, i, :], src, ident[:]),
                   reads=list(reads) + ['ident'], writes=[nm])
            return PTb[bank], nm

        lnst = T("lnst", [128, 2, 6], F32)
        lnmv = T("lnmv", [128, 2], F32)
        lnrs = T("lnrs", [128, 1], F32)

        def layer_norm(dst, src, gtab, btab, sname, dname, tname):
            for c in range(2):
                op('dve', lambda e, c=c: e.bn_stats(out=lnst[:, c, :], in_=src[:, c * 512:(c + 1) * 512]),
                   reads=sname, writes=['lnst'])
            op('dve', lambda e: e.bn_aggr(out=lnmv[:], in_=lnst[:]), reads=['lnst'], writes=['lnmv'])
            op('act', lambda e: e.activation(out=lnrs[:], in_=lnmv[:, 1:2], func=AF.Sqrt, bias=eps_t[:, 0:1], scale=1.0),
               reads=['lnmv', 'eps'], writes=['lnrs'])
            op('dve', lambda e: e.reciprocal(out=lnrs[:], in_=lnrs[:]), reads=['lnrs'], writes=['lnrs'])
            op('dve', lambda e: e.tensor_scalar(out=dst, in0=src, scalar1=lnmv[:, 0:1], scalar2=lnrs[:, 0:1],
                                                op0=ALU.subtract, op1=ALU.mult),
               reads=sname + ['lnmv', 'lnrs'], writes=dname)
            op('pool', lambda e: e.tensor_tensor(out=dst, in0=dst, in1=gtab, op=ALU.mult), reads=dname + [tname], writes=dname)
            op('pool', lambda e: e.tensor_tensor(out=dst, in0=dst, in1=btab, op=ALU.add), reads=dname + [tname], writes=dname)

        _stop(1)
        with contextlib.ExitStack() as stA:
            def TA(name, shape, dt=BF16):
                return stA.enter_context(nc.sbuf_tensor("s_" + name, list(shape), dt))

            kbias = TA("kbias", [128, 64], F32)
            dma('sp', lambda e: e.dma_start(out=kbias[:], in_=kbias_d), writes=['kbias'])
            kbc = TA("kbc", [128, 4], F32)
            dma('sp', lambda e: e.dma_start(out=kbc[:], in_=kbc_d), writes=['kbc'])
            dec = TA("dec", [128, 16], F32)
            dma('sp', lambda e: e.dma_start(out=dec[:], in_=dec_d), writes=['dec'])
            gC = TA("gC", [128, 256], F32)
            dma('sp', lambda e: e.dma_start(out=gC[:], in_=gC_d), writes=['gC'])
            retg = TA("retg", [128, 512], F32)
            dma('sp', lambda e: e.dma_start(out=retg[:], in_=retg_d), writes=['retg'])

            def causal(o):
                return masks[:, 384 - 128 * o:896 - 128 * o]

            def wlo(o):
                return masks[:, 896 + 384 - 128 * o:896 + 896 - 128 * o]

            _stop(2)
            Wkv = TA("Wkv", [128, 8, 1792])
            for kc in range(8):
                dma('pool', lambda e, kc=kc: e.dma_start(out=Wkv[:, kc, :], in_=win_d[kc, :, 0:1792],
                                                         max_dma_last_dim=4096), writes=['Wkv'])
            Wt = [TA("WtA", [128, 8, 536]), TA("WtB", [128, 8, 536])]

            def load_wt(i, src_fn, ncols):
                for kc in range(8):
                    dma('pool', lambda e, kc=kc: e.dma_start(out=Wt[i][:, kc, 0:ncols], in_=src_fn(kc),
                                                             max_dma_last_dim=4096), writes=['Wt%d' % i])

            wck = TA("wck", [64, 32, 64]); wcv = TA("wcv", [64, 32, 64])
            dma('pool', lambda e: e.dma_start(out=wck[:], in_=wck_d, max_dma_last_dim=4096), writes=['wck'])
            dma('pool', lambda e: e.dma_start(out=wcv[:], in_=wcv_d, max_dma_last_dim=4096), writes=['wcv'])
            posk = TA("posk", [64, 32]); posv = TA("posv", [64, 32])
            dma('pool', lambda e: e.dma_start(out=posk[:], in_=posk_d), writes=['posk'])
            dma('pool', lambda e: e.dma_start(out=posv[:], in_=posv_d), writes=['posv'])
            ln1 = TA("ln1", [128, 2, D], F32)
            dma('sp', lambda e: e.dma_start(out=ln1[:, 0, :], in_=ln_d[0]), writes=['ln1'])
            dma('sp', lambda e: e.dma_start(out=ln1[:, 1, :], in_=ln_d[1]), writes=['ln1'])

            KslT = TA("KslT", [128, 2, SEQ])
            for g in range(2):
                for r in range(2):
                    dma('pool', lambda e, g=g, r=r: e.dma_start(out=KslT[64:128, g, r * 4096:(r + 1) * 4096],
                                                              in_=E_d, max_dma_last_dim=4096), writes=['KslT_E'])
            Vsl = TA("Vsl", [128, 64, 2, 65])
            op('pool', lambda e: e.memset(Vsl[:, :, :, 64:65], 1.0), writes=['Vsl_ones'])
            KwT = TA("KwT", [128, 2, 1024])
            op('pool', lambda e: e.memset(KwT[:], 0.0), writes=['KwT%d' % i for i in range(8)])
            Vw = TA("Vw", [128, 8, 2, 65])
            op('pool', lambda e: e.memset(Vw[:, :, :, 64:65], 1.0), writes=['Vw_ones'])
            KcT = TA("KcT", [64, 2, 528]); VcT = TA("VcT", [64, 2, 528])
            op('pool', lambda e: e.memset(KcT[:], 0.0), writes=['KcT'])
            op('pool', lambda e: e.memset(VcT[:], 0.0), writes=['VcT'])
            kcT = TA("kcT", [128, 2, 512])
            op('pool', lambda e: e.memset(kcT[:], 0.0), writes=['kcT'])
            Rc = TA("Rc", [128, 4, 2, 193])
            op('pool', lambda e: e.memset(Rc[:], 0.0), writes=['Rc'])
            for g in range(2):
                dma('pool', lambda e, g=g: e.dma_start(out=Rc[:, :, g, 0:128], in_=mimp_d), writes=['Rc'])
            op('pool', lambda e: e.memset(Rc[:, :, :, 192:193], 1.0), reads=['Rc'], writes=['Rc'])
            pbk = TA("pbk", [64, 1], F32)
            pbv = TA("pbv", [1, 64])
            ones1 = TA("ones1", [1, 128])
            op('pool', lambda e: e.memset(ones1[:], 1.0), writes=['ones1'])
            for r in range(32):
                mm(P[5][0:64, 0:1], wck[:, r, :], posk[:, r:r + 1], r == 0, r == 31, ['wck', 'posk'], ['P5'])
            op('act', lambda e: e.copy(out=pbk[:], in_=P[5][0:64, 0:1]), reads=['P5'], writes=['pbk'])
            for r in range(32):
                mm(P[5][0:1, 64:128], posv[:, r:r + 1], wcv[:, r, :], r == 0, r == 31, ['wcv', 'posv', 'pbk'], ['P5'])
            op('act', lambda e: e.copy(out=pbv[:], in_=P[5][0:1, 64:128]), reads=['P5'], writes=['pbv'])

            _stop(3)
            tabs = TA("tabs", [128, 4, 80], F32)
            big0 = TA("big0", [128, 1024], F32)
            stage = big0[:, 0:512]
            kvb = TA("kvb", [128, 512])
            wstage = TA("wstage", [128, 256], F32)
            wb = TA("wb", [128, 256])
            rtmp = big0[:, 512:1024]
            ta = TA("ta", [128, 256], F32)
            tb = TA("tb", [128, 256], F32)
            ktil = TA("ktil", [128, 512])
            rvb = TA("rvb", [128, 512])
            Sst = TA("Sst", [128, 256], F32)
            op('pool', lambda e: e.memset(Sst[:], 0.0), writes=['Sst'])
            SbZ = [TA("SbZ%d" % i, [128, 256]) for i in range(2)]
            for i in range(2):
                op('pool', lambda e, i=i: e.memset(SbZ[i][:], 0.0), writes=['Sb'])
            stmp = TA("stmp", [128, 256], F32)
            gat = TA("gat", [128, 4, 24], F32)
            qtil = TA("qtil", [128, 512])
            qtilT = TA("qtilT", [128, 4, 128])
            kz = [TA("kz%d" % i, [128, 4, 128]) for i in range(2)]
            for i in range(2):
                op('pool', lambda e, i=i: e.memset(kz[i][:], 0.0), writes=['ktilT'])
            rgs = TA("rgs", [128, 512])
            pt_rr = [0]
            onsa = TA("onsa", [128, 4, 256], F32)
            onsab = TA("onsab", [128, 256])
            innb = TA("innb", [128, 8, 128])
            big1 = TA("big1", [128, 1024], F32)
            orf = big1[:, 0:512]
            osq = big1[:, 512:1024]
            gsm = TA("gsm", [128, 8], F32); gss = TA("gss", [128, 8], F32)
            gmu = TA("gmu", [128, 8], F32); grs = TA("grs", [128, 8], F32); gm2 = TA("gm2", [128, 8], F32)
            mixret = TA("mixret", [128, 512])
            rec = TA("rec", [128, 1], F32); scg = TA("scg", [128, 1], F32)
            xo = big0
            xr = big1

            def do_rope(dst3, src3, cos2, sin2, H, half, reads, writes):
                cb = bc(cos2.unsqueeze(1), [128, H, half])
                sb = bc(sin2.unsqueeze(1), [128, H, half])
                x1 = src3[:, :, 0:half]; x2 = src3[:, :, half:2 * half]
                A = ta[:, 0:H * half].rearrange("p (h d) -> p h d", h=H)
                B = tb[:, 0:H * half].rearrange("p (h d) -> p h d", h=H)
                rd = list(reads) + ['tabs']
                op('dve', lambda e: e.tensor_tensor(out=A, in0=x1, in1=cb, op=ALU.mult), reads=rd, writes=['ta'])
                _stop(201)
                op('dve', lambda e: e.tensor_tensor(out=B, in0=x2, in1=sb, op=ALU.mult), reads=rd, writes=['tb'])
                _stop(202)
                op('dve', lambda e: e.tensor_tensor(out=dst3[:, :, 0:half], in0=A, in1=B, op=ALU.subtract),
                   reads=['ta', 'tb'], writes=writes)
                _stop(203)
                op('dve', lambda e: e.tensor_tensor(out=A, in0=x1, in1=sb, op=ALU.mult), reads=rd, writes=['ta'])
                op('dve', lambda e: e.tensor_tensor(out=B, in0=x2, in1=cb, op=ALU.mult), reads=rd, writes=['tb'])
                op('dve', lambda e: e.tensor_tensor(out=dst3[:, :, half:2 * half], in0=A, in1=B, op=ALU.add),
                   reads=['ta', 'tb'], writes=writes)

            def rope_ip(buf3, cos2, sin2, H, half, bname):
                cb = bc(cos2.unsqueeze(1), [128, H, half])
                sb = bc(sin2.unsqueeze(1), [128, H, half])
                x1 = buf3[:, :, 0:half]; x2 = buf3[:, :, half:2 * half]
                A = ta[:, 0:H * half].rearrange("p (h d) -> p h d", h=H)
                B = tb[:, 0:H * half].rearrange("p (h d) -> p h d", h=H)
                C = stmp[:, 0:H * half].rearrange("p (h d) -> p h d", h=H)
                Dd = osq[:, 0:H * half].rearrange("p (h d) -> p h d", h=H)
                rd = [bname, 'tabs']
                op('dve', lambda e: e.tensor_tensor(out=A, in0=x1, in1=cb, op=ALU.mult), reads=rd, writes=['ta'])
                op('dve', lambda e: e.tensor_tensor(out=B, in0=x2, in1=sb, op=ALU.mult), reads=rd, writes=['tb'])
                op('dve', lambda e: e.tensor_tensor(out=C, in0=x1, in1=sb, op=ALU.mult), reads=rd, writes=['stmp'])
                op('dve', lambda e: e.tensor_tensor(out=Dd, in0=x2, in1=cb, op=ALU.mult), reads=rd, writes=['osq'])
                op('dve', lambda e: e.tensor_tensor(out=x1, in0=A, in1=B, op=ALU.subtract), reads=['ta', 'tb', bname], writes=[bname])
                op('dve', lambda e: e.tensor_tensor(out=x2, in0=C, in1=Dd, op=ALU.add), reads=['stmp', 'osq', bname], writes=[bname])

            xsel = [None]

            def projn(pi, u, W, c0, c1, wname):
                xt = xsel[0] if xsel[0] is not None else xTb
                for kc in range(8):
                    mm(P[pi][:, 0:c1 - c0], xt[:, kc, u * 128:(u + 1) * 128], W[:, kc, c0:c1],
                       kc == 0, kc == 7, [('xTb' if xsel[0] is not None else 'xTb%d' % u), wname], ['P%d' % pi])

            def exp_pt(psi, c0, c1, bias_ap, breads):
                i = pt_rr[0]; pt_rr[0] = (i + 1) % 3
                pt = PT3[i]
                op('act', lambda e: e.activation(out=pt[:, c0:c1], in_=P[psi][:, c0:c1], func=AF.Exp, bias=bias_ap, scale=0.125),
                   reads=['P%d' % psi] + breads, writes=[('ptile%d' % i) if i < 2 else 'mixret'])
                return pt, (('ptile%d' % i) if i < 2 else 'mixret')

            def ot_finish(pacc, h4, h, br):
                op('act', lambda e: e.copy(out=orf[0:65, :], in_=P[pacc][0:65, :]), reads=['P%d' % pacc], writes=['orf'])
                for u in range(4):
                    op('pe', lambda e, u=u: e.transpose(P[5][:, u * 66:u * 66 + 65], orf[0:65, u * 128:(u + 1) * 128], identf[0:65, 0:65]),
                       reads=['orf', 'identf'], writes=['P5'])
                finish_branch(lambda u: P[5][:, u * 66:u * 66 + 65], lambda u: 'P5', h4, h, br, False, 64)

            def pipeline(n_items, stage1, stage2, depth=2):
                q = []
                for i in range(n_items):
                    q.append(stage1(i))
                    if len(q) > depth:
                        stage2(*q.pop(0))
                while q:
                    stage2(*q.pop(0))

            def finish_branch(pacc, pn, h4, h, br, first, ow):
                for u in range(4):
                    pa = pacc(u)
                    op('dve', lambda e, pa=pa: e.tensor_scalar(out=rec[:], in0=pa[:, ow:ow + 1], scalar1=1e-30, scalar2=None, op0=ALU.max),
                       reads=[pn(u)], writes=['rec'])
                    op('dve', lambda e: e.reciprocal(out=rec[:], in_=rec[:]), reads=['rec'], writes=['rec'])
                    op('dve', lambda e, u=u: e.tensor_tensor(out=scg[:], in0=rec[:], in1=gat[:, u, h * 3 + br:h * 3 + br + 1], op=ALU.mult),
                       reads=['rec', 'gat'], writes=['scg'])
                    dst = onsa[:, u, h4 * 64:(h4 + 1) * 64]
                    if first:
                        op('dve', lambda e, pa=pa, dst=dst: e.tensor_scalar(out=dst, in0=pa[:, ow - 64:ow], scalar1=scg[:, 0:1],
                                                                             scalar2=None, op0=ALU.mult),
                           reads=[pn(u), 'scg'], writes=['onsa'])
                    else:
                        op('dve', lambda e, pa=pa, dst=dst: e.scalar_tensor_tensor(out=dst, in0=pa[:, ow - 64:ow], scalar=scg[:, 0:1],
                                                                                    in1=dst, op0=ALU.mult, op1=ALU.add),
                           reads=[pn(u), 'scg', 'onsa'], writes=['onsa'])
                    if br == 0:
                        dsti = impacc[:, u, :]
                        if h4 == 0:
                            op('dve', lambda e, pa=pa, dsti=dsti: e.tensor_scalar(out=dsti, in0=pa[:, 0:128], scalar1=rec[:, 0:1],
                                                                                   scalar2=None, op0=ALU.mult),
                               reads=[pn(u), 'rec'], writes=['impacc'])
                        else:
                            op('dve', lambda e, pa=pa, dsti=dsti: e.scalar_tensor_tensor(out=dsti, in0=pa[:, 0:128], scalar=rec[:, 0:1],
                                                                                          in1=dsti, op0=ALU.mult, op1=ALU.add),
                               reads=[pn(u), 'rec', 'impacc'], writes=['impacc'])

            def compress_tile(s):
                    ctp, row0 = s // 4, 32 * (s % 4)
                    tp96 = {'tile_position': (0, 96)} if row0 == 96 else {}
                    for r in range(32):
                        mm(P[4][0:64, 0:64].rearrange("p (g n) -> p g n", g=2), wck[:, r, :], KcT[:, :, r:r + 497:16], r == 0, r == 31,
                           ['wck', 'KcT'], ['P4'])
                    op('act', lambda e: e.activation(out=kcT[0:64, :, 32 * s:32 * s + 32],
                                                     in_=P[4][0:64, 0:64].rearrange("p (g n) -> p g n", g=2),
                                                     func=AF.Identity, bias=pbk[:, 0:1], scale=1.0),
                       reads=['P4', 'pbk'], writes=['kcT'])
                    for g in range(2):
                        for r in range(33):
                            if r < 32:
                                mm(P[5][row0:row0 + 32, g * 64:(g + 1) * 64], VcT[:, g, r:r + 497:16], wcv[:, r, :],
                                   r == 0, False, ['wcv', 'VcT'], ['P5'], **tp96)
                            else:
                                mm(P[5][row0:row0 + 32, g * 64:(g + 1) * 64], ones1[0:1, 0:32], pbv[0:1, :],
                                   False, True, ['ones1', 'pbv'], ['P5'], **tp96)
                        op('act', lambda e, g=g: e.copy(out=Rc[row0:row0 + 32, ctp, g, 128:192],
                                                         in_=P[5][row0:row0 + 32, g * 64:(g + 1) * 64]),
                           reads=['P5'], writes=['Rc'])
                    op('pool', lambda e: e.tensor_copy(out=KcT[:, :, 0:16], in_=KcT[:, :, 512:528]), reads=['KcT'], writes=['KcT'])
                    op('pool', lambda e: e.tensor_copy(out=VcT[:, :, 0:16], in_=VcT[:, :, 512:528]), reads=['VcT'], writes=['VcT'])


            stP = contextlib.ExitStack()

            def TP(name, shape, dt=BF16):
                return stP.enter_context(nc.sbuf_tensor("s_" + name, list(shape), dt))
            masks = TP("masks", [128, 2304])
            dma('pool', lambda e: e.dma_start(out=masks[:], in_=masks_d, max_dma_last_dim=4096), writes=['masks'])
            tri = TP("tri", [128, 128], F32)
            dma('sp', lambda e: e.dma_start(out=tri[:], in_=tri_d), writes=['tri'])
            Qa = TP("Qa", [128, 2, 4, 512])
            op('pool', lambda e: e.memset(Qa[:], 0.0), writes=['Qa0', 'Qa1'])
            selT = TP("selT", [128, 2, 512])
            PTt = [TP("PT%d" % i, [128, 512]) for i in range(2)]
            impacc = TP("impacc", [128, 4, 128], F32)
            bon = TP("bon", [128, 128], F32)
            score = TP("score", [128, 128], F32)
            sc2 = TP("sc2", [128, 128], F32)
            m8a = TP("m8a", [128, 8], F32); m8b = TP("m8b", [128, 8], F32)
            selb = TP("selb", [128, 256])
            qb = TP("qb", [128, 4, 512])
            onrm = TP("onrm", [128, 4, 512])
            xTb = TP("xTb", [128, 8, 512])
            mixT = TP("mixT", [128, 8, 512])
            _stop(4)
            load_wt(0, lambda kc: win_d[kc, :, C_RQ:C_RQ + 512], 512)

            PT3 = [PTt[0], PTt[1], mixret]
            SB3 = [0, 1, 4]
            kvbL = [(kvb, 'kvb'), (mixret, 'mixret')]
            ktilL = [(ktil, 'ktil'), (PTt[0], 'ptile0')]
            rvbL = [(rvb, 'rvb'), (rgs, 'rgs')]
            wbL = [(wb, 'wb'), (onsab, 'onsab')]
            for s in range(int(os.environ.get('K_NT', NT))):
                own = (s % 4 == 3)
                _CUR_S[0] = s
                k = s // 4
                wtile = (s % 4 in (2, 3))
                wr = s % 2
                def load_x(sn):
                    for uu in range(4):
                        dma('pool', lambda e, uu=uu: e.dma_start(
                            out=xTb[:, :, uu * 128:(uu + 1) * 128],
                            in_=xT_d[:, :, sn * 512 + uu * 128:sn * 512 + (uu + 1) * 128].rearrange("k p t -> p k t")),
                            writes=['xTb%d' % uu])
                if s == 0 or (s % 4 == 0):
                    load_x(s)
                dma('sp', lambda e: e.dma_start(out=tabs[:], in_=tabs_d[4 * s:4 * s + 4].rearrange("u p c -> p u c")),
                    writes=['tabs'])
                _stop(10)
                def part1(u):
                        kt = 4 * s + u
                        cosN = tabs[:, u, 0:8]; sinN = tabs[:, u, 8:16]
                        cosR = tabs[:, u, 16:48]; sinR = tabs[:, u, 48:80]
                        kvb, kvbn = kvbL[u % 2]
                        ktil, ktiln = ktilL[u % 2]
                        rvb, rvbn = rvbL[u % 2]
                        wb, wbn = wbL[u % 2]
                        projn(0, u, Wkv, C_KVA, C_KVA + 512, 'Wkv')
                        op('act', lambda e: e.copy(out=stage[:], in_=P[0][:]), reads=['P0'], writes=['stage'])
                        rope_ip(stage[:, 256:384].rearrange("p (g d) -> p g d", g=2), cosN, sinN, 2, 8, 'stage')
                        if own:
                            dma('sp', lambda e, u=u: e.dma_start(out=kv_d[4 * k + u], in_=stage[:]), reads=['stage'])
                        op('act', lambda e: e.copy(out=kvb[:], in_=stage[:]), reads=['stage'], writes=[kvbn])
                        projn(1, u, Wkv, C_RK, C_RK + 512, 'Wkv')
                        projn(2, u, Wkv, C_RV, C_RV + 512, 'Wkv')
                        op('act', lambda e: e.copy(out=rtmp[:], in_=P[1][:]), reads=['P1'], writes=['rtmp'])
                        rope_ip(rtmp[:].rearrange("p (h d) -> p h d", h=8), cosR, sinR, 8, 32, 'rtmp')
                        op('dve', lambda e: e.tensor_tensor(out=ktil[:].rearrange("p (h d) -> p h d", h=8),
                                                            in0=rtmp[:].rearrange("p (h d) -> p h d", h=8),
                                                            in1=bc(dec[:, 8:16].unsqueeze(2), [128, 8, 64]), op=ALU.mult),
                           reads=['rtmp', 'dec'], writes=[ktiln])
                        op('act', lambda e: e.copy(out=rvb[:], in_=P[2][:]), reads=['P2'], writes=[rvbn])
                        if wtile:
                            projn(3, u, Wkv, C_WIN, C_WIN + 256, 'Wkv')
                            op('act', lambda e: e.copy(out=wstage[:], in_=P[3][:, 0:256]), reads=['P3'], writes=['wstage'])
                            rope_ip(wstage[:, 0:128].rearrange("p (g d) -> p g d", g=2), cosN, sinN, 2, 8, 'wstage')
                            if s == NT - 1:
                                dma('sp', lambda e, u=u: e.dma_start(out=wout_d[u], in_=wstage[:]), reads=['wstage'])
                            op('pool', lambda e: e.tensor_copy(out=wb[:], in_=wstage[:]), reads=['wstage'], writes=[wbn])

                def part2(u):
                        kt = 4 * s + u
                        cosN = tabs[:, u, 0:8]; sinN = tabs[:, u, 8:16]
                        cosR = tabs[:, u, 16:48]; sinR = tabs[:, u, 48:80]
                        kvb, kvbn = kvbL[u % 2]
                        ktil, ktiln = ktilL[u % 2]
                        rvb, rvbn = rvbL[u % 2]
                        wb, wbn = wbL[u % 2]
                        srcs = [kvb[:, i * 64:(i + 1) * 64] for i in range(6)]
                        rds = [kvbn]
                        if wtile:
                            srcs += [wb[:, 0:64], wb[:, 64:128]]
                            rds.append(wbn)
                        pbk_, pbn_ = tbatch(srcs, rds)
                        op('act', lambda e: e.copy(out=KcT[:, :, 16 + u * 128:16 + (u + 1) * 128], in_=pbk_[0:64, 0:2, :]),
                           reads=[pbn_], writes=['KcT'])
                        op('act', lambda e: e.copy(out=VcT[:, :, 16 + u * 128:16 + (u + 1) * 128], in_=pbk_[0:64, 2:4, :]),
                           reads=[pbn_], writes=['VcT'])
                        op('dve', lambda e: e.tensor_copy(out=KslT[0:64, :, kt * 128:(kt + 1) * 128], in_=pbk_[0:64, 4:6, :]),
                           reads=[pbn_], writes=['KslT%d' % kt])
                        op('pool', lambda e, kt=kt: e.tensor_copy(out=Vsl[:, kt, :, 0:64],
                                                                   in_=kvb[:, 384:512].rearrange("p (g d) -> p g d", g=2)),
                           reads=[kvbn], writes=['Vsl%d' % kt])
                        if wtile:
                            wkt = wr * 4 + u
                            op('act', lambda e: e.copy(out=KwT[0:64, :, wkt * 128:(wkt + 1) * 128], in_=pbk_[0:64, 6:8, :]),
                               reads=[pbn_], writes=['KwT%d' % wkt])
                            op('pool', lambda e, wkt=wkt: e.tensor_copy(out=Vw[:, wkt, :, 0:64],
                                                                         in_=wb[:, 128:256].rearrange("p (g d) -> p g d", g=2)),
                               reads=[wbn], writes=['Vw%d' % wkt])
                        if own:
                            op('act', lambda e: e.copy(out=SbZ[0][0:64, :], in_=Sst[0:64, :]), reads=['Sst'], writes=['Sb'])
                            op('act', lambda e: e.copy(out=SbZ[1][64:128, :], in_=Sst[64:128, :]), reads=['Sst'], writes=['Sb'])
                            projn(3, u, Wt[0], 0, 512, 'Wt0')
                            op('act', lambda e: e.copy(out=rtmp[:], in_=P[3][:]), reads=['P3'], writes=['rtmp'])
                            rope_ip(rtmp[:].rearrange("p (h d) -> p h d", h=8), cosR, sinR, 8, 32, 'rtmp')
                            op('dve', lambda e: e.tensor_tensor(out=qtil[:].rearrange("p (h d) -> p h d", h=8),
                                                                in0=rtmp[:].rearrange("p (h d) -> p h d", h=8),
                                                                in1=bc(dec[:, 0:8].unsqueeze(2), [128, 8, 64]), op=ALU.mult),
                               reads=['rtmp', 'dec'], writes=['qtil'])
                            for hp in range(4):
                                slot = tr_rr[0]; tr_rr[0] = (slot + 1) % 2
                                pst = PTb[slot][:, 0, :]
                                op('pe', lambda e, pst=pst, hp=hp: e.transpose(pst, ktil[:, hp * 128:(hp + 1) * 128], ident[:]),
                                   reads=[ktiln, 'ident'], writes=['PTr%d' % slot])
                                op('act', lambda e, pst=pst, hp=hp: e.copy(out=kz[0][0:64, hp, :], in_=pst[0:64, :]),
                                   reads=['PTr%d' % slot], writes=['ktilT'])
                                op('act', lambda e, pst=pst, hp=hp: e.copy(out=kz[1][64:128, hp, :], in_=pst[64:128, :]),
                                   reads=['PTr%d' % slot], writes=['ktilT'])
                                transpose_to(qtilT[:, hp, :], qtil[:, hp * 128:(hp + 1) * 128], 128, 128, ['qtil'], ['qtilT'], evac='dve')
                            for h in range(8):
                                hp, h2 = h // 2, h % 2
                                bp = 64 * h2
                                pb = 4 + h // 4
                                mm(P[pb][:, (h % 4) * 128:(h % 4 + 1) * 128], kz[h2][:, hp, :], qtilT[:, hp, :],
                                   True, True, ['ktilT', 'qtilT'], ['P%d' % pb])
                            for half in range(2):
                                op('dve', lambda e, half=half: e.tensor_tensor(
                                    out=innb[:, half * 4:(half + 1) * 4, :],
                                    in0=P[4 + half][:].rearrange("p (h i) -> p h i", h=4),
                                    in1=bc(tri[:].unsqueeze(1), [128, 4, 128]), op=ALU.mult),
                                   reads=['P%d' % (4 + half), 'tri'], writes=['innb%d' % half])
                            for h in range(8):
                                hp, h2 = h // 2, h % 2
                                bp = 64 * h2
                                mm(P[3][:, h * 64:(h + 1) * 64], innb[:, h, :], rvb[:, h * 64:(h + 1) * 64], True, False,
                                   ['innb%d' % (h // 4), rvbn, 'rtmp'], ['P3'])
                                mm(P[3][:, h * 64:(h + 1) * 64], qtilT[:, hp, :], SbZ[h2][:, hp * 64:(hp + 1) * 64],
                                   False, True, ['qtilT', 'Sb'], ['P3'])
                            op('act', lambda e: e.copy(out=orf[:], in_=P[3][:]), reads=['P3'], writes=['orf'])
                            orf3 = orf[:].rearrange("p (h d) -> p h d", h=8)
                            osq3 = osq[:].rearrange("p (h d) -> p h d", h=8)
                            op('dve', lambda e: e.tensor_reduce(out=gsm[:], in_=orf3, axis=AX.X, op=ALU.add), reads=['orf'], writes=['gsm'])
                            op('pool', lambda e: e.tensor_tensor(out=osq[:], in0=orf[:], in1=orf[:], op=ALU.mult), reads=['orf'], writes=['osq'])
                            op('dve', lambda e: e.tensor_reduce(out=gss[:], in_=osq3, axis=AX.X, op=ALU.add), reads=['osq'], writes=['gss'])
                            op('dve', lambda e: e.tensor_scalar(out=gmu[:], in0=gsm[:], scalar1=1.0 / 64, scalar2=None, op0=ALU.mult),
                               reads=['gsm'], writes=['gmu'])
                            op('dve', lambda e: e.tensor_tensor(out=gm2[:], in0=gmu[:], in1=gmu[:], op=ALU.mult), reads=['gmu'], writes=['gm2'])
                            op('dve', lambda e: e.scalar_tensor_tensor(out=grs[:], in0=gss[:], scalar=1.0 / 64, in1=gm2[:],
                                                                       op0=ALU.mult, op1=ALU.subtract),
                               reads=['gss', 'gm2'], writes=['grs'])
                            op('act', lambda e: e.activation(out=grs[:], in_=grs[:], func=AF.Sqrt, bias=eps_t[:, 0:1], scale=1.0),
                               reads=['grs', 'eps'], writes=['grs'])
                            op('dve', lambda e: e.reciprocal(out=grs[:], in_=grs[:]), reads=['grs'], writes=['grs'])
                            op('dve', lambda e: e.tensor_tensor(out=osq3, in0=orf3, in1=bc(gmu[:].unsqueeze(2), [128, 8, 64]), op=ALU.subtract),
                               reads=['orf', 'gmu', 'gss'], writes=['osq'])
                            op('dve', lambda e: e.tensor_tensor(out=osq3, in0=osq3, in1=bc(grs[:].unsqueeze(2), [128, 8, 64]), op=ALU.mult),
                               reads=['osq', 'grs'], writes=['osq'])
                            op('pool', lambda e, u=u: e.tensor_tensor(out=onrm[:, u, :], in0=osq[:], in1=retg[:], op=ALU.mult),
                               reads=['osq', 'retg'], writes=['onrm%d' % u])
                        for h in range(8):
                            hp, h2 = h // 2, h % 2
                            mm(P[5][h2 * 64:(h2 + 1) * 64, hp * 64:(hp + 1) * 64], ktil[:, h * 64:(h + 1) * 64],
                               rvb[:, h * 64:(h + 1) * 64], True, True, [ktiln, rvbn], ['P5'])
                        op('dve', lambda e: e.tensor_tensor(out=stmp[:], in0=P[5][:, 0:256], in1=Sst[:], op=ALU.add),
                           reads=['P5', 'Sst'], writes=['stmp'])
                        op('dve', lambda e: e.tensor_tensor(out=Sst[:], in0=stmp[:], in1=gC[:], op=ALU.mult),
                           reads=['stmp', 'gC', 'Sb'], writes=['Sst'])


                part1(0)
                for u in range(4):
                    if u + 1 < 4:
                        part1(u + 1)
                    elif (not own) and s + 1 < NT:
                        load_x(s + 1)
                    part2(u)

                _stop(15)
                compress_tile(s)
                _stop(16)
                if s == NT - 1:
                    dma('sp', lambda e: e.dma_start(out=ret_d, in_=Sst[:]), reads=['Sst'])
                if not own or os.environ.get('K_OWN', '1') == '0':
                    continue

                load_wt(1, lambda kc: win_d[kc, :, C_Q:C_Q + 536], 536)
                for u in range(4):
                    cosN = tabs[:, u, 0:8]; sinN = tabs[:, u, 8:16]
                    projn(0, u, Wt[1], 0, 512, 'Wt1')
                    projn(1, u, Wt[1], 512, 536, 'Wt1')
                    op('act', lambda e, u=u: e.copy(out=qb[:, u, :], in_=P[0][:]), reads=['P0'], writes=['qb'])
                    do_rope(qb[:, u, :].rearrange("p (h d) -> p h d", h=8), P[0][:].rearrange("p (h d) -> p h d", h=8),
                            cosN, sinN, 8, 8, ['P0'], ['qb'])
                    op('act', lambda e, u=u: e.activation(out=gat[:, u, :], in_=P[1][:, 0:24], func=AF.Sigmoid),
                       reads=['P1'], writes=['gat'])
                load_wt(0, lambda kc: win_d[kc, :, C_RG:C_RG + 512], 512)
                for u in range(4):
                    projn(2, u, Wt[0], 0, 512, 'Wt0')
                    op('act', lambda e: e.activation(out=rgs[:], in_=P[2][:], func=AF.Silu), reads=['P2'], writes=['rgs'])
                    op('pool', lambda e, u=u: e.tensor_tensor(out=mixret[:], in0=onrm[:, u, :], in1=rgs[:], op=ALU.mult),
                       reads=['onrm%d' % u, 'rgs'], writes=['mixret'])
                    for c in range(4):
                        transpose_to(mixT[:, 4 + c, u * 128:(u + 1) * 128], mixret[:, c * 128:(c + 1) * 128], 128, 128,
                                     ['mixret'], ['mixT'])

                for g in range(2):
                    for h4 in range(4):
                        h = 4 * g + h4
                        for u in range(4):
                            transpose_to(Qa[0:64, 0, h4, u * 128:(u + 1) * 128], qb[:, u, h * 64:(h + 1) * 64], 128, 64,
                                         ['qb'], ['Qa0'], evac='dve')
                    op('pool', lambda e: e.tensor_copy(out=Qa[0:64, 1, :, :], in_=Qa[0:64, 0, :, :]), reads=['Qa0'], writes=['Qa1'])
                    for h4 in range(4):
                        h = 4 * g + h4

                        def c_s1(ct, h4=h4):
                            psi = SB3[ct % 3]
                            mm(P[psi][:], kcT[:, g, ct * 128:(ct + 1) * 128], Qa[:, 0, h4, :], True, ct != k,
                               ['kcT', 'Qa0'], ['P%d' % psi])
                            if ct == k:
                                mm(P[psi][:], ident[:], masks[:, 1792:2304], False, True, ['ident', 'masks'], ['P%d' % psi])
                            return (ct,) + exp_pt(psi, 0, 512, kbc[:, ct:ct + 1], ['kbc'])

                        def c_s2(pct, pt, ptn):
                            for u in range(4):
                                pb = 2 + u // 2
                                mm(P[pb][:, (u % 2) * 193:(u % 2 + 1) * 193], pt[:, u * 128:(u + 1) * 128], Rc[:, pct, g, :],
                                   (pct == 0 and u % 2 == 0), pct == k, [ptn, 'Rc'], ['P%d' % pb], skip_group_check=True)
                        pipeline(k + 1, c_s1, c_s2)
                        finish_branch(lambda u: P[2 + u // 2][:, (u % 2) * 193:(u % 2 + 1) * 193],
                                      lambda u: 'P%d' % (2 + u // 2), h4, h, 0, True, 192)
                    for u in range(4):
                        dma('sp', lambda e, u=u: e.dma_start(out=bon[:], in_=bonus_d[4 * k + u]), writes=['bon'])
                        op('dve', lambda e, u=u: e.tensor_tensor(out=score[:], in0=impacc[:, u, :], in1=bon[:], op=ALU.add),
                           reads=['impacc', 'bon'], writes=['score'])
                        op('dve', lambda e: e.max(out=m8a[:], in_=score[:]), reads=['score'], writes=['m8a'])
                        op('dve', lambda e: e.match_replace(out=sc2[:], in_to_replace=m8a[:], in_values=score[:], imm_value=-3e38),
                           reads=['score', 'm8a'], writes=['sc2'])
                        op('dve', lambda e: e.max(out=m8b[:], in_=sc2[:]), reads=['sc2'], writes=['m8b'])
                        op('dve', lambda e: e.tensor_scalar(out=sc2[:], in0=score[:], scalar1=m8b[:, 7:8], scalar2=-1.0,
                                                            op0=ALU.is_ge, op1=ALU.add),
                           reads=['score', 'm8b'], writes=['sc2'])
                        op('dve', lambda e: e.tensor_scalar(out=selb[:, 0:128], in0=sc2[:], scalar1=-NEGB, scalar2=None, op0=ALU.mult),
                           reads=['sc2'], writes=['selb'])
                        op('pool', lambda e: e.tensor_copy(out=selb[:, 128:256], in_=selb[:, 0:128]), reads=['selb'], writes=['selb'])
                        transpose_to(selT[:, 1, u * 128:(u + 1) * 128], selb[:, 0:128], 128, 128, ['selb'], ['selT'])
                        transpose_to(selT[:, 0, u * 128:(u + 1) * 128], selb[:, 64:192], 128, 128, ['selb'], ['selT'])
                    for r in range(2):
                        for h4 in range(4):
                            op('pool', lambda e, r=r, h4=h4: e.tensor_copy(out=Qa[64:128, r, h4, :], in_=selT[64:128, r, :]),
                               reads=['selT'], writes=['Qa%d' % r])
                    nkt = 4 * s + 4
                    for h4 in range(4):
                        h = 4 * g + h4

                        def s_s1(kt, h4=h4):
                            r = kt // 32
                            o = kt - 4 * s
                            c0 = 128 * o if o > 0 else 0
                            psi = SB3[kt % 3]
                            mm(P[psi][:, c0:512], KslT[:, g, kt * 128:(kt + 1) * 128], Qa[:, r, h4, c0:512], True, o < 0,
                               ['KslT%d' % kt, 'KslT_E', 'Qa%d' % r], ['P%d' % psi])
                            if o >= 0:
                                mm(P[psi][:, c0:512], ident[:], causal(o)[:, c0:512], False, True, ['ident', 'masks'], ['P%d' % psi])
                            return (kt, o) + exp_pt(psi, c0, 512, kbias[:, kt:kt + 1], ['kbias'])

                        def s_s2(pkt, po, pt, ptn):
                            c0 = 128 * po if po > 0 else 0
                            mm(P[2][0:65, c0:512], Vsl[:, pkt, g, :], pt[:, c0:512], pkt == 0, pkt == nkt - 1,
                               [ptn, 'Vsl%d' % pkt, 'Vsl_ones'], ['P2'])
                        pipeline(nkt, s_s1, s_s2)
                        ot_finish(2, h4, h, 1)
                    for h4 in range(4):
                        h = 4 * g + h4

                        def w_s1(o8, h4=h4):
                            psi = SB3[o8 % 3]
                            if o8 < 4:
                                c0, c1 = 0, 128 * (o8 + 1)
                                msk = wlo(o8)
                            else:
                                c0, c1 = 128 * (o8 - 4), 512
                                msk = causal(o8 - 4)
                            mm(P[psi][:, c0:c1], KwT[:, g, o8 * 128:(o8 + 1) * 128], Qa[:, 0, h4, c0:c1], True, False,
                               ['KwT%d' % o8, 'Qa0'], ['P%d' % psi])
                            mm(P[psi][:, c0:c1], ident[:], msk[:, c0:c1], False, True, ['ident', 'masks'], ['P%d' % psi])
                            kt = 4 * (s - 1) + o8
                            return (o8, c0, c1) + exp_pt(psi, c0, c1, kbias[:, kt:kt + 1], ['kbias'])

                        def w_s2(p8, pc0, pc1, pt, ptn):
                            mm(P[3][0:65, pc0:pc1], Vw[:, p8, g, :], pt[:, pc0:pc1], p8 == 0, p8 == 7,
                               [ptn, 'Vw%d' % p8, 'Vw_ones'], ['P3'], skip_group_check=True)
                        pipeline(8, w_s1, w_s2)
                        ot_finish(3, h4, h, 2)
                    for u in range(4):
                        op('act', lambda e, u=u: e.copy(out=onsab[:], in_=onsa[:, u, :]), reads=['onsa'], writes=['onsab'])
                        for c2 in range(2):
                            transpose_to(mixT[:, 2 * g + c2, u * 128:(u + 1) * 128], onsab[:, c2 * 128:(c2 + 1) * 128], 128, 128,
                                         ['onsab'], ['mixT'], evac='dve')

                load_wt(0, lambda kc: wo_d[kc, :, 0:512], 512)
                load_wt(1, lambda kc: wo_d[kc, :, 512:1024], 512)
                for u in range(4):
                    dma('sp', lambda e, u=u: e.dma_start(out=xo[:], in_=xown_d[4 * k + u]), writes=['stage', 'rtmp'])
                    for half in range(2):
                        for c in range(8):
                            mm(P[half][:], mixT[:, c, u * 128:(u + 1) * 128], Wt[half][:, c, 0:512], c == 0, c == 7,
                               ['mixT', 'Wt%d' % half], ['P%d' % half])
                        op('dve', lambda e, half=half: e.scalar_tensor_tensor(out=xr[:, half * 512:(half + 1) * 512],
                                                                              in0=xo[:, half * 512:(half + 1) * 512], scalar=ALPHA,
                                                                              in1=P[half][:], op0=ALU.mult, op1=ALU.add),
                           reads=['stage', 'rtmp', 'P%d' % half], writes=['orf', 'osq'])
                    layer_norm(xo[:], xr[:], ln1[:, 0, :], ln1[:, 1, :], ['orf', 'osq'], ['stage', 'rtmp'], 'ln1')
                    dma('sp', lambda e, u=u: e.dma_start(out=x1s_d[4 * k + u], in_=xo[:]), reads=['stage', 'rtmp'], writes=['x1s%d' % (4 * k + u)])
                if k < 3:
                    load_wt(0, lambda kc: win_d[kc, :, C_RQ:C_RQ + 512], 512)

            S.barrier()
            stP.close()
            if DO_SAMPLE:
                u = 0
                xTbs = TA("xTbs", [128, 8, 128])
                xsel[0] = xTbs
                mixTs = TA("mixTs", [128, 8, 128])
                KcTs = TA("KcTs", [64, 2, 1040]); VcTs = TA("VcTs", [64, 2, 1040])

                def compress8(s8):
                    ctp, row0 = s8 // 2, 64 * (s8 % 2)
                    for r in range(32):
                        mm(P[4][0:64, 0:128].rearrange("p (g n) -> p g n", g=2), wck[:, r, :], KcTs[:, :, r:r + 1009:16], r == 0, r == 31,
                           ['wck', 'KcTs'], ['P4'])
                    op('act', lambda e: e.activation(out=kcT[0:64, :, 64 * s8:64 * s8 + 64],
                                                     in_=P[4][0:64, 0:128].rearrange("p (g n) -> p g n", g=2),
                                                     func=AF.Identity, bias=pbk[:, 0:1], scale=1.0),
                       reads=['P4', 'pbk'], writes=['kcT'])
                    for g in range(2):
                        for r in range(33):
                            if r < 32:
                                mm(P[5][row0:row0 + 64, g * 64:(g + 1) * 64], VcTs[:, g, r:r + 1009:16], wcv[:, r, :],
                                   r == 0, False, ['wcv', 'VcTs'], ['P5'])
                            else:
                                mm(P[5][row0:row0 + 64, g * 64:(g + 1) * 64], ones1[0:1, 0:64], pbv[0:1, :],
                                   False, True, ['ones1', 'pbv'], ['P5'])
                        op('act', lambda e, g=g: e.copy(out=Rc[row0:row0 + 64, ctp, g, 128:192],
                                                         in_=P[5][row0:row0 + 64, g * 64:(g + 1) * 64]),
                           reads=['P5'], writes=['Rc'])
                    op('pool', lambda e: e.tensor_copy(out=KcTs[:, :, 0:16], in_=KcTs[:, :, 1024:1040]), reads=['KcTs'], writes=['KcTs'])
                    op('pool', lambda e: e.tensor_copy(out=VcTs[:, :, 0:16], in_=VcTs[:, :, 1024:1040]), reads=['VcTs'], writes=['VcTs'])
                qbs = TA("qbs", [128, 1, 512])
                onrm_s = TA("onrm_s", [128, 1, 512])
                ohs = TA("ohs", [128, 4], F32)
                dma('sp', lambda e: e.dma_start(out=ohs[:], in_=oh_d), writes=['ohs'])
                kbself = TA("kbself", [128, 4], F32)
                dma('sp', lambda e: e.dma_start(out=kbself[:], in_=kbself_d), writes=['kbself'])
                kbws = TA("kbws", [128, 4], F32)
                dma('sp', lambda e: e.dma_start(out=kbws[:], in_=kbws_d), writes=['kbws'])
                kbcs = TA("kbcs", [128, 4], F32)
                dma('sp', lambda e: e.dma_start(out=kbcs[:], in_=kbcs_d), writes=['kbcs'])
                bons = TA("bons", [1, 128], F32)
                dma('sp', lambda e: e.dma_start(out=bons[:], in_=bons_d), writes=['bons'])
                decs = TA("decs", [128, 16], F32)
                dma('sp', lambda e: e.dma_start(out=decs[:], in_=decs_d), writes=['decs'])
                gC1 = TA("gC1", [128, 256], F32)
                dma('sp', lambda e: e.dma_start(out=gC1[:], in_=gC1_d), writes=['gC1'])
                zcol = TA("zcol", [128, 1], F32)
                op('pool', lambda e: e.memset(zcol[:], 0.0), writes=['zcol'])
                oneb = TA("oneb", [1, 1])
                op('pool', lambda e: e.memset(oneb[:], 1.0), writes=['oneb'])
                pti = TA("pti", [128, 256], I32)
                dma('sp', lambda e: e.dma_start(out=pti[:], in_=pt_d), writes=['pti'])
                ptf = TA("ptf", [128, 256], F32)
                iop = TA("iop", [128, 1], I32)
                iof = TA("iof", [128, 1], F32)
                idx = TA("idx", [128, 256], I32)
                op('pool', lambda e: e.iota(iop[:], pattern=[[0, 1]], base=0, channel_multiplier=1), writes=['iop'])
                op('dve', lambda e: e.tensor_copy(out=iof[:], in_=iop[:]), reads=['iop'], writes=['iof'])
                op('dve', lambda e: e.tensor_copy(out=ptf[:], in_=pti[:]), reads=['pti'], writes=['ptf'])
                op('dve', lambda e: e.tensor_scalar(out=ptf[:], in0=ptf[:], scalar1=128.0, scalar2=iof[:, 0:1],
                                                    op0=ALU.mult, op1=ALU.add), reads=['ptf', 'iof'], writes=['ptf'])
                op('dve', lambda e: e.tensor_copy(out=idx[:], in_=ptf[:]), reads=['ptf'], writes=['idx'])

                KnT = TA("KnT", [128, 2, 128]); Vn = TA("Vn", [128, 2, 65])
                KwnT = TA("KwnT", [128, 2, 128]); Vwn = TA("Vwn", [128, 2, 65])
                for t_ in (KnT, KwnT):
                    op('pool', lambda e, t_=t_: e.memset(t_[:], 0.0), writes=['Knew'])
                for t_ in (Vn, Vwn):
                    op('pool', lambda e, t_=t_: e.memset(t_[:, :, 64:65], 1.0), writes=['Vnew'])
                QTs = TA("QTs", [64, 8, 128])
                Qs = TA("Qs", [128, 2, 4])
                op('pool', lambda e: e.memset(Qs[:], 0.0), writes=['Qs'])
                pts = TA("pts", [128, 64])
                accs = TA("accs", [4, 193], F32)
                rec4 = TA("rec4", [4, 1], F32)
                gcol = TA("gcol", [4, 3], F32)
                sc4 = TA("sc4", [4, 1], F32)
                ocmb = TA("ocmb", [4, 64], F32)
                srow = TA("srow", [1, 128], F32)
                srow2 = TA("srow2", [1, 128], F32)
                s8a = TA("s8a", [1, 8], F32); s8b = TA("s8b", [1, 8], F32)
                selr = TA("selr", [1, 256])
                qz = [TA("qz%d" % b, [128, 4, 128]) for b in range(4)]
                kvbs = [TA("kvbs%d" % i, [128, 512]) for i in range(3)]
                SbS = [[TA("SbS%d_%d" % (b, i), [128, 256]) for i in range(2)] for b in range(4)]
                SsT = [TA("SsT%d" % b, [128, 256], F32) for b in range(4)]
                ktz = TA("ktz", [128, 512])

                for kc in range(8):
                    dma('pool', lambda e, kc=kc: e.dma_start(out=xTbs[:, kc, :], in_=xsT_d[kc]), writes=['xTb'])
                dma('sp', lambda e: e.dma_start(out=tabs[:, 0, :], in_=tabs_s_d), writes=['tabs'])
                cosN = tabs[:, 0, 0:8]; sinN = tabs[:, 0, 8:16]
                cosR = tabs[:, 0, 16:48]; sinR = tabs[:, 0, 48:80]
                load_wt(0, lambda kc: win_d[kc, :, C_RQ:C_RQ + 512], 512)
                projn(0, u, Wkv, C_KVA, C_KVA + 512, 'Wkv')
                op('act', lambda e: e.copy(out=stage[:], in_=P[0][:]), reads=['P0'], writes=['stage'])
                do_rope(stage[:, 256:384].rearrange("p (g d) -> p g d", g=2),
                        P[0][:, 256:384].rearrange("p (g d) -> p g d", g=2), cosN, sinN, 2, 8, ['P0', 'stage'], ['stage'])
                dma('sp', lambda e: e.dma_start(out=kvs_d, in_=stage[:]), reads=['stage'])
                op('pool', lambda e: e.tensor_copy(out=kvb[:], in_=stage[:]), reads=['stage'], writes=['kvb'])
                for g in range(2):
                    transpose_to(KnT[0:64, g, :], kvb[:, 256 + g * 64:256 + (g + 1) * 64], 128, 64, ['kvb'], ['Knew'])
                op('pool', lambda e: e.tensor_copy(out=Vn[:, :, 0:64], in_=kvb[:, 384:512].rearrange("p (g d) -> p g d", g=2)),
                   reads=['kvb'], writes=['Vnew'])
                projn(1, u, Wkv, C_RK, C_RK + 512, 'Wkv')
                projn(2, u, Wkv, C_RV, C_RV + 512, 'Wkv')
                op('act', lambda e: e.copy(out=rtmp[:], in_=P[1][:]), reads=['P1'], writes=['rtmp'])
                do_rope(rtmp[:].rearrange("p (h d) -> p h d", h=8), P[1][:].rearrange("p (h d) -> p h d", h=8),
                        cosR, sinR, 8, 32, ['P1'], ['rtmp'])
                op('dve', lambda e: e.tensor_tensor(out=ktil[:].rearrange("p (h d) -> p h d", h=8),
                                                    in0=rtmp[:].rearrange("p (h d) -> p h d", h=8),
                                                    in1=bc(decs[:, 8:16].unsqueeze(2), [128, 8, 64]), op=ALU.mult),
                   reads=['rtmp', 'decs'], writes=['ktil'])
                op('act', lambda e: e.copy(out=rvb[:], in_=P[2][:]), reads=['P2'], writes=['rvb'])
                projn(3, u, Wkv, C_WIN, C_WIN + 256, 'Wkv')
                op('act', lambda e: e.copy(out=wstage[:], in_=P[3][:, 0:256]), reads=['P3'], writes=['wstage'])
                do_rope(wstage[:, 0:128].rearrange("p (g d) -> p g d", g=2),
                        P[3][:, 0:128].rearrange("p (g d) -> p g d", g=2), cosN, sinN, 2, 8, ['P3', 'wstage'], ['wstage'])
                for b in range(4):
                    dma('sp', lambda e, b=b: e.dma_start(out=wins_d[b, 511:512, :], in_=wstage[b:b + 1, :]), reads=['wstage'])
                op('pool', lambda e: e.tensor_copy(out=wb[:], in_=wstage[:]), reads=['wstage'], writes=['wb'])
                for g in range(2):
                    transpose_to(KwnT[0:64, g, :], wb[:, g * 64:(g + 1) * 64], 128, 64, ['wb'], ['Knew'])
                op('pool', lambda e: e.tensor_copy(out=Vwn[:, :, 0:64], in_=wb[:, 128:256].rearrange("p (g d) -> p g d", g=2)),
                   reads=['wb'], writes=['Vnew'])
                for b in range(4):
                    for h2 in range(2):
                        dma('sp', lambda e, b=b, h2=h2: e.dma_start(
                            out=SsT[b][h2 * 64:(h2 + 1) * 64, :].rearrange("p (hp e) -> p hp e", hp=4),
                            in_=stin_d[b].rearrange("(hp h2) d e -> h2 d hp e", h2=2)[h2]), writes=['SsT%d' % b])
                    op('act', lambda e, b=b: e.copy(out=SbS[b][0][0:64, :], in_=SsT[b][0:64, :]), reads=['SsT%d' % b], writes=['SbS'])
                    op('act', lambda e, b=b: e.copy(out=SbS[b][1][64:128, :], in_=SsT[b][64:128, :]), reads=['SsT%d' % b], writes=['SbS'])
                    op('pool', lambda e, b=b: e.memset(SbS[b][0][64:128, :], 0.0), writes=['SbS'])
                    op('pool', lambda e, b=b: e.memset(SbS[b][1][0:64, :], 0.0), writes=['SbS'])
                projn(3, u, Wt[0], 0, 512, 'Wt0')
                op('act', lambda e: e.copy(out=rtmp[:], in_=P[3][:]), reads=['P3'], writes=['rtmp'])
                do_rope(rtmp[:].rearrange("p (h d) -> p h d", h=8), P[3][:].rearrange("p (h d) -> p h d", h=8),
                        cosR, sinR, 8, 32, ['P3'], ['rtmp'])
                op('dve', lambda e: e.tensor_tensor(out=qtil[:].rearrange("p (h d) -> p h d", h=8),
                                                    in0=rtmp[:].rearrange("p (h d) -> p h d", h=8),
                                                    in1=bc(decs[:, 0:8].unsqueeze(2), [128, 8, 64]), op=ALU.mult),
                   reads=['rtmp', 'decs'], writes=['qtil'])
                for hp in range(4):
                    slot = tr_rr[0]; tr_rr[0] = (slot + 1) % 2
                    pst = PTb[slot][:, 0, :]
                    op('pe', lambda e, pst=pst, hp=hp: e.transpose(pst, ktil[:, hp * 128:(hp + 1) * 128], ident[:]),
                       reads=['ktil', 'ident'], writes=['PTr%d' % slot])
                    op('act', lambda e, pst=pst, hp=hp: e.copy(out=kz[0][0:64, hp, :], in_=pst[0:64, :]),
                       reads=['PTr%d' % slot], writes=['ktilT'])
                    op('act', lambda e, pst=pst, hp=hp: e.copy(out=kz[1][64:128, hp, :], in_=pst[64:128, :]),
                       reads=['PTr%d' % slot], writes=['ktilT'])
                    transpose_to(qtilT[:, hp, :], qtil[:, hp * 128:(hp + 1) * 128], 128, 128, ['qtil'], ['qtilT'], evac='dve')
                for b in range(4):
                    op('pool', lambda e, b=b: e.memset(qz[b][:], 0.0), writes=['qz'])
                    op('pool', lambda e, b=b: e.tensor_copy(out=qz[b][:, :, b:b + 1], in_=qtilT[:, :, b:b + 1]),
                       reads=['qtilT', 'qz'], writes=['qz'])
                for h in range(8):
                    hp, h2 = h // 2, h % 2
                    pb = 4 + h // 4
                    mm(P[pb][:, (h % 4) * 128:(h % 4 + 1) * 128], kz[h2][:, hp, :], qtilT[:, hp, :],
                       True, True, ['ktilT', 'qtilT'], ['P%d' % pb])
                for half in range(2):
                    op('dve', lambda e, half=half: e.tensor_tensor(
                        out=innb[:, half * 4:(half + 1) * 4, :],
                        in0=P[4 + half][:].rearrange("p (h i) -> p h i", h=4),
                        in1=bc(identf[:].unsqueeze(1), [128, 4, 128]), op=ALU.mult),
                       reads=['P%d' % (4 + half), 'identf'], writes=['innb%d' % half])
                for h in range(8):
                    hp, h2 = h // 2, h % 2
                    mm(P[3][:, h * 64:(h + 1) * 64], innb[:, h, :], rvb[:, h * 64:(h + 1) * 64], True, False,
                       ['innb%d' % (h // 4), 'rvb', 'rtmp'], ['P3'])
                    for b in range(4):
                        mm(P[3][:, h * 64:(h + 1) * 64], qz[b][:, hp, :], SbS[b][h2][:, hp * 64:(hp + 1) * 64],
                           False, b == 3, ['qz', 'SbS'], ['P3'])
                op('act', lambda e: e.copy(out=orf[:], in_=P[3][:]), reads=['P3'], writes=['orf'])
                orf3 = orf[:].rearrange("p (h d) -> p h d", h=8)
                osq3 = osq[:].rearrange("p (h d) -> p h d", h=8)
                op('dve', lambda e: e.tensor_reduce(out=gsm[:], in_=orf3, axis=AX.X, op=ALU.add), reads=['orf'], writes=['gsm'])
                op('pool', lambda e: e.tensor_tensor(out=osq[:], in0=orf[:], in1=orf[:], op=ALU.mult), reads=['orf'], writes=['osq'])
                op('dve', lambda e: e.tensor_reduce(out=gss[:], in_=osq3, axis=AX.X, op=ALU.add), reads=['osq'], writes=['gss'])
                op('dve', lambda e: e.tensor_scalar(out=gmu[:], in0=gsm[:], scalar1=1.0 / 64, scalar2=None, op0=ALU.mult),
                   reads=['gsm'], writes=['gmu'])
                op('dve', lambda e: e.tensor_tensor(out=gm2[:], in0=gmu[:], in1=gmu[:], op=ALU.mult), reads=['gmu'], writes=['gm2'])
                op('dve', lambda e: e.scalar_tensor_tensor(out=grs[:], in0=gss[:], scalar=1.0 / 64, in1=gm2[:],
                                                           op0=ALU.mult, op1=ALU.subtract),
                   reads=['gss', 'gm2'], writes=['grs'])
                op('act', lambda e: e.activation(out=grs[:], in_=grs[:], func=AF.Sqrt, bias=eps_t[:, 0:1], scale=1.0),
                   reads=['grs', 'eps'], writes=['grs'])
                op('dve', lambda e: e.reciprocal(out=grs[:], in_=grs[:]), reads=['grs'], writes=['grs'])
                op('dve', lambda e: e.tensor_tensor(out=osq3, in0=orf3, in1=bc(gmu[:].unsqueeze(2), [128, 8, 64]), op=ALU.subtract),
                   reads=['orf', 'gmu', 'gss'], writes=['osq'])
                op('dve', lambda e: e.tensor_tensor(out=osq3, in0=osq3, in1=bc(grs[:].unsqueeze(2), [128, 8, 64]), op=ALU.mult),
                   reads=['osq', 'grs'], writes=['osq'])
                op('pool', lambda e: e.tensor_tensor(out=onrm_s[:, 0, :], in0=osq[:], in1=retg[:], op=ALU.mult),
                   reads=['osq', 'retg'], writes=['onrm0'])
                for b in range(4):
                    op('dve', lambda e, b=b: e.tensor_scalar(out=ktz[:], in0=ktil[:], scalar1=ohs[:, b:b + 1], scalar2=None, op0=ALU.mult),
                       reads=['ktil', 'ohs'], writes=['ktz'])
                    for h in range(8):
                        hp, h2 = h // 2, h % 2
                        mm(P[5][h2 * 64:(h2 + 1) * 64, hp * 64:(hp + 1) * 64], ktz[:, h * 64:(h + 1) * 64],
                           rvb[:, h * 64:(h + 1) * 64], True, True, ['ktz', 'rvb'], ['P5'])
                    op('dve', lambda e, b=b: e.tensor_tensor(out=stmp[:], in0=P[5][:, 0:256], in1=SsT[b][:], op=ALU.add),
                       reads=['P5', 'SsT%d' % b], writes=['stmp'])
                    op('dve', lambda e, b=b: e.tensor_tensor(out=SsT[b][:], in0=stmp[:], in1=gC1[:], op=ALU.mult),
                       reads=['stmp', 'gC1', 'SbS'], writes=['SsT%d' % b])
                    dma('sp', lambda e, b=b: e.dma_start(out=rets_d[b], in_=SsT[b][:]), reads=['SsT%d' % b])
                load_wt(1, lambda kc: win_d[kc, :, C_Q:C_Q + 536], 536)
                projn(0, u, Wt[1], 0, 512, 'Wt1')
                projn(1, u, Wt[1], 512, 536, 'Wt1')
                op('act', lambda e: e.copy(out=qbs[:, 0, :], in_=P[0][:]), reads=['P0'], writes=['qb'])
                do_rope(qbs[:, 0, :].rearrange("p (h d) -> p h d", h=8), P[0][:].rearrange("p (h d) -> p h d", h=8),
                        cosN, sinN, 8, 8, ['P0', 'qb'], ['qb'])
                op('act', lambda e: e.activation(out=gat[:, 0, :], in_=P[1][:, 0:24], func=AF.Sigmoid), reads=['P1'], writes=['gat'])
                load_wt(0, lambda kc: win_d[kc, :, C_RG:C_RG + 512], 512)
                projn(2, u, Wt[0], 0, 512, 'Wt0')
                op('act', lambda e: e.activation(out=rgs[:], in_=P[2][:], func=AF.Silu), reads=['P2'], writes=['rgs'])
                op('pool', lambda e: e.tensor_tensor(out=mixret[:], in0=onrm_s[:, 0, :], in1=rgs[:], op=ALU.mult),
                   reads=['onrm0', 'rgs'], writes=['mixret'])
                for c in range(4):
                    transpose_to(mixTs[:, 4 + c, :], mixret[:, c * 128:(c + 1) * 128], 128, 128, ['mixret'], ['mixT'])
                for h in range(8):
                    transpose_to(QTs[:, h, :], qbs[:, 0, h * 64:(h + 1) * 64], 128, 64, ['qb'], ['QTs'], evac='dve')

                for b in range(4):
                    op('pool', lambda e: e.memset(KcTs[:, :, 0:16], 0.0), reads=['KcTs'], writes=['KcTs'])
                    op('pool', lambda e: e.memset(VcTs[:, :, 0:16], 0.0), reads=['VcTs'], writes=['VcTs'])
                    for pg in range(64):
                        kt = pg
                        ru = pg % 8
                        kb_i = (b * 64 + pg) % 3
                        kvr = kvbs[kb_i]
                        kvn = 'kvbs%d' % kb_i
                        dma('pool', lambda e, b=b, pg=pg, kvr=kvr: e.indirect_dma_start(
                            out=kvr[:], out_offset=None, in_=ckv_d,
                            in_offset=bass.IndirectOffsetOnAxis(ap=idx[:, b * 64 + pg:b * 64 + pg + 1], axis=0)),
                            reads=['idx'], writes=[kvn])
                        pbk_, pbn_ = tbatch([kvr[:, i * 64:(i + 1) * 64] for i in range(6)], [kvn])
                        op('act', lambda e, pbk_=pbk_, ru=ru: e.copy(out=KcTs[:, :, 16 + ru * 128:16 + (ru + 1) * 128], in_=pbk_[0:64, 0:2, :]),
                           reads=[pbn_], writes=['KcTs'])
                        op('act', lambda e, pbk_=pbk_, ru=ru: e.copy(out=VcTs[:, :, 16 + ru * 128:16 + (ru + 1) * 128], in_=pbk_[0:64, 2:4, :]),
                           reads=[pbn_], writes=['VcTs'])
                        op('dve', lambda e, pbk_=pbk_, kt=kt: e.tensor_copy(out=KslT[0:64, :, kt * 128:(kt + 1) * 128], in_=pbk_[0:64, 4:6, :]),
                           reads=[pbn_], writes=['KslT%d' % kt])
                        op('dve', lambda e, kt=kt, kvr=kvr: e.tensor_copy(out=Vsl[:, kt, :, 0:64],
                                                                           in_=kvr[:, 384:512].rearrange("p (g d) -> p g d", g=2)),
                           reads=[kvn], writes=['Vsl%d' % kt])
                        if ru == 7:
                            compress8(pg // 8)
                    for w in range(4):
                        dma('sp', lambda e, b=b, w=w: e.dma_start(out=wstage[:], in_=cwin_d[b, w * 128:(w + 1) * 128, :]), writes=['wstage'])
                        if w == 0:
                            dma('sp', lambda e, b=b: e.dma_start(out=wins_d[b, 0:127, :], in_=wstage[1:128, :]), reads=['wstage'])
                        else:
                            dma('sp', lambda e, b=b, w=w: e.dma_start(out=wins_d[b, 128 * w - 1:128 * w + 127, :], in_=wstage[:]),
                                reads=['wstage'])
                        op('pool', lambda e: e.tensor_copy(out=wb[:], in_=wstage[:]), reads=['wstage'], writes=['wb'])
                        for g in range(2):
                            transpose_to(KwT[0:64, g, w * 128:(w + 1) * 128], wb[:, g * 64:(g + 1) * 64], 128, 64, ['wb'], ['KwT%d' % w])
                        op('pool', lambda e, w=w: e.tensor_copy(out=Vw[:, w, :, 0:64],
                                                                 in_=wb[:, 128:256].rearrange("p (g d) -> p g d", g=2)),
                           reads=['wb'], writes=['Vw%d' % w])
                    for g in range(2):
                        for r in range(2):
                            op('dve', lambda e, g=g, r=r, b=b: e.tensor_copy(out=Qs[0:64, r, :], in_=QTs[:, 4 * g:4 * g + 4, b]),
                               reads=['QTs', 'Qs'], writes=['Qs'])
                        for br in range(3):
                            mm(P[4][0:4, br:br + 1], gat[:, 0, g * 12 + br:g * 12 + br + 10:3], ohs[:, b:b + 1], True, True,
                               ['gat', 'ohs'], ['P4'])
                        op('act', lambda e: e.copy(out=gcol[:], in_=P[4][0:4, 0:3]), reads=['P4'], writes=['gcol'])
                        for ct in range(4):
                            mm(P[0][:, 4 * ct:4 * ct + 4], kcT[:, g, ct * 128:(ct + 1) * 128], Qs[:, 0, :], True, True,
                               ['kcT', 'Qs'], ['P0'])
                        for ct in range(4):
                            op('act', lambda e, ct=ct: e.activation(out=pts[:, 4 * ct:4 * ct + 4], in_=P[0][:, 4 * ct:4 * ct + 4],
                                                                    func=AF.Exp, bias=kbcs[:, ct:ct + 1], scale=0.125),
                               reads=['P0', 'kbcs'], writes=['pts'])
                        for ct in range(4):
                            mm(P[2][0:4, 0:193], pts[:, 4 * ct:4 * ct + 4], Rc[:, ct, g, :], ct == 0, ct == 3, ['pts', 'Rc'], ['P2'])
                        op('act', lambda e: e.copy(out=accs[:], in_=P[2][0:4, 0:193]), reads=['P2'], writes=['accs'])
                        op('dve', lambda e: e.tensor_scalar(out=rec4[:], in0=accs[:, 192:193], scalar1=1e-30, scalar2=None, op0=ALU.max),
                           reads=['accs'], writes=['rec4'])
                        op('dve', lambda e: e.reciprocal(out=rec4[:], in_=rec4[:]), reads=['rec4'], writes=['rec4'])
                        mm(P[3][0:1, 0:128], rec4[:, 0:1], accs[:, 0:128], True, True, ['rec4', 'accs'], ['P3'])
                        op('dve', lambda e: e.tensor_tensor(out=sc4[:], in0=rec4[:], in1=gcol[:, 0:1], op=ALU.mult),
                           reads=['rec4', 'gcol'], writes=['sc4'])
                        op('dve', lambda e: e.tensor_scalar(out=ocmb[:], in0=accs[:, 128:192], scalar1=sc4[:, 0:1], scalar2=None, op0=ALU.mult),
                           reads=['accs', 'sc4'], writes=['ocmb'])
                        op('dve', lambda e: e.tensor_tensor(out=srow[:], in0=P[3][0:1, 0:128], in1=bons[:], op=ALU.add),
                           reads=['P3', 'bons'], writes=['srow'])
                        op('dve', lambda e: e.max(out=s8a[:], in_=srow[:]), reads=['srow'], writes=['s8a'])
                        op('dve', lambda e: e.match_replace(out=srow2[:], in_to_replace=s8a[:], in_values=srow[:], imm_value=-3e38),
                           reads=['srow', 's8a'], writes=['srow2'])
                        op('dve', lambda e: e.max(out=s8b[:], in_=srow2[:]), reads=['srow2'], writes=['s8b'])
                        op('dve', lambda e: e.tensor_scalar(out=srow2[:], in0=srow[:], scalar1=s8b[:, 6:7], scalar2=-1.0,
                                                            op0=ALU.is_ge, op1=ALU.add), reads=['srow', 's8b'], writes=['srow2'])
                        op('dve', lambda e: e.tensor_scalar(out=selr[:, 0:128], in0=srow2[:], scalar1=-NEGB, scalar2=None, op0=ALU.mult),
                           reads=['srow2'], writes=['selr'])
                        op('dve', lambda e: e.tensor_copy(out=selr[:, 128:256], in_=selr[:, 0:128]), reads=['selr'], writes=['selr'])
                        mm(P[4][:, 8:9], selr[0:1, 0:128], oneb[0:1, 0:1], True, True, ['selr', 'oneb', 'gcol'], ['P4'])
                        mm(P[4][:, 9:10], selr[0:1, 64:192], oneb[0:1, 0:1], True, True, ['selr', 'oneb'], ['P4'])
                        op('dve', lambda e: e.tensor_copy(out=Qs[64:128, 1, :], in_=bc(P[4][64:128, 8:9], [64, 4])), reads=['P4', 'Qs'], writes=['Qs'])
                        op('dve', lambda e: e.tensor_copy(out=Qs[64:128, 0, :], in_=bc(P[4][64:128, 9:10], [64, 4])), reads=['P4', 'Qs'], writes=['Qs'])
                        for grp in range(4):
                            psi = grp % 2
                            for i in range(16):
                                kt = 16 * grp + i
                                mm(P[psi][:, 4 * i:4 * i + 4], KslT[:, g, kt * 128:(kt + 1) * 128], Qs[:, kt // 32, :], True, True,
                                   ['KslT%d' % kt, 'KslT_E', 'Qs'], ['P%d' % psi])
                            op('act', lambda e, psi=psi: e.activation(out=pts[:], in_=P[psi][:, 0:64], func=AF.Exp, bias=zcol[:, 0:1], scale=0.125),
                               reads=['P%d' % psi, 'zcol'], writes=['pts'])
                            for i in range(16):
                                kt = 16 * grp + i
                                mm(P[2][0:4, 0:65], pts[:, 4 * i:4 * i + 4], Vsl[:, kt, g, :], grp == 0 and i == 0, False,
                                   ['pts', 'Vsl%d' % kt, 'Vsl_ones'], ['P2'])
                        mm(P[0][:, 0:4], KnT[:, g, :], Qs[:, 0, :], True, True, ['Knew', 'Qs'], ['P0'])
                        op('act', lambda e, b=b: e.activation(out=pts[:, 0:4], in_=P[0][:, 0:4], func=AF.Exp, bias=kbself[:, b:b + 1], scale=0.125),
                           reads=['P0', 'kbself'], writes=['pts'])
                        mm(P[2][0:4, 0:65], pts[:, 0:4], Vn[:, g, :], False, True, ['pts', 'Vnew'], ['P2'])
                        for br, pbk_ in ((1, 2),):
                            op('act', lambda e: e.copy(out=accs[:, 0:65], in_=P[2][0:4, 0:65]), reads=['P2'], writes=['accs'])
                            op('dve', lambda e: e.reciprocal(out=rec4[:], in_=accs[:, 64:65]), reads=['accs'], writes=['rec4'])
                            op('dve', lambda e: e.tensor_tensor(out=sc4[:], in0=rec4[:], in1=gcol[:, 1:2], op=ALU.mult),
                               reads=['rec4', 'gcol'], writes=['sc4'])
                            op('dve', lambda e: e.scalar_tensor_tensor(out=ocmb[:], in0=accs[:, 0:64], scalar=sc4[:, 0:1], in1=ocmb[:],
                                                                       op0=ALU.mult, op1=ALU.add),
                               reads=['accs', 'sc4', 'ocmb'], writes=['ocmb'])
                        for w in range(4):
                            mm(P[1][:, 4 * w:4 * w + 4], KwT[:, g, w * 128:(w + 1) * 128], Qs[:, 0, :], True, True,
                               ['KwT%d' % w, 'Qs'], ['P1'])
                        mm(P[1][:, 16:20], KwnT[:, g, :], Qs[:, 0, :], True, True, ['Knew', 'Qs'], ['P1'])
                        for w in range(4):
                            op('act', lambda e, w=w: e.activation(out=pts[:, 4 * w:4 * w + 4], in_=P[1][:, 4 * w:4 * w + 4], func=AF.Exp,
                                                                  bias=kbws[:, w:w + 1], scale=0.125),
                               reads=['P1', 'kbws'], writes=['pts'])
                        op('act', lambda e, b=b: e.activation(out=pts[:, 16:20], in_=P[1][:, 16:20], func=AF.Exp, bias=kbself[:, b:b + 1], scale=0.125),
                           reads=['P1', 'kbself'], writes=['pts'])
                        for w in range(4):
                            mm(P[2][0:4, 0:65], pts[:, 4 * w:4 * w + 4], Vw[:, w, g, :], w == 0, False, ['pts', 'Vw%d' % w, 'Vw_ones'], ['P2'])
                        mm(P[2][0:4, 0:65], pts[:, 16:20], Vwn[:, g, :], False, True, ['pts', 'Vnew'], ['P2'])
                        op('act', lambda e: e.copy(out=accs[:, 0:65], in_=P[2][0:4, 0:65]), reads=['P2'], writes=['accs'])
                        op('dve', lambda e: e.reciprocal(out=rec4[:], in_=accs[:, 64:65]), reads=['accs'], writes=['rec4'])
                        op('dve', lambda e: e.tensor_tensor(out=sc4[:], in0=rec4[:], in1=gcol[:, 2:3], op=ALU.mult),
                           reads=['rec4', 'gcol'], writes=['sc4'])
                        op('dve', lambda e: e.scalar_tensor_tensor(out=ocmb[:], in0=accs[:, 0:64], scalar=sc4[:, 0:1], in1=ocmb[:],
                                                                   op0=ALU.mult, op1=ALU.add),
                           reads=['accs', 'sc4', 'ocmb'], writes=['ocmb'])
                        dma('sp', lambda e, b=b, g=g: e.dma_start(out=ons_d[b, 4 * g:4 * g + 4, :], in_=ocmb[:]), reads=['ocmb'], writes=['ons'])

                op('pool', lambda e: e.memset(onsa[:, 0:2, :], 0.0), reads=['onsa'], writes=['onsa'])
                dma('sp', lambda e: e.dma_start(out=onsa[0:4, 0:2, :].rearrange("p a c -> p (a c)"),
                                                in_=ons_d.rearrange("b h d -> b (h d)")), reads=['ons', 'onsa'], writes=['onsa'])
                op('act', lambda e: e.copy(out=mixret[:], in_=onsa[:, 0:2, :].rearrange("p a c -> p (a c)")), reads=['onsa'], writes=['mixret'])
                for c in range(4):
                    transpose_to(mixTs[:, c, :], mixret[:, c * 128:(c + 1) * 128], 128, 128, ['mixret'], ['mixT'], evac='dve')
                load_wt(0, lambda kc: wo_d[kc, :, 0:512], 512)
                load_wt(1, lambda kc: wo_d[kc, :, 512:1024], 512)
                dma('sp', lambda e: e.dma_start(out=xo[:], in_=xs_own_d), writes=['stage', 'rtmp'])
                for half in range(2):
                    for c in range(8):
                        mm(P[half][:], mixTs[:, c, :], Wt[half][:, c, 0:512], c == 0, c == 7, ['mixT', 'Wt%d' % half], ['P%d' % half])
                    op('dve', lambda e, half=half: e.scalar_tensor_tensor(out=xr[:, half * 512:(half + 1) * 512],
                                                                          in0=xo[:, half * 512:(half + 1) * 512], scalar=ALPHA,
                                                                          in1=P[half][:], op0=ALU.mult, op1=ALU.add),
                       reads=['stage', 'rtmp', 'P%d' % half], writes=['orf', 'osq'])
                layer_norm(xo[:], xr[:], ln1[:, 0, :], ln1[:, 1, :], ['orf', 'osq'], ['stage', 'rtmp'], 'ln1')
                dma('sp', lambda e: e.dma_start(out=x1s_d[16], in_=xo[:]), reads=['stage', 'rtmp'], writes=['x1s16'])

        _stop(5)
        S.barrier()
        with contextlib.ExitStack() as stB:
            def TB(name, shape, dt=BF16):
                return stB.enter_context(nc.sbuf_tensor("s_" + name, list(shape), dt))
            Wup = TB("Wup", [128, 8, 4096])
            Wdn = TB("Wdn", [128, 32, D])
            for c4 in range(4):
                for kc in range(8):
                    dma('pool', lambda e, kc=kc, c4=c4: e.dma_start(out=Wup[:, kc, c4 * 1024:(c4 + 1) * 1024],
                                                                  in_=wup_d[kc, :, c4 * 1024:(c4 + 1) * 1024], max_dma_last_dim=4096),
                        writes=['Wup%d' % c4])
            for fc in range(32):
                dma('pool', lambda e, fc=fc: e.dma_start(out=Wdn[:, fc, :], in_=wdn_d[fc], max_dma_last_dim=4096),
                    writes=['Wdn%d' % (fc // 8)])
            ln2 = TB("ln2", [128, 2, D], F32)
            dma('sp', lambda e: e.dma_start(out=ln2[:, 0, :], in_=ln_d[2]), writes=['ln2'])
            dma('sp', lambda e: e.dma_start(out=ln2[:, 1, :], in_=ln_d[3]), writes=['ln2'])
            x1f = TB("x1f", [128, 4, D], F32)
            x1b = TB("x1b", [128, D])
            x1T = TB("x1T", [128, 8, 512])
            hr = [TB("hr%d" % i, [128, 512], F32) for i in range(2)]
            hT = TB("hT", [128, 32, 512])
            xr2 = TB("xr2", [128, D], F32)
            yo = TB("yo", [128, D], F32)
            for k in ([4] if os.environ.get('K_MLP') == 's' else range(int(os.environ.get('K_MLP', 5 if DO_SAMPLE else 4)))):
                nsub = 4 if k < 4 else 1
                NTOK = 128 * nsub
                for u in range(nsub):
                    dma('sp', lambda e, u=u: e.dma_start(out=x1f[:, u, :], in_=x1s_d[4 * k + u]),
                        reads=['x1s%d' % (4 * k + u)], writes=['x1f%d' % u])
                    op('pool', lambda e, u=u: e.tensor_copy(out=x1b[:], in_=x1f[:, u, :]), reads=['x1f%d' % u], writes=['x1b'])
                    for kc in range(8):
                        transpose_to(x1T[:, kc, u * 128:(u + 1) * 128], x1b[:, kc * 128:(kc + 1) * 128], 128, 128,
                                     ['x1b'], ['x1T'], evac=('act' if kc % 2 else 'dve'))
                for fc in range(32):
                    psi = fc % 2
                    for kc in range(8):
                        mm(P[psi][:, 0:NTOK], Wup[:, kc, fc * 128:(fc + 1) * 128], x1T[:, kc, 0:NTOK], kc == 0, kc == 7,
                           ['Wup%d' % (fc // 8), 'x1T'], ['P%d' % psi])
                    op('act', lambda e, psi=psi: e.activation(out=hr[psi][:, 0:NTOK], in_=P[psi][:, 0:NTOK], func=AF.Relu),
                       reads=['P%d' % psi], writes=['hr%d' % psi])
                    op('pool', lambda e, psi=psi, fc=fc: e.tensor_tensor(out=hT[:, fc, 0:NTOK], in0=hr[psi][:, 0:NTOK], in1=hr[psi][:, 0:NTOK], op=ALU.mult),
                       reads=['hr%d' % psi], writes=['hT'])
                for u in range(nsub):
                    for half in range(2):
                        pb = 2 + half
                        for fc in range(32):
                            mm(P[pb][:], hT[:, fc, u * 128:(u + 1) * 128], Wdn[:, fc, half * 512:(half + 1) * 512],
                               fc == 0, fc == 31, ['hT', 'Wdn%d' % (fc // 8)], ['P%d' % pb])
                        op('dve', lambda e, half=half, pb=pb, u=u: e.scalar_tensor_tensor(
                            out=xr2[:, half * 512:(half + 1) * 512], in0=x1f[:, u, half * 512:(half + 1) * 512], scalar=ALPHA,
                            in1=P[pb][:], op0=ALU.mult, op1=ALU.add),
                           reads=['x1f%d' % u, 'P%d' % pb], writes=['xr2'])
                    layer_norm(yo[:], xr2[:], ln2[:, 0, :], ln2[:, 1, :], ['xr2'], ['yo'], 'ln2')
                    dma('sp', lambda e, u=u, k=k: e.dma_start(out=(y_d[4 * k + u] if k < 4 else ys_d), in_=yo[:]), reads=['yo'])

        S.finish('sp')
        print("instructions:", S.ninstr, "sems:", S.nsem)
    return nc


_PERM = np.concatenate([np.arange(512, 1024), np.arange(1816, 2328), np.arange(2328, 2840),
                        np.arange(1024, 1280), np.arange(0, 512), np.arange(1280, 1304),
                        np.arange(1304, 1816), np.arange(2840, 3352)])


def _const_tables():
    f = np.float32
    key = np.arange(128)[:, None]
    tp = np.arange(896)[None, :] - 384
    mc = np.where(key <= tp, 0.0, NEGB)
    wl = np.where(tp < key, 0.0, NEGB)
    t = np.arange(512)[None, :]
    cm = np.where(t >= 16 * key - 1521, 0.0, NEGB)
    masks = np.concatenate([mc, wl, cm], axis=1).astype(f)
    tri = (np.arange(128)[None, :] >= np.arange(128)[:, None]).astype(f)
    E = (np.arange(4096)[None, :] // 64 == np.arange(64)[:, None]).astype(f)
    mimp = np.zeros((512, 128), f)
    for jb in range(128):
        for c, w in ((4 * jb - 1, 1.0), (4 * jb, 2.0), (4 * jb + 1, 2.0), (4 * jb + 2, 2.0), (4 * jb + 3, 1.0)):
            if 0 <= c + 1 < 512:
                mimp[c + 1, jb] = w
    mimp = mimp.reshape(4, 128, 128).transpose(1, 0, 2).copy()
    gam = 1.0 - 2.0 ** (-5.0 - np.arange(8, dtype=np.float64))
    tl = np.arange(128, dtype=np.float64)[:, None]
    dec = np.concatenate([gam[None, :] ** (tl + 1), gam[None, :] ** (-(tl + 1)) / 8.0], axis=1).astype(f)
    gC = np.zeros((128, 4, 64), f)
    for h in range(8):
        gC[(h % 2) * 64:(h % 2 + 1) * 64, h // 2, :] = gam[h] ** 128
    return dict(masks=masks, tri=tri, Eoh=E, mimp=mimp, dec=dec, gC=gC.reshape(128, 256), gam=gam)


def _rope_tabs(pos):
    invn = 500000.0 ** (-np.arange(8, dtype=np.float64) / 8)
    invr = 10000.0 ** (-np.arange(32, dtype=np.float64) / 32)
    an = pos[:, None] * invn[None, :]
    ar = pos[:, None] * invr[None, :]
    return np.concatenate([np.cos(an), np.sin(an), np.cos(ar), np.sin(ar)], axis=1).astype(np.float32)


def kernel(x_prompt, x_sample, cache_kv, cache_win, state_ret, page_table, w_in, w_cmp_k, w_cmp_v,
           pos_cmp_k, pos_cmp_v, ret_norm_g, w_o, ln1_g, ln1_b, w_up, w_down, ln2_g, ln2_b):
    f = np.float32
    asf = lambda a: np.ascontiguousarray(np.asarray(a), dtype=f)
    x_prompt = asf(x_prompt)
    ct = _const_tables()
    shared = dict(
        w_in=np.ascontiguousarray(asf(w_in)[0][:, _PERM].reshape(8, 128, 3352)),
        w_o=asf(w_o)[0].reshape(8, 128, D),
        w_up=asf(w_up)[0].reshape(8, 128, 4096),
        w_down=asf(w_down)[0].reshape(32, 128, D),
        wck=np.ascontiguousarray(asf(w_cmp_k)[0].transpose(1, 0, 2)),
        wcv=np.ascontiguousarray(asf(w_cmp_v)[0].transpose(1, 0, 2)),
        posk=np.ascontiguousarray(asf(pos_cmp_k)[0].T),
        posv=np.ascontiguousarray(asf(pos_cmp_v)[0].T),
        retg=np.ascontiguousarray(np.broadcast_to(asf(ret_norm_g)[0][None, :], (128, 512))),
        lnp=np.ascontiguousarray(np.stack([np.broadcast_to(asf(v)[0][None, :], (128, D))
                                           for v in (ln1_g, ln1_b, ln2_g, ln2_b)])),
        masks=ct['masks'], tri=ct['tri'], Eoh=ct['Eoh'], mimp=ct['mimp'], dec=ct['dec'], gC=ct['gC'],
    )
    gam = ct['gam']
    ohs = np.zeros((128, 4), f); ohs[np.arange(4), np.arange(4)] = 1.0
    kbself = np.full((128, 4), NEGB, f); kbself[np.arange(4), np.arange(4)] = 0.0
    kbws = np.zeros((128, 4), f); kbws[0, 0] = NEGB
    kbcs = np.zeros((128, 4), f); kbcs[0, 0] = NEGB
    bons = np.zeros((1, 128), f); bons[0, 0] = 1e4; bons[0, 127] = 1e4
    decs = np.ascontiguousarray(np.broadcast_to(ct['dec'][0:1, :], (128, 16)))
    gC1 = np.zeros((128, 4, 64), f)
    for h in range(8):
        gC1[(h % 2) * 64:(h % 2 + 1) * 64, h // 2, :] = gam[h]
    tabs_s = np.ascontiguousarray(np.broadcast_to(_rope_tabs(np.array([float(PAST)]))[0:1, :], (128, 80)))
    ckv = np.ascontiguousarray(np.asarray(cache_kv, dtype=f)).reshape(2560 * 128, 512)
    cwin = np.asarray(cache_win, dtype=f)[0].reshape(32, 512, 256)
    stin = np.asarray(state_ret, dtype=f)[0]
    ptab = np.asarray(page_table).astype(np.int32)
    xsamp = asf(x_sample)[:, 0, :]
    shared.update(ohs=ohs, kbself=kbself, kbws=kbws, kbcs=kbcs, bons=bons, decs=decs, gC1=gC1.reshape(128, 256),
                  tabs_s=tabs_s, cache_kv=ckv)
    in_maps = []
    for c in range(8):
        b, j = c // 4, c % 4
        off = 512 * (3 - j)
        xs = np.zeros((SEQ, D), f)
        xs[off:] = x_prompt[b, :SEQ - off]
        xT = np.ascontiguousarray(xs.T).reshape(8, 128, SEQ)
        xown = np.stack([xs[512 * (4 * k + 3):512 * (4 * k + 4)] for k in range(4)]).reshape(16, 128, D)
        sp = np.arange(SEQ)
        tpos = np.maximum(sp - off, 0).astype(np.float64)
        tabs = _rope_tabs(tpos).reshape(64, 128, 80)
        kbias = np.where(sp >= off, 0.0, NEGB).astype(f).reshape(64, 128).T.copy()
        cp = np.arange(512)
        kbc = np.where((cp - 1) >= 32 * (3 - j), 0.0, NEGB).astype(f).reshape(4, 128).T.copy()
        bonus = np.zeros((16, 128, 128), f)
        blk = np.arange(128)[None, :] - 8 * (3 - j)
        for k in range(4):
            for u in range(4):
                tt = 512 * (4 * k + 3) + 128 * u + np.arange(128) - off
                cur = (tt // 64)[:, None]
                forced = (blk == 0) | (blk == cur) | (blk == cur - 1)
                bo = np.where(forced, 1e4, 0.0)
                bo = np.where((blk < 0) | (blk > cur), -1e30, bo)
                bonus[4 * k + u] = bo
        m = dict(shared)
        m.update(xT=xT, xown=np.ascontiguousarray(xown), tabs=tabs, kbias=kbias, kbias_c=kbc, bonus=bonus)
        xs4 = np.zeros((128, D), f); xs4[0:4] = xsamp[4 * c:4 * c + 4]
        m.update(xsT=np.ascontiguousarray(xs4.T).reshape(8, 128, 128), xs_own=xs4,
                 cache_win=np.ascontiguousarray(cwin[4 * c:4 * c + 4]), state_in=np.ascontiguousarray(stin[4 * c:4 * c + 4]),
                 pt_rep=np.ascontiguousarray(np.broadcast_to(ptab[4 * c:4 * c + 4].reshape(1, 256), (128, 256))))
        in_maps.append(m)

    try:
        nc = build_program()
    except _Stop:
        nc = _CUR[0].nc
    res = run_bass_kernel_spmd(nc, in_maps, core_ids=list(range(8)))
    R = res.results

    y_prompt = np.zeros((2, SEQ, D), f)
    kv_prompt = np.zeros((1, 2, SEQ, 4, 2, 64), f)
    win_prompt = np.zeros((1, 2, 512, 2, 2, 64), f)
    ret_prompt = np.zeros((1, 2, 8, 64, 64), f)
    for c in range(8):
        b, j = c // 4, c % 4
        yo = R[c]["y_own"].reshape(4, 512, D)
        kvo = R[c]["kv_own"].reshape(4, 512, 4, 2, 64)
        for k in range(4):
            i = 4 * k + j
            y_prompt[b, 512 * i:512 * (i + 1)] = yo[k]
            kv_prompt[0, b, 512 * i:512 * (i + 1)] = kvo[k]
        if j == 3:
            win_prompt[0, b] = R[c]["win_out"].reshape(512, 2, 2, 64)
            ro = R[c]["ret_out"].reshape(2, 64, 4, 64)
            ret_prompt[0, b] = ro.transpose(2, 0, 1, 3).reshape(8, 64, 64)
    y_sample = np.zeros((32, 1, D), f)
    kv_sample = np.zeros((1, 32, 1, 4, 2, 64), f)
    win_sample = np.zeros((1, 32, 512, 2, 2, 64), f)
    ret_sample = np.zeros((1, 32, 8, 64, 64), f)
    for c in range(8):
        y_sample[4 * c:4 * c + 4, 0] = R[c]["y_s"][0:4]
        kv_sample[0, 4 * c:4 * c + 4, 0] = R[c]["kv_s"][0:4].reshape(4, 4, 2, 64)
        win_sample[0, 4 * c:4 * c + 4] = R[c]["win_s"].reshape(4, 512, 2, 2, 64)
        rs = R[c]["ret_s"].reshape(4, 2, 64, 4, 64)
        ret_sample[0, 4 * c:4 * c + 4] = rs.transpose(0, 3, 1, 2, 4).reshape(4, 8, 64, 64)
    return (y_prompt, y_sample, kv_prompt, kv_sample, win_prompt, win_sample, ret_prompt, ret_sample)
```

```python
import contextlib
import os
import numpy as np
import concourse.bass as bass
import concourse.mybir as mybir
from concourse.bass_utils import run_bass_kernel_spmd

F32 = mybir.dt.float32
BF16 = mybir.dt.bfloat16
I32 = mybir.dt.int32
U32 = mybir.dt.uint32
AF = mybir.ActivationFunctionType
ALU = mybir.AluOpType
AX = mybir.AxisListType

D = 1024
SEQ = 8192
NT = 16
ALPHA = 2.0 ** 0.25
BETA = 8.0 ** -0.25
LN_EPS = 1e-5
NEGB = -30000.0
PAST = 8192
DO_SAMPLE = True


class _Stop(Exception):
    pass


_CUR = [None]
_CUR_S = [-1]


def _stop(n):
    if int(os.environ.get('K_STOP', 999)) == n and int(os.environ.get('K_STOP_S', _CUR_S[0])) == _CUR_S[0]:
        _CUR[0].finish('sp')
        raise _Stop()


class Sched:
    EPOCH = 30000
    NDMA = 16

    def __init__(self, nc, stack):
        self.nc = nc
        self.stack = stack
        self.eng = {'pe': nc.tensor, 'act': nc.scalar, 'dve': nc.vector,
                    'pool': nc.gpsimd, 'sp': nc.sync}
        self.cur_sem = {}
        self.cnt = {}
        self.nsem = 0
        for e in ('pe', 'act', 'dve', 'pool'):
            self._new_epoch(e)
        self.dma_sems = [self._alloc_sem('dma%d' % i) for i in range(2 * self.NDMA)]
        self.dma_cnt = [0] * (2 * self.NDMA)
        self.dma_rr = {'sp': 0, 'pool': 0, 'act': 0}
        self.known = {e: {} for e in self.eng}
        self.last_w = {}
        self.readers = {}
        self.ninstr = 0

    def _alloc_sem(self, name):
        self.nsem += 1
        return self.stack.enter_context(self.nc.semaphore('%s_%d' % (name, self.nsem)))

    def _new_epoch(self, e):
        self.cur_sem[e] = self._alloc_sem('e_' + e)
        self.cnt[e] = 0

    def _wait(self, e, tok):
        if tok is None:
            return
        sem, val, src = tok
        if src == e and e == 'pe':
            return
        k = self.known[e]
        if k.get(id(sem), 0) >= val:
            return
        self.eng[e].wait_ge(sem, val)
        k[id(sem)] = val

    def _deps(self, e, reads, writes):
        for b in reads:
            self._wait(e, self.last_w.get(b))
            if b[0] == 'P':
                for t in self.readers.get(b, ()):
                    if t[2] != e:
                        self._wait(e, t)
        for b in writes:
            self._wait(e, self.last_w.get(b))
            for t in self.readers.get(b, ()):
                self._wait(e, t)

    def _commit(self, tok, reads, writes):
        for b in reads:
            self.readers.setdefault(b, []).append(tok)
        for b in writes:
            self.last_w[b] = tok
            self.readers[b] = []

    def op(self, e, fn, reads=(), writes=()):
        self._deps(e, reads, writes)
        if self.cnt[e] >= self.EPOCH:
            self._new_epoch(e)
        ins = fn(self.eng[e])
        self.cnt[e] += 1
        sem = self.cur_sem[e]
        ins.then_inc(sem, 1)
        tok = (sem, self.cnt[e], e)
        self._commit(tok, reads, writes)
        self.ninstr += 1
        return tok

    def dma(self, q, fn, reads=(), writes=()):
        self._deps(q, reads, writes)
        i = self.dma_rr[q] + (self.NDMA if q == 'pool' else 0)
        self.dma_rr[q] = (self.dma_rr[q] + 1) % self.NDMA
        sem = self.dma_sems[i]
        if self.dma_cnt[i] > 0:
            self._wait(q, (sem, 16 * self.dma_cnt[i], 'dma'))
        ins = fn(self.eng[q])
        self.dma_cnt[i] += 1
        ins.then_inc(sem, 16)
        tok = (sem, 16 * self.dma_cnt[i], 'dma')
        self._commit(tok, reads, writes)
        self.ninstr += 1
        return tok

    def barrier(self):
        for e in ('sp', 'pool', 'act', 'dve', 'pe'):
            self.finish(e)

    def finish(self, e='sp'):
        for i, sem in enumerate(self.dma_sems):
            if self.dma_cnt[i]:
                self._wait(e, (sem, 16 * self.dma_cnt[i], 'dma'))
        for x in ('pe', 'act', 'dve', 'pool'):
            if self.cnt[x]:
                self._wait(e, (self.cur_sem[x], self.cnt[x], x))


C_KVA, C_RK, C_RV, C_WIN, C_Q, C_GT, C_RQ, C_RG = 0, 512, 1024, 1536, 1792, 2304, 2328, 2840


def build_program():
    nc = bass.Bass("TRN2", target_bir_lowering=False)

    def din(name, shape, dt=F32):
        return nc.dram_tensor(name, list(shape), dt, kind="ExternalInput").ap()

    def dout(name, shape, dt=F32):
        return nc.dram_tensor(name, list(shape), dt, kind="ExternalOutput").ap()

    xT_d = din("xT", [8, 128, SEQ])
    xown_d = din("xown", [16, 128, D])
    tabs_d = din("tabs", [64, 128, 80])
    dec_d = din("dec", [128, 16])
    gC_d = din("gC", [128, 256])
    win_d = din("w_in", [8, 128, 3352])
    wo_d = din("w_o", [8, 128, D])
    wup_d = din("w_up", [8, 128, 4096])
    wdn_d = din("w_down", [32, 128, D])
    wck_d = din("wck", [64, 32, 64])
    wcv_d = din("wcv", [64, 32, 64])
    posk_d = din("posk", [64, 32])
    posv_d = din("posv", [64, 32])
    retg_d = din("retg", [128, 512])
    ln_d = din("lnp", [4, 128, D])
    kbias_d = din("kbias", [128, 64])
    kbc_d = din("kbias_c", [128, 4])
    bonus_d = din("bonus", [16, 128, 128])
    masks_d = din("masks", [128, 2304])
    tri_d = din("tri", [128, 128])
    E_d = din("Eoh", [64, 4096])
    mimp_d = din("mimp", [128, 4, 128])

    y_d = dout("y_own", [16, 128, D])
    kv_d = dout("kv_own", [16, 128, 512])
    wout_d = dout("win_out", [4, 128, 256])
    ret_d = dout("ret_out", [128, 256])
    x1s_d = nc.dram_tensor("x1_scratch", [17, 128, D], F32, kind="Internal").ap()

    oh_d = din("ohs", [128, 4]); kbself_d = din("kbself", [128, 4]); kbws_d = din("kbws", [128, 4]); kbcs_d = din("kbcs", [128, 4])
    bons_d = din("bons", [1, 128]); decs_d = din("decs", [128, 16]); gC1_d = din("gC1", [128, 256])
    pt_d = din("pt_rep", [128, 256], I32)
    xsT_d = din("xsT", [8, 128, 128]); tabs_s_d = din("tabs_s", [128, 80]); xs_own_d = din("xs_own", [128, D])
    ckv_d = din("cache_kv", [2560 * 128, 512]); cwin_d = din("cache_win", [4, 512, 256]); stin_d = din("state_in", [4, 8, 64, 64])
    kvs_d = dout("kv_s", [128, 512]); wins_d = dout("win_s", [4, 512, 256]); rets_d = dout("ret_s", [4, 128, 256]); ys_d = dout("y_s", [128, D])
    ons_d = nc.dram_tensor("ons_scratch", [4, 8, 64], F32, kind="Internal").ap()

    with contextlib.ExitStack() as st:
        S = Sched(nc, st)
        _CUR[0] = S
        op, dma = S.op, S.dma

        def T(name, shape, dt=BF16):
            return st.enter_context(nc.sbuf_tensor("s_" + name, list(shape), dt))

        P = [st.enter_context(nc.psum_tensor("P%d" % i, [128, 512], F32)) for i in range(6)]
        PTb = [st.enter_context(nc.psum_tensor("PTr%d" % i, [128, 8, 128], BF16)) for i in range(2)]
        tr_rr = [0]

        def mm(out, lhsT, rhs, start, stop, reads, writes, **kw):
            return op('pe', lambda e: e.matmul(out, lhsT=lhsT, rhs=rhs, start=start, stop=stop, **kw),
                      reads=reads, writes=writes)

        def bc(ap, shape):
            return ap.to_broadcast(list(shape))

        ident = T("ident", [128, 128])
        op('pool', lambda e: e.memset(ident[:], 1.0), writes=['ident'])
        op('pool', lambda e: e.affine_select(out=ident[:], in_=ident[:], pattern=[[-1, 128]],
                                             compare_op=ALU.is_equal, fill=0.0, base=0,
                                             channel_multiplier=1), reads=['ident'], writes=['ident'])
        identf = T("identf", [128, 128], F32)
        op('act', lambda e: e.copy(out=identf[:], in_=ident[:]), reads=['ident'], writes=['identf'])
        eps_t = T("eps_t", [128, 1], F32)
        op('pool', lambda e: e.memset(eps_t[:], LN_EPS), writes=['eps'])

        def transpose_to(dst, src, rows, cols, reads, writes, evac='act'):
            slot = tr_rr[0]
            tr_rr[0] = (slot + 1) % 2
            pst = PTb[slot][0:cols, 0, 0:rows]
            nm = 'PTr%d' % slot
            op('pe', lambda e: e.transpose(pst, src, ident[0:rows, 0:rows]),
               reads=list(reads) + ['ident'], writes=[nm])
            if evac == 'act':
                op('act', lambda e: e.copy(out=dst, in_=pst), reads=[nm], writes=writes)
            else:
                op('dve', lambda e: e.tensor_copy(out=dst, in_=pst), reads=[nm], writes=writes)

        def tbatch(srcs, reads):
            bank = tr_rr[0]
            tr_rr[0] = (bank + 1) % 2
            nm = 'PTr%d' % bank
            for i, src in enumerate(srcs):
                op('pe', lambda e, i=i, src=src: e.transpose(PTb[bank][0:64, i, :], src, ident[:]),
                   reads=list(reads) + ['ident'], writes=[nm])
            return PTb[bank], nm

        lnst = T("lnst", [128, 2, 6], F32)
        lnmv = T("lnmv", [128, 2], F32)
        lnrs = T("lnrs", [128, 1], F32)

        def layer_norm(dst, src, gtab, btab, sname, dname, tname):
            for c in range(2):
                op('dve', lambda e, c=c: e.bn_stats(out=lnst[:, c, :], in_=src[:, c * 512:(c + 1) * 512]),
                   reads=sname, writes=['lnst'])
            op('dve', lambda e: e.bn_aggr(out=lnmv[:], in_=lnst[:]), reads=['lnst'], writes=['lnmv'])
            op('act', lambda e: e.activation(out=lnrs[:], in_=lnmv[:, 1:2], func=AF.Sqrt, bias=eps_t[:, 0:1], scale=1.0),
               reads=['lnmv', 'eps'], writes=['lnrs'])
            op('dve', lambda e: e.reciprocal(out=lnrs[:], in_=lnrs[:]), reads=['lnrs'], writes=['lnrs'])
            op('dve', lambda e: e.tensor_scalar(out=dst, in0=src, scalar1=lnmv[:, 0:1], scalar2=lnrs[:, 0:1],
                                                op0=ALU.subtract, op1=ALU.mult),
               reads=sname + ['lnmv', 'lnrs'], writes=dname)
            op('dve', lambda e: e.tensor_tensor(out=dst, in0=dst, in1=gtab, op=ALU.mult), reads=dname + [tname], writes=dname)
            op('dve', lambda e: e.tensor_tensor(out=dst, in0=dst, in1=btab, op=ALU.add), reads=dname + [tname], writes=dname)

        _stop(1)
        with contextlib.ExitStack() as stA:
            def TA(name, shape, dt=BF16):
                return stA.enter_context(nc.sbuf_tensor("s_" + name, list(shape), dt))

            kbias = TA("kbias", [128, 64], F32)
            dma('sp', lambda e: e.dma_start(out=kbias[:], in_=kbias_d), writes=['kbias'])
            kbc = TA("kbc", [128, 4], F32)
            dma('sp', lambda e: e.dma_start(out=kbc[:], in_=kbc_d), writes=['kbc'])
            dec = TA("dec", [128, 16], F32)
            dma('sp', lambda e: e.dma_start(out=dec[:], in_=dec_d), writes=['dec'])
            gC = TA("gC", [128, 256], F32)
            dma('sp', lambda e: e.dma_start(out=gC[:], in_=gC_d), writes=['gC'])
            retg = TA("retg", [128, 512], F32)
            dma('sp', lambda e: e.dma_start(out=retg[:], in_=retg_d), writes=['retg'])

            def causal(o):
                return masks[:, 384 - 128 * o:896 - 128 * o]

            def wlo(o):
                return masks[:, 896 + 384 - 128 * o:896 + 896 - 128 * o]

            _stop(2)
            Wkv = TA("Wkv", [128, 8, 1792])
            for kc in range(8):
                dma('pool', lambda e, kc=kc: e.dma_start(out=Wkv[:, kc, :], in_=win_d[kc, :, 0:1792],
                                                         max_dma_last_dim=4096), writes=['Wkv'])
            Wt = [TA("WtA", [128, 8, 536]), TA("WtB", [128, 8, 536])]

            def load_wt(i, src_fn, ncols):
                for kc in range(8):
                    dma('pool', lambda e, kc=kc: e.dma_start(out=Wt[i][:, kc, 0:ncols], in_=src_fn(kc),
                                                             max_dma_last_dim=4096), writes=['Wt%d' % i])

            wck = TA("wck", [64, 32, 64]); wcv = TA("wcv", [64, 32, 64])
            dma('pool', lambda e: e.dma_start(out=wck[:], in_=wck_d, max_dma_last_dim=4096), writes=['wck'])
            dma('pool', lambda e: e.dma_start(out=wcv[:], in_=wcv_d, max_dma_last_dim=4096), writes=['wcv'])
            posk = TA("posk", [64, 32]); posv = TA("posv", [64, 32])
            dma('pool', lambda e: e.dma_start(out=posk[:], in_=posk_d), writes=['posk'])
            dma('pool', lambda e: e.dma_start(out=posv[:], in_=posv_d), writes=['posv'])
            ln1 = TA("ln1", [128, 2, D], F32)
            dma('sp', lambda e: e.dma_start(out=ln1[:, 0, :], in_=ln_d[0]), writes=['ln1'])
            dma('sp', lambda e: e.dma_start(out=ln1[:, 1, :], in_=ln_d[1]), writes=['ln1'])

            KslT = TA("KslT", [128, 2, SEQ])
            for g in range(2):
                for r in range(2):
                    dma('pool', lambda e, g=g, r=r: e.dma_start(out=KslT[64:128, g, r * 4096:(r + 1) * 4096],
                                                              in_=E_d, max_dma_last_dim=4096), writes=['KslT_E'])
            Vsl = TA("Vsl", [128, 64, 2, 65])
            op('pool', lambda e: e.memset(Vsl[:, :, :, 64:65], 1.0), writes=['Vsl_ones'])
            KwT = TA("KwT", [128, 2, 1024])
            op('pool', lambda e: e.memset(KwT[:], 0.0), writes=['KwT%d' % i for i in range(8)])
            Vw = TA("Vw", [128, 8, 2, 65])
            op('pool', lambda e: e.memset(Vw[:, :, :, 64:65], 1.0), writes=['Vw_ones'])
            KcT = TA("KcT", [64, 2, 528]); VcT = TA("VcT", [64, 2, 528])
            op('pool', lambda e: e.memset(KcT[:], 0.0), writes=['KcT'])
            op('pool', lambda e: e.memset(VcT[:], 0.0), writes=['VcT'])
            kcT = TA("kcT", [128, 2, 512])
            op('pool', lambda e: e.memset(kcT[:], 0.0), writes=['kcT'])
            Rc = TA("Rc", [128, 4, 2, 193])
            op('pool', lambda e: e.memset(Rc[:], 0.0), writes=['Rc'])
            for g in range(2):
                dma('pool', lambda e, g=g: e.dma_start(out=Rc[:, :, g, 0:128], in_=mimp_d), writes=['Rc'])
            op('pool', lambda e: e.memset(Rc[:, :, :, 192:193], 1.0), reads=['Rc'], writes=['Rc'])
            pbk = TA("pbk", [64, 1], F32)
            pbv = TA("pbv", [1, 64])
            ones1 = TA("ones1", [1, 128])
            op('pool', lambda e: e.memset(ones1[:], 1.0), writes=['ones1'])
            for r in range(32):
                mm(P[5][0:64, 0:1], wck[:, r, :], posk[:, r:r + 1], r == 0, r == 31, ['wck', 'posk'], ['P5'])
            op('act', lambda e: e.copy(out=pbk[:], in_=P[5][0:64, 0:1]), reads=['P5'], writes=['pbk'])
            for r in range(32):
                mm(P[5][0:1, 64:128], posv[:, r:r + 1], wcv[:, r, :], r == 0, r == 31, ['wcv', 'posv', 'pbk'], ['P5'])
            op('act', lambda e: e.copy(out=pbv[:], in_=P[5][0:1, 64:128]), reads=['P5'], writes=['pbv'])

            _stop(3)
            tabs = TA("tabs", [128, 4, 80], F32)
            big0 = TA("big0", [128, 1024], F32)
            stage = big0[:, 0:512]
            kvb = TA("kvb", [128, 512])
            wstage = TA("wstage", [128, 256], F32)
            wb = TA("wb", [128, 256])
            rtmp = big0[:, 512:1024]
            ta = TA("ta", [128, 256], F32)
            tb = TA("tb", [128, 256], F32)
            ktil = TA("ktil", [128, 512])
            rvb = TA("rvb", [128, 512])
            Sst = TA("Sst", [128, 256], F32)
            op('pool', lambda e: e.memset(Sst[:], 0.0), writes=['Sst'])
            SbZ = [TA("SbZ%d" % i, [128, 256]) for i in range(2)]
            for i in range(2):
                op('pool', lambda e, i=i: e.memset(SbZ[i][:], 0.0), writes=['Sb'])
            stmp = TA("stmp", [128, 256], F32)
            gat = TA("gat", [128, 4, 24], F32)
            qtil = TA("qtil", [128, 512])
            qtilT = TA("qtilT", [128, 4, 128])
            kz = [TA("kz%d" % i, [128, 4, 128]) for i in range(2)]
            for i in range(2):
                op('pool', lambda e, i=i: e.memset(kz[i][:], 0.0), writes=['ktilT'])
            rgs = TA("rgs", [128, 512])
            pt_rr = [0]
            onsa = TA("onsa", [128, 4, 256], F32)
            onsab = TA("onsab", [128, 256])
            innb = TA("innb", [128, 8, 128])
            big1 = TA("big1", [128, 1024], F32)
            orf = big1[:, 0:512]
            osq = big1[:, 512:1024]
            gsm = TA("gsm", [128, 8], F32); gss = TA("gss", [128, 8], F32)
            gmu = TA("gmu", [128, 8], F32); grs = TA("grs", [128, 8], F32); gm2 = TA("gm2", [128, 8], F32)
            mixret = TA("mixret", [128, 512])
            rec = TA("rec", [128, 1], F32); scg = TA("scg", [128, 1], F32)
            xo = big0
            xr = big1

            def do_rope(dst3, src3, cos2, sin2, H, half, reads, writes):
                cb = bc(cos2.unsqueeze(1), [128, H, half])
                sb = bc(sin2.unsqueeze(1), [128, H, half])
                x1 = src3[:, :, 0:half]; x2 = src3[:, :, half:2 * half]
                A = ta[:, 0:H * half].rearrange("p (h d) -> p h d", h=H)
                B = tb[:, 0:H * half].rearrange("p (h d) -> p h d", h=H)
                rd = list(reads) + ['tabs']
                op('dve', lambda e: e.tensor_tensor(out=A, in0=x1, in1=cb, op=ALU.mult), reads=rd, writes=['ta'])
                _stop(201)
                op('dve', lambda e: e.tensor_tensor(out=B, in0=x2, in1=sb, op=ALU.mult), reads=rd, writes=['tb'])
                _stop(202)
                op('dve', lambda e: e.tensor_tensor(out=dst3[:, :, 0:half], in0=A, in1=B, op=ALU.subtract),
                   reads=['ta', 'tb'], writes=writes)
                _stop(203)
                op('dve', lambda e: e.tensor_tensor(out=A, in0=x1, in1=sb, op=ALU.mult), reads=rd, writes=['ta'])
                op('dve', lambda e: e.tensor_tensor(out=B, in0=x2, in1=cb, op=ALU.mult), reads=rd, writes=['tb'])
                op('dve', lambda e: e.tensor_tensor(out=dst3[:, :, half:2 * half], in0=A, in1=B, op=ALU.add),
                   reads=['ta', 'tb'], writes=writes)

            def rope_ip(buf3, cos2, sin2, H, half, bname):
                cb = bc(cos2.unsqueeze(1), [128, H, half])
                sb = bc(sin2.unsqueeze(1), [128, H, half])
                x1 = buf3[:, :, 0:half]; x2 = buf3[:, :, half:2 * half]
                A = ta[:, 0:H * half].rearrange("p (h d) -> p h d", h=H)
                B = tb[:, 0:H * half].rearrange("p (h d) -> p h d", h=H)
                C = stmp[:, 0:H * half].rearrange("p (h d) -> p h d", h=H)
                Dd = osq[:, 0:H * half].rearrange("p (h d) -> p h d", h=H)
                rd = [bname, 'tabs']
                op('dve', lambda e: e.tensor_tensor(out=A, in0=x1, in1=cb, op=ALU.mult), reads=rd, writes=['ta'])
                op('dve', lambda e: e.tensor_tensor(out=B, in0=x2, in1=sb, op=ALU.mult), reads=rd, writes=['tb'])
                op('dve', lambda e: e.tensor_tensor(out=C, in0=x1, in1=sb, op=ALU.mult), reads=rd, writes=['stmp'])
                op('dve', lambda e: e.tensor_tensor(out=Dd, in0=x2, in1=cb, op=ALU.mult), reads=rd, writes=['osq'])
                op('dve', lambda e: e.tensor_tensor(out=x1, in0=A, in1=B, op=ALU.subtract), reads=['ta', 'tb', bname], writes=[bname])
                op('dve', lambda e: e.tensor_tensor(out=x2, in0=C, in1=Dd, op=ALU.add), reads=['stmp', 'osq', bname], writes=[bname])

            xsel = [None]

            def projn(pi, u, W, c0, c1, wname):
                xt = xsel[0] if xsel[0] is not None else xTb
                for kc in range(8):
                    mm(P[pi][:, 0:c1 - c0], xt[:, kc, u * 128:(u + 1) * 128], W[:, kc, c0:c1],
                       kc == 0, kc == 7, [('xTb' if xsel[0] is not None else 'xTb%d' % u), wname], ['P%d' % pi])

            def exp_pt(psi, c0, c1, bias_ap, breads):
                i = pt_rr[0]; pt_rr[0] = (i + 1) % 3
                pt = PT3[i]
                op('act', lambda e: e.activation(out=pt[:, c0:c1], in_=P[psi][:, c0:c1], func=AF.Exp, bias=bias_ap, scale=0.125),
                   reads=['P%d' % psi] + breads, writes=[('ptile%d' % i) if i < 2 else 'mixret'])
                return pt, (('ptile%d' % i) if i < 2 else 'mixret')

            def ot_finish(pacc, h4, h, br):
                op('act', lambda e: e.copy(out=orf[0:65, :], in_=P[pacc][0:65, :]), reads=['P%d' % pacc], writes=['orf'])
                for u in range(4):
                    op('pe', lambda e, u=u: e.transpose(P[5][:, u * 66:u * 66 + 65], orf[0:65, u * 128:(u + 1) * 128], identf[0:65, 0:65]),
                       reads=['orf', 'identf'], writes=['P5'])
                finish_branch(lambda u: P[5][:, u * 66:u * 66 + 65], lambda u: 'P5', h4, h, br, False, 64)

            def pipeline(n_items, stage1, stage2, depth=2):
                q = []
                for i in range(n_items):
                    q.append(stage1(i))
                    if len(q) > depth:
                        stage2(*q.pop(0))
                while q:
                    stage2(*q.pop(0))

            def finish_branch(pacc, pn, h4, h, br, first, ow):
                for u in range(4):
                    pa = pacc(u)
                    op('dve', lambda e, pa=pa: e.tensor_scalar(out=rec[:], in0=pa[:, ow:ow + 1], scalar1=1e-30, scalar2=None, op0=ALU.max),
                       reads=[pn(u)], writes=['rec'])
                    op('dve', lambda e: e.reciprocal(out=rec[:], in_=rec[:]), reads=['rec'], writes=['rec'])
                    op('dve', lambda e, u=u: e.tensor_tensor(out=scg[:], in0=rec[:], in1=gat[:, u, h * 3 + br:h * 3 + br + 1], op=ALU.mult),
                       reads=['rec', 'gat'], writes=['scg'])
                    dst = onsa[:, u, h4 * 64:(h4 + 1) * 64]
                    if first:
                        op('dve', lambda e, pa=pa, dst=dst: e.tensor_scalar(out=dst, in0=pa[:, ow - 64:ow], scalar1=scg[:, 0:1],
                                                                             scalar2=None, op0=ALU.mult),
                           reads=[pn(u), 'scg'], writes=['onsa'])
                    else:
                        op('dve', lambda e, pa=pa, dst=dst: e.scalar_tensor_tensor(out=dst, in0=pa[:, ow - 64:ow], scalar=scg[:, 0:1],
                                                                                    in1=dst, op0=ALU.mult, op1=ALU.add),
                           reads=[pn(u), 'scg', 'onsa'], writes=['onsa'])
                    if br == 0:
                        dsti = impacc[:, u, :]
                        if h4 == 0:
                            op('dve', lambda e, pa=pa, dsti=dsti: e.tensor_scalar(out=dsti, in0=pa[:, 0:128], scalar1=rec[:, 0:1],
                                                                                   scalar2=None, op0=ALU.mult),
                               reads=[pn(u), 'rec'], writes=['impacc'])
                        else:
                            op('dve', lambda e, pa=pa, dsti=dsti: e.scalar_tensor_tensor(out=dsti, in0=pa[:, 0:128], scalar=rec[:, 0:1],
                                                                                          in1=dsti, op0=ALU.mult, op1=ALU.add),
                               reads=[pn(u), 'rec', 'impacc'], writes=['impacc'])

            def compress_tile(s):
                    ctp, row0 = s // 4, 32 * (s % 4)
                    tp96 = {'tile_position': (0, 96)} if row0 == 96 else {}
                    for r in range(32):
                        mm(P[4][0:64, 0:64].rearrange("p (g n) -> p g n", g=2), wck[:, r, :], KcT[:, :, r:r + 497:16], r == 0, r == 31,
                           ['wck', 'KcT'], ['P4'])
                    op('act', lambda e: e.activation(out=kcT[0:64, :, 32 * s:32 * s + 32],
                                                     in_=P[4][0:64, 0:64].rearrange("p (g n) -> p g n", g=2),
                                                     func=AF.Identity, bias=pbk[:, 0:1], scale=1.0),
                       reads=['P4', 'pbk'], writes=['kcT'])
                    for g in range(2):
                        for r in range(33):
                            if r < 32:
                                mm(P[5][row0:row0 + 32, g * 64:(g + 1) * 64], VcT[:, g, r:r + 497:16], wcv[:, r, :],
                                   r == 0, False, ['wcv', 'VcT'], ['P5'], **tp96)
                            else:
                                mm(P[5][row0:row0 + 32, g * 64:(g + 1) * 64], ones1[0:1, 0:32], pbv[0:1, :],
                                   False, True, ['ones1', 'pbv'], ['P5'], **tp96)
                        op('act', lambda e, g=g: e.copy(out=Rc[row0:row0 + 32, ctp, g, 128:192],
                                                         in_=P[5][row0:row0 + 32, g * 64:(g + 1) * 64]),
                           reads=['P5'], writes=['Rc'])
                    op('pool', lambda e: e.tensor_copy(out=KcT[:, :, 0:16], in_=KcT[:, :, 512:528]), reads=['KcT'], writes=['KcT'])
                    op('pool', lambda e: e.tensor_copy(out=VcT[:, :, 0:16], in_=VcT[:, :, 512:528]), reads=['VcT'], writes=['VcT'])


            stP = contextlib.ExitStack()

            def TP(name, shape, dt=BF16):
                return stP.enter_context(nc.sbuf_tensor("s_" + name, list(shape), dt))
            masks = TP("masks", [128, 2304])
            dma('pool', lambda e: e.dma_start(out=masks[:], in_=masks_d, max_dma_last_dim=4096), writes=['masks'])
            tri = TP("tri", [128, 128], F32)
            dma('sp', lambda e: e.dma_start(out=tri[:], in_=tri_d), writes=['tri'])
            Qa = TP("Qa", [128, 2, 4, 512])
            op('pool', lambda e: e.memset(Qa[:], 0.0), writes=['Qa0', 'Qa1'])
            selT = TP("selT", [128, 2, 512])
            PTt = [TP("PT%d" % i, [128, 512]) for i in range(2)]
            impacc = TP("impacc", [128, 4, 128], F32)
            bon = TP("bon", [128, 128], F32)
            score = TP("score", [128, 128], F32)
            sc2 = TP("sc2", [128, 128], F32)
            m8a = TP("m8a", [128, 8], F32); m8b = TP("m8b", [128, 8], F32)
            selb = TP("selb", [128, 256])
            qb = TP("qb", [128, 4, 512])
            onrm = TP("onrm", [128, 4, 512])
            xTb = TP("xTb", [128, 8, 512])
            mixT = TP("mixT", [128, 8, 512])
            _stop(4)
            load_wt(0, lambda kc: win_d[kc, :, C_RQ:C_RQ + 512], 512)

            PT3 = [PTt[0], PTt[1], mixret]
            SB3 = [0, 1, 4]
            kvbL = [(kvb, 'kvb'), (mixret, 'mixret')]
            ktilL = [(ktil, 'ktil'), (PTt[0], 'ptile0')]
            rvbL = [(rvb, 'rvb'), (rgs, 'rgs')]
            wbL = [(wb, 'wb'), (onsab, 'onsab')]
            for s in range(int(os.environ.get('K_NT', NT))):
                own = (s % 4 == 3)
                _CUR_S[0] = s
                k = s // 4
                wtile = (s % 4 in (2, 3))
                wr = s % 2
                def load_x(sn):
                    for uu in range(4):
                        dma('pool', lambda e, uu=uu: e.dma_start(
                            out=xTb[:, :, uu * 128:(uu + 1) * 128],
                            in_=xT_d[:, :, sn * 512 + uu * 128:sn * 512 + (uu + 1) * 128].rearrange("k p t -> p k t")),
                            writes=['xTb%d' % uu])
                if s == 0 or (s % 4 == 0):
                    load_x(s)
                dma('sp', lambda e: e.dma_start(out=tabs[:], in_=tabs_d[4 * s:4 * s + 4].rearrange("u p c -> p u c")),
                    writes=['tabs'])
                _stop(10)
                def part1(u):
                        kt = 4 * s + u
                        cosN = tabs[:, u, 0:8]; sinN = tabs[:, u, 8:16]
                        cosR = tabs[:, u, 16:48]; sinR = tabs[:, u, 48:80]
                        kvb, kvbn = kvbL[u % 2]
                        ktil, ktiln = ktilL[u % 2]
                        rvb, rvbn = rvbL[u % 2]
                        wb, wbn = wbL[u % 2]
                        projn(0, u, Wkv, C_KVA, C_KVA + 512, 'Wkv')
                        op('act', lambda e: e.copy(out=stage[:], in_=P[0][:]), reads=['P0'], writes=['stage'])
                        rope_ip(stage[:, 256:384].rearrange("p (g d) -> p g d", g=2), cosN, sinN, 2, 8, 'stage')
                        if own:
                            dma('sp', lambda e, u=u: e.dma_start(out=kv_d[4 * k + u], in_=stage[:]), reads=['stage'])
                        op('act', lambda e: e.copy(out=kvb[:], in_=stage[:]), reads=['stage'], writes=[kvbn])
                        projn(1, u, Wkv, C_RK, C_RK + 512, 'Wkv')
                        projn(2, u, Wkv, C_RV, C_RV + 512, 'Wkv')
                        op('act', lambda e: e.copy(out=rtmp[:], in_=P[1][:]), reads=['P1'], writes=['rtmp'])
                        rope_ip(rtmp[:].rearrange("p (h d) -> p h d", h=8), cosR, sinR, 8, 32, 'rtmp')
                        op('dve', lambda e: e.tensor_tensor(out=ktil[:].rearrange("p (h d) -> p h d", h=8),
                                                            in0=rtmp[:].rearrange("p (h d) -> p h d", h=8),
                                                            in1=bc(dec[:, 8:16].unsqueeze(2), [128, 8, 64]), op=ALU.mult),
                           reads=['rtmp', 'dec'], writes=[ktiln])
                        op('act', lambda e: e.copy(out=rvb[:], in_=P[2][:]), reads=['P2'], writes=[rvbn])
                        if wtile:
                            projn(3, u, Wkv, C_WIN, C_WIN + 256, 'Wkv')
                            op('act', lambda e: e.copy(out=wstage[:], in_=P[3][:, 0:256]), reads=['P3'], writes=['wstage'])
                            rope_ip(wstage[:, 0:128].rearrange("p (g d) -> p g d", g=2), cosN, sinN, 2, 8, 'wstage')
                            if s == NT - 1:
                                dma('sp', lambda e, u=u: e.dma_start(out=wout_d[u], in_=wstage[:]), reads=['wstage'])
                            op('pool', lambda e: e.tensor_copy(out=wb[:], in_=wstage[:]), reads=['wstage'], writes=[wbn])

                def part2(u):
                        kt = 4 * s + u
                        cosN = tabs[:, u, 0:8]; sinN = tabs[:, u, 8:16]
                        cosR = tabs[:, u, 16:48]; sinR = tabs[:, u, 48:80]
                        kvb, kvbn = kvbL[u % 2]
                        ktil, ktiln = ktilL[u % 2]
                        rvb, rvbn = rvbL[u % 2]
                        wb, wbn = wbL[u % 2]
                        srcs = [kvb[:, i * 64:(i + 1) * 64] for i in range(6)]
                        rds = [kvbn]
                        if wtile:
                            srcs += [wb[:, 0:64], wb[:, 64:128]]
                            rds.append(wbn)
                        pbk_, pbn_ = tbatch(srcs, rds)
                        op('act', lambda e: e.copy(out=KcT[:, :, 16 + u * 128:16 + (u + 1) * 128], in_=pbk_[0:64, 0:2, :]),
                           reads=[pbn_], writes=['KcT'])
                        op('act', lambda e: e.copy(out=VcT[:, :, 16 + u * 128:16 + (u + 1) * 128], in_=pbk_[0:64, 2:4, :]),
                           reads=[pbn_], writes=['VcT'])
                        op('dve', lambda e: e.tensor_copy(out=KslT[0:64, :, kt * 128:(kt + 1) * 128], in_=pbk_[0:64, 4:6, :]),
                           reads=[pbn_], writes=['KslT%d' % kt])
                        op('pool', lambda e, kt=kt: e.tensor_copy(out=Vsl[:, kt, :, 0:64],
                                                                   in_=kvb[:, 384:512].rearrange("p (g d) -> p g d", g=2)),
                           reads=[kvbn], writes=['Vsl%d' % kt])
                        if wtile:
                            wkt = wr * 4 + u
                            op('act', lambda e: e.copy(out=KwT[0:64, :, wkt * 128:(wkt + 1) * 128], in_=pbk_[0:64, 6:8, :]),
                               reads=[pbn_], writes=['KwT%d' % wkt])
                            op('pool', lambda e, wkt=wkt: e.tensor_copy(out=Vw[:, wkt, :, 0:64],
                                                                         in_=wb[:, 128:256].rearrange("p (g d) -> p g d", g=2)),
                               reads=[wbn], writes=['Vw%d' % wkt])
                        if own:
                            op('act', lambda e: e.copy(out=SbZ[0][0:64, :], in_=Sst[0:64, :]), reads=['Sst'], writes=['Sb'])
                            op('act', lambda e: e.copy(out=SbZ[1][64:128, :], in_=Sst[64:128, :]), reads=['Sst'], writes=['Sb'])
                            projn(3, u, Wt[0], 0, 512, 'Wt0')
                            op('act', lambda e: e.copy(out=rtmp[:], in_=P[3][:]), reads=['P3'], writes=['rtmp'])
                            rope_ip(rtmp[:].rearrange("p (h d) -> p h d", h=8), cosR, sinR, 8, 32, 'rtmp')
                            op('dve', lambda e: e.tensor_tensor(out=qtil[:].rearrange("p (h d) -> p h d", h=8),
                                                                in0=rtmp[:].rearrange("p (h d) -> p h d", h=8),
                                                                in1=bc(dec[:, 0:8].unsqueeze(2), [128, 8, 64]), op=ALU.mult),
                               reads=['rtmp', 'dec'], writes=['qtil'])
                            for hp in range(4):
                                slot = tr_rr[0]; tr_rr[0] = (slot + 1) % 2
                                pst = PTb[slot][:, 0, :]
                                op('pe', lambda e, pst=pst, hp=hp: e.transpose(pst, ktil[:, hp * 128:(hp + 1) * 128], ident[:]),
                                   reads=[ktiln, 'ident'], writes=['PTr%d' % slot])
                                op('act', lambda e, pst=pst, hp=hp: e.copy(out=kz[0][0:64, hp, :], in_=pst[0:64, :]),
                                   reads=['PTr%d' % slot], writes=['ktilT'])
                                op('act', lambda e, pst=pst, hp=hp: e.copy(out=kz[1][64:128, hp, :], in_=pst[64:128, :]),
                                   reads=['PTr%d' % slot], writes=['ktilT'])
                                transpose_to(qtilT[:, hp, :], qtil[:, hp * 128:(hp + 1) * 128], 128, 128, ['qtil'], ['qtilT'], evac='dve')
                            for h in range(8):
                                hp, h2 = h // 2, h % 2
                                bp = 64 * h2
                                pb = 4 + h // 4
                                mm(P[pb][:, (h % 4) * 128:(h % 4 + 1) * 128], kz[h2][:, hp, :], qtilT[:, hp, :],
                                   True, True, ['ktilT', 'qtilT'], ['P%d' % pb])
                            for half in range(2):
                                op('dve', lambda e, half=half: e.tensor_tensor(
                                    out=innb[:, half * 4:(half + 1) * 4, :],
                                    in0=P[4 + half][:].rearrange("p (h i) -> p h i", h=4),
                                    in1=bc(tri[:].unsqueeze(1), [128, 4, 128]), op=ALU.mult),
                                   reads=['P%d' % (4 + half), 'tri'], writes=['innb%d' % half])
                            for h in range(8):
                                hp, h2 = h // 2, h % 2
                                bp = 64 * h2
                                mm(P[3][:, h * 64:(h + 1) * 64], innb[:, h, :], rvb[:, h * 64:(h + 1) * 64], True, False,
                                   ['innb%d' % (h // 4), rvbn, 'rtmp'], ['P3'])
                                mm(P[3][:, h * 64:(h + 1) * 64], qtilT[:, hp, :], SbZ[h2][:, hp * 64:(hp + 1) * 64],
                                   False, True, ['qtilT', 'Sb'], ['P3'])
                            op('act', lambda e: e.copy(out=orf[:], in_=P[3][:]), reads=['P3'], writes=['orf'])
                            orf3 = orf[:].rearrange("p (h d) -> p h d", h=8)
                            osq3 = osq[:].rearrange("p (h d) -> p h d", h=8)
                            op('dve', lambda e: e.tensor_reduce(out=gsm[:], in_=orf3, axis=AX.X, op=ALU.add), reads=['orf'], writes=['gsm'])
                            op('pool', lambda e: e.tensor_tensor(out=osq[:], in0=orf[:], in1=orf[:], op=ALU.mult), reads=['orf'], writes=['osq'])
                            op('dve', lambda e: e.tensor_reduce(out=gss[:], in_=osq3, axis=AX.X, op=ALU.add), reads=['osq'], writes=['gss'])
                            op('dve', lambda e: e.tensor_scalar(out=gmu[:], in0=gsm[:], scalar1=1.0 / 64, scalar2=None, op0=ALU.mult),
                               reads=['gsm'], writes=['gmu'])
                            op('dve', lambda e: e.tensor_tensor(out=gm2[:], in0=gmu[:], in1=gmu[:], op=ALU.mult), reads=['gmu'], writes=['gm2'])
                            op('dve', lambda e: e.scalar_tensor_tensor(out=grs[:], in0=gss[:], scalar=1.0 / 64, in1=gm2[:],
                                                                       op0=ALU.mult, op1=ALU.subtract),
                               reads=['gss', 'gm2'], writes=['grs'])
                            op('act', lambda e: e.activation(out=grs[:], in_=grs[:], func=AF.Sqrt, bias=eps_t[:, 0:1], scale=1.0),
                               reads=['grs', 'eps'], writes=['grs'])
                            op('dve', lambda e: e.reciprocal(out=grs[:], in_=grs[:]), reads=['grs'], writes=['grs'])
                            op('dve', lambda e: e.tensor_tensor(out=osq3, in0=orf3, in1=bc(gmu[:].unsqueeze(2), [128, 8, 64]), op=ALU.subtract),
                               reads=['orf', 'gmu', 'gss'], writes=['osq'])
                            op('dve', lambda e: e.tensor_tensor(out=osq3, in0=osq3, in1=bc(grs[:].unsqueeze(2), [128, 8, 64]), op=ALU.mult),
                               reads=['osq', 'grs'], writes=['osq'])
                            op('pool', lambda e, u=u: e.tensor_tensor(out=onrm[:, u, :], in0=osq[:], in1=retg[:], op=ALU.mult),
                               reads=['osq', 'retg'], writes=['onrm%d' % u])
                        for h in range(8):
                            hp, h2 = h // 2, h % 2
                            mm(P[5][h2 * 64:(h2 + 1) * 64, hp * 64:(hp + 1) * 64], ktil[:, h * 64:(h + 1) * 64],
                               rvb[:, h * 64:(h + 1) * 64], True, True, [ktiln, rvbn], ['P5'])
                        op('dve', lambda e: e.tensor_tensor(out=stmp[:], in0=P[5][:, 0:256], in1=Sst[:], op=ALU.add),
                           reads=['P5', 'Sst'], writes=['stmp'])
                        op('dve', lambda e: e.tensor_tensor(out=Sst[:], in0=stmp[:], in1=gC[:], op=ALU.mult),
                           reads=['stmp', 'gC', 'Sb'], writes=['Sst'])


                part1(0)
                for u in range(4):
                    if u + 1 < 4:
                        part1(u + 1)
                    elif (not own) and s + 1 < NT:
                        load_x(s + 1)
                    part2(u)

                _stop(15)
                compress_tile(s)
                _stop(16)
                if s == NT - 1:
                    dma('sp', lambda e: e.dma_start(out=ret_d, in_=Sst[:]), reads=['Sst'])
                if not own or os.environ.get('K_OWN', '1') == '0':
                    continue

                load_wt(1, lambda kc: win_d[kc, :, C_Q:C_Q + 536], 536)
                for u in range(4):
                    cosN = tabs[:, u, 0:8]; sinN = tabs[:, u, 8:16]
                    projn(0, u, Wt[1], 0, 512, 'Wt1')
                    projn(1, u, Wt[1], 512, 536, 'Wt1')
                    op('act', lambda e, u=u: e.copy(out=qb[:, u, :], in_=P[0][:]), reads=['P0'], writes=['qb'])
                    do_rope(qb[:, u, :].rearrange("p (h d) -> p h d", h=8), P[0][:].rearrange("p (h d) -> p h d", h=8),
                            cosN, sinN, 8, 8, ['P0'], ['qb'])
                    op('act', lambda e, u=u: e.activation(out=gat[:, u, :], in_=P[1][:, 0:24], func=AF.Sigmoid),
                       reads=['P1'], writes=['gat'])
                load_wt(0, lambda kc: win_d[kc, :, C_RG:C_RG + 512], 512)
                for u in range(4):
                    projn(2, u, Wt[0], 0, 512, 'Wt0')
                    op('act', lambda e: e.activation(out=rgs[:], in_=P[2][:], func=AF.Silu), reads=['P2'], writes=['rgs'])
                    op('pool', lambda e, u=u: e.tensor_tensor(out=mixret[:], in0=onrm[:, u, :], in1=rgs[:], op=ALU.mult),
                       reads=['onrm%d' % u, 'rgs'], writes=['mixret'])
                    for c in range(4):
                        transpose_to(mixT[:, 4 + c, u * 128:(u + 1) * 128], mixret[:, c * 128:(c + 1) * 128], 128, 128,
                                     ['mixret'], ['mixT'])

                for g in range(2):
                    for h4 in range(4):
                        h = 4 * g + h4
                        for u in range(4):
                            transpose_to(Qa[0:64, 0, h4, u * 128:(u + 1) * 128], qb[:, u, h * 64:(h + 1) * 64], 128, 64,
                                         ['qb'], ['Qa0'], evac='dve')
                    op('pool', lambda e: e.tensor_copy(out=Qa[0:64, 1, :, :], in_=Qa[0:64, 0, :, :]), reads=['Qa0'], writes=['Qa1'])
                    for h4 in range(4):
                        h = 4 * g + h4

                        def c_s1(ct, h4=h4):
                            psi = SB3[ct % 3]
                            mm(P[psi][:], kcT[:, g, ct * 128:(ct + 1) * 128], Qa[:, 0, h4, :], True, ct != k,
                               ['kcT', 'Qa0'], ['P%d' % psi])
                            if ct == k:
                                mm(P[psi][:], ident[:], masks[:, 1792:2304], False, True, ['ident', 'masks'], ['P%d' % psi])
                            return (ct,) + exp_pt(psi, 0, 512, kbc[:, ct:ct + 1], ['kbc'])

                        def c_s2(pct, pt, ptn):
                            for u in range(4):
                                pb = 2 + u // 2
                                mm(P[pb][:, (u % 2) * 193:(u % 2 + 1) * 193], pt[:, u * 128:(u + 1) * 128], Rc[:, pct, g, :],
                                   (pct == 0 and u % 2 == 0), pct == k, [ptn, 'Rc'], ['P%d' % pb], skip_group_check=True)
                        pipeline(k + 1, c_s1, c_s2)
                        finish_branch(lambda u: P[2 + u // 2][:, (u % 2) * 193:(u % 2 + 1) * 193],
                                      lambda u: 'P%d' % (2 + u // 2), h4, h, 0, True, 192)
                    for u in range(4):
                        dma('sp', lambda e, u=u: e.dma_start(out=bon[:], in_=bonus_d[4 * k + u]), writes=['bon'])
                        op('dve', lambda e, u=u: e.tensor_tensor(out=score[:], in0=impacc[:, u, :], in1=bon[:], op=ALU.add),
                           reads=['impacc', 'bon'], writes=['score'])
                        op('dve', lambda e: e.max(out=m8a[:], in_=score[:]), reads=['score'], writes=['m8a'])
                        op('dve', lambda e: e.match_replace(out=sc2[:], in_to_replace=m8a[:], in_values=score[:], imm_value=-3e38),
                           reads=['score', 'm8a'], writes=['sc2'])
                        op('dve', lambda e: e.max(out=m8b[:], in_=sc2[:]), reads=['sc2'], writes=['m8b'])
                        op('dve', lambda e: e.tensor_scalar(out=sc2[:], in0=score[:], scalar1=m8b[:, 7:8], scalar2=-1.0,
                                                            op0=ALU.is_ge, op1=ALU.add),
                           reads=['score', 'm8b'], writes=['sc2'])
                        op('dve', lambda e: e.tensor_scalar(out=selb[:, 0:128], in0=sc2[:], scalar1=-NEGB, scalar2=None, op0=ALU.mult),
                           reads=['sc2'], writes=['selb'])
                        op('pool', lambda e: e.tensor_copy(out=selb[:, 128:256], in_=selb[:, 0:128]), reads=['selb'], writes=['selb'])
                        transpose_to(selT[:, 1, u * 128:(u + 1) * 128], selb[:, 0:128], 128, 128, ['selb'], ['selT'])
                        transpose_to(selT[:, 0, u * 128:(u + 1) * 128], selb[:, 64:192], 128, 128, ['selb'], ['selT'])
                    for r in range(2):
                        for h4 in range(4):
                            op('pool', lambda e, r=r, h4=h4: e.tensor_copy(out=Qa[64:128, r, h4, :], in_=selT[64:128, r, :]),
                               reads=['selT'], writes=['Qa%d' % r])
                    nkt = 4 * s + 4
                    for h4 in range(4):
                        h = 4 * g + h4

                        def s_s1(kt, h4=h4):
                            r = kt // 32
                            o = kt - 4 * s
                            c0 = 128 * o if o > 0 else 0
                            psi = SB3[kt % 3]
                            mm(P[psi][:, c0:512], KslT[:, g, kt * 128:(kt + 1) * 128], Qa[:, r, h4, c0:512], True, o < 0,
                               ['KslT%d' % kt, 'KslT_E', 'Qa%d' % r], ['P%d' % psi])
                            if o >= 0:
                                mm(P[psi][:, c0:512], ident[:], causal(o)[:, c0:512], False, True, ['ident', 'masks'], ['P%d' % psi])
                            return (kt, o) + exp_pt(psi, c0, 512, kbias[:, kt:kt + 1], ['kbias'])

                        def s_s2(pkt, po, pt, ptn):
                            c0 = 128 * po if po > 0 else 0
                            mm(P[2][0:65, c0:512], Vsl[:, pkt, g, :], pt[:, c0:512], pkt == 0, pkt == nkt - 1,
                               [ptn, 'Vsl%d' % pkt, 'Vsl_ones'], ['P2'])
                        pipeline(nkt, s_s1, s_s2)
                        ot_finish(2, h4, h, 1)
                    for h4 in range(4):
                        h = 4 * g + h4

                        def w_s1(o8, h4=h4):
                            psi = SB3[o8 % 3]
                            if o8 < 4:
                                c0, c1 = 0, 128 * (o8 + 1)
                                msk = wlo(o8)
                            else:
                                c0, c1 = 128 * (o8 - 4), 512
                                msk = causal(o8 - 4)
                            mm(P[psi][:, c0:c1], KwT[:, g, o8 * 128:(o8 + 1) * 128], Qa[:, 0, h4, c0:c1], True, False,
                               ['KwT%d' % o8, 'Qa0'], ['P%d' % psi])
                            mm(P[psi][:, c0:c1], ident[:], msk[:, c0:c1], False, True, ['ident', 'masks'], ['P%d' % psi])
                            kt = 4 * (s - 1) + o8
                            return (o8, c0, c1) + exp_pt(psi, c0, c1, kbias[:, kt:kt + 1], ['kbias'])

                        def w_s2(p8, pc0, pc1, pt, ptn):
                            mm(P[3][0:65, pc0:pc1], Vw[:, p8, g, :], pt[:, pc0:pc1], p8 == 0, p8 == 7,
                               [ptn, 'Vw%d' % p8, 'Vw_ones'], ['P3'], skip_group_check=True)
                        pipeline(8, w_s1, w_s2)
                        ot_finish(3, h4, h, 2)
                    for u in range(4):
                        op('act', lambda e, u=u: e.copy(out=onsab[:], in_=onsa[:, u, :]), reads=['onsa'], writes=['onsab'])
                        for c2 in range(2):
                            transpose_to(mixT[:, 2 * g + c2, u * 128:(u + 1) * 128], onsab[:, c2 * 128:(c2 + 1) * 128], 128, 128,
                                         ['onsab'], ['mixT'], evac='dve')

                load_wt(0, lambda kc: wo_d[kc, :, 0:512], 512)
                load_wt(1, lambda kc: wo_d[kc, :, 512:1024], 512)
                for u in range(4):
                    dma('sp', lambda e, u=u: e.dma_start(out=xo[:], in_=xown_d[4 * k + u]), writes=['stage', 'rtmp'])
                    for half in range(2):
                        for c in range(8):
                            mm(P[half][:], mixT[:, c, u * 128:(u + 1) * 128], Wt[half][:, c, 0:512], c == 0, c == 7,
                               ['mixT', 'Wt%d' % half], ['P%d' % half])
                        op('dve', lambda e, half=half: e.scalar_tensor_tensor(out=xr[:, half * 512:(half + 1) * 512],
                                                                              in0=xo[:, half * 512:(half + 1) * 512], scalar=ALPHA,
                                                                              in1=P[half][:], op0=ALU.mult, op1=ALU.add),
                           reads=['stage', 'rtmp', 'P%d' % half], writes=['orf', 'osq'])
                    layer_norm(xo[:], xr[:], ln1[:, 0, :], ln1[:, 1, :], ['orf', 'osq'], ['stage', 'rtmp'], 'ln1')
                    dma('sp', lambda e, u=u: e.dma_start(out=x1s_d[4 * k + u], in_=xo[:]), reads=['stage', 'rtmp'], writes=['x1s%d' % (4 * k + u)])
                if k < 3:
                    load_wt(0, lambda kc: win_d[kc, :, C_RQ:C_RQ + 512], 512)

            S.barrier()
            stP.close()
            if DO_SAMPLE:
                u = 0
                xTbs = TA("xTbs", [128, 8, 128])
                xsel[0] = xTbs
                mixTs = TA("mixTs", [128, 8, 128])
                KcTs = TA("KcTs", [64, 2, 1040]); VcTs = TA("VcTs", [64, 2, 1040])

                def compress8(s8):
                    ctp, row0 = s8 // 2, 64 * (s8 % 2)
                    for r in range(32):
                        mm(P[4][0:64, 0:128].rearrange("p (g n) -> p g n", g=2), wck[:, r, :], KcTs[:, :, r:r + 1009:16], r == 0, r == 31,
                           ['wck', 'KcTs'], ['P4'])
                    op('act', lambda e: e.activation(out=kcT[0:64, :, 64 * s8:64 * s8 + 64],
                                                     in_=P[4][0:64, 0:128].rearrange("p (g n) -> p g n", g=2),
                                                     func=AF.Identity, bias=pbk[:, 0:1], scale=1.0),
                       reads=['P4', 'pbk'], writes=['kcT'])
                    for g in range(2):
                        for r in range(33):
                            if r < 32:
                                mm(P[5][row0:row0 + 64, g * 64:(g + 1) * 64], VcTs[:, g, r:r + 1009:16], wcv[:, r, :],
                                   r == 0, False, ['wcv', 'VcTs'], ['P5'])
                            else:
                                mm(P[5][row0:row0 + 64, g * 64:(g + 1) * 64], ones1[0:1, 0:64], pbv[0:1, :],
                                   False, True, ['ones1', 'pbv'], ['P5'])
                        op('act', lambda e, g=g: e.copy(out=Rc[row0:row0 + 64, ctp, g, 128:192],
                                                         in_=P[5][row0:row0 + 64, g * 64:(g + 1) * 64]),
                           reads=['P5'], writes=['Rc'])
                    op('pool', lambda e: e.tensor_copy(out=KcTs[:, :, 0:16], in_=KcTs[:, :, 1024:1040]), reads=['KcTs'], writes=['KcTs'])
                    op('pool', lambda e: e.tensor_copy(out=VcTs[:, :, 0:16], in_=VcTs[:, :, 1024:1040]), reads=['VcTs'], writes=['VcTs'])
                qbs = TA("qbs", [128, 1, 512])
                onrm_s = TA("onrm_s", [128, 1, 512])
                ohs = TA("ohs", [128, 4], F32)
                dma('sp', lambda e: e.dma_start(out=ohs[:], in_=oh_d), writes=['ohs'])
                kbself = TA("kbself", [128, 4], F32)
                dma('sp', lambda e: e.dma_start(out=kbself[:], in_=kbself_d), writes=['kbself'])
                kbws = TA("kbws", [128, 4], F32)
                dma('sp', lambda e: e.dma_start(out=kbws[:], in_=kbws_d), writes=['kbws'])
                kbcs = TA("kbcs", [128, 4], F32)
                dma('sp', lambda e: e.dma_start(out=kbcs[:], in_=kbcs_d), writes=['kbcs'])
                bons = TA("bons", [1, 128], F32)
                dma('sp', lambda e: e.dma_start(out=bons[:], in_=bons_d), writes=['bons'])
                decs = TA("decs", [128, 16], F32)
                dma('sp', lambda e: e.dma_start(out=decs[:], in_=decs_d), writes=['decs'])
                gC1 = TA("gC1", [128, 256], F32)
                dma('sp', lambda e: e.dma_start(out=gC1[:], in_=gC1_d), writes=['gC1'])
                zcol = TA("zcol", [128, 1], F32)
                op('pool', lambda e: e.memset(zcol[:], 0.0), writes=['zcol'])
                oneb = TA("oneb", [1, 1])
                op('pool', lambda e: e.memset(oneb[:], 1.0), writes=['oneb'])
                pti = TA("pti", [128, 256], I32)
                dma('sp', lambda e: e.dma_start(out=pti[:], in_=pt_d), writes=['pti'])
                ptf = TA("ptf", [128, 256], F32)
                iop = TA("iop", [128, 1], I32)
                iof = TA("iof", [128, 1], F32)
                idx = TA("idx", [128, 256], I32)
                op('pool', lambda e: e.iota(iop[:], pattern=[[0, 1]], base=0, channel_multiplier=1), writes=['iop'])
                op('dve', lambda e: e.tensor_copy(out=iof[:], in_=iop[:]), reads=['iop'], writes=['iof'])
                op('dve', lambda e: e.tensor_copy(out=ptf[:], in_=pti[:]), reads=['pti'], writes=['ptf'])
                op('dve', lambda e: e.tensor_scalar(out=ptf[:], in0=ptf[:], scalar1=128.0, scalar2=iof[:, 0:1],
                                                    op0=ALU.mult, op1=ALU.add), reads=['ptf', 'iof'], writes=['ptf'])
                op('dve', lambda e: e.tensor_copy(out=idx[:], in_=ptf[:]), reads=['ptf'], writes=['idx'])

                KnT = TA("KnT", [128, 2, 128]); Vn = TA("Vn", [128, 2, 65])
                KwnT = TA("KwnT", [128, 2, 128]); Vwn = TA("Vwn", [128, 2, 65])
                for t_ in (KnT, KwnT):
                    op('pool', lambda e, t_=t_: e.memset(t_[:], 0.0), writes=['Knew'])
                for t_ in (Vn, Vwn):
                    op('pool', lambda e, t_=t_: e.memset(t_[:, :, 64:65], 1.0), writes=['Vnew'])
                QTs = TA("QTs", [64, 8, 128])
                Qs = TA("Qs", [128, 2, 4])
                op('pool', lambda e: e.memset(Qs[:], 0.0), writes=['Qs'])
                pts = TA("pts", [128, 64])
                accs = TA("accs", [4, 193], F32)
                rec4 = TA("rec4", [4, 1], F32)
                gcol = TA("gcol", [4, 3], F32)
                sc4 = TA("sc4", [4, 1], F32)
                ocmb = TA("ocmb", [4, 64], F32)
                srow = TA("srow", [1, 128], F32)
                srow2 = TA("srow2", [1, 128], F32)
                s8a = TA("s8a", [1, 8], F32); s8b = TA("s8b", [1, 8], F32)
                selr = TA("selr", [1, 256])
                qz = [TA("qz%d" % b, [128, 4, 128]) for b in range(4)]
                kvbs = [TA("kvbs%d" % i, [128, 512]) for i in range(3)]
                SbS = [[TA("SbS%d_%d" % (b, i), [128, 256]) for i in range(2)] for b in range(4)]
                SsT = [TA("SsT%d" % b, [128, 256], F32) for b in range(4)]
                ktz = TA("ktz", [128, 512])

                for kc in range(8):
                    dma('pool', lambda e, kc=kc: e.dma_start(out=xTbs[:, kc, :], in_=xsT_d[kc]), writes=['xTb'])
                dma('sp', lambda e: e.dma_start(out=tabs[:, 0, :], in_=tabs_s_d), writes=['tabs'])
                cosN = tabs[:, 0, 0:8]; sinN = tabs[:, 0, 8:16]
                cosR = tabs[:, 0, 16:48]; sinR = tabs[:, 0, 48:80]
                load_wt(0, lambda kc: win_d[kc, :, C_RQ:C_RQ + 512], 512)
                projn(0, u, Wkv, C_KVA, C_KVA + 512, 'Wkv')
                op('act', lambda e: e.copy(out=stage[:], in_=P[0][:]), reads=['P0'], writes=['stage'])
                do_rope(stage[:, 256:384].rearrange("p (g d) -> p g d", g=2),
                        P[0][:, 256:384].rearrange("p (g d) -> p g d", g=2), cosN, sinN, 2, 8, ['P0', 'stage'], ['stage'])
                dma('sp', lambda e: e.dma_start(out=kvs_d, in_=stage[:]), reads=['stage'])
                op('pool', lambda e: e.tensor_copy(out=kvb[:], in_=stage[:]), reads=['stage'], writes=['kvb'])
                for g in range(2):
                    transpose_to(KnT[0:64, g, :], kvb[:, 256 + g * 64:256 + (g + 1) * 64], 128, 64, ['kvb'], ['Knew'])
                op('pool', lambda e: e.tensor_copy(out=Vn[:, :, 0:64], in_=kvb[:, 384:512].rearrange("p (g d) -> p g d", g=2)),
                   reads=['kvb'], writes=['Vnew'])
                projn(1, u, Wkv, C_RK, C_RK + 512, 'Wkv')
                projn(2, u, Wkv, C_RV, C_RV + 512, 'Wkv')
                op('act', lambda e: e.copy(out=rtmp[:], in_=P[1][:]), reads=['P1'], writes=['rtmp'])
                do_rope(rtmp[:].rearrange("p (h d) -> p h d", h=8), P[1][:].rearrange("p (h d) -> p h d", h=8),
                        cosR, sinR, 8, 32, ['P1'], ['rtmp'])
                op('dve', lambda e: e.tensor_tensor(out=ktil[:].rearrange("p (h d) -> p h d", h=8),
                                                    in0=rtmp[:].rearrange("p (h d) -> p h d", h=8),
                                                    in1=bc(decs[:, 8:16].unsqueeze(2), [128, 8, 64]), op=ALU.mult),
                   reads=['rtmp', 'decs'], writes=['ktil'])
                op('act', lambda e: e.copy(out=rvb[:], in_=P[2][:]), reads=['P2'], writes=['rvb'])
                projn(3, u, Wkv, C_WIN, C_WIN + 256, 'Wkv')
                op('act', lambda e: e.copy(out=wstage[:], in_=P[3][:, 0:256]), reads=['P3'], writes=['wstage'])
                do_rope(wstage[:, 0:128].rearrange("p (g d) -> p g d", g=2),
                        P[3][:, 0:128].rearrange("p (g d) -> p g d", g=2), cosN, sinN, 2, 8, ['P3', 'wstage'], ['wstage'])
                for b in range(4):
                    dma('sp', lambda e, b=b: e.dma_start(out=wins_d[b, 511:512, :], in_=wstage[b:b + 1, :]), reads=['wstage'])
                op('pool', lambda e: e.tensor_copy(out=wb[:], in_=wstage[:]), reads=['wstage'], writes=['wb'])
                for g in range(2):
                    transpose_to(KwnT[0:64, g, :], wb[:, g * 64:(g + 1) * 64], 128, 64, ['wb'], ['Knew'])
                op('pool', lambda e: e.tensor_copy(out=Vwn[:, :, 0:64], in_=wb[:, 128:256].rearrange("p (g d) -> p g d", g=2)),
                   reads=['wb'], writes=['Vnew'])
                for b in range(4):
                    for h2 in range(2):
                        dma('sp', lambda e, b=b, h2=h2: e.dma_start(
                            out=SsT[b][h2 * 64:(h2 + 1) * 64, :].rearrange("p (hp e) -> p hp e", hp=4),
                            in_=stin_d[b].rearrange("(hp h2) d e -> h2 d hp e", h2=2)[h2]), writes=['SsT%d' % b])
                    op('act', lambda e, b=b: e.copy(out=SbS[b][0][0:64, :], in_=SsT[b][0:64, :]), reads=['SsT%d' % b], writes=['SbS'])
                    op('act', lambda e, b=b: e.copy(out=SbS[b][1][64:128, :], in_=SsT[b][64:128, :]), reads=['SsT%d' % b], writes=['SbS'])
                    op('pool', lambda e, b=b: e.memset(SbS[b][0][64:128, :], 0.0), writes=['SbS'])
                    op('pool', lambda e, b=b: e.memset(SbS[b][1][0:64, :], 0.0), writes=['SbS'])
                projn(3, u, Wt[0], 0, 512, 'Wt0')
                op('act', lambda e: e.copy(out=rtmp[:], in_=P[3][:]), reads=['P3'], writes=['rtmp'])
                do_rope(rtmp[:].rearrange("p (h d) -> p h d", h=8), P[3][:].rearrange("p (h d) -> p h d", h=8),
                        cosR, sinR, 8, 32, ['P3'], ['rtmp'])
                op('dve', lambda e: e.tensor_tensor(out=qtil[:].rearrange("p (h d) -> p h d", h=8),
                                                    in0=rtmp[:].rearrange("p (h d) -> p h d", h=8),
                                                    in1=bc(decs[:, 0:8].unsqueeze(2), [128, 8, 64]), op=ALU.mult),
                   reads=['rtmp', 'decs'], writes=['qtil'])
                for hp in range(4):
                    slot = tr_rr[0]; tr_rr[0] = (slot + 1) % 2
                    pst = PTb[slot][:, 0, :]
                    op('pe', lambda e, pst=pst, hp=hp: e.transpose(pst, ktil[:, hp * 128:(hp + 1) * 128], ident[:]),
                       reads=['ktil', 'ident'], writes=['PTr%d' % slot])
                    op('act', lambda e, pst=pst, hp=hp: e.copy(out=kz[0][0:64, hp, :], in_=pst[0:64, :]),
                       reads=['PTr%d' % slot], writes=['ktilT'])
                    op('act', lambda e, pst=pst, hp=hp: e.copy(out=kz[1][64:128, hp, :], in_=pst[64:128, :]),
                       reads=['PTr%d' % slot], writes=['ktilT'])
                    transpose_to(qtilT[:, hp, :], qtil[:, hp * 128:(hp + 1) * 128], 128, 128, ['qtil'], ['qtilT'], evac='dve')
                for b in range(4):
                    op('pool', lambda e, b=b: e.memset(qz[b][:], 0.0), writes=['qz'])
                    op('pool', lambda e, b=b: e.tensor_copy(out=qz[b][:, :, b:b + 1], in_=qtilT[:, :, b:b + 1]),
                       reads=['qtilT', 'qz'], writes=['qz'])
                for h in range(8):
                    hp, h2 = h // 2, h % 2
                    pb = 4 + h // 4
                    mm(P[pb][:, (h % 4) * 128:(h % 4 + 1) * 128], kz[h2][:, hp, :], qtilT[:, hp, :],
                       True, True, ['ktilT', 'qtilT'], ['P%d' % pb])
                for half in range(2):
                    op('dve', lambda e, half=half: e.tensor_tensor(
                        out=innb[:, half * 4:(half + 1) * 4, :],
                        in0=P[4 + half][:].rearrange("p (h i) -> p h i", h=4),
                        in1=bc(identf[:].unsqueeze(1), [128, 4, 128]), op=ALU.mult),
                       reads=['P%d' % (4 + half), 'identf'], writes=['innb%d' % half])
                for h in range(8):
                    hp, h2 = h // 2, h % 2
                    mm(P[3][:, h * 64:(h + 1) * 64], innb[:, h, :], rvb[:, h * 64:(h + 1) * 64], True, False,
                       ['innb%d' % (h // 4), 'rvb', 'rtmp'], ['P3'])
                    for b in range(4):
                        mm(P[3][:, h * 64:(h + 1) * 64], qz[b][:, hp, :], SbS[b][h2][:, hp * 64:(hp + 1) * 64],
                           False, b == 3, ['qz', 'SbS'], ['P3'])
                op('act', lambda e: e.copy(out=orf[:], in_=P[3][:]), reads=['P3'], writes=['orf'])
                orf3 = orf[:].rearrange("p (h d) -> p h d", h=8)
                osq3 = osq[:].rearrange("p (h d) -> p h d", h=8)
                op('dve', lambda e: e.tensor_reduce(out=gsm[:], in_=orf3, axis=AX.X, op=ALU.add), reads=['orf'], writes=['gsm'])
                op('pool', lambda e: e.tensor_tensor(out=osq[:], in0=orf[:], in1=orf[:], op=ALU.mult), reads=['orf'], writes=['osq'])
                op('dve', lambda e: e.tensor_reduce(out=gss[:], in_=osq3, axis=AX.X, op=ALU.add), reads=['osq'], writes=['gss'])
                op('dve', lambda e: e.tensor_scalar(out=gmu[:], in0=gsm[:], scalar1=1.0 / 64, scalar2=None, op0=ALU.mult),
                   reads=['gsm'], writes=['gmu'])
                op('dve', lambda e: e.tensor_tensor(out=gm2[:], in0=gmu[:], in1=gmu[:], op=ALU.mult), reads=['gmu'], writes=['gm2'])
                op('dve', lambda e: e.scalar_tensor_tensor(out=grs[:], in0=gss[:], scalar=1.0 / 64, in1=gm2[:],
                                                           op0=ALU.mult, op1=ALU.subtract),
                   reads=['gss', 'gm2'], writes=['grs'])
                op('act', lambda e: e.activation(out=grs[:], in_=grs[:], func=AF.Sqrt, bias=eps_t[:, 0:1], scale=1.0),
                   reads=['grs', 'eps'], writes=['grs'])
                op('dve', lambda e: e.reciprocal(out=grs[:], in_=grs[:]), reads=['grs'], writes=['grs'])
                op('dve', lambda e: e.tensor_tensor(out=osq3, in0=orf3, in1=bc(gmu[:].unsqueeze(2), [128, 8, 64]), op=ALU.subtract),
                   reads=['orf', 'gmu', 'gss'], writes=['osq'])
                op('dve', lambda e: e.tensor_tensor(out=osq3, in0=osq3, in1=bc(grs[:].unsqueeze(2), [128, 8, 64]), op=ALU.mult),
                   reads=['osq', 'grs'], writes=['osq'])
                op('pool', lambda e: e.tensor_tensor(out=onrm_s[:, 0, :], in0=osq[:], in1=retg[:], op=ALU.mult),
                   reads=['osq', 'retg'], writes=['onrm0'])
                for b in range(4):
                    op('dve', lambda e, b=b: e.tensor_scalar(out=ktz[:], in0=ktil[:], scalar1=ohs[:, b:b + 1], scalar2=None, op0=ALU.mult),
                       reads=['ktil', 'ohs'], writes=['ktz'])
                    for h in range(8):
                        hp, h2 = h // 2, h % 2
                        mm(P[5][h2 * 64:(h2 + 1) * 64, hp * 64:(hp + 1) * 64], ktz[:, h * 64:(h + 1) * 64],
                           rvb[:, h * 64:(h + 1) * 64], True, True, ['ktz', 'rvb'], ['P5'])
                    op('dve', lambda e, b=b: e.tensor_tensor(out=stmp[:], in0=P[5][:, 0:256], in1=SsT[b][:], op=ALU.add),
                       reads=['P5', 'SsT%d' % b], writes=['stmp'])
                    op('dve', lambda e, b=b: e.tensor_tensor(out=SsT[b][:], in0=stmp[:], in1=gC1[:], op=ALU.mult),
                       reads=['stmp', 'gC1', 'SbS'], writes=['SsT%d' % b])
                    dma('sp', lambda e, b=b: e.dma_start(out=rets_d[b], in_=SsT[b][:]), reads=['SsT%d' % b])
                load_wt(1, lambda kc: win_d[kc, :, C_Q:C_Q + 536], 536)
                projn(0, u, Wt[1], 0, 512, 'Wt1')
                projn(1, u, Wt[1], 512, 536, 'Wt1')
                op('act', lambda e: e.copy(out=qbs[:, 0, :], in_=P[0][:]), reads=['P0'], writes=['qb'])
                do_rope(qbs[:, 0, :].rearrange("p (h d) -> p h d", h=8), P[0][:].rearrange("p (h d) -> p h d", h=8),
                        cosN, sinN, 8, 8, ['P0', 'qb'], ['qb'])
                op('act', lambda e: e.activation(out=gat[:, 0, :], in_=P[1][:, 0:24], func=AF.Sigmoid), reads=['P1'], writes=['gat'])
                load_wt(0, lambda kc: win_d[kc, :, C_RG:C_RG + 512], 512)
                projn(2, u, Wt[0], 0, 512, 'Wt0')
                op('act', lambda e: e.activation(out=rgs[:], in_=P[2][:], func=AF.Silu), reads=['P2'], writes=['rgs'])
                op('pool', lambda e: e.tensor_tensor(out=mixret[:], in0=onrm_s[:, 0, :], in1=rgs[:], op=ALU.mult),
                   reads=['onrm0', 'rgs'], writes=['mixret'])
                for c in range(4):
                    transpose_to(mixTs[:, 4 + c, :], mixret[:, c * 128:(c + 1) * 128], 128, 128, ['mixret'], ['mixT'])
                for h in range(8):
                    transpose_to(QTs[:, h, :], qbs[:, 0, h * 64:(h + 1) * 64], 128, 64, ['qb'], ['QTs'], evac='dve')

                for b in range(4):
                    op('pool', lambda e: e.memset(KcTs[:, :, 0:16], 0.0), reads=['KcTs'], writes=['KcTs'])
                    op('pool', lambda e: e.memset(VcTs[:, :, 0:16], 0.0), reads=['VcTs'], writes=['VcTs'])
                    for pg in range(64):
                        kt = pg
                        ru = pg % 8
                        kb_i = (b * 64 + pg) % 3
                        kvr = kvbs[kb_i]
                        kvn = 'kvbs%d' % kb_i
                        dma('pool', lambda e, b=b, pg=pg, kvr=kvr: e.indirect_dma_start(
                            out=kvr[:], out_offset=None, in_=ckv_d,
                            in_offset=bass.IndirectOffsetOnAxis(ap=idx[:, b * 64 + pg:b * 64 + pg + 1], axis=0)),
                            reads=['idx'], writes=[kvn])
                        pbk_, pbn_ = tbatch([kvr[:, i * 64:(i + 1) * 64] for i in range(6)], [kvn])
                        op('act', lambda e, pbk_=pbk_, ru=ru: e.copy(out=KcTs[:, :, 16 + ru * 128:16 + (ru + 1) * 128], in_=pbk_[0:64, 0:2, :]),
                           reads=[pbn_], writes=['KcTs'])
                        op('act', lambda e, pbk_=pbk_, ru=ru: e.copy(out=VcTs[:, :, 16 + ru * 128:16 + (ru + 1) * 128], in_=pbk_[0:64, 2:4, :]),
                           reads=[pbn_], writes=['VcTs'])
                        op('dve', lambda e, pbk_=pbk_, kt=kt: e.tensor_copy(out=KslT[0:64, :, kt * 128:(kt + 1) * 128], in_=pbk_[0:64, 4:6, :]),
                           reads=[pbn_], writes=['KslT%d' % kt])
                        op('dve', lambda e, kt=kt, kvr=kvr: e.tensor_copy(out=Vsl[:, kt, :, 0:64],
                                                                           in_=kvr[:, 384:512].rearrange("p (g d) -> p g d", g=2)),
                           reads=[kvn], writes=['Vsl%d' % kt])
                        if ru == 7:
                            compress8(pg // 8)
                    for w in range(4):
                        dma('sp', lambda e, b=b, w=w: e.dma_start(out=wstage[:], in_=cwin_d[b, w * 128:(w + 1) * 128, :]), writes=['wstage'])
                        if w == 0:
                            dma('sp', lambda e, b=b: e.dma_start(out=wins_d[b, 0:127, :], in_=wstage[1:128, :]), reads=['wstage'])
                        else:
                            dma('sp', lambda e, b=b, w=w: e.dma_start(out=wins_d[b, 128 * w - 1:128 * w + 127, :], in_=wstage[:]),
                                reads=['wstage'])
                        op('pool', lambda e: e.tensor_copy(out=wb[:], in_=wstage[:]), reads=['wstage'], writes=['wb'])
                        for g in range(2):
                            transpose_to(KwT[0:64, g, w * 128:(w + 1) * 128], wb[:, g * 64:(g + 1) * 64], 128, 64, ['wb'], ['KwT%d' % w])
                        op('pool', lambda e, w=w: e.tensor_copy(out=Vw[:, w, :, 0:64],
                                                                 in_=wb[:, 128:256].rearrange("p (g d) -> p g d", g=2)),
                           reads=['wb'], writes=['Vw%d' % w])
                    for g in range(2):
                        for r in range(2):
                            op('dve', lambda e, g=g, r=r, b=b: e.tensor_copy(out=Qs[0:64, r, :], in_=QTs[:, 4 * g:4 * g + 4, b]),
                               reads=['QTs', 'Qs'], writes=['Qs'])
                        for br in range(3):
                            mm(P[4][0:4, br:br + 1], gat[:, 0, g * 12 + br:g * 12 + br + 10:3], ohs[:, b:b + 1], True, True,
                               ['gat', 'ohs'], ['P4'])
                        op('act', lambda e: e.copy(out=gcol[:], in_=P[4][0:4, 0:3]), reads=['P4'], writes=['gcol'])
                        for ct in range(4):
                            mm(P[0][:, 4 * ct:4 * ct + 4], kcT[:, g, ct * 128:(ct + 1) * 128], Qs[:, 0, :], True, True,
                               ['kcT', 'Qs'], ['P0'])
                        for ct in range(4):
                            op('act', lambda e, ct=ct: e.activation(out=pts[:, 4 * ct:4 * ct + 4], in_=P[0][:, 4 * ct:4 * ct + 4],
                                                                    func=AF.Exp, bias=kbcs[:, ct:ct + 1], scale=0.125),
                               reads=['P0', 'kbcs'], writes=['pts'])
                        for ct in range(4):
                            mm(P[2][0:4, 0:193], pts[:, 4 * ct:4 * ct + 4], Rc[:, ct, g, :], ct == 0, ct == 3, ['pts', 'Rc'], ['P2'])
                        op('act', lambda e: e.copy(out=accs[:], in_=P[2][0:4, 0:193]), reads=['P2'], writes=['accs'])
                        op('dve', lambda e: e.tensor_scalar(out=rec4[:], in0=accs[:, 192:193], scalar1=1e-30, scalar2=None, op0=ALU.max),
                           reads=['accs'], writes=['rec4'])
                        op('dve', lambda e: e.reciprocal(out=rec4[:], in_=rec4[:]), reads=['rec4'], writes=['rec4'])
                        mm(P[3][0:1, 0:128], rec4[:, 0:1], accs[:, 0:128], True, True, ['rec4', 'accs'], ['P3'])
                        op('dve', lambda e: e.tensor_tensor(out=sc4[:], in0=rec4[:], in1=gcol[:, 0:1], op=ALU.mult),
                           reads=['rec4', 'gcol'], writes=['sc4'])
                        op('dve', lambda e: e.tensor_scalar(out=ocmb[:], in0=accs[:, 128:192], scalar1=sc4[:, 0:1], scalar2=None, op0=ALU.mult),
                           reads=['accs', 'sc4'], writes=['ocmb'])
                        op('dve', lambda e: e.tensor_tensor(out=srow[:], in0=P[3][0:1, 0:128], in1=bons[:], op=ALU.add),
                           reads=['P3', 'bons'], writes=['srow'])
                        op('dve', lambda e: e.max(out=s8a[:], in_=srow[:]), reads=['srow'], writes=['s8a'])
                        op('dve', lambda e: e.match_replace(out=srow2[:], in_to_replace=s8a[:], in_values=srow[:], imm_value=-3e38),
                           reads=['srow', 's8a'], writes=['srow2'])
                        op('dve', lambda e: e.max(out=s8b[:], in_=srow2[:]), reads=['srow2'], writes=['s8b'])
                        op('dve', lambda e: e.tensor_scalar(out=srow2[:], in0=srow[:], scalar1=s8b[:, 6:7], scalar2=-1.0,
                                                            op0=ALU.is_ge, op1=ALU.add), reads=['srow', 's8b'], writes=['srow2'])
                        op('dve', lambda e: e.tensor_scalar(out=selr[:, 0:128], in0=srow2[:], scalar1=-NEGB, scalar2=None, op0=ALU.mult),
                           reads=['srow2'], writes=['selr'])
                        op('dve', lambda e: e.tensor_copy(out=selr[:, 128:256], in_=selr[:, 0:128]), reads=['selr'], writes=['selr'])
                        mm(P[4][:, 8:9], selr[0:1, 0:128], oneb[0:1, 0:1], True, True, ['selr', 'oneb', 'gcol'], ['P4'])
                        mm(P[4][:, 9:10], selr[0:1, 64:192], oneb[0:1, 0:1], True, True, ['selr', 'oneb'], ['P4'])
                        op('dve', lambda e: e.tensor_copy(out=Qs[64:128, 1, :], in_=bc(P[4][64:128, 8:9], [64, 4])), reads=['P4', 'Qs'], writes=['Qs'])
                        op('dve', lambda e: e.tensor_copy(out=Qs[64:128, 0, :], in_=bc(P[4][64:128, 9:10], [64, 4])), reads=['P4', 'Qs'], writes=['Qs'])
                        for grp in range(4):
                            psi = grp % 2
                            for i in range(16):
                                kt = 16 * grp + i
                                mm(P[psi][:, 4 * i:4 * i + 4], KslT[:, g, kt * 128:(kt + 1) * 128], Qs[:, kt // 32, :], True, True,
                                   ['KslT%d' % kt, 'KslT_E', 'Qs'], ['P%d' % psi])
                            op('act', lambda e, psi=psi: e.activation(out=pts[:], in_=P[psi][:, 0:64], func=AF.Exp, bias=zcol[:, 0:1], scale=0.125),
                               reads=['P%d' % psi, 'zcol'], writes=['pts'])
                            for i in range(16):
                                kt = 16 * grp + i
                                mm(P[2][0:4, 0:65], pts[:, 4 * i:4 * i + 4], Vsl[:, kt, g, :], grp == 0 and i == 0, False,
                                   ['pts', 'Vsl%d' % kt, 'Vsl_ones'], ['P2'])
                        mm(P[0][:, 0:4], KnT[:, g, :], Qs[:, 0, :], True, True, ['Knew', 'Qs'], ['P0'])
                        op('act', lambda e, b=b: e.activation(out=pts[:, 0:4], in_=P[0][:, 0:4], func=AF.Exp, bias=kbself[:, b:b + 1], scale=0.125),
                           reads=['P0', 'kbself'], writes=['pts'])
                        mm(P[2][0:4, 0:65], pts[:, 0:4], Vn[:, g, :], False, True, ['pts', 'Vnew'], ['P2'])
                        for br, pbk_ in ((1, 2),):
                            op('act', lambda e: e.copy(out=accs[:, 0:65], in_=P[2][0:4, 0:65]), reads=['P2'], writes=['accs'])
                            op('dve', lambda e: e.reciprocal(out=rec4[:], in_=accs[:, 64:65]), reads=['accs'], writes=['rec4'])
                            op('dve', lambda e: e.tensor_tensor(out=sc4[:], in0=rec4[:], in1=gcol[:, 1:2], op=ALU.mult),
                               reads=['rec4', 'gcol'], writes=['sc4'])
                            op('dve', lambda e: e.scalar_tensor_tensor(out=ocmb[:], in0=accs[:, 0:64], scalar=sc4[:, 0:1], in1=ocmb[:],
                                                                       op0=ALU.mult, op1=ALU.add),
                               reads=['accs', 'sc4', 'ocmb'], writes=['ocmb'])
                        for w in range(4):
                            mm(P[1][:, 4 * w:4 * w + 4], KwT[:, g, w * 128:(w + 1) * 128], Qs[:, 0, :], True, True,
                               ['KwT%d' % w, 'Qs'], ['P1'])
                        mm(P[1][:, 16:20], KwnT[:, g, :], Qs[:, 0, :], True, True, ['Knew', 'Qs'], ['P1'])
                        for w in range(4):
                            op('act', lambda e, w=w: e.activation(out=pts[:, 4 * w:4 * w + 4], in_=P[1][:, 4 * w:4 * w + 4], func=AF.Exp,
                                                                  bias=kbws[:, w:w + 1], scale=0.125),
                               reads=['P1', 'kbws'], writes=['pts'])
                        op('act', lambda e, b=b: e.activation(out=pts[:, 16:20], in_=P[1][:, 16:20], func=AF.Exp, bias=kbself[:, b:b + 1], scale=0.125),
                           reads=['P1', 'kbself'], writes=['pts'])
                        for w in range(4):
                            mm(P[2][0:4, 0:65], pts[:, 4 * w:4 * w + 4], Vw[:, w, g, :], w == 0, False, ['pts', 'Vw%d' % w, 'Vw_ones'], ['P2'])
                        mm(P[2][0:4, 0:65], pts[:, 16:20], Vwn[:, g, :], False, True, ['pts', 'Vnew'], ['P2'])
                        op('act', lambda e: e.copy(out=accs[:, 0:65], in_=P[2][0:4, 0:65]), reads=['P2'], writes=['accs'])
                        op('dve', lambda e: e.reciprocal(out=rec4[:], in_=accs[:, 64:65]), reads=['accs'], writes=['rec4'])
                        op('dve', lambda e: e.tensor_tensor(out=sc4[:], in0=rec4[:], in1=gcol[:, 2:3], op=ALU.mult),
                           reads=['rec4', 'gcol'], writes=['sc4'])
                        op('dve', lambda e: e.scalar_tensor_tensor(out=ocmb[:], in0=accs[:, 0:64], scalar=sc4[:, 0:1], in1=ocmb[:],
                                                                   op0=ALU.mult, op1=ALU.add),
                           reads=['accs', 'sc4', 'ocmb'], writes=['ocmb'])
                        dma('sp', lambda e, b=b, g=g: e.dma_start(out=ons_d[b, 4 * g:4 * g + 4, :], in_=ocmb[:]), reads=['ocmb'], writes=['ons'])

                op('pool', lambda e: e.memset(onsa[:, 0:2, :], 0.0), reads=['onsa'], writes=['onsa'])
                dma('sp', lambda e: e.dma_start(out=onsa[0:4, 0:2, :].rearrange("p a c -> p (a c)"),
                                                in_=ons_d.rearrange("b h d -> b (h d)")), reads=['ons', 'onsa'], writes=['onsa'])
                op('act', lambda e: e.copy(out=mixret[:], in_=onsa[:, 0:2, :].rearrange("p a c -> p (a c)")), reads=['onsa'], writes=['mixret'])
                for c in range(4):
                    transpose_to(mixTs[:, c, :], mixret[:, c * 128:(c + 1) * 128], 128, 128, ['mixret'], ['mixT'], evac='dve')
                load_wt(0, lambda kc: wo_d[kc, :, 0:512], 512)
                load_wt(1, lambda kc: wo_d[kc, :, 512:1024], 512)
                dma('sp', lambda e: e.dma_start(out=xo[:], in_=xs_own_d), writes=['stage', 'rtmp'])
                for half in range(2):
                    for c in range(8):
                        mm(P[half][:], mixTs[:, c, :], Wt[half][:, c, 0:512], c == 0, c == 7, ['mixT', 'Wt%d' % half], ['P%d' % half])
                    op('dve', lambda e, half=half: e.scalar_tensor_tensor(out=xr[:, half * 512:(half + 1) * 512],
                                                                          in0=xo[:, half * 512:(half + 1) * 512], scalar=ALPHA,
                                                                          in1=P[half][:], op0=ALU.mult, op1=ALU.add),
                       reads=['stage', 'rtmp', 'P%d' % half], writes=['orf', 'osq'])
                layer_norm(xo[:], xr[:], ln1[:, 0, :], ln1[:, 1, :], ['orf', 'osq'], ['stage', 'rtmp'], 'ln1')
                dma('sp', lambda e: e.dma_start(out=x1s_d[16], in_=xo[:]), reads=['stage', 'rtmp'], writes=['x1s16'])

        _stop(5)
        S.barrier()
        with contextlib.ExitStack() as stB:
            def TB(name, shape, dt=BF16):
                return stB.enter_context(nc.sbuf_tensor("s_" + name, list(shape), dt))
            Wup = TB("Wup", [128, 8, 4096])
            Wdn = TB("Wdn", [128, 32, D])
            for c4 in range(4):
                for kc in range(8):
                    dma('pool', lambda e, kc=kc, c4=c4: e.dma_start(out=Wup[:, kc, c4 * 1024:(c4 + 1) * 1024],
                                                                  in_=wup_d[kc, :, c4 * 1024:(c4 + 1) * 1024], max_dma_last_dim=4096),
                        writes=['Wup%d' % c4])
            for fc in range(32):
                dma('pool', lambda e, fc=fc: e.dma_start(out=Wdn[:, fc, :], in_=wdn_d[fc], max_dma_last_dim=4096),
                    writes=['Wdn%d' % (fc // 8)])
            ln2 = TB("ln2", [128, 2, D], F32)
            dma('sp', lambda e: e.dma_start(out=ln2[:, 0, :], in_=ln_d[2]), writes=['ln2'])
            dma('sp', lambda e: e.dma_start(out=ln2[:, 1, :], in_=ln_d[3]), writes=['ln2'])
            x1f = TB("x1f", [128, 4, D], F32)
            x1b = TB("x1b", [128, D])
            x1T = TB("x1T", [128, 8, 512])
            hr = [TB("hr%d" % i, [128, 512], F32) for i in range(2)]
            hT = TB("hT", [128, 32, 512])
            xr2 = TB("xr2", [128, D], F32)
            yo = TB("yo", [128, D], F32)
            for k in ([4] if os.environ.get('K_MLP') == 's' else range(int(os.environ.get('K_MLP', 5 if DO_SAMPLE else 4)))):
                nsub = 4 if k < 4 else 1
                NTOK = 128 * nsub
                for u in range(nsub):
                    dma('sp', lambda e, u=u: e.dma_start(out=x1f[:, u, :], in_=x1s_d[4 * k + u]),
                        reads=['x1s%d' % (4 * k + u)], writes=['x1f%d' % u])
                    op('pool', lambda e, u=u: e.tensor_copy(out=x1b[:], in_=x1f[:, u, :]), reads=['x1f%d' % u], writes=['x1b'])
                    for kc in range(8):
                        transpose_to(x1T[:, kc, u * 128:(u + 1) * 128], x1b[:, kc * 128:(kc + 1) * 128], 128, 128,
                                     ['x1b'], ['x1T'], evac=('act' if kc % 2 else 'dve'))
                for fc in range(32):
                    psi = fc % 2
                    for kc in range(8):
                        mm(P[psi][:, 0:NTOK], Wup[:, kc, fc * 128:(fc + 1) * 128], x1T[:, kc, 0:NTOK], kc == 0, kc == 7,
                           ['Wup%d' % (fc // 8), 'x1T'], ['P%d' % psi])
                    op('act', lambda e, psi=psi: e.activation(out=hr[psi][:, 0:NTOK], in_=P[psi][:, 0:NTOK], func=AF.Relu),
                       reads=['P%d' % psi], writes=['hr%d' % psi])
                    op('pool', lambda e, psi=psi, fc=fc: e.tensor_tensor(out=hT[:, fc, 0:NTOK], in0=hr[psi][:, 0:NTOK], in1=hr[psi][:, 0:NTOK], op=ALU.mult),
                       reads=['hr%d' % psi], writes=['hT'])
                for u in range(nsub):
                    for half in range(2):
                        pb = 2 + half
                        for fc in range(32):
                            mm(P[pb][:], hT[:, fc, u * 128:(u + 1) * 128], Wdn[:, fc, half * 512:(half + 1) * 512],
                               fc == 0, fc == 31, ['hT', 'Wdn%d' % (fc // 8)], ['P%d' % pb])
                        op('dve', lambda e, half=half, pb=pb, u=u: e.scalar_tensor_tensor(
                            out=xr2[:, half * 512:(half + 1) * 512], in0=x1f[:, u, half * 512:(half + 1) * 512], scalar=ALPHA,
                            in1=P[pb][:], op0=ALU.mult, op1=ALU.add),
                           reads=['x1f%d' % u, 'P%d' % pb], writes=['xr2'])
                    layer_norm(yo[:], xr2[:], ln2[:, 0, :], ln2[:, 1, :], ['xr2'], ['yo'], 'ln2')
                    dma('sp', lambda e, u=u, k=k: e.dma_start(out=(y_d[4 * k + u] if k < 4 else ys_d), in_=yo[:]), reads=['yo'])

        S.finish('sp')
        print("instructions:", S.ninstr, "sems:", S.nsem)
    return nc


_PERM = np.concatenate([np.arange(512, 1024), np.arange(1816, 2328), np.arange(2328, 2840),
                        np.arange(1024, 1280), np.arange(0, 512), np.arange(1280, 1304),
                        np.arange(1304, 1816), np.arange(2840, 3352)])


def _const_tables():
    f = np.float32
    key = np.arange(128)[:, None]
    tp = np.arange(896)[None, :] - 384
    mc = np.where(key <= tp, 0.0, NEGB)
    wl = np.where(tp < key, 0.0, NEGB)
    t = np.arange(512)[None, :]
    cm = np.where(t >= 16 * key - 1521, 0.0, NEGB)
    masks = np.concatenate([mc, wl, cm], axis=1).astype(f)
    tri = (np.arange(128)[None, :] >= np.arange(128)[:, None]).astype(f)
    E = (np.arange(4096)[None, :] // 64 == np.arange(64)[:, None]).astype(f)
    mimp = np.zeros((512, 128), f)
    for jb in range(128):
        for c, w in ((4 * jb - 1, 1.0), (4 * jb, 2.0), (4 * jb + 1, 2.0), (4 * jb + 2, 2.0), (4 * jb + 3, 1.0)):
            if 0 <= c + 1 < 512:
                mimp[c + 1, jb] = w
    mimp = mimp.reshape(4, 128, 128).transpose(1, 0, 2).copy()
    gam = 1.0 - 2.0 ** (-5.0 - np.arange(8, dtype=np.float64))
    tl = np.arange(128, dtype=np.float64)[:, None]
    dec = np.concatenate([gam[None, :] ** (tl + 1), gam[None, :] ** (-(tl + 1)) / 8.0], axis=1).astype(f)
    gC = np.zeros((128, 4, 64), f)
    for h in range(8):
        gC[(h % 2) * 64:(h % 2 + 1) * 64, h // 2, :] = gam[h] ** 128
    return dict(masks=masks, tri=tri, Eoh=E, mimp=mimp, dec=dec, gC=gC.reshape(128, 256), gam=gam)


def _rope_tabs(pos):
    invn = 500000.0 ** (-np.arange(8, dtype=np.float64) / 8)
    invr = 10000.0 ** (-np.arange(32, dtype=np.float64) / 32)
    an = pos[:, None] * invn[None, :]
    ar = pos[:, None] * invr[None, :]
    return np.concatenate([np.cos(an), np.sin(an), np.cos(ar), np.sin(ar)], axis=1).astype(np.float32)


def kernel(x_prompt, x_sample, cache_kv, cache_win, state_ret, page_table, w_in, w_cmp_k, w_cmp_v,
           pos_cmp_k, pos_cmp_v, ret_norm_g, w_o, ln1_g, ln1_b, w_up, w_down, ln2_g, ln2_b):
    f = np.float32
    asf = lambda a: np.ascontiguousarray(np.asarray(a), dtype=f)
    x_prompt = asf(x_prompt)
    ct = _const_tables()
    shared = dict(
        w_in=np.ascontiguousarray(asf(w_in)[0][:, _PERM].reshape(8, 128, 3352)),
        w_o=asf(w_o)[0].reshape(8, 128, D),
        w_up=asf(w_up)[0].reshape(8, 128, 4096),
        w_down=asf(w_down)[0].reshape(32, 128, D),
        wck=np.ascontiguousarray(asf(w_cmp_k)[0].transpose(1, 0, 2)),
        wcv=np.ascontiguousarray(asf(w_cmp_v)[0].transpose(1, 0, 2)),
        posk=np.ascontiguousarray(asf(pos_cmp_k)[0].T),
        posv=np.ascontiguousarray(asf(pos_cmp_v)[0].T),
        retg=np.ascontiguousarray(np.broadcast_to(asf(ret_norm_g)[0][None, :], (128, 512))),
        lnp=np.ascontiguousarray(np.stack([np.broadcast_to(asf(v)[0][None, :], (128, D))
                                           for v in (ln1_g, ln1_b, ln2_g, ln2_b)])),
        masks=ct['masks'], tri=ct['tri'], Eoh=ct['Eoh'], mimp=ct['mimp'], dec=ct['dec'], gC=ct['gC'],
    )
    gam = ct['gam']
    ohs = np.zeros((128, 4), f); ohs[np.arange(4), np.arange(4)] = 1.0
    kbself = np.full((128, 4), NEGB, f); kbself[np.arange(4), np.arange(4)] = 0.0
    kbws = np.zeros((128, 4), f); kbws[0, 0] = NEGB
    kbcs = np.zeros((128, 4), f); kbcs[0, 0] = NEGB
    bons = np.zeros((1, 128), f); bons[0, 0] = 1e4; bons[0, 127] = 1e4
    decs = np.ascontiguousarray(np.broadcast_to(ct['dec'][0:1, :], (128, 16)))
    gC1 = np.zeros((128, 4, 64), f)
    for h in range(8):
        gC1[(h % 2) * 64:(h % 2 + 1) * 64, h // 2, :] = gam[h]
    tabs_s = np.ascontiguousarray(np.broadcast_to(_rope_tabs(np.array([float(PAST)]))[0:1, :], (128, 80)))
    ckv = np.ascontiguousarray(np.asarray(cache_kv, dtype=f)).reshape(2560 * 128, 512)
    cwin = np.asarray(cache_win, dtype=f)[0].reshape(32, 512, 256)
    stin = np.asarray(state_ret, dtype=f)[0]
    ptab = np.asarray(page_table).astype(np.int32)
    xsamp = asf(x_sample)[:, 0, :]
    shared.update(ohs=ohs, kbself=kbself, kbws=kbws, kbcs=kbcs, bons=bons, decs=decs, gC1=gC1.reshape(128, 256),
                  tabs_s=tabs_s, cache_kv=ckv)
    in_maps = []
    for c in range(8):
        b, j = c // 4, c % 4
        off = 512 * (3 - j)
        xs = np.zeros((SEQ, D), f)
        xs[off:] = x_prompt[b, :SEQ - off]
        xT = np.ascontiguousarray(xs.T).reshape(8, 128, SEQ)
        xown = np.stack([xs[512 * (4 * k + 3):512 * (4 * k + 4)] for k in range(4)]).reshape(16, 128, D)
        sp = np.arange(SEQ)
        tpos = np.maximum(sp - off, 0).astype(np.float64)
        tabs = _rope_tabs(tpos).reshape(64, 128, 80)
        kbias = np.where(sp >= off, 0.0, NEGB).astype(f).reshape(64, 128).T.copy()
        cp = np.arange(512)
        kbc = np.where((cp - 1) >= 32 * (3 - j), 0.0, NEGB).astype(f).reshape(4, 128).T.copy()
        bonus = np.zeros((16, 128, 128), f)
        blk = np.arange(128)[None, :] - 8 * (3 - j)
        for k in range(4):
            for u in range(4):
                tt = 512 * (4 * k + 3) + 128 * u + np.arange(128) - off
                cur = (tt // 64)[:, None]
                forced = (blk == 0) | (blk == cur) | (blk == cur - 1)
                bo = np.where(forced, 1e4, 0.0)
                bo = np.where((blk < 0) | (blk > cur), -1e30, bo)
                bonus[4 * k + u] = bo
        m = dict(shared)
        m.update(xT=xT, xown=np.ascontiguousarray(xown), tabs=tabs, kbias=kbias, kbias_c=kbc, bonus=bonus)
        xs4 = np.zeros((128, D), f); xs4[0:4] = xsamp[4 * c:4 * c + 4]
        m.update(xsT=np.ascontiguousarray(xs4.T).reshape(8, 128, 128), xs_own=xs4,
                 cache_win=np.ascontiguousarray(cwin[4 * c:4 * c + 4]), state_in=np.ascontiguousarray(stin[4 * c:4 * c + 4]),
                 pt_rep=np.ascontiguousarray(np.broadcast_to(ptab[4 * c:4 * c + 4].reshape(1, 256), (128, 256))))
        in_maps.append(m)

    try:
        nc = build_program()
    except _Stop:
        nc = _CUR[0].nc
    res = run_bass_kernel_spmd(nc, in_maps, core_ids=list(range(8)))
    R = res.results

    y_prompt = np.zeros((2, SEQ, D), f)
    kv_prompt = np.zeros((1, 2, SEQ, 4, 2, 64), f)
    win_prompt = np.zeros((1, 2, 512, 2, 2, 64), f)
    ret_prompt = np.zeros((1, 2, 8, 64, 64), f)
    for c in range(8):
        b, j = c // 4, c % 4
        yo = R[c]["y_own"].reshape(4, 512, D)
        kvo = R[c]["kv_own"].reshape(4, 512, 4, 2, 64)
        for k in range(4):
            i = 4 * k + j
            y_prompt[b, 512 * i:512 * (i + 1)] = yo[k]
            kv_prompt[0, b, 512 * i:512 * (i + 1)] = kvo[k]
        if j == 3:
            win_prompt[0, b] = R[c]["win_out"].reshape(512, 2, 2, 64)
            ro = R[c]["ret_out"].reshape(2, 64, 4, 64)
            ret_prompt[0, b] = ro.transpose(2, 0, 1, 3).reshape(8, 64, 64)
    y_sample = np.zeros((32, 1, D), f)
    kv_sample = np.zeros((1, 32, 1, 4, 2, 64), f)
    win_sample = np.zeros((1, 32, 512, 2, 2, 64), f)
    ret_sample = np.zeros((1, 32, 8, 64, 64), f)
    for c in range(8):
        y_sample[4 * c:4 * c + 4, 0] = R[c]["y_s"][0:4]
        kv_sample[0, 4 * c:4 * c + 4, 0] = R[c]["kv_s"][0:4].reshape(4, 4, 2, 64)
        win_sample[0, 4 * c:4 * c + 4] = R[c]["win_s"].reshape(4, 512, 2, 2, 64)
        rs = R[c]["ret_s"].reshape(4, 2, 64, 4, 64)
        ret_sample[0, 4 * c:4 * c + 4] = rs.transpose(0, 3, 1, 2, 4).reshape(4, 8, 64, 64)
    return (y_prompt, y_sample, kv_prompt, kv_sample, win_prompt, win_sample, ret_prompt, ret_sample)
```

```python
import contextlib
import os
import numpy as np
import concourse.bass as bass
import concourse.mybir as mybir
from concourse.bass_utils import run_bass_kernel_spmd

F32 = mybir.dt.float32
BF16 = mybir.dt.bfloat16
I32 = mybir.dt.int32
U32 = mybir.dt.uint32
AF = mybir.ActivationFunctionType
ALU = mybir.AluOpType
AX = mybir.AxisListType

D = 1024
SEQ = 8192
NT = 16
ALPHA = 2.0 ** 0.25
BETA = 8.0 ** -0.25
LN_EPS = 1e-5
NEGB = -30000.0
PAST = 8192
DO_SAMPLE = True


class _Stop(Exception):
    pass


_CUR = [None]
_CUR_S = [-1]


def _stop(n):
    if int(os.environ.get('K_STOP', 999)) == n and int(os.environ.get('K_STOP_S', _CUR_S[0])) == _CUR_S[0]:
        _CUR[0].finish('sp')
        raise _Stop()


class Sched:
    EPOCH = 30000
    NDMA = 16

    def __init__(self, nc, stack):
        self.nc = nc
        self.stack = stack
        self.eng = {'pe': nc.tensor, 'act': nc.scalar, 'dve': nc.vector,
                    'pool': nc.gpsimd, 'sp': nc.sync}
        self.cur_sem = {}
        self.cnt = {}
        self.nsem = 0
        for e in ('pe', 'act', 'dve', 'pool'):
            self._new_epoch(e)
        self.dma_sems = [self._alloc_sem('dma%d' % i) for i in range(2 * self.NDMA)]
        self.dma_cnt = [0] * (2 * self.NDMA)
        self.dma_rr = {'sp': 0, 'pool': 0, 'act': 0}
        self.known = {e: {} for e in self.eng}
        self.last_w = {}
        self.readers = {}
        self.ninstr = 0

    def _alloc_sem(self, name):
        self.nsem += 1
        return self.stack.enter_context(self.nc.semaphore('%s_%d' % (name, self.nsem)))

    def _new_epoch(self, e):
        self.cur_sem[e] = self._alloc_sem('e_' + e)
        self.cnt[e] = 0

    def _wait(self, e, tok):
        if tok is None:
            return
        sem, val, src = tok
        if src == e and e == 'pe':
            return
        k = self.known[e]
        if k.get(id(sem), 0) >= val:
            return
        self.eng[e].wait_ge(sem, val)
        k[id(sem)] = val

    def _deps(self, e, reads, writes):
        for b in reads:
            self._wait(e, self.last_w.get(b))
            if b[0] == 'P':
                for t in self.readers.get(b, ()):
                    if t[2] != e:
                        self._wait(e, t)
        for b in writes:
            self._wait(e, self.last_w.get(b))
            for t in self.readers.get(b, ()):
                self._wait(e, t)

    def _commit(self, tok, reads, writes):
        for b in reads:
            self.readers.setdefault(b, []).append(tok)
        for b in writes:
            self.last_w[b] = tok
            self.readers[b] = []

    def op(self, e, fn, reads=(), writes=()):
        self._deps(e, reads, writes)
        if self.cnt[e] >= self.EPOCH:
            self._new_epoch(e)
        ins = fn(self.eng[e])
        self.cnt[e] += 1
        sem = self.cur_sem[e]
        ins.then_inc(sem, 1)
        tok = (sem, self.cnt[e], e)
        self._commit(tok, reads, writes)
        self.ninstr += 1
        return tok

    def dma(self, q, fn, reads=(), writes=()):
        self._deps(q, reads, writes)
        i = self.dma_rr[q] + (self.NDMA if q == 'pool' else 0)
        self.dma_rr[q] = (self.dma_rr[q] + 1) % self.NDMA
        sem = self.dma_sems[i]
        if self.dma_cnt[i] > 0:
            self._wait(q, (sem, 16 * self.dma_cnt[i], 'dma'))
        ins = fn(self.eng[q])
        self.dma_cnt[i] += 1
        ins.then_inc(sem, 16)
        tok = (sem, 16 * self.dma_cnt[i], 'dma')
        self._commit(tok, reads, writes)
        self.ninstr += 1
        return tok

    def barrier(self):
        for e in ('sp', 'pool', 'act', 'dve', 'pe'):
            self.finish(e)

    def finish(self, e='sp'):
        for i, sem in enumerate(self.dma_sems):
            if self.dma_cnt[i]:
                self._wait(e, (sem, 16 * self.dma_cnt[i], 'dma'))
        for x in ('pe', 'act', 'dve', 'pool'):
            if self.cnt[x]:
                self._wait(e, (self.cur_sem[x], self.cnt[x], x))


C_KVA, C_RK, C_RV, C_WIN, C_Q, C_GT, C_RQ, C_RG = 0, 512, 1024, 1536, 1792, 2304, 2328, 2840


def build_program():
    nc = bass.Bass("TRN2", target_bir_lowering=False)

    def din(name, shape, dt=F32):
        return nc.dram_tensor(name, list(shape), dt, kind="ExternalInput").ap()

    def dout(name, shape, dt=F32):
        return nc.dram_tensor(name, list(shape), dt, kind="ExternalOutput").ap()

    xT_d = din("xT", [8, 128, SEQ])
    xown_d = din("xown", [16, 128, D])
    tabs_d = din("tabs", [64, 128, 80])
    dec_d = din("dec", [128, 16])
    gC_d = din("gC", [128, 256])
    win_d = din("w_in", [8, 128, 3352])
    wo_d = din("w_o", [8, 128, D])
    wup_d = din("w_up", [8, 128, 4096])
    wdn_d = din("w_down", [32, 128, D])
    wck_d = din("wck", [64, 32, 64])
    wcv_d = din("wcv", [64, 32, 64])
    posk_d = din("posk", [64, 32])
    posv_d = din("posv", [64, 32])
    retg_d = din("retg", [128, 512])
    ln_d = din("lnp", [4, 128, D])
    kbias_d = din("kbias", [128, 64])
    kbc_d = din("kbias_c", [128, 4])
    bonus_d = din("bonus", [16, 128, 128])
    masks_d = din("masks", [128, 2304])
    tri_d = din("tri", [128, 128])
    E_d = din("Eoh", [64, 4096])
    mimp_d = din("mimp", [128, 4, 128])

    y_d = dout("y_own", [16, 128, D])
    kv_d = dout("kv_own", [16, 128, 512])
    wout_d = dout("win_out", [4, 128, 256])
    ret_d = dout("ret_out", [128, 256])
    x1s_d = nc.dram_tensor("x1_scratch", [17, 128, D], F32, kind="Internal").ap()

    oh_d = din("ohs", [128, 4]); kbself_d = din("kbself", [128, 4]); kbws_d = din("kbws", [128, 4]); kbcs_d = din("kbcs", [128, 4])
    bons_d = din("bons", [1, 128]); decs_d = din("decs", [128, 16]); gC1_d = din("gC1", [128, 256])
    pt_d = din("pt_rep", [128, 256], I32)
    xsT_d = din("xsT", [8, 128, 128]); tabs_s_d = din("tabs_s", [128, 80]); xs_own_d = din("xs_own", [128, D])
    ckv_d = din("cache_kv", [2560 * 128, 512]); cwin_d = din("cache_win", [4, 512, 256]); stin_d = din("state_in", [4, 8, 64, 64])
    kvs_d = dout("kv_s", [128, 512]); wins_d = dout("win_s", [4, 512, 256]); rets_d = dout("ret_s", [4, 128, 256]); ys_d = dout("y_s", [128, D])
    ons_d = nc.dram_tensor("ons_scratch", [4, 8, 64], F32, kind="Internal").ap()

    with contextlib.ExitStack() as st:
        S = Sched(nc, st)
        _CUR[0] = S
        op, dma = S.op, S.dma

        def T(name, shape, dt=BF16):
            return st.enter_context(nc.sbuf_tensor("s_" + name, list(shape), dt))

        P = [st.enter_context(nc.psum_tensor("P%d" % i, [128, 512], F32)) for i in range(6)]
        PTb = [st.enter_context(nc.psum_tensor("PTr%d" % i, [128, 8, 128], BF16)) for i in range(2)]
        tr_rr = [0]

        def mm(out, lhsT, rhs, start, stop, reads, writes, **kw):
            return op('pe', lambda e: e.matmul(out, lhsT=lhsT, rhs=rhs, start=start, stop=stop, **kw),
                      reads=reads, writes=writes)

        def bc(ap, shape):
            return ap.to_broadcast(list(shape))

        ident = T("ident", [128, 128])
        op('pool', lambda e: e.memset(ident[:], 1.0), writes=['ident'])
        op('pool', lambda e: e.affine_select(out=ident[:], in_=ident[:], pattern=[[-1, 128]],
                                             compare_op=ALU.is_equal, fill=0.0, base=0,
                                             channel_multiplier=1), reads=['ident'], writes=['ident'])
        identf = T("identf", [128, 128], F32)
        op('act', lambda e: e.copy(out=identf[:], in_=ident[:]), reads=['ident'], writes=['identf'])
        eps_t = T("eps_t", [128, 1], F32)
        op('pool', lambda e: e.memset(eps_t[:], LN_EPS), writes=['eps'])

        def transpose_to(dst, src, rows, cols, reads, writes, evac='act'):
            slot = tr_rr[0]
            tr_rr[0] = (slot + 1) % 2
            pst = PTb[slot][0:cols, 0, 0:rows]
            nm = 'PTr%d' % slot
            op('pe', lambda e: e.transpose(pst, src, ident[0:rows, 0:rows]),
               reads=list(reads) + ['ident'], writes=[nm])
            if evac == 'act':
                op('act', lambda e: e.copy(out=dst, in_=pst), reads=[nm], writes=writes)
            else:
                op('dve', lambda e: e.tensor_copy(out=dst, in_=pst), reads=[nm], writes=writes)

        def tbatch(srcs, reads):
            bank = tr_rr[0]
            tr_rr[0] = (bank + 1) % 2
            nm = 'PTr%d' % bank
            for i, src in enumerate(srcs):
                op('pe', lambda e, i=i, src=src: e.transpose(PTb[bank][0:64, i, :], src, ident[:]),
                   reads=list(reads) + ['ident'], writes=[nm])
            return PTb[bank], nm

        lnst = T("lnst", [128, 2, 6], F32)
        lnmv = T("lnmv", [128, 2], F32)
        lnrs = T("lnrs", [128, 1], F32)

        def layer_norm(dst, src, gtab, btab, sname, dname, tname):
            for c in range(2):
                op('dve', lambda e, c=c: e.bn_stats(out=lnst[:, c, :], in_=src[:, c * 512:(c + 1) * 512]),
                   reads=sname, writes=['lnst'])
            op('dve', lambda e: e.bn_aggr(out=lnmv[:], in_=lnst[:]), reads=['lnst'], writes=['lnmv'])
            op('act', lambda e: e.activation(out=lnrs[:], in_=lnmv[:, 1:2], func=AF.Sqrt, bias=eps_t[:, 0:1], scale=1.0),
               reads=['lnmv', 'eps'], writes=['lnrs'])
            op('dve', lambda e: e.reciprocal(out=lnrs[:], in_=lnrs[:]), reads=['lnrs'], writes=['lnrs'])
            op('dve', lambda e: e.tensor_scalar(out=dst, in0=src, scalar1=lnmv[:, 0:1], scalar2=lnrs[:, 0:1],
                                                op0=ALU.subtract, op1=ALU.mult),
               reads=sname + ['lnmv', 'lnrs'], writes=dname)
            op('dve', lambda e: e.tensor_tensor(out=dst, in0=dst, in1=gtab, op=ALU.mult), reads=dname + [tname], writes=dname)
            op('dve', lambda e: e.tensor_tensor(out=dst, in0=dst, in1=btab, op=ALU.add), reads=dname + [tname], writes=dname)

        _stop(1)
        with contextlib.ExitStack() as stA:
            def TA(name, shape, dt=BF16):
                return stA.enter_context(nc.sbuf_tensor("s_" + name, list(shape), dt))

            kbias = TA("kbias", [128, 64], F32)
            dma('sp', lambda e: e.dma_start(out=kbias[:], in_=kbias_d), writes=['kbias'])
            kbc = TA("kbc", [128, 4], F32)
            dma('sp', lambda e: e.dma_start(out=kbc[:], in_=kbc_d), writes=['kbc'])
            dec = TA("dec", [128, 16], F32)
            dma('sp', lambda e: e.dma_start(out=dec[:], in_=dec_d), writes=['dec'])
            gC = TA("gC", [128, 256], F32)
            dma('sp', lambda e: e.dma_start(out=gC[:], in_=gC_d), writes=['gC'])
            retg = TA("retg", [128, 512], F32)
            dma('sp', lambda e: e.dma_start(out=retg[:], in_=retg_d), writes=['retg'])

            def causal(o):
                return masks[:, 384 - 128 * o:896 - 128 * o]

            def wlo(o):
                return masks[:, 896 + 384 - 128 * o:896 + 896 - 128 * o]

            _stop(2)
            Wkv = TA("Wkv", [128, 8, 1792])
            for kc in range(8):
                dma('pool', lambda e, kc=kc: e.dma_start(out=Wkv[:, kc, :], in_=win_d[kc, :, 0:1792],
                                                         max_dma_last_dim=4096), writes=['Wkv'])
            Wt = [TA("WtA", [128, 8, 536]), TA("WtB", [128, 8, 536])]

            def load_wt(i, src_fn, ncols):
                for kc in range(8):
                    dma('pool', lambda e, kc=kc: e.dma_start(out=Wt[i][:, kc, 0:ncols], in_=src_fn(kc),
                                                             max_dma_last_dim=4096), writes=['Wt%d' % i])

            wck = TA("wck", [64, 32, 64]); wcv = TA("wcv", [64, 32, 64])
            dma('pool', lambda e: e.dma_start(out=wck[:], in_=wck_d, max_dma_last_dim=4096), writes=['wck'])
            dma('pool', lambda e: e.dma_start(out=wcv[:], in_=wcv_d, max_dma_last_dim=4096), writes=['wcv'])
            posk = TA("posk", [64, 32]); posv = TA("posv", [64, 32])
            dma('pool', lambda e: e.dma_start(out=posk[:], in_=posk_d), writes=['posk'])
            dma('pool', lambda e: e.dma_start(out=posv[:], in_=posv_d), writes=['posv'])
            ln1 = TA("ln1", [128, 2, D], F32)
            dma('sp', lambda e: e.dma_start(out=ln1[:, 0, :], in_=ln_d[0]), writes=['ln1'])
            dma('sp', lambda e: e.dma_start(out=ln1[:, 1, :], in_=ln_d[1]), writes=['ln1'])

            KslT = TA("KslT", [128, 2, SEQ])
            for g in range(2):
                for r in range(2):
                    dma('pool', lambda e, g=g, r=r: e.dma_start(out=KslT[64:128, g, r * 4096:(r + 1) * 4096],
                                                              in_=E_d, max_dma_last_dim=4096), writes=['KslT_E'])
            Vsl = TA("Vsl", [128, 64, 2, 65])
            op('pool', lambda e: e.memset(Vsl[:, :, :, 64:65], 1.0), writes=['Vsl_ones'])
            KwT = TA("KwT", [128, 2, 1024])
            op('pool', lambda e: e.memset(KwT[:], 0.0), writes=['KwT%d' % i for i in range(8)])
            Vw = TA("Vw", [128, 8, 2, 65])
            op('pool', lambda e: e.memset(Vw[:, :, :, 64:65], 1.0), writes=['Vw_ones'])
            KcT = TA("KcT", [64, 2, 528]); VcT = TA("VcT", [64, 2, 528])
            op('pool', lambda e: e.memset(KcT[:], 0.0), writes=['KcT'])
            op('pool', lambda e: e.memset(VcT[:], 0.0), writes=['VcT'])
            kcT = TA("kcT", [128, 2, 512])
            op('pool', lambda e: e.memset(kcT[:], 0.0), writes=['kcT'])
            Rc = TA("Rc", [128, 4, 2, 193])
            op('pool', lambda e: e.memset(Rc[:], 0.0), writes=['Rc'])
            for g in range(2):
                dma('pool', lambda e, g=g: e.dma_start(out=Rc[:, :, g, 0:128], in_=mimp_d), writes=['Rc'])
            op('pool', lambda e: e.memset(Rc[:, :, :, 192:193], 1.0), reads=['Rc'], writes=['Rc'])
            pbk = TA("pbk", [64, 1], F32)
            pbv = TA("pbv", [1, 64])
            ones1 = TA("ones1", [1, 128])
            op('pool', lambda e: e.memset(ones1[:], 1.0), writes=['ones1'])
            for r in range(32):
                mm(P[5][0:64, 0:1], wck[:, r, :], posk[:, r:r + 1], r == 0, r == 31, ['wck', 'posk'], ['P5'])
            op('act', lambda e: e.copy(out=pbk[:], in_=P[5][0:64, 0:1]), reads=['P5'], writes=['pbk'])
            for r in range(32):
                mm(P[5][0:1, 64:128], posv[:, r:r + 1], wcv[:, r, :], r == 0, r == 31, ['wcv', 'posv', 'pbk'], ['P5'])
            op('act', lambda e: e.copy(out=pbv[:], in_=P[5][0:1, 64:128]), reads=['P5'], writes=['pbv'])

            _stop(3)
            tabs = TA("tabs", [128, 4, 80], F32)
            big0 = TA("big0", [128, 1024], F32)
            stage = big0[:, 0:512]
            kvb = TA("kvb", [128, 512])
            wstage = TA("wstage", [128, 256], F32)
            wb = TA("wb", [128, 256])
            rtmp = big0[:, 512:1024]
            ta = TA("ta", [128, 256], F32)
            tb = TA("tb", [128, 256], F32)
            ktil = TA("ktil", [128, 512])
            rvb = TA("rvb", [128, 512])
            Sst = TA("Sst", [128, 256], F32)
            op('pool', lambda e: e.memset(Sst[:], 0.0), writes=['Sst'])
            SbZ = [TA("SbZ%d" % i, [128, 256]) for i in range(2)]
            for i in range(2):
                op('pool', lambda e, i=i: e.memset(SbZ[i][:], 0.0), writes=['Sb'])
            stmp = TA("stmp", [128, 256], F32)
            gat = TA("gat", [128, 4, 24], F32)
            qtil = TA("qtil", [128, 512])
            qtilT = TA("qtilT", [128, 4, 128])
            kz = [TA("kz%d" % i, [128, 4, 128]) for i in range(2)]
            for i in range(2):
                op('pool', lambda e, i=i: e.memset(kz[i][:], 0.0), writes=['ktilT'])
            rgs = TA("rgs", [128, 512])
            pt_rr = [0]
            onsa = TA("onsa", [128, 4, 256], F32)
            onsab = TA("onsab", [128, 256])
            innb = TA("innb", [128, 8, 128])
            big1 = TA("big1", [128, 1024], F32)
            orf = big1[:, 0:512]
            osq = big1[:, 512:1024]
            gsm = TA("gsm", [128, 8], F32); gss = TA("gss", [128, 8], F32)
            gmu = TA("gmu", [128, 8], F32); grs = TA("grs", [128, 8], F32); gm2 = TA("gm2", [128, 8], F32)
            mixret = TA("mixret", [128, 512])
            rec = TA("rec", [128, 1], F32); scg = TA("scg", [128, 1], F32)
            xo = big0
            xr = big1

            def do_rope(dst3, src3, cos2, sin2, H, half, reads, writes):
                cb = bc(cos2.unsqueeze(1), [128, H, half])
                sb = bc(sin2.unsqueeze(1), [128, H, half])
                x1 = src3[:, :, 0:half]; x2 = src3[:, :, half:2 * half]
                A = ta[:, 0:H * half].rearrange("p (h d) -> p h d", h=H)
                B = tb[:, 0:H * half].rearrange("p (h d) -> p h d", h=H)
                rd = list(reads) + ['tabs']
                op('dve', lambda e: e.tensor_tensor(out=A, in0=x1, in1=cb, op=ALU.mult), reads=rd, writes=['ta'])
                _stop(201)
                op('dve', lambda e: e.tensor_tensor(out=B, in0=x2, in1=sb, op=ALU.mult), reads=rd, writes=['tb'])
                _stop(202)
                op('dve', lambda e: e.tensor_tensor(out=dst3[:, :, 0:half], in0=A, in1=B, op=ALU.subtract),
                   reads=['ta', 'tb'], writes=writes)
                _stop(203)
                op('dve', lambda e: e.tensor_tensor(out=A, in0=x1, in1=sb, op=ALU.mult), reads=rd, writes=['ta'])
                op('dve', lambda e: e.tensor_tensor(out=B, in0=x2, in1=cb, op=ALU.mult), reads=rd, writes=['tb'])
                op('dve', lambda e: e.tensor_tensor(out=dst3[:, :, half:2 * half], in0=A, in1=B, op=ALU.add),
                   reads=['ta', 'tb'], writes=writes)

            def rope_ip(buf3, cos2, sin2, H, half, bname):
                cb = bc(cos2.unsqueeze(1), [128, H, half])
                sb = bc(sin2.unsqueeze(1), [128, H, half])
                x1 = buf3[:, :, 0:half]; x2 = buf3[:, :, half:2 * half]
                A = ta[:, 0:H * half].rearrange("p (h d) -> p h d", h=H)
                B = tb[:, 0:H * half].rearrange("p (h d) -> p h d", h=H)
                C = stmp[:, 0:H * half].rearrange("p (h d) -> p h d", h=H)
                Dd = osq[:, 0:H * half].rearrange("p (h d) -> p h d", h=H)
                rd = [bname, 'tabs']
                op('dve', lambda e: e.tensor_tensor(out=A, in0=x1, in1=cb, op=ALU.mult), reads=rd, writes=['ta'])
                op('dve', lambda e: e.tensor_tensor(out=B, in0=x2, in1=sb, op=ALU.mult), reads=rd, writes=['tb'])
                op('dve', lambda e: e.tensor_tensor(out=C, in0=x1, in1=sb, op=ALU.mult), reads=rd, writes=['stmp'])
                op('dve', lambda e: e.tensor_tensor(out=Dd, in0=x2, in1=cb, op=ALU.mult), reads=rd, writes=['osq'])
                op('dve', lambda e: e.tensor_tensor(out=x1, in0=A, in1=B, op=ALU.subtract), reads=['ta', 'tb', bname], writes=[bname])
                op('dve', lambda e: e.tensor_tensor(out=x2, in0=C, in1=Dd, op=ALU.add), reads=['stmp', 'osq', bname], writes=[bname])

            xsel = [None]

            def projn(pi, u, W, c0, c1, wname):
                xt = xsel[0] if xsel[0] is not None else xTb
                for kc in range(8):
                    mm(P[pi][:, 0:c1 - c0], xt[:, kc, u * 128:(u + 1) * 128], W[:, kc, c0:c1],
                       kc == 0, kc == 7, [('xTb' if xsel[0] is not None else 'xTb%d' % u), wname], ['P%d' % pi])

            def exp_pt(psi, c0, c1, bias_ap, breads):
                i = pt_rr[0]; pt_rr[0] = (i + 1) % 3
                pt = PT3[i]
                op('act', lambda e: e.activation(out=pt[:, c0:c1], in_=P[psi][:, c0:c1], func=AF.Exp, bias=bias_ap, scale=0.125),
                   reads=['P%d' % psi] + breads, writes=[('ptile%d' % i) if i < 2 else 'mixret'])
                return pt, (('ptile%d' % i) if i < 2 else 'mixret')

            def ot_finish(pacc, h4, h, br):
                op('act', lambda e: e.copy(out=orf[0:65, :], in_=P[pacc][0:65, :]), reads=['P%d' % pacc], writes=['orf'])
                for u in range(4):
                    op('pe', lambda e, u=u: e.transpose(P[5][:, u * 66:u * 66 + 65], orf[0:65, u * 128:(u + 1) * 128], identf[0:65, 0:65]),
                       reads=['orf', 'identf'], writes=['P5'])
                finish_branch(lambda u: P[5][:, u * 66:u * 66 + 65], lambda u: 'P5', h4, h, br, False, 64)

            def pipeline(n_items, stage1, stage2, depth=2):
                q = []
                for i in range(n_items):
                    q.append(stage1(i))
                    if len(q) > depth:
                        stage2(*q.pop(0))
                while q:
                    stage2(*q.pop(0))

            def finish_branch(pacc, pn, h4, h, br, first, ow):
                for u in range(4):
                    pa = pacc(u)
                    op('dve', lambda e, pa=pa: e.tensor_scalar(out=rec[:], in0=pa[:, ow:ow + 1], scalar1=1e-30, scalar2=None, op0=ALU.max),
                       reads=[pn(u)], writes=['rec'])
                    op('dve', lambda e: e.reciprocal(out=rec[:], in_=rec[:]), reads=['rec'], writes=['rec'])
                    op('dve', lambda e, u=u: e.tensor_tensor(out=scg[:], in0=rec[:], in1=gat[:, u, h * 3 + br:h * 3 + br + 1], op=ALU.mult),
                       reads=['rec', 'gat'], writes=['scg'])
                    dst = onsa[:, u, h4 * 64:(h4 + 1) * 64]
                    if first:
                        op('dve', lambda e, pa=pa, dst=dst: e.tensor_scalar(out=dst, in0=pa[:, ow - 64:ow], scalar1=scg[:, 0:1],
                                                                             scalar2=None, op0=ALU.mult),
                           reads=[pn(u), 'scg'], writes=['onsa'])
                    else:
                        op('dve', lambda e, pa=pa, dst=dst: e.scalar_tensor_tensor(out=dst, in0=pa[:, ow - 64:ow], scalar=scg[:, 0:1],
                                                                                    in1=dst, op0=ALU.mult, op1=ALU.add),
                           reads=[pn(u), 'scg', 'onsa'], writes=['onsa'])
                    if br == 0:
                        dsti = impacc[:, u, :]
                        if h4 == 0:
                            op('dve', lambda e, pa=pa, dsti=dsti: e.tensor_scalar(out=dsti, in0=pa[:, 0:128], scalar1=rec[:, 0:1],
                                                                                   scalar2=None, op0=ALU.mult),
                               reads=[pn(u), 'rec'], writes=['impacc'])
                        else:
                            op('dve', lambda e, pa=pa, dsti=dsti: e.scalar_tensor_tensor(out=dsti, in0=pa[:, 0:128], scalar=rec[:, 0:1],
                                                                                          in1=dsti, op0=ALU.mult, op1=ALU.add),
                               reads=[pn(u), 'rec', 'impacc'], writes=['impacc'])

            def compress_tile(s):
                    ctp, row0 = s // 4, 32 * (s % 4)
                    tp96 = {'tile_position': (0, 96)} if row0 == 96 else {}
                    for r in range(32):
                        mm(P[4][0:64, 0:64].rearrange("p (g n) -> p g n", g=2), wck[:, r, :], KcT[:, :, r:r + 497:16], r == 0, r == 31,
                           ['wck', 'KcT'], ['P4'])
                    op('act', lambda e: e.activation(out=kcT[0:64, :, 32 * s:32 * s + 32],
                                                     in_=P[4][0:64, 0:64].rearrange("p (g n) -> p g n", g=2),
                                                     func=AF.Identity, bias=pbk[:, 0:1], scale=1.0),
                       reads=['P4', 'pbk'], writes=['kcT'])
                    for g in range(2):
                        for r in range(33):
                            if r < 32:
                                mm(P[5][row0:row0 + 32, g * 64:(g + 1) * 64], VcT[:, g, r:r + 497:16], wcv[:, r, :],
                                   r == 0, False, ['wcv', 'VcT'], ['P5'], **tp96)
                            else:
                                mm(P[5][row0:row0 + 32, g * 64:(g + 1) * 64], ones1[0:1, 0:32], pbv[0:1, :],
                                   False, True, ['ones1', 'pbv'], ['P5'], **tp96)
                        op('act', lambda e, g=g: e.copy(out=Rc[row0:row0 + 32, ctp, g, 128:192],
                                                         in_=P[5][row0:row0 + 32, g * 64:(g + 1) * 64]),
                           reads=['P5'], writes=['Rc'])
                    op('dve', lambda e: e.tensor_copy(out=KcT[:, :, 0:16], in_=KcT[:, :, 512:528]), reads=['KcT'], writes=['KcT'])
                    op('dve', lambda e: e.tensor_copy(out=VcT[:, :, 0:16], in_=VcT[:, :, 512:528]), reads=['VcT'], writes=['VcT'])


            stP = contextlib.ExitStack()

            def TP(name, shape, dt=BF16):
                return stP.enter_context(nc.sbuf_tensor("s_" + name, list(shape), dt))
            masks = TP("masks", [128, 2304])
            dma('pool', lambda e: e.dma_start(out=masks[:], in_=masks_d, max_dma_last_dim=4096), writes=['masks'])
            tri = TP("tri", [128, 128], F32)
            dma('sp', lambda e: e.dma_start(out=tri[:], in_=tri_d), writes=['tri'])
            Qa = TP("Qa", [128, 2, 4, 512])
            op('pool', lambda e: e.memset(Qa[:], 0.0), writes=['Qa0', 'Qa1'])
            selT = TP("selT", [128, 2, 512])
            PTt = [TP("PT%d" % i, [128, 512]) for i in range(2)]
            impacc = TP("impacc", [128, 4, 128], F32)
            bon = TP("bon", [128, 128], F32)
            score = TP("score", [128, 128], F32)
            sc2 = TP("sc2", [128, 128], F32)
            m8a = TP("m8a", [128, 8], F32); m8b = TP("m8b", [128, 8], F32)
            selb = TP("selb", [128, 256])
            qb = TP("qb", [128, 4, 512])
            onrm = TP("onrm", [128, 4, 512])
            xTb = TP("xTb", [128, 8, 512])
            mixT = TP("mixT", [128, 8, 512])
            _stop(4)
            load_wt(0, lambda kc: win_d[kc, :, C_RQ:C_RQ + 512], 512)

            PT3 = [PTt[0], PTt[1], mixret]
            SB3 = [0, 1, 4]
            kvbL = [(kvb, 'kvb'), (mixret, 'mixret')]
            ktilL = [(ktil, 'ktil'), (PTt[0], 'ptile0')]
            rvbL = [(rvb, 'rvb'), (rgs, 'rgs')]
            wbL = [(wb, 'wb'), (onsab, 'onsab')]
            for s in range(int(os.environ.get('K_NT', NT))):
                own = (s % 4 == 3)
                _CUR_S[0] = s
                k = s // 4
                wtile = (s % 4 in (2, 3))
                wr = s % 2
                def load_x(sn):
                    for uu in range(4):
                        dma('pool', lambda e, uu=uu: e.dma_start(
                            out=xTb[:, :, uu * 128:(uu + 1) * 128],
                            in_=xT_d[:, :, sn * 512 + uu * 128:sn * 512 + (uu + 1) * 128].rearrange("k p t -> p k t")),
                            writes=['xTb%d' % uu])
                if s == 0 or (s % 4 == 0):
                    load_x(s)
                dma('sp', lambda e: e.dma_start(out=tabs[:], in_=tabs_d[4 * s:4 * s + 4].rearrange("u p c -> p u c")),
                    writes=['tabs'])
                _stop(10)
                def part1(u):
                        kt = 4 * s + u
                        cosN = tabs[:, u, 0:8]; sinN = tabs[:, u, 8:16]
                        cosR = tabs[:, u, 16:48]; sinR = tabs[:, u, 48:80]
                        kvb, kvbn = kvbL[u % 2]
                        ktil, ktiln = ktilL[u % 2]
                        rvb, rvbn = rvbL[u % 2]
                        wb, wbn = wbL[u % 2]
                        projn(0, u, Wkv, C_KVA, C_KVA + 512, 'Wkv')
                        op('act', lambda e: e.copy(out=stage[:], in_=P[0][:]), reads=['P0'], writes=['stage'])
                        rope_ip(stage[:, 256:384].rearrange("p (g d) -> p g d", g=2), cosN, sinN, 2, 8, 'stage')
                        if own:
                            dma('sp', lambda e, u=u: e.dma_start(out=kv_d[4 * k + u], in_=stage[:]), reads=['stage'])
                        op('act', lambda e: e.copy(out=kvb[:], in_=stage[:]), reads=['stage'], writes=[kvbn])
                        projn(1, u, Wkv, C_RK, C_RK + 512, 'Wkv')
                        projn(2, u, Wkv, C_RV, C_RV + 512, 'Wkv')
                        op('act', lambda e: e.copy(out=rtmp[:], in_=P[1][:]), reads=['P1'], writes=['rtmp'])
                        rope_ip(rtmp[:].rearrange("p (h d) -> p h d", h=8), cosR, sinR, 8, 32, 'rtmp')
                        op('dve', lambda e: e.tensor_tensor(out=ktil[:].rearrange("p (h d) -> p h d", h=8),
                                                            in0=rtmp[:].rearrange("p (h d) -> p h d", h=8),
                                                            in1=bc(dec[:, 8:16].unsqueeze(2), [128, 8, 64]), op=ALU.mult),
                           reads=['rtmp', 'dec'], writes=[ktiln])
                        op('act', lambda e: e.copy(out=rvb[:], in_=P[2][:]), reads=['P2'], writes=[rvbn])
                        if wtile:
                            projn(3, u, Wkv, C_WIN, C_WIN + 256, 'Wkv')
                            op('act', lambda e: e.copy(out=wstage[:], in_=P[3][:, 0:256]), reads=['P3'], writes=['wstage'])
                            rope_ip(wstage[:, 0:128].rearrange("p (g d) -> p g d", g=2), cosN, sinN, 2, 8, 'wstage')
                            if s == NT - 1:
                                dma('sp', lambda e, u=u: e.dma_start(out=wout_d[u], in_=wstage[:]), reads=['wstage'])
                            op('act', lambda e: e.copy(out=wb[:], in_=wstage[:]), reads=['wstage'], writes=[wbn])

                def part2(u):
                        kt = 4 * s + u
                        cosN = tabs[:, u, 0:8]; sinN = tabs[:, u, 8:16]
                        cosR = tabs[:, u, 16:48]; sinR = tabs[:, u, 48:80]
                        kvb, kvbn = kvbL[u % 2]
                        ktil, ktiln = ktilL[u % 2]
                        rvb, rvbn = rvbL[u % 2]
                        wb, wbn = wbL[u % 2]
                        srcs = [kvb[:, i * 64:(i + 1) * 64] for i in range(6)]
                        rds = [kvbn]
                        if wtile:
                            srcs += [wb[:, 0:64], wb[:, 64:128]]
                            rds.append(wbn)
                        pbk_, pbn_ = tbatch(srcs, rds)
                        op('act', lambda e: e.copy(out=KcT[:, :, 16 + u * 128:16 + (u + 1) * 128], in_=pbk_[0:64, 0:2, :]),
                           reads=[pbn_], writes=['KcT'])
                        op('act', lambda e: e.copy(out=VcT[:, :, 16 + u * 128:16 + (u + 1) * 128], in_=pbk_[0:64, 2:4, :]),
                           reads=[pbn_], writes=['VcT'])
                        op('dve', lambda e: e.tensor_copy(out=KslT[0:64, :, kt * 128:(kt + 1) * 128], in_=pbk_[0:64, 4:6, :]),
                           reads=[pbn_], writes=['KslT%d' % kt])
                        op('dve', lambda e, kt=kt: e.tensor_copy(out=Vsl[:, kt, :, 0:64],
                                                                   in_=kvb[:, 384:512].rearrange("p (g d) -> p g d", g=2)),
                           reads=[kvbn], writes=['Vsl%d' % kt])
                        if wtile:
                            wkt = wr * 4 + u
                            op('act', lambda e: e.copy(out=KwT[0:64, :, wkt * 128:(wkt + 1) * 128], in_=pbk_[0:64, 6:8, :]),
                               reads=[pbn_], writes=['KwT%d' % wkt])
                            op('act', lambda e, wkt=wkt: e.copy(out=Vw[:, wkt, :, 0:64],
                                                                         in_=wb[:, 128:256].rearrange("p (g d) -> p g d", g=2)),
                               reads=[wbn], writes=['Vw%d' % wkt])
                        if own:
                            op('act', lambda e: e.copy(out=SbZ[0][0:64, :], in_=Sst[0:64, :]), reads=['Sst'], writes=['Sb'])
                            op('act', lambda e: e.copy(out=SbZ[1][64:128, :], in_=Sst[64:128, :]), reads=['Sst'], writes=['Sb'])
                            projn(3, u, Wt[0], 0, 512, 'Wt0')
                            op('act', lambda e: e.copy(out=rtmp[:], in_=P[3][:]), reads=['P3'], writes=['rtmp'])
                            rope_ip(rtmp[:].rearrange("p (h d) -> p h d", h=8), cosR, sinR, 8, 32, 'rtmp')
                            op('dve', lambda e: e.tensor_tensor(out=qtil[:].rearrange("p (h d) -> p h d", h=8),
                                                                in0=rtmp[:].rearrange("p (h d) -> p h d", h=8),
                                                                in1=bc(dec[:, 0:8].unsqueeze(2), [128, 8, 64]), op=ALU.mult),
                               reads=['rtmp', 'dec'], writes=['qtil'])
                            for hp in range(4):
                                slot = tr_rr[0]; tr_rr[0] = (slot + 1) % 2
                                pst = PTb[slot][:, 0, :]
                                op('pe', lambda e, pst=pst, hp=hp: e.transpose(pst, ktil[:, hp * 128:(hp + 1) * 128], ident[:]),
                                   reads=[ktiln, 'ident'], writes=['PTr%d' % slot])
                                op('act', lambda e, pst=pst, hp=hp: e.copy(out=kz[0][0:64, hp, :], in_=pst[0:64, :]),
                                   reads=['PTr%d' % slot], writes=['ktilT'])
                                op('act', lambda e, pst=pst, hp=hp: e.copy(out=kz[1][64:128, hp, :], in_=pst[64:128, :]),
                                   reads=['PTr%d' % slot], writes=['ktilT'])
                                transpose_to(qtilT[:, hp, :], qtil[:, hp * 128:(hp + 1) * 128], 128, 128, ['qtil'], ['qtilT'], evac='dve')
                            for h in range(8):
                                hp, h2 = h // 2, h % 2
                                bp = 64 * h2
                                pb = 4 + h // 4
                                mm(P[pb][:, (h % 4) * 128:(h % 4 + 1) * 128], kz[h2][:, hp, :], qtilT[:, hp, :],
                                   True, True, ['ktilT', 'qtilT'], ['P%d' % pb])
                            for half in range(2):
                                op('dve', lambda e, half=half: e.tensor_tensor(
                                    out=innb[:, half * 4:(half + 1) * 4, :],
                                    in0=P[4 + half][:].rearrange("p (h i) -> p h i", h=4),
                                    in1=bc(tri[:].unsqueeze(1), [128, 4, 128]), op=ALU.mult),
                                   reads=['P%d' % (4 + half), 'tri'], writes=['innb%d' % half])
                            for h in range(8):
                                hp, h2 = h // 2, h % 2
                                bp = 64 * h2
                                mm(P[3][:, h * 64:(h + 1) * 64], innb[:, h, :], rvb[:, h * 64:(h + 1) * 64], True, False,
                                   ['innb%d' % (h // 4), rvbn, 'rtmp'], ['P3'])
                                mm(P[3][:, h * 64:(h + 1) * 64], qtilT[:, hp, :], SbZ[h2][:, hp * 64:(hp + 1) * 64],
                                   False, True, ['qtilT', 'Sb'], ['P3'])
                            op('act', lambda e: e.copy(out=orf[:], in_=P[3][:]), reads=['P3'], writes=['orf'])
                            orf3 = orf[:].rearrange("p (h d) -> p h d", h=8)
                            osq3 = osq[:].rearrange("p (h d) -> p h d", h=8)
                            op('dve', lambda e: e.tensor_reduce(out=gsm[:], in_=orf3, axis=AX.X, op=ALU.add), reads=['orf'], writes=['gsm'])
                            op('dve', lambda e: e.tensor_tensor(out=osq[:], in0=orf[:], in1=orf[:], op=ALU.mult), reads=['orf'], writes=['osq'])
                            op('dve', lambda e: e.tensor_reduce(out=gss[:], in_=osq3, axis=AX.X, op=ALU.add), reads=['osq'], writes=['gss'])
                            op('dve', lambda e: e.tensor_scalar(out=gmu[:], in0=gsm[:], scalar1=1.0 / 64, scalar2=None, op0=ALU.mult),
                               reads=['gsm'], writes=['gmu'])
                            op('dve', lambda e: e.tensor_tensor(out=gm2[:], in0=gmu[:], in1=gmu[:], op=ALU.mult), reads=['gmu'], writes=['gm2'])
                            op('dve', lambda e: e.scalar_tensor_tensor(out=grs[:], in0=gss[:], scalar=1.0 / 64, in1=gm2[:],
                                                                       op0=ALU.mult, op1=ALU.subtract),
                               reads=['gss', 'gm2'], writes=['grs'])
                            op('act', lambda e: e.activation(out=grs[:], in_=grs[:], func=AF.Sqrt, bias=eps_t[:, 0:1], scale=1.0),
                               reads=['grs', 'eps'], writes=['grs'])
                            op('dve', lambda e: e.reciprocal(out=grs[:], in_=grs[:]), reads=['grs'], writes=['grs'])
                            op('dve', lambda e: e.tensor_tensor(out=osq3, in0=orf3, in1=bc(gmu[:].unsqueeze(2), [128, 8, 64]), op=ALU.subtract),
                               reads=['orf', 'gmu', 'gss'], writes=['osq'])
                            op('dve', lambda e: e.tensor_tensor(out=osq3, in0=osq3, in1=bc(grs[:].unsqueeze(2), [128, 8, 64]), op=ALU.mult),
                               reads=['osq', 'grs'], writes=['osq'])
                            op('dve', lambda e, u=u: e.tensor_tensor(out=onrm[:, u, :], in0=osq[:], in1=retg[:], op=ALU.mult),
                               reads=['osq', 'retg'], writes=['onrm%d' % u])
                        for h in range(8):
                            hp, h2 = h // 2, h % 2
                            mm(P[5][h2 * 64:(h2 + 1) * 64, hp * 64:(hp + 1) * 64], ktil[:, h * 64:(h + 1) * 64],
                               rvb[:, h * 64:(h + 1) * 64], True, True, [ktiln, rvbn], ['P5'])
                        op('dve', lambda e: e.tensor_tensor(out=stmp[:], in0=P[5][:, 0:256], in1=Sst[:], op=ALU.add),
                           reads=['P5', 'Sst'], writes=['stmp'])
                        op('dve', lambda e: e.tensor_tensor(out=Sst[:], in0=stmp[:], in1=gC[:], op=ALU.mult),
                           reads=['stmp', 'gC', 'Sb'], writes=['Sst'])


                part1(0)
                for u in range(4):
                    if u + 1 < 4:
                        part1(u + 1)
                    elif (not own) and s + 1 < NT:
                        load_x(s + 1)
                    part2(u)

                _stop(15)
                compress_tile(s)
                _stop(16)
                if s == NT - 1:
                    dma('sp', lambda e: e.dma_start(out=ret_d, in_=Sst[:]), reads=['Sst'])
                if not own or os.environ.get('K_OWN', '1') == '0':
                    continue

                load_wt(1, lambda kc: win_d[kc, :, C_Q:C_Q + 536], 536)
                for u in range(4):
                    cosN = tabs[:, u, 0:8]; sinN = tabs[:, u, 8:16]
                    projn(0, u, Wt[1], 0, 512, 'Wt1')
                    projn(1, u, Wt[1], 512, 536, 'Wt1')
                    op('act', lambda e, u=u: e.copy(out=qb[:, u, :], in_=P[0][:]), reads=['P0'], writes=['qb'])
                    do_rope(qb[:, u, :].rearrange("p (h d) -> p h d", h=8), P[0][:].rearrange("p (h d) -> p h d", h=8),
                            cosN, sinN, 8, 8, ['P0'], ['qb'])
                    op('act', lambda e, u=u: e.activation(out=gat[:, u, :], in_=P[1][:, 0:24], func=AF.Sigmoid),
                       reads=['P1'], writes=['gat'])
                load_wt(0, lambda kc: win_d[kc, :, C_RG:C_RG + 512], 512)
                for u in range(4):
                    projn(2, u, Wt[0], 0, 512, 'Wt0')
                    op('act', lambda e: e.activation(out=rgs[:], in_=P[2][:], func=AF.Silu), reads=['P2'], writes=['rgs'])
                    op('dve', lambda e, u=u: e.tensor_tensor(out=mixret[:], in0=onrm[:, u, :], in1=rgs[:], op=ALU.mult),
                       reads=['onrm%d' % u, 'rgs'], writes=['mixret'])
                    for c in range(4):
                        transpose_to(mixT[:, 4 + c, u * 128:(u + 1) * 128], mixret[:, c * 128:(c + 1) * 128], 128, 128,
                                     ['mixret'], ['mixT'])

                for g in range(2):
                    for h4 in range(4):
                        h = 4 * g + h4
                        for u in range(4):
                            transpose_to(Qa[0:64, 0, h4, u * 128:(u + 1) * 128], qb[:, u, h * 64:(h + 1) * 64], 128, 64,
                                         ['qb'], ['Qa0'], evac='dve')
                    op('dve', lambda e: e.tensor_copy(out=Qa[0:64, 1, :, :], in_=Qa[0:64, 0, :, :]), reads=['Qa0'], writes=['Qa1'])
                    for h4 in range(4):
                        h = 4 * g + h4

                        def c_s1(ct, h4=h4):
                            psi = SB3[ct % 3]
                            mm(P[psi][:], kcT[:, g, ct * 128:(ct + 1) * 128], Qa[:, 0, h4, :], True, ct != k,
                               ['kcT', 'Qa0'], ['P%d' % psi])
                            if ct == k:
                                mm(P[psi][:], ident[:], masks[:, 1792:2304], False, True, ['ident', 'masks'], ['P%d' % psi])
                            return (ct,) + exp_pt(psi, 0, 512, kbc[:, ct:ct + 1], ['kbc'])

                        def c_s2(pct, pt, ptn):
                            for u in range(4):
                                pb = 2 + u // 2
                                mm(P[pb][:, (u % 2) * 193:(u % 2 + 1) * 193], pt[:, u * 128:(u + 1) * 128], Rc[:, pct, g, :],
                                   (pct == 0 and u % 2 == 0), pct == k, [ptn, 'Rc'], ['P%d' % pb], skip_group_check=True)
                        pipeline(k + 1, c_s1, c_s2)
                        finish_branch(lambda u: P[2 + u // 2][:, (u % 2) * 193:(u % 2 + 1) * 193],
                                      lambda u: 'P%d' % (2 + u // 2), h4, h, 0, True, 192)
                    for u in range(4):
                        dma('sp', lambda e, u=u: e.dma_start(out=bon[:], in_=bonus_d[4 * k + u]), writes=['bon'])
                        op('dve', lambda e, u=u: e.tensor_tensor(out=score[:], in0=impacc[:, u, :], in1=bon[:], op=ALU.add),
                           reads=['impacc', 'bon'], writes=['score'])
                        op('dve', lambda e: e.max(out=m8a[:], in_=score[:]), reads=['score'], writes=['m8a'])
                        op('dve', lambda e: e.match_replace(out=sc2[:], in_to_replace=m8a[:], in_values=score[:], imm_value=-3e38),
                           reads=['score', 'm8a'], writes=['sc2'])
                        op('dve', lambda e: e.max(out=m8b[:], in_=sc2[:]), reads=['sc2'], writes=['m8b'])
                        op('dve', lambda e: e.tensor_scalar(out=sc2[:], in0=score[:], scalar1=m8b[:, 7:8], scalar2=-1.0,
                                                            op0=ALU.is_ge, op1=ALU.add),
                           reads=['score', 'm8b'], writes=['sc2'])
                        op('dve', lambda e: e.tensor_scalar(out=selb[:, 0:128], in0=sc2[:], scalar1=-NEGB, scalar2=None, op0=ALU.mult),
                           reads=['sc2'], writes=['selb'])
                        op('dve', lambda e: e.tensor_copy(out=selb[:, 128:256], in_=selb[:, 0:128]), reads=['selb'], writes=['selb'])
                        transpose_to(selT[:, 1, u * 128:(u + 1) * 128], selb[:, 0:128], 128, 128, ['selb'], ['selT'])
                        transpose_to(selT[:, 0, u * 128:(u + 1) * 128], selb[:, 64:192], 128, 128, ['selb'], ['selT'])
                    for r in range(2):
                        for h4 in range(4):
                            op('dve', lambda e, r=r, h4=h4: e.tensor_copy(out=Qa[64:128, r, h4, :], in_=selT[64:128, r, :]),
                               reads=['selT'], writes=['Qa%d' % r])
                    nkt = 4 * s + 4
                    for h4 in range(4):
                        h = 4 * g + h4

                        def s_s1(kt, h4=h4):
                            r = kt // 32
                            o = kt - 4 * s
                            c0 = 128 * o if o > 0 else 0
                            psi = SB3[kt % 3]
                            mm(P[psi][:, c0:512], KslT[:, g, kt * 128:(kt + 1) * 128], Qa[:, r, h4, c0:512], True, o < 0,
                               ['KslT%d' % kt, 'KslT_E', 'Qa%d' % r], ['P%d' % psi])
                            if o >= 0:
                                mm(P[psi][:, c0:512], ident[:], causal(o)[:, c0:512], False, True, ['ident', 'masks'], ['P%d' % psi])
                            return (kt, o) + exp_pt(psi, c0, 512, kbias[:, kt:kt + 1], ['kbias'])

                        def s_s2(pkt, po, pt, ptn):
                            c0 = 128 * po if po > 0 else 0
                            mm(P[2][0:65, c0:512], Vsl[:, pkt, g, :], pt[:, c0:512], pkt == 0, pkt == nkt - 1,
                               [ptn, 'Vsl%d' % pkt, 'Vsl_ones'], ['P2'])
                        pipeline(nkt, s_s1, s_s2)
                        ot_finish(2, h4, h, 1)
                    for h4 in range(4):
                        h = 4 * g + h4

                        def w_s1(o8, h4=h4):
                            psi = SB3[o8 % 3]
                            if o8 < 4:
                                c0, c1 = 0, 128 * (o8 + 1)
                                msk = wlo(o8)
                            else:
                                c0, c1 = 128 * (o8 - 4), 512
                                msk = causal(o8 - 4)
                            mm(P[psi][:, c0:c1], KwT[:, g, o8 * 128:(o8 + 1) * 128], Qa[:, 0, h4, c0:c1], True, False,
                               ['KwT%d' % o8, 'Qa0'], ['P%d' % psi])
                            mm(P[psi][:, c0:c1], ident[:], msk[:, c0:c1], False, True, ['ident', 'masks'], ['P%d' % psi])
                            kt = 4 * (s - 1) + o8
                            return (o8, c0, c1) + exp_pt(psi, c0, c1, kbias[:, kt:kt + 1], ['kbias'])

                        def w_s2(p8, pc0, pc1, pt, ptn):
                            mm(P[3][0:65, pc0:pc1], Vw[:, p8, g, :], pt[:, pc0:pc1], p8 == 0, p8 == 7,
                               [ptn, 'Vw%d' % p8, 'Vw_ones'], ['P3'], skip_group_check=True)
                        pipeline(8, w_s1, w_s2)
                        ot_finish(3, h4, h, 2)
                    for u in range(4):
                        op('act', lambda e, u=u: e.copy(out=onsab[:], in_=onsa[:, u, :]), reads=['onsa'], writes=['onsab'])
                        for c2 in range(2):
                            transpose_to(mixT[:, 2 * g + c2, u * 128:(u + 1) * 128], onsab[:, c2 * 128:(c2 + 1) * 128], 128, 128,
                                         ['onsab'], ['mixT'], evac='dve')

                load_wt(0, lambda kc: wo_d[kc, :, 0:512], 512)
                load_wt(1, lambda kc: wo_d[kc, :, 512:1024], 512)
                for u in range(4):
                    dma('sp', lambda e, u=u: e.dma_start(out=xo[:], in_=xown_d[4 * k + u]), writes=['stage', 'rtmp'])
                    for half in range(2):
                        for c in range(8):
                            mm(P[half][:], mixT[:, c, u * 128:(u + 1) * 128], Wt[half][:, c, 0:512], c == 0, c == 7,
                               ['mixT', 'Wt%d' % half], ['P%d' % half])
                        op('dve', lambda e, half=half: e.scalar_tensor_tensor(out=xr[:, half * 512:(half + 1) * 512],
                                                                              in0=xo[:, half * 512:(half + 1) * 512], scalar=ALPHA,
                                                                              in1=P[half][:], op0=ALU.mult, op1=ALU.add),
                           reads=['stage', 'rtmp', 'P%d' % half], writes=['orf', 'osq'])
                    layer_norm(xo[:], xr[:], ln1[:, 0, :], ln1[:, 1, :], ['orf', 'osq'], ['stage', 'rtmp'], 'ln1')
                    dma('sp', lambda e, u=u: e.dma_start(out=x1s_d[4 * k + u], in_=xo[:]), reads=['stage', 'rtmp'], writes=['x1s%d' % (4 * k + u)])
                if k < 3:
                    load_wt(0, lambda kc: win_d[kc, :, C_RQ:C_RQ + 512], 512)

            S.barrier()
            stP.close()
            if DO_SAMPLE:
                u = 0
                xTbs = TA("xTbs", [128, 8, 128])
                xsel[0] = xTbs
                mixTs = TA("mixTs", [128, 8, 128])
                KcTs = TA("KcTs", [64, 2, 1040]); VcTs = TA("VcTs", [64, 2, 1040])

                def compress8(s8):
                    ctp, row0 = s8 // 2, 64 * (s8 % 2)
                    for r in range(32):
                        mm(P[4][0:64, 0:128].rearrange("p (g n) -> p g n", g=2), wck[:, r, :], KcTs[:, :, r:r + 1009:16], r == 0, r == 31,
                           ['wck', 'KcTs'], ['P4'])
                    op('act', lambda e: e.activation(out=kcT[0:64, :, 64 * s8:64 * s8 + 64],
                                                     in_=P[4][0:64, 0:128].rearrange("p (g n) -> p g n", g=2),
                                                     func=AF.Identity, bias=pbk[:, 0:1], scale=1.0),
                       reads=['P4', 'pbk'], writes=['kcT'])
                    for g in range(2):
                        for r in range(33):
                            if r < 32:
                                mm(P[5][row0:row0 + 64, g * 64:(g + 1) * 64], VcTs[:, g, r:r + 1009:16], wcv[:, r, :],
                                   r == 0, False, ['wcv', 'VcTs'], ['P5'])
                            else:
                                mm(P[5][row0:row0 + 64, g * 64:(g + 1) * 64], ones1[0:1, 0:64], pbv[0:1, :],
                                   False, True, ['ones1', 'pbv'], ['P5'])
                        op('act', lambda e, g=g: e.copy(out=Rc[row0:row0 + 64, ctp, g, 128:192],
                                                         in_=P[5][row0:row0 + 64, g * 64:(g + 1) * 64]),
                           reads=['P5'], writes=['Rc'])
                    op('dve', lambda e: e.tensor_copy(out=KcTs[:, :, 0:16], in_=KcTs[:, :, 1024:1040]), reads=['KcTs'], writes=['KcTs'])
                    op('dve', lambda e: e.tensor_copy(out=VcTs[:, :, 0:16], in_=VcTs[:, :, 1024:1040]), reads=['VcTs'], writes=['VcTs'])
                qbs = TA("qbs", [128, 1, 512])
                onrm_s = TA("onrm_s", [128, 1, 512])
                ohs = TA("ohs", [128, 4], F32)
                dma('sp', lambda e: e.dma_start(out=ohs[:], in_=oh_d), writes=['ohs'])
                kbself = TA("kbself", [128, 4], F32)
                dma('sp', lambda e: e.dma_start(out=kbself[:], in_=kbself_d), writes=['kbself'])
                kbws = TA("kbws", [128, 4], F32)
                dma('sp', lambda e: e.dma_start(out=kbws[:], in_=kbws_d), writes=['kbws'])
                kbcs = TA("kbcs", [128, 4], F32)
                dma('sp', lambda e: e.dma_start(out=kbcs[:], in_=kbcs_d), writes=['kbcs'])
                bons = TA("bons", [1, 128], F32)
                dma('sp', lambda e: e.dma_start(out=bons[:], in_=bons_d), writes=['bons'])
                decs = TA("decs", [128, 16], F32)
                dma('sp', lambda e: e.dma_start(out=decs[:], in_=decs_d), writes=['decs'])
                gC1 = TA("gC1", [128, 256], F32)
                dma('sp', lambda e: e.dma_start(out=gC1[:], in_=gC1_d), writes=['gC1'])
                zcol = TA("zcol", [128, 1], F32)
                op('pool', lambda e: e.memset(zcol[:], 0.0), writes=['zcol'])
                oneb = TA("oneb", [1, 1])
                op('pool', lambda e: e.memset(oneb[:], 1.0), writes=['oneb'])
                pti = TA("pti", [128, 256], I32)
                dma('sp', lambda e: e.dma_start(out=pti[:], in_=pt_d), writes=['pti'])
                ptf = TA("ptf", [128, 256], F32)
                iop = TA("iop", [128, 1], I32)
                iof = TA("iof", [128, 1], F32)
                idx = TA("idx", [128, 256], I32)
                op('pool', lambda e: e.iota(iop[:], pattern=[[0, 1]], base=0, channel_multiplier=1), writes=['iop'])
                op('dve', lambda e: e.tensor_copy(out=iof[:], in_=iop[:]), reads=['iop'], writes=['iof'])
                op('dve', lambda e: e.tensor_copy(out=ptf[:], in_=pti[:]), reads=['pti'], writes=['ptf'])
                op('dve', lambda e: e.tensor_scalar(out=ptf[:], in0=ptf[:], scalar1=128.0, scalar2=iof[:, 0:1],
                                                    op0=ALU.mult, op1=ALU.add), reads=['ptf', 'iof'], writes=['ptf'])
                op('dve', lambda e: e.tensor_copy(out=idx[:], in_=ptf[:]), reads=['ptf'], writes=['idx'])

                KnT = TA("KnT", [128, 2, 128]); Vn = TA("Vn", [128, 2, 65])
                KwnT = TA("KwnT", [128, 2, 128]); Vwn = TA("Vwn", [128, 2, 65])
                for t_ in (KnT, KwnT):
                    op('pool', lambda e, t_=t_: e.memset(t_[:], 0.0), writes=['Knew'])
                for t_ in (Vn, Vwn):
                    op('pool', lambda e, t_=t_: e.memset(t_[:, :, 64:65], 1.0), writes=['Vnew'])
                QTs = TA("QTs", [64, 8, 128])
                Qs = TA("Qs", [128, 2, 4])
                op('pool', lambda e: e.memset(Qs[:], 0.0), writes=['Qs'])
                pts = TA("pts", [128, 64])
                accs = TA("accs", [4, 193], F32)
                rec4 = TA("rec4", [4, 1], F32)
                gcol = TA("gcol", [4, 3], F32)
                sc4 = TA("sc4", [4, 1], F32)
                ocmb = TA("ocmb", [4, 64], F32)
                srow = TA("srow", [1, 128], F32)
                srow2 = TA("srow2", [1, 128], F32)
                s8a = TA("s8a", [1, 8], F32); s8b = TA("s8b", [1, 8], F32)
                selr = TA("selr", [1, 256])
                qz = [TA("qz%d" % b, [128, 4, 128]) for b in range(4)]
                kvbs = [TA("kvbs%d" % i, [128, 512]) for i in range(3)]
                SbS = [[TA("SbS%d_%d" % (b, i), [128, 256]) for i in range(2)] for b in range(4)]
                SsT = [TA("SsT%d" % b, [128, 256], F32) for b in range(4)]
                ktz = TA("ktz", [128, 512])

                for kc in range(8):
                    dma('pool', lambda e, kc=kc: e.dma_start(out=xTbs[:, kc, :], in_=xsT_d[kc]), writes=['xTb'])
                dma('sp', lambda e: e.dma_start(out=tabs[:, 0, :], in_=tabs_s_d), writes=['tabs'])
                cosN = tabs[:, 0, 0:8]; sinN = tabs[:, 0, 8:16]
                cosR = tabs[:, 0, 16:48]; sinR = tabs[:, 0, 48:80]
                load_wt(0, lambda kc: win_d[kc, :, C_RQ:C_RQ + 512], 512)
                projn(0, u, Wkv, C_KVA, C_KVA + 512, 'Wkv')
                op('act', lambda e: e.copy(out=stage[:], in_=P[0][:]), reads=['P0'], writes=['stage'])
                do_rope(stage[:, 256:384].rearrange("p (g d) -> p g d", g=2),
                        P[0][:, 256:384].rearrange("p (g d) -> p g d", g=2), cosN, sinN, 2, 8, ['P0', 'stage'], ['stage'])
                dma('sp', lambda e: e.dma_start(out=kvs_d, in_=stage[:]), reads=['stage'])
                op('pool', lambda e: e.tensor_copy(out=kvb[:], in_=stage[:]), reads=['stage'], writes=['kvb'])
                for g in range(2):
                    transpose_to(KnT[0:64, g, :], kvb[:, 256 + g * 64:256 + (g + 1) * 64], 128, 64, ['kvb'], ['Knew'])
                op('pool', lambda e: e.tensor_copy(out=Vn[:, :, 0:64], in_=kvb[:, 384:512].rearrange("p (g d) -> p g d", g=2)),
                   reads=['kvb'], writes=['Vnew'])
                projn(1, u, Wkv, C_RK, C_RK + 512, 'Wkv')
                projn(2, u, Wkv, C_RV, C_RV + 512, 'Wkv')
                op('act', lambda e: e.copy(out=rtmp[:], in_=P[1][:]), reads=['P1'], writes=['rtmp'])
                do_rope(rtmp[:].rearrange("p (h d) -> p h d", h=8), P[1][:].rearrange("p (h d) -> p h d", h=8),
                        cosR, sinR, 8, 32, ['P1'], ['rtmp'])
                op('dve', lambda e: e.tensor_tensor(out=ktil[:].rearrange("p (h d) -> p h d", h=8),
                                                    in0=rtmp[:].rearrange("p (h d) -> p h d", h=8),
                                                    in1=bc(decs[:, 8:16].unsqueeze(2), [128, 8, 64]), op=ALU.mult),
                   reads=['rtmp', 'decs'], writes=['ktil'])
                op('act', lambda e: e.copy(out=rvb[:], in_=P[2][:]), reads=['P2'], writes=['rvb'])
                projn(3, u, Wkv, C_WIN, C_WIN + 256, 'Wkv')
                op('act', lambda e: e.copy(out=wstage[:], in_=P[3][:, 0:256]), reads=['P3'], writes=['wstage'])
                do_rope(wstage[:, 0:128].rearrange("p (g d) -> p g d", g=2),
                        P[3][:, 0:128].rearrange("p (g d) -> p g d", g=2), cosN, sinN, 2, 8, ['P3', 'wstage'], ['wstage'])
                for b in range(4):
                    dma('sp', lambda e, b=b: e.dma_start(out=wins_d[b, 511:512, :], in_=wstage[b:b + 1, :]), reads=['wstage'])
                op('pool', lambda e: e.tensor_copy(out=wb[:], in_=wstage[:]), reads=['wstage'], writes=['wb'])
                for g in range(2):
                    transpose_to(KwnT[0:64, g, :], wb[:, g * 64:(g + 1) * 64], 128, 64, ['wb'], ['Knew'])
                op('pool', lambda e: e.tensor_copy(out=Vwn[:, :, 0:64], in_=wb[:, 128:256].rearrange("p (g d) -> p g d", g=2)),
                   reads=['wb'], writes=['Vnew'])
                for b in range(4):
                    for h2 in range(2):
                        dma('sp', lambda e, b=b, h2=h2: e.dma_start(
                            out=SsT[b][h2 * 64:(h2 + 1) * 64, :].rearrange("p (hp e) -> p hp e", hp=4),
                            in_=stin_d[b].rearrange("(hp h2) d e -> h2 d hp e", h2=2)[h2]), writes=['SsT%d' % b])
                    op('act', lambda e, b=b: e.copy(out=SbS[b][0][0:64, :], in_=SsT[b][0:64, :]), reads=['SsT%d' % b], writes=['SbS'])
                    op('act', lambda e, b=b: e.copy(out=SbS[b][1][64:128, :], in_=SsT[b][64:128, :]), reads=['SsT%d' % b], writes=['SbS'])
                    op('pool', lambda e, b=b: e.memset(SbS[b][0][64:128, :], 0.0), writes=['SbS'])
                    op('pool', lambda e, b=b: e.memset(SbS[b][1][0:64, :], 0.0), writes=['SbS'])
                projn(3, u, Wt[0], 0, 512, 'Wt0')
                op('act', lambda e: e.copy(out=rtmp[:], in_=P[3][:]), reads=['P3'], writes=['rtmp'])
                do_rope(rtmp[:].rearrange("p (h d) -> p h d", h=8), P[3][:].rearrange("p (h d) -> p h d", h=8),
                        cosR, sinR, 8, 32, ['P3'], ['rtmp'])
                op('dve', lambda e: e.tensor_tensor(out=qtil[:].rearrange("p (h d) -> p h d", h=8),
                                                    in0=rtmp[:].rearrange("p (h d) -> p h d", h=8),
                                                    in1=bc(decs[:, 0:8].unsqueeze(2), [128, 8, 64]), op=ALU.mult),
                   reads=['rtmp', 'decs'], writes=['qtil'])
                for hp in range(4):
                    slot = tr_rr[0]; tr_rr[0] = (slot + 1) % 2
                    pst = PTb[slot][:, 0, :]
                    op('pe', lambda e, pst=pst, hp=hp: e.transpose(pst, ktil[:, hp * 128:(hp + 1) * 128], ident[:]),
                       reads=['ktil', 'ident'], writes=['PTr%d' % slot])
                    op('act', lambda e, pst=pst, hp=hp: e.copy(out=kz[0][0:64, hp, :], in_=pst[0:64, :]),
                       reads=['PTr%d' % slot], writes=['ktilT'])
                    op('act', lambda e, pst=pst, hp=hp: e.copy(out=kz[1][64:128, hp, :], in_=pst[64:128, :]),
                       reads=['PTr%d' % slot], writes=['ktilT'])
                    transpose_to(qtilT[:, hp, :], qtil[:, hp * 128:(hp + 1) * 128], 128, 128, ['qtil'], ['qtilT'], evac='dve')
                for b in range(4):
                    op('pool', lambda e, b=b: e.memset(qz[b][:], 0.0), writes=['qz'])
                    op('pool', lambda e, b=b: e.tensor_copy(out=qz[b][:, :, b:b + 1], in_=qtilT[:, :, b:b + 1]),
                       reads=['qtilT', 'qz'], writes=['qz'])
                for h in range(8):
                    hp, h2 = h // 2, h % 2
                    pb = 4 + h // 4
                    mm(P[pb][:, (h % 4) * 128:(h % 4 + 1) * 128], kz[h2][:, hp, :], qtilT[:, hp, :],
                       True, True, ['ktilT', 'qtilT'], ['P%d' % pb])
                for half in range(2):
                    op('dve', lambda e, half=half: e.tensor_tensor(
                        out=innb[:, half * 4:(half + 1) * 4, :],
                        in0=P[4 + half][:].rearrange("p (h i) -> p h i", h=4),
                        in1=bc(identf[:].unsqueeze(1), [128, 4, 128]), op=ALU.mult),
                       reads=['P%d' % (4 + half), 'identf'], writes=['innb%d' % half])
                for h in range(8):
                    hp, h2 = h // 2, h % 2
                    mm(P[3][:, h * 64:(h + 1) * 64], innb[:, h, :], rvb[:, h * 64:(h + 1) * 64], True, False,
                       ['innb%d' % (h // 4), 'rvb', 'rtmp'], ['P3'])
                    for b in range(4):
                        mm(P[3][:, h * 64:(h + 1) * 64], qz[b][:, hp, :], SbS[b][h2][:, hp * 64:(hp + 1) * 64],
                           False, b == 3, ['qz', 'SbS'], ['P3'])
                op('act', lambda e: e.copy(out=orf[:], in_=P[3][:]), reads=['P3'], writes=['orf'])
                orf3 = orf[:].rearrange("p (h d) -> p h d", h=8)
                osq3 = osq[:].rearrange("p (h d) -> p h d", h=8)
                op('dve', lambda e: e.tensor_reduce(out=gsm[:], in_=orf3, axis=AX.X, op=ALU.add), reads=['orf'], writes=['gsm'])
                op('dve', lambda e: e.tensor_tensor(out=osq[:], in0=orf[:], in1=orf[:], op=ALU.mult), reads=['orf'], writes=['osq'])
                op('dve', lambda e: e.tensor_reduce(out=gss[:], in_=osq3, axis=AX.X, op=ALU.add), reads=['osq'], writes=['gss'])
                op('dve', lambda e: e.tensor_scalar(out=gmu[:], in0=gsm[:], scalar1=1.0 / 64, scalar2=None, op0=ALU.mult),
                   reads=['gsm'], writes=['gmu'])
                op('dve', lambda e: e.tensor_tensor(out=gm2[:], in0=gmu[:], in1=gmu[:], op=ALU.mult), reads=['gmu'], writes=['gm2'])
                op('dve', lambda e: e.scalar_tensor_tensor(out=grs[:], in0=gss[:], scalar=1.0 / 64, in1=gm2[:],
                                                           op0=ALU.mult, op1=ALU.subtract),
                   reads=['gss', 'gm2'], writes=['grs'])
                op('act', lambda e: e.activation(out=grs[:], in_=grs[:], func=AF.Sqrt, bias=eps_t[:, 0:1], scale=1.0),
                   reads=['grs', 'eps'], writes=['grs'])
                op('dve', lambda e: e.reciprocal(out=grs[:], in_=grs[:]), reads=['grs'], writes=['grs'])
                op('dve', lambda e: e.tensor_tensor(out=osq3, in0=orf3, in1=bc(gmu[:].unsqueeze(2), [128, 8, 64]), op=ALU.subtract),
                   reads=['orf', 'gmu', 'gss'], writes=['osq'])
                op('dve', lambda e: e.tensor_tensor(out=osq3, in0=osq3, in1=bc(grs[:].unsqueeze(2), [128, 8, 64]), op=ALU.mult),
                   reads=['osq', 'grs'], writes=['osq'])
                op('dve', lambda e: e.tensor_tensor(out=onrm_s[:, 0, :], in0=osq[:], in1=retg[:], op=ALU.mult),
                   reads=['osq', 'retg'], writes=['onrm0'])
                for b in range(4):
                    op('dve', lambda e, b=b: e.tensor_scalar(out=ktz[:], in0=ktil[:], scalar1=ohs[:, b:b + 1], scalar2=None, op0=ALU.mult),
                       reads=['ktil', 'ohs'], writes=['ktz'])
                    for h in range(8):
                        hp, h2 = h // 2, h % 2
                        mm(P[5][h2 * 64:(h2 + 1) * 64, hp * 64:(hp + 1) * 64], ktz[:, h * 64:(h + 1) * 64],
                           rvb[:, h * 64:(h + 1) * 64], True, True, ['ktz', 'rvb'], ['P5'])
                    op('dve', lambda e, b=b: e.tensor_tensor(out=stmp[:], in0=P[5][:, 0:256], in1=SsT[b][:], op=ALU.add),
                       reads=['P5', 'SsT%d' % b], writes=['stmp'])
                    op('dve', lambda e, b=b: e.tensor_tensor(out=SsT[b][:], in0=stmp[:], in1=gC1[:], op=ALU.mult),
                       reads=['stmp', 'gC1', 'SbS'], writes=['SsT%d' % b])
                    dma('sp', lambda e, b=b: e.dma_start(out=rets_d[b], in_=SsT[b][:]), reads=['SsT%d' % b])
                load_wt(1, lambda kc: win_d[kc, :, C_Q:C_Q + 536], 536)
                projn(0, u, Wt[1], 0, 512, 'Wt1')
                projn(1, u, Wt[1], 512, 536, 'Wt1')
                op('act', lambda e: e.copy(out=qbs[:, 0, :], in_=P[0][:]), reads=['P0'], writes=['qb'])
                do_rope(qbs[:, 0, :].rearrange("p (h d) -> p h d", h=8), P[0][:].rearrange("p (h d) -> p h d", h=8),
                        cosN, sinN, 8, 8, ['P0', 'qb'], ['qb'])
                op('act', lambda e: e.activation(out=gat[:, 0, :], in_=P[1][:, 0:24], func=AF.Sigmoid), reads=['P1'], writes=['gat'])
                load_wt(0, lambda kc: win_d[kc, :, C_RG:C_RG + 512], 512)
                projn(2, u, Wt[0], 0, 512, 'Wt0')
                op('act', lambda e: e.activation(out=rgs[:], in_=P[2][:], func=AF.Silu), reads=['P2'], writes=['rgs'])
                op('dve', lambda e: e.tensor_tensor(out=mixret[:], in0=onrm_s[:, 0, :], in1=rgs[:], op=ALU.mult),
                   reads=['onrm0', 'rgs'], writes=['mixret'])
                for c in range(4):
                    transpose_to(mixTs[:, 4 + c, :], mixret[:, c * 128:(c + 1) * 128], 128, 128, ['mixret'], ['mixT'])
                for h in range(8):
                    transpose_to(QTs[:, h, :], qbs[:, 0, h * 64:(h + 1) * 64], 128, 64, ['qb'], ['QTs'], evac='dve')

                for b in range(4):
                    op('pool', lambda e: e.memset(KcTs[:, :, 0:16], 0.0), reads=['KcTs'], writes=['KcTs'])
                    op('pool', lambda e: e.memset(VcTs[:, :, 0:16], 0.0), reads=['VcTs'], writes=['VcTs'])
                    for pg in range(64):
                        kt = pg
                        ru = pg % 8
                        kb_i = (b * 64 + pg) % 3
                        kvr = kvbs[kb_i]
                        kvn = 'kvbs%d' % kb_i
                        dma('pool', lambda e, b=b, pg=pg, kvr=kvr: e.indirect_dma_start(
                            out=kvr[:], out_offset=None, in_=ckv_d,
                            in_offset=bass.IndirectOffsetOnAxis(ap=idx[:, b * 64 + pg:b * 64 + pg + 1], axis=0)),
                            reads=['idx'], writes=[kvn])
                        pbk_, pbn_ = tbatch([kvr[:, i * 64:(i + 1) * 64] for i in range(6)], [kvn])
                        op('act', lambda e, pbk_=pbk_, ru=ru: e.copy(out=KcTs[:, :, 16 + ru * 128:16 + (ru + 1) * 128], in_=pbk_[0:64, 0:2, :]),
                           reads=[pbn_], writes=['KcTs'])
                        op('act', lambda e, pbk_=pbk_, ru=ru: e.copy(out=VcTs[:, :, 16 + ru * 128:16 + (ru + 1) * 128], in_=pbk_[0:64, 2:4, :]),
                           reads=[pbn_], writes=['VcTs'])
                        op('dve', lambda e, pbk_=pbk_, kt=kt: e.tensor_copy(out=KslT[0:64, :, kt * 128:(kt + 1) * 128], in_=pbk_[0:64, 4:6, :]),
                           reads=[pbn_], writes=['KslT%d' % kt])
                        op('dve', lambda e, kt=kt, kvr=kvr: e.tensor_copy(out=Vsl[:, kt, :, 0:64],
                                                                           in_=kvr[:, 384:512].rearrange("p (g d) -> p g d", g=2)),
                           reads=[kvn], writes=['Vsl%d' % kt])
                        if ru == 7:
                            compress8(pg // 8)
                    for w in range(4):
                        dma('sp', lambda e, b=b, w=w: e.dma_start(out=wstage[:], in_=cwin_d[b, w * 128:(w + 1) * 128, :]), writes=['wstage'])
                        if w == 0:
                            dma('sp', lambda e, b=b: e.dma_start(out=wins_d[b, 0:127, :], in_=wstage[1:128, :]), reads=['wstage'])
                        else:
                            dma('sp', lambda e, b=b, w=w: e.dma_start(out=wins_d[b, 128 * w - 1:128 * w + 127, :], in_=wstage[:]),
                                reads=['wstage'])
                        op('pool', lambda e: e.tensor_copy(out=wb[:], in_=wstage[:]), reads=['wstage'], writes=['wb'])
                        for g in range(2):
                            transpose_to(KwT[0:64, g, w * 128:(w + 1) * 128], wb[:, g * 64:(g + 1) * 64], 128, 64, ['wb'], ['KwT%d' % w])
                        op('pool', lambda e, w=w: e.tensor_copy(out=Vw[:, w, :, 0:64],
                                                                 in_=wb[:, 128:256].rearrange("p (g d) -> p g d", g=2)),
                           reads=['wb'], writes=['Vw%d' % w])
                    for g in range(2):
                        for r in range(2):
                            op('dve', lambda e, g=g, r=r, b=b: e.tensor_copy(out=Qs[0:64, r, :], in_=QTs[:, 4 * g:4 * g + 4, b]),
                               reads=['QTs', 'Qs'], writes=['Qs'])
                        for br in range(3):
                            mm(P[4][0:4, br:br + 1], gat[:, 0, g * 12 + br:g * 12 + br + 10:3], ohs[:, b:b + 1], True, True,
                               ['gat', 'ohs'], ['P4'])
                        op('act', lambda e: e.copy(out=gcol[:], in_=P[4][0:4, 0:3]), reads=['P4'], writes=['gcol'])
                        for ct in range(4):
                            mm(P[0][:, 4 * ct:4 * ct + 4], kcT[:, g, ct * 128:(ct + 1) * 128], Qs[:, 0, :], True, True,
                               ['kcT', 'Qs'], ['P0'])
                        for ct in range(4):
                            op('act', lambda e, ct=ct: e.activation(out=pts[:, 4 * ct:4 * ct + 4], in_=P[0][:, 4 * ct:4 * ct + 4],
                                                                    func=AF.Exp, bias=kbcs[:, ct:ct + 1], scale=0.125),
                               reads=['P0', 'kbcs'], writes=['pts'])
                        for ct in range(4):
                            mm(P[2][0:4, 0:193], pts[:, 4 * ct:4 * ct + 4], Rc[:, ct, g, :], ct == 0, ct == 3, ['pts', 'Rc'], ['P2'])
                        op('act', lambda e: e.copy(out=accs[:], in_=P[2][0:4, 0:193]), reads=['P2'], writes=['accs'])
                        op('dve', lambda e: e.tensor_scalar(out=rec4[:], in0=accs[:, 192:193], scalar1=1e-30, scalar2=None, op0=ALU.max),
                           reads=['accs'], writes=['rec4'])
                        op('dve', lambda e: e.reciprocal(out=rec4[:], in_=rec4[:]), reads=['rec4'], writes=['rec4'])
                        mm(P[3][0:1, 0:128], rec4[:, 0:1], accs[:, 0:128], True, True, ['rec4', 'accs'], ['P3'])
                        op('dve', lambda e: e.tensor_tensor(out=sc4[:], in0=rec4[:], in1=gcol[:, 0:1], op=ALU.mult),
                           reads=['rec4', 'gcol'], writes=['sc4'])
                        op('dve', lambda e: e.tensor_scalar(out=ocmb[:], in0=accs[:, 128:192], scalar1=sc4[:, 0:1], scalar2=None, op0=ALU.mult),
                           reads=['accs', 'sc4'], writes=['ocmb'])
                        op('dve', lambda e: e.tensor_tensor(out=srow[:], in0=P[3][0:1, 0:128], in1=bons[:], op=ALU.add),
                           reads=['P3', 'bons'], writes=['srow'])
                        op('dve', lambda e: e.max(out=s8a[:], in_=srow[:]), reads=['srow'], writes=['s8a'])
                        op('dve', lambda e: e.match_replace(out=srow2[:], in_to_replace=s8a[:], in_values=srow[:], imm_value=-3e38),
                           reads=['srow', 's8a'], writes=['srow2'])
                        op('dve', lambda e: e.max(out=s8b[:], in_=srow2[:]), reads=['srow2'], writes=['s8b'])
                        op('dve', lambda e: e.tensor_scalar(out=srow2[:], in0=srow[:], scalar1=s8b[:, 6:7], scalar2=-1.0,
                                                            op0=ALU.is_ge, op1=ALU.add), reads=['srow', 's8b'], writes=['srow2'])
                        op('dve', lambda e: e.tensor_scalar(out=selr[:, 0:128], in0=srow2[:], scalar1=-NEGB, scalar2=None, op0=ALU.mult),
                           reads=['srow2'], writes=['selr'])
                        op('dve', lambda e: e.tensor_copy(out=selr[:, 128:256], in_=selr[:, 0:128]), reads=['selr'], writes=['selr'])
                        mm(P[4][:, 8:9], selr[0:1, 0:128], oneb[0:1, 0:1], True, True, ['selr', 'oneb', 'gcol'], ['P4'])
                        mm(P[4][:, 9:10], selr[0:1, 64:192], oneb[0:1, 0:1], True, True, ['selr', 'oneb'], ['P4'])
                        op('dve', lambda e: e.tensor_copy(out=Qs[64:128, 1, :], in_=bc(P[4][64:128, 8:9], [64, 4])), reads=['P4', 'Qs'], writes=['Qs'])
                        op('dve', lambda e: e.tensor_copy(out=Qs[64:128, 0, :], in_=bc(P[4][64:128, 9:10], [64, 4])), reads=['P4', 'Qs'], writes=['Qs'])
                        for grp in range(4):
                            psi = grp % 2
                            for i in range(16):
                                kt = 16 * grp + i
                                mm(P[psi][:, 4 * i:4 * i + 4], KslT[:, g, kt * 128:(kt + 1) * 128], Qs[:, kt // 32, :], True, True,
                                   ['KslT%d' % kt, 'KslT_E', 'Qs'], ['P%d' % psi])
                            op('act', lambda e, psi=psi: e.activation(out=pts[:], in_=P[psi][:, 0:64], func=AF.Exp, bias=zcol[:, 0:1], scale=0.125),
                               reads=['P%d' % psi, 'zcol'], writes=['pts'])
                            for i in range(16):
                                kt = 16 * grp + i
                                mm(P[2][0:4, 0:65], pts[:, 4 * i:4 * i + 4], Vsl[:, kt, g, :], grp == 0 and i == 0, False,
                                   ['pts', 'Vsl%d' % kt, 'Vsl_ones'], ['P2'])
                        mm(P[0][:, 0:4], KnT[:, g, :], Qs[:, 0, :], True, True, ['Knew', 'Qs'], ['P0'])
                        op('act', lambda e, b=b: e.activation(out=pts[:, 0:4], in_=P[0][:, 0:4], func=AF.Exp, bias=kbself[:, b:b + 1], scale=0.125),
                           reads=['P0', 'kbself'], writes=['pts'])
                        mm(P[2][0:4, 0:65], pts[:, 0:4], Vn[:, g, :], False, True, ['pts', 'Vnew'], ['P2'])
                        for br, pbk_ in ((1, 2),):
                            op('act', lambda e: e.copy(out=accs[:, 0:65], in_=P[2][0:4, 0:65]), reads=['P2'], writes=['accs'])
                            op('dve', lambda e: e.reciprocal(out=rec4[:], in_=accs[:, 64:65]), reads=['accs'], writes=['rec4'])
                            op('dve', lambda e: e.tensor_tensor(out=sc4[:], in0=rec4[:], in1=gcol[:, 1:2], op=ALU.mult),
                               reads=['rec4', 'gcol'], writes=['sc4'])
                            op('dve', lambda e: e.scalar_tensor_tensor(out=ocmb[:], in0=accs[:, 0:64], scalar=sc4[:, 0:1], in1=ocmb[:],
                                                                       op0=ALU.mult, op1=ALU.add),
                               reads=['accs', 'sc4', 'ocmb'], writes=['ocmb'])
                        for w in range(4):
                            mm(P[1][:, 4 * w:4 * w + 4], KwT[:, g, w * 128:(w + 1) * 128], Qs[:, 0, :], True, True,
                               ['KwT%d' % w, 'Qs'], ['P1'])
                        mm(P[1][:, 16:20], KwnT[:, g, :], Qs[:, 0, :], True, True, ['Knew', 'Qs'], ['P1'])
                        for w in range(4):
                            op('act', lambda e, w=w: e.activation(out=pts[:, 4 * w:4 * w + 4], in_=P[1][:, 4 * w:4 * w + 4], func=AF.Exp,
                                                                  bias=kbws[:, w:w + 1], scale=0.125),
                               reads=['P1', 'kbws'], writes=['pts'])
                        op('act', lambda e, b=b: e.activation(out=pts[:, 16:20], in_=P[1][:, 16:20], func=AF.Exp, bias=kbself[:, b:b + 1], scale=0.125),
                           reads=['P1', 'kbself'], writes=['pts'])
                        for w in range(4):
                            mm(P[2][0:4, 0:65], pts[:, 4 * w:4 * w + 4], Vw[:, w, g, :], w == 0, False, ['pts', 'Vw%d' % w, 'Vw_ones'], ['P2'])
                        mm(P[2][0:4, 0:65], pts[:, 16:20], Vwn[:, g, :], False, True, ['pts', 'Vnew'], ['P2'])
                        op('act', lambda e: e.copy(out=accs[:, 0:65], in_=P[2][0:4, 0:65]), reads=['P2'], writes=['accs'])
                        op('dve', lambda e: e.reciprocal(out=rec4[:], in_=accs[:, 64:65]), reads=['accs'], writes=['rec4'])
                        op('dve', lambda e: e.tensor_tensor(out=sc4[:], in0=rec4[:], in1=gcol[:, 2:3], op=ALU.mult),
                           reads=['rec4', 'gcol'], writes=['sc4'])
                        op('dve', lambda e: e.scalar_tensor_tensor(out=ocmb[:], in0=accs[:, 0:64], scalar=sc4[:, 0:1], in1=ocmb[:],
                                                                   op0=ALU.mult, op1=ALU.add),
                           reads=['accs', 'sc4', 'ocmb'], writes=['ocmb'])
                        dma('sp', lambda e, b=b, g=g: e.dma_start(out=ons_d[b, 4 * g:4 * g + 4, :], in_=ocmb[:]), reads=['ocmb'], writes=['ons'])

                op('pool', lambda e: e.memset(onsa[:, 0:2, :], 0.0), reads=['onsa'], writes=['onsa'])
                dma('sp', lambda e: e.dma_start(out=onsa[0:4, 0:2, :].rearrange("p a c -> p (a c)"),
                                                in_=ons_d.rearrange("b h d -> b (h d)")), reads=['ons', 'onsa'], writes=['onsa'])
                op('act', lambda e: e.copy(out=mixret[:], in_=onsa[:, 0:2, :].rearrange("p a c -> p (a c)")), reads=['onsa'], writes=['mixret'])
                for c in range(4):
                    transpose_to(mixTs[:, c, :], mixret[:, c * 128:(c + 1) * 128], 128, 128, ['mixret'], ['mixT'], evac='dve')
                load_wt(0, lambda kc: wo_d[kc, :, 0:512], 512)
                load_wt(1, lambda kc: wo_d[kc, :, 512:1024], 512)
                dma('sp', lambda e: e.dma_start(out=xo[:], in_=xs_own_d), writes=['stage', 'rtmp'])
                for half in range(2):
                    for c in range(8):
                        mm(P[half][:], mixTs[:, c, :], Wt[half][:, c, 0:512], c == 0, c == 7, ['mixT', 'Wt%d' % half], ['P%d' % half])
                    op('dve', lambda e, half=half: e.scalar_tensor_tensor(out=xr[:, half * 512:(half + 1) * 512],
                                                                          in0=xo[:, half * 512:(half + 1) * 512], scalar=ALPHA,
                                                                          in1=P[half][:], op0=ALU.mult, op1=ALU.add),
                       reads=['stage', 'rtmp', 'P%d' % half], writes=['orf', 'osq'])
                layer_norm(xo[:], xr[:], ln1[:, 0, :], ln1[:, 1, :], ['orf', 'osq'], ['stage', 'rtmp'], 'ln1')
                dma('sp', lambda e: e.dma_start(out=x1s_d[16], in_=xo[:]), reads=['stage', 'rtmp'], writes=['x1s16'])

        _stop(5)
        S.barrier()
        with contextlib.ExitStack() as stB:
            def TB(name, shape, dt=BF16):
                return stB.enter_context(nc.sbuf_tensor("s_" + name, list(shape), dt))
            Wup = TB("Wup", [128, 8, 4096])
            Wdn = TB("Wdn", [128, 32, D])
            for c4 in range(4):
                for kc in range(8):
                    dma('pool', lambda e, kc=kc, c4=c4: e.dma_start(out=Wup[:, kc, c4 * 1024:(c4 + 1) * 1024],
                                                                  in_=wup_d[kc, :, c4 * 1024:(c4 + 1) * 1024], max_dma_last_dim=4096),
                        writes=['Wup%d' % c4])
            for fc in range(32):
                dma('pool', lambda e, fc=fc: e.dma_start(out=Wdn[:, fc, :], in_=wdn_d[fc], max_dma_last_dim=4096),
                    writes=['Wdn%d' % (fc // 8)])
            ln2 = TB("ln2", [128, 2, D], F32)
            dma('sp', lambda e: e.dma_start(out=ln2[:, 0, :], in_=ln_d[2]), writes=['ln2'])
            dma('sp', lambda e: e.dma_start(out=ln2[:, 1, :], in_=ln_d[3]), writes=['ln2'])
            x1f = TB("x1f", [128, 4, D], F32)
            x1b = TB("x1b", [128, D])
            x1T = TB("x1T", [128, 8, 512])
            hr = [TB("hr%d" % i, [128, 512], F32) for i in range(2)]
            hT = TB("hT", [128, 32, 512])
            xr2 = TB("xr2", [128, D], F32)
            yo = TB("yo", [128, D], F32)
            for k in ([4] if os.environ.get('K_MLP') == 's' else range(int(os.environ.get('K_MLP', 5 if DO_SAMPLE else 4)))):
                nsub = 4 if k < 4 else 1
                NTOK = 128 * nsub
                for u in range(nsub):
                    dma('sp', lambda e, u=u: e.dma_start(out=x1f[:, u, :], in_=x1s_d[4 * k + u]),
                        reads=['x1s%d' % (4 * k + u)], writes=['x1f%d' % u])
                    op('pool', lambda e, u=u: e.tensor_copy(out=x1b[:], in_=x1f[:, u, :]), reads=['x1f%d' % u], writes=['x1b'])
                    for kc in range(8):
                        transpose_to(x1T[:, kc, u * 128:(u + 1) * 128], x1b[:, kc * 128:(kc + 1) * 128], 128, 128,
                                     ['x1b'], ['x1T'], evac=('act' if kc % 2 else 'dve'))
                for fc in range(32):
                    psi = fc % 2
                    for kc in range(8):
                        mm(P[psi][:, 0:NTOK], Wup[:, kc, fc * 128:(fc + 1) * 128], x1T[:, kc, 0:NTOK], kc == 0, kc == 7,
                           ['Wup%d' % (fc // 8), 'x1T'], ['P%d' % psi])
                    op('act', lambda e, psi=psi: e.activation(out=hr[psi][:, 0:NTOK], in_=P[psi][:, 0:NTOK], func=AF.Relu),
                       reads=['P%d' % psi], writes=['hr%d' % psi])
                    op('pool', lambda e, psi=psi, fc=fc: e.tensor_tensor(out=hT[:, fc, 0:NTOK], in0=hr[psi][:, 0:NTOK], in1=hr[psi][:, 0:NTOK], op=ALU.mult),
                       reads=['hr%d' % psi], writes=['hT'])
                for u in range(nsub):
                    for half in range(2):
                        pb = 2 + half
                        for fc in range(32):
                            mm(P[pb][:], hT[:, fc, u * 128:(u + 1) * 128], Wdn[:, fc, half * 512:(half + 1) * 512],
                               fc == 0, fc == 31, ['hT', 'Wdn%d' % (fc // 8)], ['P%d' % pb])
                        op('dve', lambda e, half=half, pb=pb, u=u: e.scalar_tensor_tensor(
                            out=xr2[:, half * 512:(half + 1) * 512], in0=x1f[:, u, half * 512:(half + 1) * 512], scalar=ALPHA,
                            in1=P[pb][:], op0=ALU.mult, op1=ALU.add),
                           reads=['x1f%d' % u, 'P%d' % pb], writes=['xr2'])
                    layer_norm(yo[:], xr2[:], ln2[:, 0, :], ln2[:, 1, :], ['xr2'], ['yo'], 'ln2')
                    dma('sp', lambda e, u=u, k=k: e.dma_start(out=(y_d[4 * k + u] if k < 4 else ys_d), in_=yo[:]), reads=['yo'])

        S.finish('sp')
        print("instructions:", S.ninstr, "sems:", S.nsem)
    return nc


_PERM = np.concatenate([np.arange(512, 1024), np.arange(1816, 2328), np.arange(2328, 2840),
                        np.arange(1024, 1280), np.arange(0, 512), np.arange(1280, 1304),
                        np.arange(1304, 1816), np.arange(2840, 3352)])


def _const_tables():
    f = np.float32
    key = np.arange(128)[:, None]
    tp = np.arange(896)[None, :] - 384
    mc = np.where(key <= tp, 0.0, NEGB)
    wl = np.where(tp < key, 0.0, NEGB)
    t = np.arange(512)[None, :]
    cm = np.where(t >= 16 * key - 1521, 0.0, NEGB)
    masks = np.concatenate([mc, wl, cm], axis=1).astype(f)
    tri = (np.arange(128)[None, :] >= np.arange(128)[:, None]).astype(f)
    E = (np.arange(4096)[None, :] // 64 == np.arange(64)[:, None]).astype(f)
    mimp = np.zeros((512, 128), f)
    for jb in range(128):
        for c, w in ((4 * jb - 1, 1.0), (4 * jb, 2.0), (4 * jb + 1, 2.0), (4 * jb + 2, 2.0), (4 * jb + 3, 1.0)):
            if 0 <= c + 1 < 512:
                mimp[c + 1, jb] = w
    mimp = mimp.reshape(4, 128, 128).transpose(1, 0, 2).copy()
    gam = 1.0 - 2.0 ** (-5.0 - np.arange(8, dtype=np.float64))
    tl = np.arange(128, dtype=np.float64)[:, None]
    dec = np.concatenate([gam[None, :] ** (tl + 1), gam[None, :] ** (-(tl + 1)) / 8.0], axis=1).astype(f)
    gC = np.zeros((128, 4, 64), f)
    for h in range(8):
        gC[(h % 2) * 64:(h % 2 + 1) * 64, h // 2, :] = gam[h] ** 128
    return dict(masks=masks, tri=tri, Eoh=E, mimp=mimp, dec=dec, gC=gC.reshape(128, 256), gam=gam)


def _rope_tabs(pos):
    invn = 500000.0 ** (-np.arange(8, dtype=np.float64) / 8)
    invr = 10000.0 ** (-np.arange(32, dtype=np.float64) / 32)
    an = pos[:, None] * invn[None, :]
    ar = pos[:, None] * invr[None, :]
    return np.concatenate([np.cos(an), np.sin(an), np.cos(ar), np.sin(ar)], axis=1).astype(np.float32)


def kernel(x_prompt, x_sample, cache_kv, cache_win, state_ret, page_table, w_in, w_cmp_k, w_cmp_v,
           pos_cmp_k, pos_cmp_v, ret_norm_g, w_o, ln1_g, ln1_b, w_up, w_down, ln2_g, ln2_b):
    f = np.float32
    asf = lambda a: np.ascontiguousarray(np.asarray(a), dtype=f)
    x_prompt = asf(x_prompt)
    ct = _const_tables()
    shared = dict(
        w_in=np.ascontiguousarray(asf(w_in)[0][:, _PERM].reshape(8, 128, 3352)),
        w_o=asf(w_o)[0].reshape(8, 128, D),
        w_up=asf(w_up)[0].reshape(8, 128, 4096),
        w_down=asf(w_down)[0].reshape(32, 128, D),
        wck=np.ascontiguousarray(asf(w_cmp_k)[0].transpose(1, 0, 2)),
        wcv=np.ascontiguousarray(asf(w_cmp_v)[0].transpose(1, 0, 2)),
        posk=np.ascontiguousarray(asf(pos_cmp_k)[0].T),
        posv=np.ascontiguousarray(asf(pos_cmp_v)[0].T),
        retg=np.ascontiguousarray(np.broadcast_to(asf(ret_norm_g)[0][None, :], (128, 512))),
        lnp=np.ascontiguousarray(np.stack([np.broadcast_to(asf(v)[0][None, :], (128, D))
                                           for v in (ln1_g, ln1_b, ln2_g, ln2_b)])),
        masks=ct['masks'], tri=ct['tri'], Eoh=ct['Eoh'], mimp=ct['mimp'], dec=ct['dec'], gC=ct['gC'],
    )
    gam = ct['gam']
    ohs = np.zeros((128, 4), f); ohs[np.arange(4), np.arange(4)] = 1.0
    kbself = np.full((128, 4), NEGB, f); kbself[np.arange(4), np.arange(4)] = 0.0
    kbws = np.zeros((128, 4), f); kbws[0, 0] = NEGB
    kbcs = np.zeros((128, 4), f); kbcs[0, 0] = NEGB
    bons = np.zeros((1, 128), f); bons[0, 0] = 1e4; bons[0, 127] = 1e4
    decs = np.ascontiguousarray(np.broadcast_to(ct['dec'][0:1, :], (128, 16)))
    gC1 = np.zeros((128, 4, 64), f)
    for h in range(8):
        gC1[(h % 2) * 64:(h % 2 + 1) * 64, h // 2, :] = gam[h]
    tabs_s = np.ascontiguousarray(np.broadcast_to(_rope_tabs(np.array([float(PAST)]))[0:1, :], (128, 80)))
    ckv = np.ascontiguousarray(np.asarray(cache_kv, dtype=f)).reshape(2560 * 128, 512)
    cwin = np.asarray(cache_win, dtype=f)[0].reshape(32, 512, 256)
    stin = np.asarray(state_ret, dtype=f)[0]
    ptab = np.asarray(page_table).astype(np.int32)
    xsamp = asf(x_sample)[:, 0, :]
    shared.update(ohs=ohs, kbself=kbself, kbws=kbws, kbcs=kbcs, bons=bons, decs=decs, gC1=gC1.reshape(128, 256),
                  tabs_s=tabs_s, cache_kv=ckv)
    in_maps = []
    for c in range(8):
        b, j = c // 4, c % 4
        off = 512 * (3 - j)
        xs = np.zeros((SEQ, D), f)
        xs[off:] = x_prompt[b, :SEQ - off]
        xT = np.ascontiguousarray(xs.T).reshape(8, 128, SEQ)
        xown = np.stack([xs[512 * (4 * k + 3):512 * (4 * k + 4)] for k in range(4)]).reshape(16, 128, D)
        sp = np.arange(SEQ)
        tpos = np.maximum(sp - off, 0).astype(np.float64)
        tabs = _rope_tabs(tpos).reshape(64, 128, 80)
        kbias = np.where(sp >= off, 0.0, NEGB).astype(f).reshape(64, 128).T.copy()
        cp = np.arange(512)
        kbc = np.where((cp - 1) >= 32 * (3 - j), 0.0, NEGB).astype(f).reshape(4, 128).T.copy()
        bonus = np.zeros((16, 128, 128), f)
        blk = np.arange(128)[None, :] - 8 * (3 - j)
        for k in range(4):
            for u in range(4):
                tt = 512 * (4 * k + 3) + 128 * u + np.arange(128) - off
                cur = (tt // 64)[:, None]
                forced = (blk == 0) | (blk == cur) | (blk == cur - 1)
                bo = np.where(forced, 1e4, 0.0)
                bo = np.where((blk < 0) | (blk > cur), -1e30, bo)
                bonus[4 * k + u] = bo
        m = dict(shared)
        m.update(xT=xT, xown=np.ascontiguousarray(xown), tabs=tabs, kbias=kbias, kbias_c=kbc, bonus=bonus)
        xs4 = np.zeros((128, D), f); xs4[0:4] = xsamp[4 * c:4 * c + 4]
        m.update(xsT=np.ascontiguousarray(xs4.T).reshape(8, 128, 128), xs_own=xs4,
                 cache_win=np.ascontiguousarray(cwin[4 * c:4 * c + 4]), state_in=np.ascontiguousarray(stin[4 * c:4 * c + 4]),
                 pt_rep=np.ascontiguousarray(np.broadcast_to(ptab[4 * c:4 * c + 4].reshape(1, 256), (128, 256))))
        in_maps.append(m)

    try:
        nc = build_program()
    except _Stop:
        nc = _CUR[0].nc
    res = run_bass_kernel_spmd(nc, in_maps, core_ids=list(range(8)))
    R = res.results

    y_prompt = np.zeros((2, SEQ, D), f)
    kv_prompt = np.zeros((1, 2, SEQ, 4, 2, 64), f)
    win_prompt = np.zeros((1, 2, 512, 2, 2, 64), f)
    ret_prompt = np.zeros((1, 2, 8, 64, 64), f)
    for c in range(8):
        b, j = c // 4, c % 4
        yo = R[c]["y_own"].reshape(4, 512, D)
        kvo = R[c]["kv_own"].reshape(4, 512, 4, 2, 64)
        for k in range(4):
            i = 4 * k + j
            y_prompt[b, 512 * i:512 * (i + 1)] = yo[k]
            kv_prompt[0, b, 512 * i:512 * (i + 1)] = kvo[k]
        if j == 3:
            win_prompt[0, b] = R[c]["win_out"].reshape(512, 2, 2, 64)
            ro = R[c]["ret_out"].reshape(2, 64, 4, 64)
            ret_prompt[0, b] = ro.transpose(2, 0, 1, 3).reshape(8, 64, 64)
    y_sample = np.zeros((32, 1, D), f)
    kv_sample = np.zeros((1, 32, 1, 4, 2, 64), f)
    win_sample = np.zeros((1, 32, 512, 2, 2, 64), f)
    ret_sample = np.zeros((1, 32, 8, 64, 64), f)
    for c in range(8):
        y_sample[4 * c:4 * c + 4, 0] = R[c]["y_s"][0:4]
        kv_sample[0, 4 * c:4 * c + 4, 0] = R[c]["kv_s"][0:4].reshape(4, 4, 2, 64)
        win_sample[0, 4 * c:4 * c + 4] = R[c]["win_s"].reshape(4, 512, 2, 2, 64)
        rs = R[c]["ret_s"].reshape(4, 2, 64, 4, 64)
        ret_sample[0, 4 * c:4 * c + 4] = rs.transpose(0, 3, 1, 2, 4).reshape(4, 8, 64, 64)
    return (y_prompt, y_sample, kv_prompt, kv_sample, win_prompt, win_sample, ret_prompt, ret_sample)
```

```python
import contextlib
import os
import numpy as np
import concourse.bass as bass
import concourse.mybir as mybir
from concourse.bass_utils import run_bass_kernel_spmd

F32 = mybir.dt.float32
BF16 = mybir.dt.bfloat16
I32 = mybir.dt.int32
U32 = mybir.dt.uint32
AF = mybir.ActivationFunctionType
ALU = mybir.AluOpType
AX = mybir.AxisListType

D = 1024
SEQ = 8192
NT = 16
ALPHA = 2.0 ** 0.25
BETA = 8.0 ** -0.25
LN_EPS = 1e-5
NEGB = -30000.0
PAST = 8192
DO_SAMPLE = True


class _Stop(Exception):
    pass


_CUR = [None]
_CUR_S = [-1]


def _stop(n):
    if int(os.environ.get('K_STOP', 999)) == n and int(os.environ.get('K_STOP_S', _CUR_S[0])) == _CUR_S[0]:
        _CUR[0].finish('sp')
        raise _Stop()


class Sched:
    EPOCH = 30000
    NDMA = 16

    def __init__(self, nc, stack):
        self.nc = nc
        self.stack = stack
        self.eng = {'pe': nc.tensor, 'act': nc.scalar, 'dve': nc.vector,
                    'pool': nc.gpsimd, 'sp': nc.sync}
        self.cur_sem = {}
        self.cnt = {}
        self.nsem = 0
        for e in ('pe', 'act', 'dve', 'pool'):
            self._new_epoch(e)
        self.dma_sems = [self._alloc_sem('dma%d' % i) for i in range(2 * self.NDMA)]
        self.dma_cnt = [0] * (2 * self.NDMA)
        self.dma_rr = {'sp': 0, 'pool': 0, 'act': 0}
        self.known = {e: {} for e in self.eng}
        self.last_w = {}
        self.readers = {}
        self.ninstr = 0

    def _alloc_sem(self, name):
        self.nsem += 1
        return self.stack.enter_context(self.nc.semaphore('%s_%d' % (name, self.nsem)))

    def _new_epoch(self, e):
        self.cur_sem[e] = self._alloc_sem('e_' + e)
        self.cnt[e] = 0

    def _wait(self, e, tok):
        if tok is None:
            return
        sem, val, src = tok
        if src == e and e == 'pe':
            return
        k = self.known[e]
        if k.get(id(sem), 0) >= val:
            return
        self.eng[e].wait_ge(sem, val)
        k[id(sem)] = val

    def _deps(self, e, reads, writes):
        for b in reads:
            self._wait(e, self.last_w.get(b))
            if b[0] == 'P':
                for t in self.readers.get(b, ()):
                    if t[2] != e:
                        self._wait(e, t)
        for b in writes:
            self._wait(e, self.last_w.get(b))
            for t in self.readers.get(b, ()):
                self._wait(e, t)

    def _commit(self, tok, reads, writes):
        for b in reads:
            self.readers.setdefault(b, []).append(tok)
        for b in writes:
            self.last_w[b] = tok
            self.readers[b] = []

    def op(self, e, fn, reads=(), writes=()):
        self._deps(e, reads, writes)
        if self.cnt[e] >= self.EPOCH:
            self._new_epoch(e)
        ins = fn(self.eng[e])
        self.cnt[e] += 1
        sem = self.cur_sem[e]
        ins.then_inc(sem, 1)
        tok = (sem, self.cnt[e], e)
        self._commit(tok, reads, writes)
        self.ninstr += 1
        return tok

    def dma(self, q, fn, reads=(), writes=()):
        self._deps(q, reads, writes)
        i = self.dma_rr[q] + (self.NDMA if q == 'pool' else 0)
        self.dma_rr[q] = (self.dma_rr[q] + 1) % self.NDMA
        sem = self.dma_sems[i]
        if self.dma_cnt[i] > 0:
            self._wait(q, (sem, 16 * self.dma_cnt[i], 'dma'))
        ins = fn(self.eng[q])
        self.dma_cnt[i] += 1
        ins.then_inc(sem, 16)
        tok = (sem, 16 * self.dma_cnt[i], 'dma')
        self._commit(tok, reads, writes)
        self.ninstr += 1
        return tok

    def barrier(self):
        for e in ('sp', 'pool', 'act', 'dve', 'pe'):
            self.finish(e)

    def finish(self, e='sp'):
        for i, sem in enumerate(self.dma_sems):
            if self.dma_cnt[i]:
                self._wait(e, (sem, 16 * self.dma_cnt[i], 'dma'))
        for x in ('pe', 'act', 'dve', 'pool'):
            if self.cnt[x]:
                self._wait(e, (self.cur_sem[x], self.cnt[x], x))


C_KVA, C_RK, C_RV, C_WIN, C_Q, C_GT, C_RQ, C_RG = 0, 512, 1024, 1536, 1792, 2304, 2328, 2840


def build_program():
    nc = bass.Bass("TRN2", target_bir_lowering=False)

    def din(name, shape, dt=F32):
        return nc.dram_tensor(name, list(shape), dt, kind="ExternalInput").ap()

    def dout(name, shape, dt=F32):
        return nc.dram_tensor(name, list(shape), dt, kind="ExternalOutput").ap()

    xT_d = din("xT", [8, 128, SEQ])
    xown_d = din("xown", [16, 128, D])
    tabs_d = din("tabs", [64, 128, 80])
    dec_d = din("dec", [128, 16])
    gC_d = din("gC", [128, 256])
    win_d = din("w_in", [8, 128, 3352])
    wo_d = din("w_o", [8, 128, D])
    wup_d = din("w_up", [8, 128, 4096])
    wdn_d = din("w_down", [32, 128, D])
    wck_d = din("wck", [64, 32, 64])
    wcv_d = din("wcv", [64, 32, 64])
    posk_d = din("posk", [64, 32])
    posv_d = din("posv", [64, 32])
    retg_d = din("retg", [128, 512])
    ln_d = din("lnp", [4, 128, D])
    kbias_d = din("kbias", [128, 64])
    kbc_d = din("kbias_c", [128, 4])
    bonus_d = din("bonus", [16, 128, 128])
    masks_d = din("masks", [128, 2304])
    tri_d = din("tri", [128, 128])
    E_d = din("Eoh", [64, 4096])
    mimp_d = din("mimp", [128, 4, 128])

    y_d = dout("y_own", [16, 128, D])
    kv_d = dout("kv_own", [16, 128, 512])
    wout_d = dout("win_out", [4, 128, 256])
    ret_d = dout("ret_out", [128, 256])
    x1s_d = nc.dram_tensor("x1_scratch", [17, 128, D], F32, kind="Internal").ap()

    oh_d = din("ohs", [128, 4]); kbself_d = din("kbself", [128, 4]); kbws_d = din("kbws", [128, 4]); kbcs_d = din("kbcs", [128, 4])
    bons_d = din("bons", [1, 128]); decs_d = din("decs", [128, 16]); gC1_d = din("gC1", [128, 256])
    pt_d = din("pt_rep", [128, 256], I32)
    xsT_d = din("xsT", [8, 128, 128]); tabs_s_d = din("tabs_s", [128, 80]); xs_own_d = din("xs_own", [128, D])
    ckv_d = din("cache_kv", [2560 * 128, 512]); cwin_d = din("cache_win", [4, 512, 256]); stin_d = din("state_in", [4, 8, 64, 64])
    kvs_d = dout("kv_s", [128, 512]); wins_d = dout("win_s", [4, 512, 256]); rets_d = dout("ret_s", [4, 128, 256]); ys_d = dout("y_s", [128, D])
    ons_d = nc.dram_tensor("ons_scratch", [4, 8, 64], F32, kind="Internal").ap()
    wupb_d = nc.dram_tensor("wup_bf16", [8, 128, 4096], BF16, kind="Internal").ap()
    wdnb_d = nc.dram_tensor("wdn_bf16", [32, 128, D], BF16, kind="Internal").ap()

    with contextlib.ExitStack() as st:
        S = Sched(nc, st)
        _CUR[0] = S
        op, dma = S.op, S.dma

        def T(name, shape, dt=BF16):
            return st.enter_context(nc.sbuf_tensor("s_" + name, list(shape), dt))

        P = [st.enter_context(nc.psum_tensor("P%d" % i, [128, 512], F32)) for i in range(6)]
        PTb = [st.enter_context(nc.psum_tensor("PTr%d" % i, [128, 8, 128], BF16)) for i in range(2)]
        tr_rr = [0]

        def mm(out, lhsT, rhs, start, stop, reads, writes, **kw):
            return op('pe', lambda e: e.matmul(out, lhsT=lhsT, rhs=rhs, start=start, stop=stop, **kw),
                      reads=reads, writes=writes)

        def bc(ap, shape):
            return ap.to_broadcast(list(shape))

        ident = T("ident", [128, 128])
        op('pool', lambda e: e.memset(ident[:], 1.0), writes=['ident'])
        op('pool', lambda e: e.affine_select(out=ident[:], in_=ident[:], pattern=[[-1, 128]],
                                             compare_op=ALU.is_equal, fill=0.0, base=0,
                                             channel_multiplier=1), reads=['ident'], writes=['ident'])
        identf = T("identf", [128, 128], F32)
        op('act', lambda e: e.copy(out=identf[:], in_=ident[:]), reads=['ident'], writes=['identf'])
        eps_t = T("eps_t", [128, 1], F32)
        op('pool', lambda e: e.memset(eps_t[:], LN_EPS), writes=['eps'])

        def transpose_to(dst, src, rows, cols, reads, writes, evac='act'):
            slot = tr_rr[0]
            tr_rr[0] = (slot + 1) % 2
            pst = PTb[slot][0:cols, 0, 0:rows]
            nm = 'PTr%d' % slot
            op('pe', lambda e: e.transpose(pst, src, ident[0:rows, 0:rows]),
               reads=list(reads) + ['ident'], writes=[nm])
            if evac == 'act':
                op('act', lambda e: e.copy(out=dst, in_=pst), reads=[nm], writes=writes)
            else:
                op('dve', lambda e: e.tensor_copy(out=dst, in_=pst), reads=[nm], writes=writes)

        def tbatch(srcs, reads):
            bank = tr_rr[0]
            tr_rr[0] = (bank + 1) % 2
            nm = 'PTr%d' % bank
            for i, src in enumerate(srcs):
                op('pe', lambda e, i=i, src=src: e.transpose(PTb[bank][0:64, i, :], src, ident[:]),
                   reads=list(reads) + ['ident'], writes=[nm])
            return PTb[bank], nm

        lnst = T("lnst", [128, 2, 6], F32)
        lnmv = T("lnmv", [128, 2], F32)
        lnrs = T("lnrs", [128, 1], F32)

        def layer_norm(dst, src, gtab, btab, sname, dname, tname):
            for c in range(2):
                op('dve', lambda e, c=c: e.bn_stats(out=lnst[:, c, :], in_=src[:, c * 512:(c + 1) * 512]),
                   reads=sname, writes=['lnst'])
            op('dve', lambda e: e.bn_aggr(out=lnmv[:], in_=lnst[:]), reads=['lnst'], writes=['lnmv'])
            op('act', lambda e: e.activation(out=lnrs[:], in_=lnmv[:, 1:2], func=AF.Sqrt, bias=eps_t[:, 0:1], scale=1.0),
               reads=['lnmv', 'eps'], writes=['lnrs'])
            op('dve', lambda e: e.reciprocal(out=lnrs[:], in_=lnrs[:]), reads=['lnrs'], writes=['lnrs'])
            op('dve', lambda e: e.tensor_scalar(out=dst, in0=src, scalar1=lnmv[:, 0:1], scalar2=lnrs[:, 0:1],
                                                op0=ALU.subtract, op1=ALU.mult),
               reads=sname + ['lnmv', 'lnrs'], writes=dname)
            op('dve', lambda e: e.tensor_tensor(out=dst, in0=dst, in1=gtab, op=ALU.mult), reads=dname + [tname], writes=dname)
            op('dve', lambda e: e.tensor_tensor(out=dst, in0=dst, in1=btab, op=ALU.add), reads=dname + [tname], writes=dname)

        _stop(1)
        with contextlib.ExitStack() as stA:
            def TA(name, shape, dt=BF16):
                return stA.enter_context(nc.sbuf_tensor("s_" + name, list(shape), dt))

            kbias = TA("kbias", [128, 64], F32)
            dma('sp', lambda e: e.dma_start(out=kbias[:], in_=kbias_d), writes=['kbias'])
            kbc = TA("kbc", [128, 4], F32)
            dma('sp', lambda e: e.dma_start(out=kbc[:], in_=kbc_d), writes=['kbc'])
            dec = TA("dec", [128, 16], F32)
            dma('sp', lambda e: e.dma_start(out=dec[:], in_=dec_d), writes=['dec'])
            gC = TA("gC", [128, 256], F32)
            dma('sp', lambda e: e.dma_start(out=gC[:], in_=gC_d), writes=['gC'])
            retg = TA("retg", [128, 512], F32)
            dma('sp', lambda e: e.dma_start(out=retg[:], in_=retg_d), writes=['retg'])

            def causal(o):
                return masks[:, 384 - 128 * o:896 - 128 * o]

            def wlo(o):
                return masks[:, 896 + 384 - 128 * o:896 + 896 - 128 * o]

            _stop(2)
            Wkv = TA("Wkv", [128, 8, 1792])
            for kc in range(8):
                dma('pool', lambda e, kc=kc: e.dma_start(out=Wkv[:, kc, :], in_=win_d[kc, :, 0:1792],
                                                         max_dma_last_dim=4096), writes=['Wkv'])
            Wt = [TA("WtA", [128, 8, 536]), TA("WtB", [128, 8, 536])]
            for kc in range(8):
                for c4 in range(4):
                    dma('pool', lambda e, kc=kc, c4=c4: e.dma_start(out=wupb_d[kc, :, c4 * 1024:(c4 + 1) * 1024],
                                                                  in_=wup_d[kc, :, c4 * 1024:(c4 + 1) * 1024], max_dma_last_dim=4096),
                        writes=['wupb%d' % c4])
            for fc in range(32):
                dma('pool', lambda e, fc=fc: e.dma_start(out=wdnb_d[fc], in_=wdn_d[fc], max_dma_last_dim=4096),
                    writes=['wdnb%d' % (fc // 8)])

            def load_wt(i, src_fn, ncols):
                for kc in range(8):
                    dma('pool', lambda e, kc=kc: e.dma_start(out=Wt[i][:, kc, 0:ncols], in_=src_fn(kc),
                                                             max_dma_last_dim=4096), writes=['Wt%d' % i])

            wck = TA("wck", [64, 32, 64]); wcv = TA("wcv", [64, 32, 64])
            dma('pool', lambda e: e.dma_start(out=wck[:], in_=wck_d, max_dma_last_dim=4096), writes=['wck'])
            dma('pool', lambda e: e.dma_start(out=wcv[:], in_=wcv_d, max_dma_last_dim=4096), writes=['wcv'])
            posk = TA("posk", [64, 32]); posv = TA("posv", [64, 32])
            dma('pool', lambda e: e.dma_start(out=posk[:], in_=posk_d), writes=['posk'])
            dma('pool', lambda e: e.dma_start(out=posv[:], in_=posv_d), writes=['posv'])
            ln1 = TA("ln1", [128, 2, D], F32)
            dma('sp', lambda e: e.dma_start(out=ln1[:, 0, :], in_=ln_d[0]), writes=['ln1'])
            dma('sp', lambda e: e.dma_start(out=ln1[:, 1, :], in_=ln_d[1]), writes=['ln1'])

            KslT = TA("KslT", [128, 2, SEQ])
            for g in range(2):
                for r in range(2):
                    dma('pool', lambda e, g=g, r=r: e.dma_start(out=KslT[64:128, g, r * 4096:(r + 1) * 4096],
                                                              in_=E_d, max_dma_last_dim=4096), writes=['KslT_E'])
            Vsl = TA("Vsl", [128, 64, 2, 65])
            op('pool', lambda e: e.memset(Vsl[:, :, :, 64:65], 1.0), writes=['Vsl_ones'])
            KwT = TA("KwT", [128, 2, 1024])
            op('pool', lambda e: e.memset(KwT[:], 0.0), writes=['KwT%d' % i for i in range(8)])
            Vw = TA("Vw", [128, 8, 2, 65])
            op('pool', lambda e: e.memset(Vw[:, :, :, 64:65], 1.0), writes=['Vw_ones'])
            KcT = TA("KcT", [64, 2, 528]); VcT = TA("VcT", [64, 2, 528])
            op('pool', lambda e: e.memset(KcT[:], 0.0), writes=['KcT'])
            op('pool', lambda e: e.memset(VcT[:], 0.0), writes=['VcT'])
            kcT = TA("kcT", [128, 2, 512])
            op('pool', lambda e: e.memset(kcT[:], 0.0), writes=['kcT'])
            Rc = TA("Rc", [128, 4, 2, 193])
            op('pool', lambda e: e.memset(Rc[:], 0.0), writes=['Rc'])
            for g in range(2):
                dma('pool', lambda e, g=g: e.dma_start(out=Rc[:, :, g, 0:128], in_=mimp_d), writes=['Rc'])
            op('pool', lambda e: e.memset(Rc[:, :, :, 192:193], 1.0), reads=['Rc'], writes=['Rc'])
            pbk = TA("pbk", [64, 1], F32)
            pbv = TA("pbv", [1, 64])
            ones1 = TA("ones1", [1, 128])
            op('pool', lambda e: e.memset(ones1[:], 1.0), writes=['ones1'])
            for r in range(32):
                mm(P[5][0:64, 0:1], wck[:, r, :], posk[:, r:r + 1], r == 0, r == 31, ['wck', 'posk'], ['P5'])
            op('act', lambda e: e.copy(out=pbk[:], in_=P[5][0:64, 0:1]), reads=['P5'], writes=['pbk'])
            for r in range(32):
                mm(P[5][0:1, 64:128], posv[:, r:r + 1], wcv[:, r, :], r == 0, r == 31, ['wcv', 'posv', 'pbk'], ['P5'])
            op('act', lambda e: e.copy(out=pbv[:], in_=P[5][0:1, 64:128]), reads=['P5'], writes=['pbv'])

            _stop(3)
            tabs = TA("tabs", [128, 4, 80], F32)
            big0 = TA("big0", [128, 1024], F32)
            stage = big0[:, 0:512]
            kvb = TA("kvb", [128, 512])
            wstage = TA("wstage", [128, 256], F32)
            wb = TA("wb", [128, 256])
            rtmp = big0[:, 512:1024]
            ta = TA("ta", [128, 256], F32)
            tb = TA("tb", [128, 256], F32)
            ktil = TA("ktil", [128, 512])
            rvb = TA("rvb", [128, 512])
            Sst = TA("Sst", [128, 256], F32)
            op('pool', lambda e: e.memset(Sst[:], 0.0), writes=['Sst'])
            SbZ = [TA("SbZ%d" % i, [128, 256]) for i in range(2)]
            for i in range(2):
                op('pool', lambda e, i=i: e.memset(SbZ[i][:], 0.0), writes=['Sb'])
            stmp = TA("stmp", [128, 256], F32)
            gat = TA("gat", [128, 4, 24], F32)
            qtil = TA("qtil", [128, 512])
            qtilT = TA("qtilT", [128, 4, 128])
            kz = [TA("kz%d" % i, [128, 4, 128]) for i in range(2)]
            for i in range(2):
                op('pool', lambda e, i=i: e.memset(kz[i][:], 0.0), writes=['ktilT'])
            rgs = TA("rgs", [128, 512])
            pt_rr = [0]
            onsa = TA("onsa", [128, 4, 256], F32)
            onsab = TA("onsab", [128, 256])
            innb = TA("innb", [128, 8, 128])
            big1 = TA("big1", [128, 1024], F32)
            orf = big1[:, 0:512]
            osq = big1[:, 512:1024]
            gsm = TA("gsm", [128, 8], F32); gss = TA("gss", [128, 8], F32)
            gmu = TA("gmu", [128, 8], F32); grs = TA("grs", [128, 8], F32); gm2 = TA("gm2", [128, 8], F32)
            mixret = TA("mixret", [128, 512])
            rec = TA("rec", [128, 1], F32); scg = TA("scg", [128, 1], F32)
            xo = big0
            xr = big1

            def do_rope(dst3, src3, cos2, sin2, H, half, reads, writes):
                cb = bc(cos2.unsqueeze(1), [128, H, half])
                sb = bc(sin2.unsqueeze(1), [128, H, half])
                x1 = src3[:, :, 0:half]; x2 = src3[:, :, half:2 * half]
                A = ta[:, 0:H * half].rearrange("p (h d) -> p h d", h=H)
                B = tb[:, 0:H * half].rearrange("p (h d) -> p h d", h=H)
                rd = list(reads) + ['tabs']
                op('dve', lambda e: e.tensor_tensor(out=A, in0=x1, in1=cb, op=ALU.mult), reads=rd, writes=['ta'])
                _stop(201)
                op('dve', lambda e: e.tensor_tensor(out=B, in0=x2, in1=sb, op=ALU.mult), reads=rd, writes=['tb'])
                _stop(202)
                op('dve', lambda e: e.tensor_tensor(out=dst3[:, :, 0:half], in0=A, in1=B, op=ALU.subtract),
                   reads=['ta', 'tb'], writes=writes)
                _stop(203)
                op('dve', lambda e: e.tensor_tensor(out=A, in0=x1, in1=sb, op=ALU.mult), reads=rd, writes=['ta'])
                op('dve', lambda e: e.tensor_tensor(out=B, in0=x2, in1=cb, op=ALU.mult), reads=rd, writes=['tb'])
                op('dve', lambda e: e.tensor_tensor(out=dst3[:, :, half:2 * half], in0=A, in1=B, op=ALU.add),
                   reads=['ta', 'tb'], writes=writes)

            def rope_ip(buf3, cos2, sin2, H, half, bname):
                cb = bc(cos2.unsqueeze(1), [128, H, half])
                sb = bc(sin2.unsqueeze(1), [128, H, half])
                x1 = buf3[:, :, 0:half]; x2 = buf3[:, :, half:2 * half]
                A = ta[:, 0:H * half].rearrange("p (h d) -> p h d", h=H)
                B = tb[:, 0:H * half].rearrange("p (h d) -> p h d", h=H)
                C = stmp[:, 0:H * half].rearrange("p (h d) -> p h d", h=H)
                Dd = osq[:, 0:H * half].rearrange("p (h d) -> p h d", h=H)
                rd = [bname, 'tabs']
                op('dve', lambda e: e.tensor_tensor(out=A, in0=x1, in1=cb, op=ALU.mult), reads=rd, writes=['ta'])
                op('dve', lambda e: e.tensor_tensor(out=B, in0=x2, in1=sb, op=ALU.mult), reads=rd, writes=['tb'])
                op('dve', lambda e: e.tensor_tensor(out=C, in0=x1, in1=sb, op=ALU.mult), reads=rd, writes=['stmp'])
                op('dve', lambda e: e.tensor_tensor(out=Dd, in0=x2, in1=cb, op=ALU.mult), reads=rd, writes=['osq'])
                op('dve', lambda e: e.tensor_tensor(out=x1, in0=A, in1=B, op=ALU.subtract), reads=['ta', 'tb', bname], writes=[bname])
                op('dve', lambda e: e.tensor_tensor(out=x2, in0=C, in1=Dd, op=ALU.add), reads=['stmp', 'osq', bname], writes=[bname])

            xsel = [None]

            def projn(pi, u, W, c0, c1, wname):
                xt = xsel[0] if xsel[0] is not None else xTb
                for kc in range(8):
                    mm(P[pi][:, 0:c1 - c0], xt[:, kc, u * 128:(u + 1) * 128], W[:, kc, c0:c1],
                       kc == 0, kc == 7, [('xTb' if xsel[0] is not None else 'xTb%d' % u), wname], ['P%d' % pi])

            def exp_pt(psi, c0, c1, bias_ap, breads):
                i = pt_rr[0]; pt_rr[0] = (i + 1) % 3
                pt = PT3[i]
                op('act', lambda e: e.activation(out=pt[:, c0:c1], in_=P[psi][:, c0:c1], func=AF.Exp, bias=bias_ap, scale=0.125),
                   reads=['P%d' % psi] + breads, writes=[('ptile%d' % i) if i < 2 else 'mixret'])
                return pt, (('ptile%d' % i) if i < 2 else 'mixret')

            def ot_finish(pacc, h4, h, br):
                op('act', lambda e: e.copy(out=orf[0:65, :], in_=P[pacc][0:65, :]), reads=['P%d' % pacc], writes=['orf'])
                for u in range(4):
                    op('pe', lambda e, u=u: e.transpose(P[5][:, u * 66:u * 66 + 65], orf[0:65, u * 128:(u + 1) * 128], identf[0:65, 0:65]),
                       reads=['orf', 'identf'], writes=['P5'])
                finish_branch(lambda u: P[5][:, u * 66:u * 66 + 65], lambda u: 'P5', h4, h, br, False, 64)

            def pipeline(n_items, stage1, stage2, depth=2):
                q = []
                for i in range(n_items):
                    q.append(stage1(i))
                    if len(q) > depth:
                        stage2(*q.pop(0))
                while q:
                    stage2(*q.pop(0))

            def finish_branch(pacc, pn, h4, h, br, first, ow):
                for u in range(4):
                    pa = pacc(u)
                    op('dve', lambda e, pa=pa: e.tensor_scalar(out=rec[:], in0=pa[:, ow:ow + 1], scalar1=1e-30, scalar2=None, op0=ALU.max),
                       reads=[pn(u)], writes=['rec'])
                    op('dve', lambda e: e.reciprocal(out=rec[:], in_=rec[:]), reads=['rec'], writes=['rec'])
                    op('dve', lambda e, u=u: e.tensor_tensor(out=scg[:], in0=rec[:], in1=gat[:, u, h * 3 + br:h * 3 + br + 1], op=ALU.mult),
                       reads=['rec', 'gat'], writes=['scg'])
                    dst = onsa[:, u, h4 * 64:(h4 + 1) * 64]
                    if first:
                        op('dve', lambda e, pa=pa, dst=dst: e.tensor_scalar(out=dst, in0=pa[:, ow - 64:ow], scalar1=scg[:, 0:1],
                                                                             scalar2=None, op0=ALU.mult),
                           reads=[pn(u), 'scg'], writes=['onsa'])
                    else:
                        op('dve', lambda e, pa=pa, dst=dst: e.scalar_tensor_tensor(out=dst, in0=pa[:, ow - 64:ow], scalar=scg[:, 0:1],
                                                                                    in1=dst, op0=ALU.mult, op1=ALU.add),
                           reads=[pn(u), 'scg', 'onsa'], writes=['onsa'])
                    if br == 0:
                        dsti = impacc[:, u, :]
                        if h4 == 0:
                            op('dve', lambda e, pa=pa, dsti=dsti: e.tensor_scalar(out=dsti, in0=pa[:, 0:128], scalar1=rec[:, 0:1],
                                                                                   scalar2=None, op0=ALU.mult),
                               reads=[pn(u), 'rec'], writes=['impacc'])
                        else:
                            op('dve', lambda e, pa=pa, dsti=dsti: e.scalar_tensor_tensor(out=dsti, in0=pa[:, 0:128], scalar=rec[:, 0:1],
                                                                                          in1=dsti, op0=ALU.mult, op1=ALU.add),
                               reads=[pn(u), 'rec', 'impacc'], writes=['impacc'])

            def compress_tile(s):
                    ctp, row0 = s // 4, 32 * (s % 4)
                    tp96 = {'tile_position': (0, 96)} if row0 == 96 else {}
                    for r in range(32):
                        mm(P[4][0:64, 0:64].rearrange("p (g n) -> p g n", g=2), wck[:, r, :], KcT[:, :, r:r + 497:16], r == 0, r == 31,
                           ['wck', 'KcT'], ['P4'])
                    op('act', lambda e: e.activation(out=kcT[0:64, :, 32 * s:32 * s + 32],
                                                     in_=P[4][0:64, 0:64].rearrange("p (g n) -> p g n", g=2),
                                                     func=AF.Identity, bias=pbk[:, 0:1], scale=1.0),
                       reads=['P4', 'pbk'], writes=['kcT'])
                    for g in range(2):
                        for r in range(33):
                            if r < 32:
                                mm(P[5][row0:row0 + 32, g * 64:(g + 1) * 64], VcT[:, g, r:r + 497:16], wcv[:, r, :],
                                   r == 0, False, ['wcv', 'VcT'], ['P5'], **tp96)
                            else:
                                mm(P[5][row0:row0 + 32, g * 64:(g + 1) * 64], ones1[0:1, 0:32], pbv[0:1, :],
                                   False, True, ['ones1', 'pbv'], ['P5'], **tp96)
                        op('act', lambda e, g=g: e.copy(out=Rc[row0:row0 + 32, ctp, g, 128:192],
                                                         in_=P[5][row0:row0 + 32, g * 64:(g + 1) * 64]),
                           reads=['P5'], writes=['Rc'])
                    op('dve', lambda e: e.tensor_copy(out=KcT[:, :, 0:16], in_=KcT[:, :, 512:528]), reads=['KcT'], writes=['KcT'])
                    op('dve', lambda e: e.tensor_copy(out=VcT[:, :, 0:16], in_=VcT[:, :, 512:528]), reads=['VcT'], writes=['VcT'])


            stP = contextlib.ExitStack()

            def TP(name, shape, dt=BF16):
                return stP.enter_context(nc.sbuf_tensor("s_" + name, list(shape), dt))
            masks = TP("masks", [128, 2304])
            dma('pool', lambda e: e.dma_start(out=masks[:], in_=masks_d, max_dma_last_dim=4096), writes=['masks'])
            tri = TP("tri", [128, 128], F32)
            dma('sp', lambda e: e.dma_start(out=tri[:], in_=tri_d), writes=['tri'])
            Qa = TP("Qa", [128, 2, 4, 512])
            op('pool', lambda e: e.memset(Qa[:], 0.0), writes=['Qa0', 'Qa1'])
            selT = TP("selT", [128, 2, 512])
            PTt = [TP("PT%d" % i, [128, 512]) for i in range(2)]
            impacc = TP("impacc", [128, 4, 128], F32)
            bon = TP("bon", [128, 128], F32)
            score = TP("score", [128, 128], F32)
            sc2 = TP("sc2", [128, 128], F32)
            m8a = TP("m8a", [128, 8], F32); m8b = TP("m8b", [128, 8], F32)
            selb = TP("selb", [128, 256])
            qb = TP("qb", [128, 4, 512])
            onrm = TP("onrm", [128, 4, 512])
            xTb = TP("xTb", [128, 8, 512])
            mixT = TP("mixT", [128, 8, 512])
            _stop(4)
            load_wt(0, lambda kc: win_d[kc, :, C_RQ:C_RQ + 512], 512)

            PT3 = [PTt[0], PTt[1], mixret]
            SB3 = [0, 1, 4]
            kvbL = [(kvb, 'kvb'), (mixret, 'mixret')]
            ktilL = [(ktil, 'ktil'), (PTt[0], 'ptile0')]
            rvbL = [(rvb, 'rvb'), (rgs, 'rgs')]
            wbL = [(wb, 'wb'), (onsab, 'onsab')]
            for s in range(int(os.environ.get('K_NT', NT))):
                own = (s % 4 == 3)
                _CUR_S[0] = s
                k = s // 4
                wtile = (s % 4 in (2, 3))
                wr = s % 2
                def load_x(sn):
                    for uu in range(4):
                        dma('pool', lambda e, uu=uu: e.dma_start(
                            out=xTb[:, :, uu * 128:(uu + 1) * 128],
                            in_=xT_d[:, :, sn * 512 + uu * 128:sn * 512 + (uu + 1) * 128].rearrange("k p t -> p k t")),
                            writes=['xTb%d' % uu])
                if s == 0 or (s % 4 == 0):
                    load_x(s)
                dma('sp', lambda e: e.dma_start(out=tabs[:], in_=tabs_d[4 * s:4 * s + 4].rearrange("u p c -> p u c")),
                    writes=['tabs'])
                _stop(10)
                def part1(u):
                        kt = 4 * s + u
                        cosN = tabs[:, u, 0:8]; sinN = tabs[:, u, 8:16]
                        cosR = tabs[:, u, 16:48]; sinR = tabs[:, u, 48:80]
                        kvb, kvbn = kvbL[u % 2]
                        ktil, ktiln = ktilL[u % 2]
                        rvb, rvbn = rvbL[u % 2]
                        wb, wbn = wbL[u % 2]
                        projn(0, u, Wkv, C_KVA, C_KVA + 512, 'Wkv')
                        op('act', lambda e: e.copy(out=stage[:], in_=P[0][:]), reads=['P0'], writes=['stage'])
                        rope_ip(stage[:, 256:384].rearrange("p (g d) -> p g d", g=2), cosN, sinN, 2, 8, 'stage')
                        if own:
                            dma('sp', lambda e, u=u: e.dma_start(out=kv_d[4 * k + u], in_=stage[:]), reads=['stage'])
                        op('act', lambda e: e.copy(out=kvb[:], in_=stage[:]), reads=['stage'], writes=[kvbn])
                        projn(1, u, Wkv, C_RK, C_RK + 512, 'Wkv')
                        projn(2, u, Wkv, C_RV, C_RV + 512, 'Wkv')
                        op('act', lambda e: e.copy(out=rtmp[:], in_=P[1][:]), reads=['P1'], writes=['rtmp'])
                        rope_ip(rtmp[:].rearrange("p (h d) -> p h d", h=8), cosR, sinR, 8, 32, 'rtmp')
                        op('dve', lambda e: e.tensor_tensor(out=ktil[:].rearrange("p (h d) -> p h d", h=8),
                                                            in0=rtmp[:].rearrange("p (h d) -> p h d", h=8),
                                                            in1=bc(dec[:, 8:16].unsqueeze(2), [128, 8, 64]), op=ALU.mult),
                           reads=['rtmp', 'dec'], writes=[ktiln])
                        op('act', lambda e: e.copy(out=rvb[:], in_=P[2][:]), reads=['P2'], writes=[rvbn])
                        if wtile:
                            projn(3, u, Wkv, C_WIN, C_WIN + 256, 'Wkv')
                            op('act', lambda e: e.copy(out=wstage[:], in_=P[3][:, 0:256]), reads=['P3'], writes=['wstage'])
                            rope_ip(wstage[:, 0:128].rearrange("p (g d) -> p g d", g=2), cosN, sinN, 2, 8, 'wstage')
                            if s == NT - 1:
                                dma('sp', lambda e, u=u: e.dma_start(out=wout_d[u], in_=wstage[:]), reads=['wstage'])
                            op('act', lambda e: e.copy(out=wb[:], in_=wstage[:]), reads=['wstage'], writes=[wbn])

                def part2(u):
                        kt = 4 * s + u
                        cosN = tabs[:, u, 0:8]; sinN = tabs[:, u, 8:16]
                        cosR = tabs[:, u, 16:48]; sinR = tabs[:, u, 48:80]
                        kvb, kvbn = kvbL[u % 2]
                        ktil, ktiln = ktilL[u % 2]
                        rvb, rvbn = rvbL[u % 2]
                        wb, wbn = wbL[u % 2]
                        srcs = [kvb[:, i * 64:(i + 1) * 64] for i in range(6)]
                        rds = [kvbn]
                        if wtile:
                            srcs += [wb[:, 0:64], wb[:, 64:128]]
                            rds.append(wbn)
                        pbk_, pbn_ = tbatch(srcs, rds)
                        op('act', lambda e: e.copy(out=KcT[:, :, 16 + u * 128:16 + (u + 1) * 128], in_=pbk_[0:64, 0:2, :]),
                           reads=[pbn_], writes=['KcT'])
                        op('act', lambda e: e.copy(out=VcT[:, :, 16 + u * 128:16 + (u + 1) * 128], in_=pbk_[0:64, 2:4, :]),
                           reads=[pbn_], writes=['VcT'])
                        op('dve', lambda e: e.tensor_copy(out=KslT[0:64, :, kt * 128:(kt + 1) * 128], in_=pbk_[0:64, 4:6, :]),
                           reads=[pbn_], writes=['KslT%d' % kt])
                        op('dve', lambda e, kt=kt: e.tensor_copy(out=Vsl[:, kt, :, 0:64],
                                                                   in_=kvb[:, 384:512].rearrange("p (g d) -> p g d", g=2)),
                           reads=[kvbn], writes=['Vsl%d' % kt])
                        if wtile:
                            wkt = wr * 4 + u
                            op('act', lambda e: e.copy(out=KwT[0:64, :, wkt * 128:(wkt + 1) * 128], in_=pbk_[0:64, 6:8, :]),
                               reads=[pbn_], writes=['KwT%d' % wkt])
                            op('act', lambda e, wkt=wkt: e.copy(out=Vw[:, wkt, :, 0:64],
                                                                         in_=wb[:, 128:256].rearrange("p (g d) -> p g d", g=2)),
                               reads=[wbn], writes=['Vw%d' % wkt])
                        if own:
                            op('act', lambda e: e.copy(out=SbZ[0][0:64, :], in_=Sst[0:64, :]), reads=['Sst'], writes=['Sb'])
                            op('act', lambda e: e.copy(out=SbZ[1][64:128, :], in_=Sst[64:128, :]), reads=['Sst'], writes=['Sb'])
                            projn(3, u, Wt[0], 0, 512, 'Wt0')
                            op('act', lambda e: e.copy(out=rtmp[:], in_=P[3][:]), reads=['P3'], writes=['rtmp'])
                            rope_ip(rtmp[:].rearrange("p (h d) -> p h d", h=8), cosR, sinR, 8, 32, 'rtmp')
                            op('dve', lambda e: e.tensor_tensor(out=qtil[:].rearrange("p (h d) -> p h d", h=8),
                                                                in0=rtmp[:].rearrange("p (h d) -> p h d", h=8),
                                                                in1=bc(dec[:, 0:8].unsqueeze(2), [128, 8, 64]), op=ALU.mult),
                               reads=['rtmp', 'dec'], writes=['qtil'])
                            for hp in range(4):
                                slot = tr_rr[0]; tr_rr[0] = (slot + 1) % 2
                                pst = PTb[slot][:, 0, :]
                                op('pe', lambda e, pst=pst, hp=hp: e.transpose(pst, ktil[:, hp * 128:(hp + 1) * 128], ident[:]),
                                   reads=[ktiln, 'ident'], writes=['PTr%d' % slot])
                                op('act', lambda e, pst=pst, hp=hp: e.copy(out=kz[0][0:64, hp, :], in_=pst[0:64, :]),
                                   reads=['PTr%d' % slot], writes=['ktilT'])
                                op('act', lambda e, pst=pst, hp=hp: e.copy(out=kz[1][64:128, hp, :], in_=pst[64:128, :]),
                                   reads=['PTr%d' % slot], writes=['ktilT'])
                                transpose_to(qtilT[:, hp, :], qtil[:, hp * 128:(hp + 1) * 128], 128, 128, ['qtil'], ['qtilT'], evac='dve')
                            for h in range(8):
                                hp, h2 = h // 2, h % 2
                                bp = 64 * h2
                                pb = 4 + h // 4
                                mm(P[pb][:, (h % 4) * 128:(h % 4 + 1) * 128], kz[h2][:, hp, :], qtilT[:, hp, :],
                                   True, True, ['ktilT', 'qtilT'], ['P%d' % pb])
                            for half in range(2):
                                op('dve', lambda e, half=half: e.tensor_tensor(
                                    out=innb[:, half * 4:(half + 1) * 4, :],
                                    in0=P[4 + half][:].rearrange("p (h i) -> p h i", h=4),
                                    in1=bc(tri[:].unsqueeze(1), [128, 4, 128]), op=ALU.mult),
                                   reads=['P%d' % (4 + half), 'tri'], writes=['innb%d' % half])
                            for h in range(8):
                                hp, h2 = h // 2, h % 2
                                bp = 64 * h2
                                mm(P[3][:, h * 64:(h + 1) * 64], innb[:, h, :], rvb[:, h * 64:(h + 1) * 64], True, False,
                                   ['innb%d' % (h // 4), rvbn, 'rtmp'], ['P3'])
                                mm(P[3][:, h * 64:(h + 1) * 64], qtilT[:, hp, :], SbZ[h2][:, hp * 64:(hp + 1) * 64],
                                   False, True, ['qtilT', 'Sb'], ['P3'])
                            op('act', lambda e: e.copy(out=orf[:], in_=P[3][:]), reads=['P3'], writes=['orf'])
                            orf3 = orf[:].rearrange("p (h d) -> p h d", h=8)
                            osq3 = osq[:].rearrange("p (h d) -> p h d", h=8)
                            op('dve', lambda e: e.tensor_reduce(out=gsm[:], in_=orf3, axis=AX.X, op=ALU.add), reads=['orf'], writes=['gsm'])
                            op('dve', lambda e: e.tensor_tensor(out=osq[:], in0=orf[:], in1=orf[:], op=ALU.mult), reads=['orf'], writes=['osq'])
                            op('dve', lambda e: e.tensor_reduce(out=gss[:], in_=osq3, axis=AX.X, op=ALU.add), reads=['osq'], writes=['gss'])
                            op('dve', lambda e: e.tensor_scalar(out=gmu[:], in0=gsm[:], scalar1=1.0 / 64, scalar2=None, op0=ALU.mult),
                               reads=['gsm'], writes=['gmu'])
                            op('dve', lambda e: e.tensor_tensor(out=gm2[:], in0=gmu[:], in1=gmu[:], op=ALU.mult), reads=['gmu'], writes=['gm2'])
                            op('dve', lambda e: e.scalar_tensor_tensor(out=grs[:], in0=gss[:], scalar=1.0 / 64, in1=gm2[:],
                                                                       op0=ALU.mult, op1=ALU.subtract),
                               reads=['gss', 'gm2'], writes=['grs'])
                            op('act', lambda e: e.activation(out=grs[:], in_=grs[:], func=AF.Sqrt, bias=eps_t[:, 0:1], scale=1.0),
                               reads=['grs', 'eps'], writes=['grs'])
                            op('dve', lambda e: e.reciprocal(out=grs[:], in_=grs[:]), reads=['grs'], writes=['grs'])
                            op('dve', lambda e: e.tensor_tensor(out=osq3, in0=orf3, in1=bc(gmu[:].unsqueeze(2), [128, 8, 64]), op=ALU.subtract),
                               reads=['orf', 'gmu', 'gss'], writes=['osq'])
                            op('dve', lambda e: e.tensor_tensor(out=osq3, in0=osq3, in1=bc(grs[:].unsqueeze(2), [128, 8, 64]), op=ALU.mult),
                               reads=['osq', 'grs'], writes=['osq'])
                            op('dve', lambda e, u=u: e.tensor_tensor(out=onrm[:, u, :], in0=osq[:], in1=retg[:], op=ALU.mult),
                               reads=['osq', 'retg'], writes=['onrm%d' % u])
                        for h in range(8):
                            hp, h2 = h // 2, h % 2
                            mm(P[5][h2 * 64:(h2 + 1) * 64, hp * 64:(hp + 1) * 64], ktil[:, h * 64:(h + 1) * 64],
                               rvb[:, h * 64:(h + 1) * 64], True, True, [ktiln, rvbn], ['P5'])
                        op('dve', lambda e: e.tensor_tensor(out=stmp[:], in0=P[5][:, 0:256], in1=Sst[:], op=ALU.add),
                           reads=['P5', 'Sst'], writes=['stmp'])
                        op('dve', lambda e: e.tensor_tensor(out=Sst[:], in0=stmp[:], in1=gC[:], op=ALU.mult),
                           reads=['stmp', 'gC', 'Sb'], writes=['Sst'])


                part1(0)
                for u in range(4):
                    if u + 1 < 4:
                        part1(u + 1)
                    elif (not own) and s + 1 < NT:
                        load_x(s + 1)
                    part2(u)

                _stop(15)
                compress_tile(s)
                _stop(16)
                if s == NT - 1:
                    dma('sp', lambda e: e.dma_start(out=ret_d, in_=Sst[:]), reads=['Sst'])
                if not own or os.environ.get('K_OWN', '1') == '0':
                    continue

                load_wt(1, lambda kc: win_d[kc, :, C_Q:C_Q + 536], 536)
                for u in range(4):
                    cosN = tabs[:, u, 0:8]; sinN = tabs[:, u, 8:16]
                    projn(0, u, Wt[1], 0, 512, 'Wt1')
                    projn(1, u, Wt[1], 512, 536, 'Wt1')
                    op('act', lambda e, u=u: e.copy(out=qb[:, u, :], in_=P[0][:]), reads=['P0'], writes=['qb'])
                    do_rope(qb[:, u, :].rearrange("p (h d) -> p h d", h=8), P[0][:].rearrange("p (h d) -> p h d", h=8),
                            cosN, sinN, 8, 8, ['P0'], ['qb'])
                    op('act', lambda e, u=u: e.activation(out=gat[:, u, :], in_=P[1][:, 0:24], func=AF.Sigmoid),
                       reads=['P1'], writes=['gat'])
                load_wt(0, lambda kc: win_d[kc, :, C_RG:C_RG + 512], 512)
                for u in range(4):
                    projn(2, u, Wt[0], 0, 512, 'Wt0')
                    op('act', lambda e: e.activation(out=rgs[:], in_=P[2][:], func=AF.Silu), reads=['P2'], writes=['rgs'])
                    op('dve', lambda e, u=u: e.tensor_tensor(out=mixret[:], in0=onrm[:, u, :], in1=rgs[:], op=ALU.mult),
                       reads=['onrm%d' % u, 'rgs'], writes=['mixret'])
                    for c in range(4):
                        transpose_to(mixT[:, 4 + c, u * 128:(u + 1) * 128], mixret[:, c * 128:(c + 1) * 128], 128, 128,
                                     ['mixret'], ['mixT'])

                for g in range(2):
                    for h4 in range(4):
                        h = 4 * g + h4
                        for u in range(4):
                            transpose_to(Qa[0:64, 0, h4, u * 128:(u + 1) * 128], qb[:, u, h * 64:(h + 1) * 64], 128, 64,
                                         ['qb'], ['Qa0'], evac='dve')
                    op('dve', lambda e: e.tensor_copy(out=Qa[0:64, 1, :, :], in_=Qa[0:64, 0, :, :]), reads=['Qa0'], writes=['Qa1'])
                    for h4 in range(4):
                        h = 4 * g + h4

                        def c_s1(ct, h4=h4):
                            psi = SB3[ct % 3]
                            mm(P[psi][:], kcT[:, g, ct * 128:(ct + 1) * 128], Qa[:, 0, h4, :], True, ct != k,
                               ['kcT', 'Qa0'], ['P%d' % psi])
                            if ct == k:
                                mm(P[psi][:], ident[:], masks[:, 1792:2304], False, True, ['ident', 'masks'], ['P%d' % psi])
                            return (ct,) + exp_pt(psi, 0, 512, kbc[:, ct:ct + 1], ['kbc'])

                        def c_s2(pct, pt, ptn):
                            for u in range(4):
                                pb = 2 + u // 2
                                mm(P[pb][:, (u % 2) * 193:(u % 2 + 1) * 193], pt[:, u * 128:(u + 1) * 128], Rc[:, pct, g, :],
                                   (pct == 0 and u % 2 == 0), pct == k, [ptn, 'Rc'], ['P%d' % pb], skip_group_check=True)
                        pipeline(k + 1, c_s1, c_s2)
                        finish_branch(lambda u: P[2 + u // 2][:, (u % 2) * 193:(u % 2 + 1) * 193],
                                      lambda u: 'P%d' % (2 + u // 2), h4, h, 0, True, 192)
                    for u in range(4):
                        dma('sp', lambda e, u=u: e.dma_start(out=bon[:], in_=bonus_d[4 * k + u]), writes=['bon'])
                        op('dve', lambda e, u=u: e.tensor_tensor(out=score[:], in0=impacc[:, u, :], in1=bon[:], op=ALU.add),
                           reads=['impacc', 'bon'], writes=['score'])
                        op('dve', lambda e: e.max(out=m8a[:], in_=score[:]), reads=['score'], writes=['m8a'])
                        op('dve', lambda e: e.match_replace(out=sc2[:], in_to_replace=m8a[:], in_values=score[:], imm_value=-3e38),
                           reads=['score', 'm8a'], writes=['sc2'])
                        op('dve', lambda e: e.max(out=m8b[:], in_=sc2[:]), reads=['sc2'], writes=['m8b'])
                        op('dve', lambda e: e.tensor_scalar(out=sc2[:], in0=score[:], scalar1=m8b[:, 7:8], scalar2=-1.0,
                                                            op0=ALU.is_ge, op1=ALU.add),
                           reads=['score', 'm8b'], writes=['sc2'])
                        op('dve', lambda e: e.tensor_scalar(out=selb[:, 0:128], in0=sc2[:], scalar1=-NEGB, scalar2=None, op0=ALU.mult),
                           reads=['sc2'], writes=['selb'])
                        op('dve', lambda e: e.tensor_copy(out=selb[:, 128:256], in_=selb[:, 0:128]), reads=['selb'], writes=['selb'])
                        transpose_to(selT[:, 1, u * 128:(u + 1) * 128], selb[:, 0:128], 128, 128, ['selb'], ['selT'])
                        transpose_to(selT[:, 0, u * 128:(u + 1) * 128], selb[:, 64:192], 128, 128, ['selb'], ['selT'])
                    for r in range(2):
                        for h4 in range(4):
                            op('dve', lambda e, r=r, h4=h4: e.tensor_copy(out=Qa[64:128, r, h4, :], in_=selT[64:128, r, :]),
                               reads=['selT'], writes=['Qa%d' % r])
                    nkt = 4 * s + 4
                    for h4 in range(4):
                        h = 4 * g + h4

                        def s_s1(kt, h4=h4):
                            r = kt // 32
                            o = kt - 4 * s
                            c0 = 128 * o if o > 0 else 0
                            psi = SB3[kt % 3]
                            mm(P[psi][:, c0:512], KslT[:, g, kt * 128:(kt + 1) * 128], Qa[:, r, h4, c0:512], True, o < 0,
                               ['KslT%d' % kt, 'KslT_E', 'Qa%d' % r], ['P%d' % psi])
                            if o >= 0:
                                mm(P[psi][:, c0:512], ident[:], causal(o)[:, c0:512], False, True, ['ident', 'masks'], ['P%d' % psi])
                            return (kt, o) + exp_pt(psi, c0, 512, kbias[:, kt:kt + 1], ['kbias'])

                        def s_s2(pkt, po, pt, ptn):
                            c0 = 128 * po if po > 0 else 0
                            mm(P[2][0:65, c0:512], Vsl[:, pkt, g, :], pt[:, c0:512], pkt == 0, pkt == nkt - 1,
                               [ptn, 'Vsl%d' % pkt, 'Vsl_ones'], ['P2'])
                        pipeline(nkt, s_s1, s_s2)
                        ot_finish(2, h4, h, 1)
                    for h4 in range(4):
                        h = 4 * g + h4

                        def w_s1(o8, h4=h4):
                            psi = SB3[o8 % 3]
                            if o8 < 4:
                                c0, c1 = 0, 128 * (o8 + 1)
                                msk = wlo(o8)
                            else:
                                c0, c1 = 128 * (o8 - 4), 512
                                msk = causal(o8 - 4)
                            mm(P[psi][:, c0:c1], KwT[:, g, o8 * 128:(o8 + 1) * 128], Qa[:, 0, h4, c0:c1], True, False,
                               ['KwT%d' % o8, 'Qa0'], ['P%d' % psi])
                            mm(P[psi][:, c0:c1], ident[:], msk[:, c0:c1], False, True, ['ident', 'masks'], ['P%d' % psi])
                            kt = 4 * (s - 1) + o8
                            return (o8, c0, c1) + exp_pt(psi, c0, c1, kbias[:, kt:kt + 1], ['kbias'])

                        def w_s2(p8, pc0, pc1, pt, ptn):
                            mm(P[3][0:65, pc0:pc1], Vw[:, p8, g, :], pt[:, pc0:pc1], p8 == 0, p8 == 7,
                               [ptn, 'Vw%d' % p8, 'Vw_ones'], ['P3'], skip_group_check=True)
                        pipeline(8, w_s1, w_s2)
                        ot_finish(3, h4, h, 2)
                    for u in range(4):
                        op('act', lambda e, u=u: e.copy(out=onsab[:], in_=onsa[:, u, :]), reads=['onsa'], writes=['onsab'])
                        for c2 in range(2):
                            transpose_to(mixT[:, 2 * g + c2, u * 128:(u + 1) * 128], onsab[:, c2 * 128:(c2 + 1) * 128], 128, 128,
                                         ['onsab'], ['mixT'], evac='dve')

                load_wt(0, lambda kc: wo_d[kc, :, 0:512], 512)
                load_wt(1, lambda kc: wo_d[kc, :, 512:1024], 512)
                for u in range(4):
                    dma('sp', lambda e, u=u: e.dma_start(out=xo[:], in_=xown_d[4 * k + u]), writes=['stage', 'rtmp'])
                    for half in range(2):
                        for c in range(8):
                            mm(P[half][:], mixT[:, c, u * 128:(u + 1) * 128], Wt[half][:, c, 0:512], c == 0, c == 7,
                               ['mixT', 'Wt%d' % half], ['P%d' % half])
                        op('dve', lambda e, half=half: e.scalar_tensor_tensor(out=xr[:, half * 512:(half + 1) * 512],
                                                                              in0=xo[:, half * 512:(half + 1) * 512], scalar=ALPHA,
                                                                              in1=P[half][:], op0=ALU.mult, op1=ALU.add),
                           reads=['stage', 'rtmp', 'P%d' % half], writes=['orf', 'osq'])
                    layer_norm(xo[:], xr[:], ln1[:, 0, :], ln1[:, 1, :], ['orf', 'osq'], ['stage', 'rtmp'], 'ln1')
                    dma('sp', lambda e, u=u: e.dma_start(out=x1s_d[4 * k + u], in_=xo[:]), reads=['stage', 'rtmp'], writes=['x1s%d' % (4 * k + u)])
                if k < 3:
                    load_wt(0, lambda kc: win_d[kc, :, C_RQ:C_RQ + 512], 512)

            S.barrier()
            stP.close()
            if DO_SAMPLE:
                u = 0
                xTbs = TA("xTbs", [128, 8, 128])
                xsel[0] = xTbs
                mixTs = TA("mixTs", [128, 8, 128])
                KcTs = TA("KcTs", [64, 2, 1040]); VcTs = TA("VcTs", [64, 2, 1040])

                def compress8(s8):
                    ctp, row0 = s8 // 2, 64 * (s8 % 2)
                    for r in range(32):
                        mm(P[4][0:64, 0:128].rearrange("p (g n) -> p g n", g=2), wck[:, r, :], KcTs[:, :, r:r + 1009:16], r == 0, r == 31,
                           ['wck', 'KcTs'], ['P4'])
                    op('act', lambda e: e.activation(out=kcT[0:64, :, 64 * s8:64 * s8 + 64],
                                                     in_=P[4][0:64, 0:128].rearrange("p (g n) -> p g n", g=2),
                                                     func=AF.Identity, bias=pbk[:, 0:1], scale=1.0),
                       reads=['P4', 'pbk'], writes=['kcT'])
                    for g in range(2):
                        for r in range(33):
                            if r < 32:
                                mm(P[5][row0:row0 + 64, g * 64:(g + 1) * 64], VcTs[:, g, r:r + 1009:16], wcv[:, r, :],
                                   r == 0, False, ['wcv', 'VcTs'], ['P5'])
                            else:
                                mm(P[5][row0:row0 + 64, g * 64:(g + 1) * 64], ones1[0:1, 0:64], pbv[0:1, :],
                                   False, True, ['ones1', 'pbv'], ['P5'])
                        op('act', lambda e, g=g: e.copy(out=Rc[row0:row0 + 64, ctp, g, 128:192],
                                                         in_=P[5][row0:row0 + 64, g * 64:(g + 1) * 64]),
                           reads=['P5'], writes=['Rc'])
                    op('dve', lambda e: e.tensor_copy(out=KcTs[:, :, 0:16], in_=KcTs[:, :, 1024:1040]), reads=['KcTs'], writes=['KcTs'])
                    op('dve', lambda e: e.tensor_copy(out=VcTs[:, :, 0:16], in_=VcTs[:, :, 1024:1040]), reads=['VcTs'], writes=['VcTs'])
                qbs = TA("qbs", [128, 1, 512])
                onrm_s = TA("onrm_s", [128, 1, 512])
                ohs = TA("ohs", [128, 4], F32)
                dma('sp', lambda e: e.dma_start(out=ohs[:], in_=oh_d), writes=['ohs'])
                kbself = TA("kbself", [128, 4], F32)
                dma('sp', lambda e: e.dma_start(out=kbself[:], in_=kbself_d), writes=['kbself'])
                kbws = TA("kbws", [128, 4], F32)
                dma('sp', lambda e: e.dma_start(out=kbws[:], in_=kbws_d), writes=['kbws'])
                kbcs = TA("kbcs", [128, 4], F32)
                dma('sp', lambda e: e.dma_start(out=kbcs[:], in_=kbcs_d), writes=['kbcs'])
                bons = TA("bons", [1, 128], F32)
                dma('sp', lambda e: e.dma_start(out=bons[:], in_=bons_d), writes=['bons'])
                decs = TA("decs", [128, 16], F32)
                dma('sp', lambda e: e.dma_start(out=decs[:], in_=decs_d), writes=['decs'])
                gC1 = TA("gC1", [128, 256], F32)
                dma('sp', lambda e: e.dma_start(out=gC1[:], in_=gC1_d), writes=['gC1'])
                zcol = TA("zcol", [128, 1], F32)
                op('pool', lambda e: e.memset(zcol[:], 0.0), writes=['zcol'])
                oneb = TA("oneb", [1, 1])
                op('pool', lambda e: e.memset(oneb[:], 1.0), writes=['oneb'])
                pti = TA("pti", [128, 256], I32)
                dma('sp', lambda e: e.dma_start(out=pti[:], in_=pt_d), writes=['pti'])
                ptf = TA("ptf", [128, 256], F32)
                iop = TA("iop", [128, 1], I32)
                iof = TA("iof", [128, 1], F32)
                idx = TA("idx", [128, 256], I32)
                op('pool', lambda e: e.iota(iop[:], pattern=[[0, 1]], base=0, channel_multiplier=1), writes=['iop'])
                op('dve', lambda e: e.tensor_copy(out=iof[:], in_=iop[:]), reads=['iop'], writes=['iof'])
                op('dve', lambda e: e.tensor_copy(out=ptf[:], in_=pti[:]), reads=['pti'], writes=['ptf'])
                op('dve', lambda e: e.tensor_scalar(out=ptf[:], in0=ptf[:], scalar1=128.0, scalar2=iof[:, 0:1],
                                                    op0=ALU.mult, op1=ALU.add), reads=['ptf', 'iof'], writes=['ptf'])
                op('dve', lambda e: e.tensor_copy(out=idx[:], in_=ptf[:]), reads=['ptf'], writes=['idx'])

                KnT = TA("KnT", [128, 2, 128]); Vn = TA("Vn", [128, 2, 65])
                KwnT = TA("KwnT", [128, 2, 128]); Vwn = TA("Vwn", [128, 2, 65])
                for t_ in (KnT, KwnT):
                    op('pool', lambda e, t_=t_: e.memset(t_[:], 0.0), writes=['Knew'])
                for t_ in (Vn, Vwn):
                    op('pool', lambda e, t_=t_: e.memset(t_[:, :, 64:65], 1.0), writes=['Vnew'])
                QTs = TA("QTs", [64, 8, 128])
                Qs = TA("Qs", [128, 2, 4])
                op('pool', lambda e: e.memset(Qs[:], 0.0), writes=['Qs'])
                pts = TA("pts", [128, 64])
                accs = TA("accs", [4, 193], F32)
                rec4 = TA("rec4", [4, 1], F32)
                gcol = TA("gcol", [4, 3], F32)
                sc4 = TA("sc4", [4, 1], F32)
                ocmb = TA("ocmb", [4, 64], F32)
                srow = TA("srow", [1, 128], F32)
                srow2 = TA("srow2", [1, 128], F32)
                s8a = TA("s8a", [1, 8], F32); s8b = TA("s8b", [1, 8], F32)
                selr = TA("selr", [1, 256])
                qz = [TA("qz%d" % b, [128, 4, 128]) for b in range(4)]
                kvbs = [TA("kvbs%d" % i, [128, 512]) for i in range(3)]
                SbS = [[TA("SbS%d_%d" % (b, i), [128, 256]) for i in range(2)] for b in range(4)]
                SsT = [TA("SsT%d" % b, [128, 256], F32) for b in range(4)]
                ktz = TA("ktz", [128, 512])

                for kc in range(8):
                    dma('pool', lambda e, kc=kc: e.dma_start(out=xTbs[:, kc, :], in_=xsT_d[kc]), writes=['xTb'])
                dma('sp', lambda e: e.dma_start(out=tabs[:, 0, :], in_=tabs_s_d), writes=['tabs'])
                cosN = tabs[:, 0, 0:8]; sinN = tabs[:, 0, 8:16]
                cosR = tabs[:, 0, 16:48]; sinR = tabs[:, 0, 48:80]
                load_wt(0, lambda kc: win_d[kc, :, C_RQ:C_RQ + 512], 512)
                projn(0, u, Wkv, C_KVA, C_KVA + 512, 'Wkv')
                op('act', lambda e: e.copy(out=stage[:], in_=P[0][:]), reads=['P0'], writes=['stage'])
                do_rope(stage[:, 256:384].rearrange("p (g d) -> p g d", g=2),
                        P[0][:, 256:384].rearrange("p (g d) -> p g d", g=2), cosN, sinN, 2, 8, ['P0', 'stage'], ['stage'])
                dma('sp', lambda e: e.dma_start(out=kvs_d, in_=stage[:]), reads=['stage'])
                op('pool', lambda e: e.tensor_copy(out=kvb[:], in_=stage[:]), reads=['stage'], writes=['kvb'])
                for g in range(2):
                    transpose_to(KnT[0:64, g, :], kvb[:, 256 + g * 64:256 + (g + 1) * 64], 128, 64, ['kvb'], ['Knew'])
                op('pool', lambda e: e.tensor_copy(out=Vn[:, :, 0:64], in_=kvb[:, 384:512].rearrange("p (g d) -> p g d", g=2)),
                   reads=['kvb'], writes=['Vnew'])
                projn(1, u, Wkv, C_RK, C_RK + 512, 'Wkv')
                projn(2, u, Wkv, C_RV, C_RV + 512, 'Wkv')
                op('act', lambda e: e.copy(out=rtmp[:], in_=P[1][:]), reads=['P1'], writes=['rtmp'])
                do_rope(rtmp[:].rearrange("p (h d) -> p h d", h=8), P[1][:].rearrange("p (h d) -> p h d", h=8),
                        cosR, sinR, 8, 32, ['P1'], ['rtmp'])
                op('dve', lambda e: e.tensor_tensor(out=ktil[:].rearrange("p (h d) -> p h d", h=8),
                                                    in0=rtmp[:].rearrange("p (h d) -> p h d", h=8),
                                                    in1=bc(decs[:, 8:16].unsqueeze(2), [128, 8, 64]), op=ALU.mult),
                   reads=['rtmp', 'decs'], writes=['ktil'])
                op('act', lambda e: e.copy(out=rvb[:], in_=P[2][:]), reads=['P2'], writes=['rvb'])
                projn(3, u, Wkv, C_WIN, C_WIN + 256, 'Wkv')
                op('act', lambda e: e.copy(out=wstage[:], in_=P[3][:, 0:256]), reads=['P3'], writes=['wstage'])
                do_rope(wstage[:, 0:128].rearrange("p (g d) -> p g d", g=2),
                        P[3][:, 0:128].rearrange("p (g d) -> p g d", g=2), cosN, sinN, 2, 8, ['P3', 'wstage'], ['wstage'])
                for b in range(4):
                    dma('sp', lambda e, b=b: e.dma_start(out=wins_d[b, 511:512, :], in_=wstage[b:b + 1, :]), reads=['wstage'])
                op('pool', lambda e: e.tensor_copy(out=wb[:], in_=wstage[:]), reads=['wstage'], writes=['wb'])
                for g in range(2):
                    transpose_to(KwnT[0:64, g, :], wb[:, g * 64:(g + 1) * 64], 128, 64, ['wb'], ['Knew'])
                op('pool', lambda e: e.tensor_copy(out=Vwn[:, :, 0:64], in_=wb[:, 128:256].rearrange("p (g d) -> p g d", g=2)),
                   reads=['wb'], writes=['Vnew'])
                for b in range(4):
                    for h2 in range(2):
                        dma('sp', lambda e, b=b, h2=h2: e.dma_start(
                            out=SsT[b][h2 * 64:(h2 + 1) * 64, :].rearrange("p (hp e) -> p hp e", hp=4),
                            in_=stin_d[b].rearrange("(hp h2) d e -> h2 d hp e", h2=2)[h2]), writes=['SsT%d' % b])
                    op('act', lambda e, b=b: e.copy(out=SbS[b][0][0:64, :], in_=SsT[b][0:64, :]), reads=['SsT%d' % b], writes=['SbS'])
                    op('act', lambda e, b=b: e.copy(out=SbS[b][1][64:128, :], in_=SsT[b][64:128, :]), reads=['SsT%d' % b], writes=['SbS'])
                    op('pool', lambda e, b=b: e.memset(SbS[b][0][64:128, :], 0.0), writes=['SbS'])
                    op('pool', lambda e, b=b: e.memset(SbS[b][1][0:64, :], 0.0), writes=['SbS'])
                projn(3, u, Wt[0], 0, 512, 'Wt0')
                op('act', lambda e: e.copy(out=rtmp[:], in_=P[3][:]), reads=['P3'], writes=['rtmp'])
                do_rope(rtmp[:].rearrange("p (h d) -> p h d", h=8), P[3][:].rearrange("p (h d) -> p h d", h=8),
                        cosR, sinR, 8, 32, ['P3'], ['rtmp'])
                op('dve', lambda e: e.tensor_tensor(out=qtil[:].rearrange("p (h d) -> p h d", h=8),
                                                    in0=rtmp[:].rearrange("p (h d) -> p h d", h=8),
                                                    in1=bc(decs[:, 0:8].unsqueeze(2), [128, 8, 64]), op=ALU.mult),
                   reads=['rtmp', 'decs'], writes=['qtil'])
                for hp in range(4):
                    slot = tr_rr[0]; tr_rr[0] = (slot + 1) % 2
                    pst = PTb[slot][:, 0, :]
                    op('pe', lambda e, pst=pst, hp=hp: e.transpose(pst, ktil[:, hp * 128:(hp + 1) * 128], ident[:]),
                       reads=['ktil', 'ident'], writes=['PTr%d' % slot])
                    op('act', lambda e, pst=pst, hp=hp: e.copy(out=kz[0][0:64, hp, :], in_=pst[0:64, :]),
                       reads=['PTr%d' % slot], writes=['ktilT'])
                    op('act', lambda e, pst=pst, hp=hp: e.copy(out=kz[1][64:128, hp, :], in_=pst[64:128, :]),
                       reads=['PTr%d' % slot], writes=['ktilT'])
                    transpose_to(qtilT[:, hp, :], qtil[:, hp * 128:(hp + 1) * 128], 128, 128, ['qtil'], ['qtilT'], evac='dve')
                for b in range(4):
                    op('pool', lambda e, b=b: e.memset(qz[b][:], 0.0), writes=['qz'])
                    op('pool', lambda e, b=b: e.tensor_copy(out=qz[b][:, :, b:b + 1], in_=qtilT[:, :, b:b + 1]),
                       reads=['qtilT', 'qz'], writes=['qz'])
                for h in range(8):
                    hp, h2 = h // 2, h % 2
                    pb = 4 + h // 4
                    mm(P[pb][:, (h % 4) * 128:(h % 4 + 1) * 128], kz[h2][:, hp, :], qtilT[:, hp, :],
                       True, True, ['ktilT', 'qtilT'], ['P%d' % pb])
                for half in range(2):
                    op('dve', lambda e, half=half: e.tensor_tensor(
                        out=innb[:, half * 4:(half + 1) * 4, :],
                        in0=P[4 + half][:].rearrange("p (h i) -> p h i", h=4),
                        in1=bc(identf[:].unsqueeze(1), [128, 4, 128]), op=ALU.mult),
                       reads=['P%d' % (4 + half), 'identf'], writes=['innb%d' % half])
                for h in range(8):
                    hp, h2 = h // 2, h % 2
                    mm(P[3][:, h * 64:(h + 1) * 64], innb[:, h, :], rvb[:, h * 64:(h + 1) * 64], True, False,
                       ['innb%d' % (h // 4), 'rvb', 'rtmp'], ['P3'])
                    for b in range(4):
                        mm(P[3][:, h * 64:(h + 1) * 64], qz[b][:, hp, :], SbS[b][h2][:, hp * 64:(hp + 1) * 64],
                           False, b == 3, ['qz', 'SbS'], ['P3'])
                op('act', lambda e: e.copy(out=orf[:], in_=P[3][:]), reads=['P3'], writes=['orf'])
                orf3 = orf[:].rearrange("p (h d) -> p h d", h=8)
                osq3 = osq[:].rearrange("p (h d) -> p h d", h=8)
                op('dve', lambda e: e.tensor_reduce(out=gsm[:], in_=orf3, axis=AX.X, op=ALU.add), reads=['orf'], writes=['gsm'])
                op('dve', lambda e: e.tensor_tensor(out=osq[:], in0=orf[:], in1=orf[:], op=ALU.mult), reads=['orf'], writes=['osq'])
                op('dve', lambda e: e.tensor_reduce(out=gss[:], in_=osq3, axis=AX.X, op=ALU.add), reads=['osq'], writes=['gss'])
                op('dve', lambda e: e.tensor_scalar(out=gmu[:], in0=gsm[:], scalar1=1.0 / 64, scalar2=None, op0=ALU.mult),
                   reads=['gsm'], writes=['gmu'])
                op('dve', lambda e: e.tensor_tensor(out=gm2[:], in0=gmu[:], in1=gmu[:], op=ALU.mult), reads=['gmu'], writes=['gm2'])
                op('dve', lambda e: e.scalar_tensor_tensor(out=grs[:], in0=gss[:], scalar=1.0 / 64, in1=gm2[:],
                                                           op0=ALU.mult, op1=ALU.subtract),
                   reads=['gss', 'gm2'], writes=['grs'])
                op('act', lambda e: e.activation(out=grs[:], in_=grs[:], func=AF.Sqrt, bias=eps_t[:, 0:1], scale=1.0),
                   reads=['grs', 'eps'], writes=['grs'])
                op('dve', lambda e: e.reciprocal(out=grs[:], in_=grs[:]), reads=['grs'], writes=['grs'])
                op('dve', lambda e: e.tensor_tensor(out=osq3, in0=orf3, in1=bc(gmu[:].unsqueeze(2), [128, 8, 64]), op=ALU.subtract),
                   reads=['orf', 'gmu', 'gss'], writes=['osq'])
                op('dve', lambda e: e.tensor_tensor(out=osq3, in0=osq3, in1=bc(grs[:].unsqueeze(2), [128, 8, 64]), op=ALU.mult),
                   reads=['osq', 'grs'], writes=['osq'])
                op('dve', lambda e: e.tensor_tensor(out=onrm_s[:, 0, :], in0=osq[:], in1=retg[:], op=ALU.mult),
                   reads=['osq', 'retg'], writes=['onrm0'])
                for b in range(4):
                    op('dve', lambda e, b=b: e.tensor_scalar(out=ktz[:], in0=ktil[:], scalar1=ohs[:, b:b + 1], scalar2=None, op0=ALU.mult),
                       reads=['ktil', 'ohs'], writes=['ktz'])
                    for h in range(8):
                        hp, h2 = h // 2, h % 2
                        mm(P[5][h2 * 64:(h2 + 1) * 64, hp * 64:(hp + 1) * 64], ktz[:, h * 64:(h + 1) * 64],
                           rvb[:, h * 64:(h + 1) * 64], True, True, ['ktz', 'rvb'], ['P5'])
                    op('dve', lambda e, b=b: e.tensor_tensor(out=stmp[:], in0=P[5][:, 0:256], in1=SsT[b][:], op=ALU.add),
                       reads=['P5', 'SsT%d' % b], writes=['stmp'])
                    op('dve', lambda e, b=b: e.tensor_tensor(out=SsT[b][:], in0=stmp[:], in1=gC1[:], op=ALU.mult),
                       reads=['stmp', 'gC1', 'SbS'], writes=['SsT%d' % b])
                    dma('sp', lambda e, b=b: e.dma_start(out=rets_d[b], in_=SsT[b][:]), reads=['SsT%d' % b])
                load_wt(1, lambda kc: win_d[kc, :, C_Q:C_Q + 536], 536)
                projn(0, u, Wt[1], 0, 512, 'Wt1')
                projn(1, u, Wt[1], 512, 536, 'Wt1')
                op('act', lambda e: e.copy(out=qbs[:, 0, :], in_=P[0][:]), reads=['P0'], writes=['qb'])
                do_rope(qbs[:, 0, :].rearrange("p (h d) -> p h d", h=8), P[0][:].rearrange("p (h d) -> p h d", h=8),
                        cosN, sinN, 8, 8, ['P0', 'qb'], ['qb'])
                op('act', lambda e: e.activation(out=gat[:, 0, :], in_=P[1][:, 0:24], func=AF.Sigmoid), reads=['P1'], writes=['gat'])
                load_wt(0, lambda kc: win_d[kc, :, C_RG:C_RG + 512], 512)
                projn(2, u, Wt[0], 0, 512, 'Wt0')
                op('act', lambda e: e.activation(out=rgs[:], in_=P[2][:], func=AF.Silu), reads=['P2'], writes=['rgs'])
                op('dve', lambda e: e.tensor_tensor(out=mixret[:], in0=onrm_s[:, 0, :], in1=rgs[:], op=ALU.mult),
                   reads=['onrm0', 'rgs'], writes=['mixret'])
                for c in range(4):
                    transpose_to(mixTs[:, 4 + c, :], mixret[:, c * 128:(c + 1) * 128], 128, 128, ['mixret'], ['mixT'])
                for h in range(8):
                    transpose_to(QTs[:, h, :], qbs[:, 0, h * 64:(h + 1) * 64], 128, 64, ['qb'], ['QTs'], evac='dve')

                for b in range(4):
                    op('pool', lambda e: e.memset(KcTs[:, :, 0:16], 0.0), reads=['KcTs'], writes=['KcTs'])
                    op('pool', lambda e: e.memset(VcTs[:, :, 0:16], 0.0), reads=['VcTs'], writes=['VcTs'])
                    for pg in range(64):
                        kt = pg
                        ru = pg % 8
                        kb_i = (b * 64 + pg) % 3
                        kvr = kvbs[kb_i]
                        kvn = 'kvbs%d' % kb_i
                        dma('pool', lambda e, b=b, pg=pg, kvr=kvr: e.indirect_dma_start(
                            out=kvr[:], out_offset=None, in_=ckv_d,
                            in_offset=bass.IndirectOffsetOnAxis(ap=idx[:, b * 64 + pg:b * 64 + pg + 1], axis=0)),
                            reads=['idx'], writes=[kvn])
                        pbk_, pbn_ = tbatch([kvr[:, i * 64:(i + 1) * 64] for i in range(6)], [kvn])
                        op('act', lambda e, pbk_=pbk_, ru=ru: e.copy(out=KcTs[:, :, 16 + ru * 128:16 + (ru + 1) * 128], in_=pbk_[0:64, 0:2, :]),
                           reads=[pbn_], writes=['KcTs'])
                        op('act', lambda e, pbk_=pbk_, ru=ru: e.copy(out=VcTs[:, :, 16 + ru * 128:16 + (ru + 1) * 128], in_=pbk_[0:64, 2:4, :]),
                           reads=[pbn_], writes=['VcTs'])
                        op('dve', lambda e, pbk_=pbk_, kt=kt: e.tensor_copy(out=KslT[0:64, :, kt * 128:(kt + 1) * 128], in_=pbk_[0:64, 4:6, :]),
                           reads=[pbn_], writes=['KslT%d' % kt])
                        op('dve', lambda e, kt=kt, kvr=kvr: e.tensor_copy(out=Vsl[:, kt, :, 0:64],
                                                                           in_=kvr[:, 384:512].rearrange("p (g d) -> p g d", g=2)),
                           reads=[kvn], writes=['Vsl%d' % kt])
                        if ru == 7:
                            compress8(pg // 8)
                    for w in range(4):
                        dma('sp', lambda e, b=b, w=w: e.dma_start(out=wstage[:], in_=cwin_d[b, w * 128:(w + 1) * 128, :]), writes=['wstage'])
                        if w == 0:
                            dma('sp', lambda e, b=b: e.dma_start(out=wins_d[b, 0:127, :], in_=wstage[1:128, :]), reads=['wstage'])
                        else:
                            dma('sp', lambda e, b=b, w=w: e.dma_start(out=wins_d[b, 128 * w - 1:128 * w + 127, :], in_=wstage[:]),
                                reads=['wstage'])
                        op('pool', lambda e: e.tensor_copy(out=wb[:], in_=wstage[:]), reads=['wstage'], writes=['wb'])
                        for g in range(2):
                            transpose_to(KwT[0:64, g, w * 128:(w + 1) * 128], wb[:, g * 64:(g + 1) * 64], 128, 64, ['wb'], ['KwT%d' % w])
                        op('pool', lambda e, w=w: e.tensor_copy(out=Vw[:, w, :, 0:64],
                                                                 in_=wb[:, 128:256].rearrange("p (g d) -> p g d", g=2)),
                           reads=['wb'], writes=['Vw%d' % w])
                    for g in range(2):
                        for r in range(2):
                            op('dve', lambda e, g=g, r=r, b=b: e.tensor_copy(out=Qs[0:64, r, :], in_=QTs[:, 4 * g:4 * g + 4, b]),
                               reads=['QTs', 'Qs'], writes=['Qs'])
                        for br in range(3):
                            mm(P[4][0:4, br:br + 1], gat[:, 0, g * 12 + br:g * 12 + br + 10:3], ohs[:, b:b + 1], True, True,
                               ['gat', 'ohs'], ['P4'])
                        op('act', lambda e: e.copy(out=gcol[:], in_=P[4][0:4, 0:3]), reads=['P4'], writes=['gcol'])
                        for ct in range(4):
                            mm(P[0][:, 4 * ct:4 * ct + 4], kcT[:, g, ct * 128:(ct + 1) * 128], Qs[:, 0, :], True, True,
                               ['kcT', 'Qs'], ['P0'])
                        for ct in range(4):
                            op('act', lambda e, ct=ct: e.activation(out=pts[:, 4 * ct:4 * ct + 4], in_=P[0][:, 4 * ct:4 * ct + 4],
                                                                    func=AF.Exp, bias=kbcs[:, ct:ct + 1], scale=0.125),
                               reads=['P0', 'kbcs'], writes=['pts'])
                        for ct in range(4):
                            mm(P[2][0:4, 0:193], pts[:, 4 * ct:4 * ct + 4], Rc[:, ct, g, :], ct == 0, ct == 3, ['pts', 'Rc'], ['P2'])
                        op('act', lambda e: e.copy(out=accs[:], in_=P[2][0:4, 0:193]), reads=['P2'], writes=['accs'])
                        op('dve', lambda e: e.tensor_scalar(out=rec4[:], in0=accs[:, 192:193], scalar1=1e-30, scalar2=None, op0=ALU.max),
                           reads=['accs'], writes=['rec4'])
                        op('dve', lambda e: e.reciprocal(out=rec4[:], in_=rec4[:]), reads=['rec4'], writes=['rec4'])
                        mm(P[3][0:1, 0:128], rec4[:, 0:1], accs[:, 0:128], True, True, ['rec4', 'accs'], ['P3'])
                        op('dve', lambda e: e.tensor_tensor(out=sc4[:], in0=rec4[:], in1=gcol[:, 0:1], op=ALU.mult),
                           reads=['rec4', 'gcol'], writes=['sc4'])
                        op('dve', lambda e: e.tensor_scalar(out=ocmb[:], in0=accs[:, 128:192], scalar1=sc4[:, 0:1], scalar2=None, op0=ALU.mult),
                           reads=['accs', 'sc4'], writes=['ocmb'])
                        op('dve', lambda e: e.tensor_tensor(out=srow[:], in0=P[3][0:1, 0:128], in1=bons[:], op=ALU.add),
                           reads=['P3', 'bons'], writes=['srow'])
                        op('dve', lambda e: e.max(out=s8a[:], in_=srow[:]), reads=['srow'], writes=['s8a'])
                        op('dve', lambda e: e.match_replace(out=srow2[:], in_to_replace=s8a[:], in_values=srow[:], imm_value=-3e38),
                           reads=['srow', 's8a'], writes=['srow2'])
                        op('dve', lambda e: e.max(out=s8b[:], in_=srow2[:]), reads=['srow2'], writes=['s8b'])
                        op('dve', lambda e: e.tensor_scalar(out=srow2[:], in0=srow[:], scalar1=s8b[:, 6:7], scalar2=-1.0,
                                                            op0=ALU.is_ge, op1=ALU.add), reads=['srow', 's8b'], writes=['srow2'])
                        op('dve', lambda e: e.tensor_scalar(out=selr[:, 0:128], in0=srow2[:], scalar1=-NEGB, scalar2=None, op0=ALU.mult),
                           reads=['srow2'], writes=['selr'])
                        op('dve', lambda e: e.tensor_copy(out=selr[:, 128:256], in_=selr[:, 0:128]), reads=['selr'], writes=['selr'])
                        mm(P[4][:, 8:9], selr[0:1, 0:128], oneb[0:1, 0:1], True, True, ['selr', 'oneb', 'gcol'], ['P4'])
                        mm(P[4][:, 9:10], selr[0:1, 64:192], oneb[0:1, 0:1], True, True, ['selr', 'oneb'], ['P4'])
                        op('dve', lambda e: e.tensor_copy(out=Qs[64:128, 1, :], in_=bc(P[4][64:128, 8:9], [64, 4])), reads=['P4', 'Qs'], writes=['Qs'])
                        op('dve', lambda e: e.tensor_copy(out=Qs[64:128, 0, :], in_=bc(P[4][64:128, 9:10], [64, 4])), reads=['P4', 'Qs'], writes=['Qs'])
                        for grp in range(4):
                            psi = grp % 2
                            for i in range(16):
                                kt = 16 * grp + i
                                mm(P[psi][:, 4 * i:4 * i + 4], KslT[:, g, kt * 128:(kt + 1) * 128], Qs[:, kt // 32, :], True, True,
                                   ['KslT%d' % kt, 'KslT_E', 'Qs'], ['P%d' % psi])
                            op('act', lambda e, psi=psi: e.activation(out=pts[:], in_=P[psi][:, 0:64], func=AF.Exp, bias=zcol[:, 0:1], scale=0.125),
                               reads=['P%d' % psi, 'zcol'], writes=['pts'])
                            for i in range(16):
                                kt = 16 * grp + i
                                mm(P[2][0:4, 0:65], pts[:, 4 * i:4 * i + 4], Vsl[:, kt, g, :], grp == 0 and i == 0, False,
                                   ['pts', 'Vsl%d' % kt, 'Vsl_ones'], ['P2'])
                        mm(P[0][:, 0:4], KnT[:, g, :], Qs[:, 0, :], True, True, ['Knew', 'Qs'], ['P0'])
                        op('act', lambda e, b=b: e.activation(out=pts[:, 0:4], in_=P[0][:, 0:4], func=AF.Exp, bias=kbself[:, b:b + 1], scale=0.125),
                           reads=['P0', 'kbself'], writes=['pts'])
                        mm(P[2][0:4, 0:65], pts[:, 0:4], Vn[:, g, :], False, True, ['pts', 'Vnew'], ['P2'])
                        for br, pbk_ in ((1, 2),):
                            op('act', lambda e: e.copy(out=accs[:, 0:65], in_=P[2][0:4, 0:65]), reads=['P2'], writes=['accs'])
                            op('dve', lambda e: e.reciprocal(out=rec4[:], in_=accs[:, 64:65]), reads=['accs'], writes=['rec4'])
                            op('dve', lambda e: e.tensor_tensor(out=sc4[:], in0=rec4[:], in1=gcol[:, 1:2], op=ALU.mult),
                               reads=['rec4', 'gcol'], writes=['sc4'])
                            op('dve', lambda e: e.scalar_tensor_tensor(out=ocmb[:], in0=accs[:, 0:64], scalar=sc4[:, 0:1], in1=ocmb[:],
                                                                       op0=ALU.mult, op1=ALU.add),
                               reads=['accs', 'sc4', 'ocmb'], writes=['ocmb'])
                        for w in range(4):
                            mm(P[1][:, 4 * w:4 * w + 4], KwT[:, g, w * 128:(w + 1) * 128], Qs[:, 0, :], True, True,
                               ['KwT%d' % w, 'Qs'], ['P1'])
                        mm(P[1][:, 16:20], KwnT[:, g, :], Qs[:, 0, :], True, True, ['Knew', 'Qs'], ['P1'])
                        for w in range(4):
                            op('act', lambda e, w=w: e.activation(out=pts[:, 4 * w:4 * w + 4], in_=P[1][:, 4 * w:4 * w + 4], func=AF.Exp,
                                                                  bias=kbws[:, w:w + 1], scale=0.125),
                               reads=['P1', 'kbws'], writes=['pts'])
                        op('act', lambda e, b=b: e.activation(out=pts[:, 16:20], in_=P[1][:, 16:20], func=AF.Exp, bias=kbself[:, b:b + 1], scale=0.125),
                           reads=['P1', 'kbself'], writes=['pts'])
                        for w in range(4):
                            mm(P[2][0:4, 0:65], pts[:, 4 * w:4 * w + 4], Vw[:, w, g, :], w == 0, False, ['pts', 'Vw%d' % w, 'Vw_ones'], ['P2'])
                        mm(P[2][0:4, 0:65], pts[:, 16:20], Vwn[:, g, :], False, True, ['pts', 'Vnew'], ['P2'])
                        op('act', lambda e: e.copy(out=accs[:, 0:65], in_=P[2][0:4, 0:65]), reads=['P2'], writes=['accs'])
                        op('dve', lambda e: e.reciprocal(out=rec4[:], in_=accs[:, 64:65]), reads=['accs'], writes=['rec4'])
                        op('dve', lambda e: e.tensor_tensor(out=sc4[:], in0=rec4[:], in1=gcol[:, 2:3], op=ALU.mult),
                           reads=['rec4', 'gcol'], writes=['sc4'])
                        op('dve', lambda e: e.scalar_tensor_tensor(out=ocmb[:], in0=accs[:, 0:64], scalar=sc4[:, 0:1], in1=ocmb[:],
                                                                   op0=ALU.mult, op1=ALU.add),
                           reads=['accs', 'sc4', 'ocmb'], writes=['ocmb'])
                        dma('sp', lambda e, b=b, g=g: e.dma_start(out=ons_d[b, 4 * g:4 * g + 4, :], in_=ocmb[:]), reads=['ocmb'], writes=['ons'])

                op('pool', lambda e: e.memset(onsa[:, 0:2, :], 0.0), reads=['onsa'], writes=['onsa'])
                dma('sp', lambda e: e.dma_start(out=onsa[0:4, 0:2, :].rearrange("p a c -> p (a c)"),
                                                in_=ons_d.rearrange("b h d -> b (h d)")), reads=['ons', 'onsa'], writes=['onsa'])
                op('act', lambda e: e.copy(out=mixret[:], in_=onsa[:, 0:2, :].rearrange("p a c -> p (a c)")), reads=['onsa'], writes=['mixret'])
                for c in range(4):
                    transpose_to(mixTs[:, c, :], mixret[:, c * 128:(c + 1) * 128], 128, 128, ['mixret'], ['mixT'], evac='dve')
                load_wt(0, lambda kc: wo_d[kc, :, 0:512], 512)
                load_wt(1, lambda kc: wo_d[kc, :, 512:1024], 512)
                dma('sp', lambda e: e.dma_start(out=xo[:], in_=xs_own_d), writes=['stage', 'rtmp'])
                for half in range(2):
                    for c in range(8):
                        mm(P[half][:], mixTs[:, c, :], Wt[half][:, c, 0:512], c == 0, c == 7, ['mixT', 'Wt%d' % half], ['P%d' % half])
                    op('dve', lambda e, half=half: e.scalar_tensor_tensor(out=xr[:, half * 512:(half + 1) * 512],
                                                                          in0=xo[:, half * 512:(half + 1) * 512], scalar=ALPHA,
                                                                          in1=P[half][:], op0=ALU.mult, op1=ALU.add),
                       reads=['stage', 'rtmp', 'P%d' % half], writes=['orf', 'osq'])
                layer_norm(xo[:], xr[:], ln1[:, 0, :], ln1[:, 1, :], ['orf', 'osq'], ['stage', 'rtmp'], 'ln1')
                dma('sp', lambda e: e.dma_start(out=x1s_d[16], in_=xo[:]), reads=['stage', 'rtmp'], writes=['x1s16'])

        _stop(5)
        S.barrier()
        with contextlib.ExitStack() as stB:
            def TB(name, shape, dt=BF16):
                return stB.enter_context(nc.sbuf_tensor("s_" + name, list(shape), dt))
            Wup = TB("Wup", [128, 8, 4096])
            Wdn = TB("Wdn", [128, 32, D])
            for c4 in range(4):
                dma('sp', lambda e, c4=c4: e.dma_start(out=Wup[:, :, c4 * 1024:(c4 + 1) * 1024],
                                                       in_=wupb_d[:, :, c4 * 1024:(c4 + 1) * 1024].rearrange("k p c -> p k c")),
                    reads=['wupb%d' % c4], writes=['Wup%d' % c4])
            for f8 in range(4):
                dma('sp', lambda e, f8=f8: e.dma_start(out=Wdn[:, f8 * 8:(f8 + 1) * 8, :],
                                                       in_=wdnb_d[f8 * 8:(f8 + 1) * 8].rearrange("f p c -> p f c")),
                    reads=['wdnb%d' % f8], writes=['Wdn%d' % f8])
            ln2 = TB("ln2", [128, 2, D], F32)
            dma('sp', lambda e: e.dma_start(out=ln2[:, 0, :], in_=ln_d[2]), writes=['ln2'])
            dma('sp', lambda e: e.dma_start(out=ln2[:, 1, :], in_=ln_d[3]), writes=['ln2'])
            x1f = TB("x1f", [128, 4, D], F32)
            x1b = TB("x1b", [128, D])
            x1T = TB("x1T", [128, 8, 512])
            hr = [TB("hr%d" % i, [128, 512], F32) for i in range(2)]
            hT = TB("hT", [128, 32, 512])
            xr2 = TB("xr2", [128, D], F32)
            yo = TB("yo", [128, D], F32)
            for k in ([4] if os.environ.get('K_MLP') == 's' else range(int(os.environ.get('K_MLP', 5 if DO_SAMPLE else 4)))):
                nsub = 4 if k < 4 else 1
                NTOK = 128 * nsub
                for u in range(nsub):
                    dma('sp', lambda e, u=u: e.dma_start(out=x1f[:, u, :], in_=x1s_d[4 * k + u]),
                        reads=['x1s%d' % (4 * k + u)], writes=['x1f%d' % u])
                    op('pool', lambda e, u=u: e.tensor_copy(out=x1b[:], in_=x1f[:, u, :]), reads=['x1f%d' % u], writes=['x1b'])
                    for kc in range(8):
                        transpose_to(x1T[:, kc, u * 128:(u + 1) * 128], x1b[:, kc * 128:(kc + 1) * 128], 128, 128,
                                     ['x1b'], ['x1T'], evac=('act' if kc % 2 else 'dve'))
                for fc in range(32):
                    psi = fc % 2
                    for kc in range(8):
                        mm(P[psi][:, 0:NTOK], Wup[:, kc, fc * 128:(fc + 1) * 128], x1T[:, kc, 0:NTOK], kc == 0, kc == 7,
                           ['Wup%d' % (fc // 8), 'x1T'], ['P%d' % psi])
                    op('act', lambda e, psi=psi: e.activation(out=hr[psi][:, 0:NTOK], in_=P[psi][:, 0:NTOK], func=AF.Relu),
                       reads=['P%d' % psi], writes=['hr%d' % psi])
                    op('pool', lambda e, psi=psi, fc=fc: e.tensor_tensor(out=hT[:, fc, 0:NTOK], in0=hr[psi][:, 0:NTOK], in1=hr[psi][:, 0:NTOK], op=ALU.mult),
                       reads=['hr%d' % psi], writes=['hT'])
                for u in range(nsub):
                    for half in range(2):
                        pb = 2 + half
                        for fc in range(32):
                            mm(P[pb][:], hT[:, fc, u * 128:(u + 1) * 128], Wdn[:, fc, half * 512:(half + 1) * 512],
                               fc == 0, fc == 31, ['hT', 'Wdn%d' % (fc // 8)], ['P%d' % pb])
                        op('dve', lambda e, half=half, pb=pb, u=u: e.scalar_tensor_tensor(
                            out=xr2[:, half * 512:(half + 1) * 512], in0=x1f[:, u, half * 512:(half + 1) * 512], scalar=ALPHA,
                            in1=P[pb][:], op0=ALU.mult, op1=ALU.add),
                           reads=['x1f%d' % u, 'P%d' % pb], writes=['xr2'])
                    layer_norm(yo[:], xr2[:], ln2[:, 0, :], ln2[:, 1, :], ['xr2'], ['yo'], 'ln2')
                    dma('sp', lambda e, u=u, k=k: e.dma_start(out=(y_d[4 * k + u] if k < 4 else ys_d), in_=yo[:]), reads=['yo'])

        S.finish('sp')
        print("instructions:", S.ninstr, "sems:", S.nsem)
    return nc


_PERM = np.concatenate([np.arange(512, 1024), np.arange(1816, 2328), np.arange(2328, 2840),
                        np.arange(1024, 1280), np.arange(0, 512), np.arange(1280, 1304),
                        np.arange(1304, 1816), np.arange(2840, 3352)])


def _const_tables():
    f = np.float32
    key = np.arange(128)[:, None]
    tp = np.arange(896)[None, :] - 384
    mc = np.where(key <= tp, 0.0, NEGB)
    wl = np.where(tp < key, 0.0, NEGB)
    t = np.arange(512)[None, :]
    cm = np.where(t >= 16 * key - 1521, 0.0, NEGB)
    masks = np.concatenate([mc, wl, cm], axis=1).astype(f)
    tri = (np.arange(128)[None, :] >= np.arange(128)[:, None]).astype(f)
    E = (np.arange(4096)[None, :] // 64 == np.arange(64)[:, None]).astype(f)
    mimp = np.zeros((512, 128), f)
    for jb in range(128):
        for c, w in ((4 * jb - 1, 1.0), (4 * jb, 2.0), (4 * jb + 1, 2.0), (4 * jb + 2, 2.0), (4 * jb + 3, 1.0)):
            if 0 <= c + 1 < 512:
                mimp[c + 1, jb] = w
    mimp = mimp.reshape(4, 128, 128).transpose(1, 0, 2).copy()
    gam = 1.0 - 2.0 ** (-5.0 - np.arange(8, dtype=np.float64))
    tl = np.arange(128, dtype=np.float64)[:, None]
    dec = np.concatenate([gam[None, :] ** (tl + 1), gam[None, :] ** (-(tl + 1)) / 8.0], axis=1).astype(f)
    gC = np.zeros((128, 4, 64), f)
    for h in range(8):
        gC[(h % 2) * 64:(h % 2 + 1) * 64, h // 2, :] = gam[h] ** 128
    return dict(masks=masks, tri=tri, Eoh=E, mimp=mimp, dec=dec, gC=gC.reshape(128, 256), gam=gam)


def _rope_tabs(pos):
    invn = 500000.0 ** (-np.arange(8, dtype=np.float64) / 8)
    invr = 10000.0 ** (-np.arange(32, dtype=np.float64) / 32)
    an = pos[:, None] * invn[None, :]
    ar = pos[:, None] * invr[None, :]
    return np.concatenate([np.cos(an), np.sin(an), np.cos(ar), np.sin(ar)], axis=1).astype(np.float32)


def kernel(x_prompt, x_sample, cache_kv, cache_win, state_ret, page_table, w_in, w_cmp_k, w_cmp_v,
           pos_cmp_k, pos_cmp_v, ret_norm_g, w_o, ln1_g, ln1_b, w_up, w_down, ln2_g, ln2_b):
    f = np.float32
    asf = lambda a: np.ascontiguousarray(np.asarray(a), dtype=f)
    x_prompt = asf(x_prompt)
    ct = _const_tables()
    shared = dict(
        w_in=np.ascontiguousarray(asf(w_in)[0][:, _PERM].reshape(8, 128, 3352)),
        w_o=asf(w_o)[0].reshape(8, 128, D),
        w_up=asf(w_up)[0].reshape(8, 128, 4096),
        w_down=asf(w_down)[0].reshape(32, 128, D),
        wck=np.ascontiguousarray(asf(w_cmp_k)[0].transpose(1, 0, 2)),
        wcv=np.ascontiguousarray(asf(w_cmp_v)[0].transpose(1, 0, 2)),
        posk=np.ascontiguousarray(asf(pos_cmp_k)[0].T),
        posv=np.ascontiguousarray(asf(pos_cmp_v)[0].T),
        retg=np.ascontiguousarray(np.broadcast_to(asf(ret_norm_g)[0][None, :], (128, 512))),
        lnp=np.ascontiguousarray(np.stack([np.broadcast_to(asf(v)[0][None, :], (128, D))
                                           for v in (ln1_g, ln1_b, ln2_g, ln2_b)])),
        masks=ct['masks'], tri=ct['tri'], Eoh=ct['Eoh'], mimp=ct['mimp'], dec=ct['dec'], gC=ct['gC'],
    )
    gam = ct['gam']
    ohs = np.zeros((128, 4), f); ohs[np.arange(4), np.arange(4)] = 1.0
    kbself = np.full((128, 4), NEGB, f); kbself[np.arange(4), np.arange(4)] = 0.0
    kbws = np.zeros((128, 4), f); kbws[0, 0] = NEGB
    kbcs = np.zeros((128, 4), f); kbcs[0, 0] = NEGB
    bons = np.zeros((1, 128), f); bons[0, 0] = 1e4; bons[0, 127] = 1e4
    decs = np.ascontiguousarray(np.broadcast_to(ct['dec'][0:1, :], (128, 16)))
    gC1 = np.zeros((128, 4, 64), f)
    for h in range(8):
        gC1[(h % 2) * 64:(h % 2 + 1) * 64, h // 2, :] = gam[h]
    tabs_s = np.ascontiguousarray(np.broadcast_to(_rope_tabs(np.array([float(PAST)]))[0:1, :], (128, 80)))
    ckv = np.ascontiguousarray(np.asarray(cache_kv, dtype=f)).reshape(2560 * 128, 512)
    cwin = np.asarray(cache_win, dtype=f)[0].reshape(32, 512, 256)
    stin = np.asarray(state_ret, dtype=f)[0]
    ptab = np.asarray(page_table).astype(np.int32)
    xsamp = asf(x_sample)[:, 0, :]
    shared.update(ohs=ohs, kbself=kbself, kbws=kbws, kbcs=kbcs, bons=bons, decs=decs, gC1=gC1.reshape(128, 256),
                  tabs_s=tabs_s, cache_kv=ckv)
    in_maps = []
    for c in range(8):
        b, j = c // 4, c % 4
        off = 512 * (3 - j)
        xs = np.zeros((SEQ, D), f)
        xs[off:] = x_prompt[b, :SEQ - off]
        xT = np.ascontiguousarray(xs.T).reshape(8, 128, SEQ)
        xown = np.stack([xs[512 * (4 * k + 3):512 * (4 * k + 4)] for k in range(4)]).reshape(16, 128, D)
        sp = np.arange(SEQ)
        tpos = np.maximum(sp - off, 0).astype(np.float64)
        tabs = _rope_tabs(tpos).reshape(64, 128, 80)
        kbias = np.where(sp >= off, 0.0, NEGB).astype(f).reshape(64, 128).T.copy()
        cp = np.arange(512)
        kbc = np.where((cp - 1) >= 32 * (3 - j), 0.0, NEGB).astype(f).reshape(4, 128).T.copy()
        bonus = np.zeros((16, 128, 128), f)
        blk = np.arange(128)[None, :] - 8 * (3 - j)
        for k in range(4):
            for u in range(4):
                tt = 512 * (4 * k + 3) + 128 * u + np.arange(128) - off
                cur = (tt // 64)[:, None]
                forced = (blk == 0) | (blk == cur) | (blk == cur - 1)
                bo = np.where(forced, 1e4, 0.0)
                bo = np.where((blk < 0) | (blk > cur), -1e30, bo)
                bonus[4 * k + u] = bo
        m = dict(shared)
        m.update(xT=xT, xown=np.ascontiguousarray(xown), tabs=tabs, kbias=kbias, kbias_c=kbc, bonus=bonus)
        xs4 = np.zeros((128, D), f); xs4[0:4] = xsamp[4 * c:4 * c + 4]
        m.update(xsT=np.ascontiguousarray(xs4.T).reshape(8, 128, 128), xs_own=xs4,
                 cache_win=np.ascontiguousarray(cwin[4 * c:4 * c + 4]), state_in=np.ascontiguousarray(stin[4 * c:4 * c + 4]),
                 pt_rep=np.ascontiguousarray(np.broadcast_to(ptab[4 * c:4 * c + 4].reshape(1, 256), (128, 256))))
        in_maps.append(m)

    try:
        nc = build_program()
    except _Stop:
        nc = _CUR[0].nc
    res = run_bass_kernel_spmd(nc, in_maps, core_ids=list(range(8)))
    R = res.results

    y_prompt = np.zeros((2, SEQ, D), f)
    kv_prompt = np.zeros((1, 2, SEQ, 4, 2, 64), f)
    win_prompt = np.zeros((1, 2, 512, 2, 2, 64), f)
    ret_prompt = np.zeros((1, 2, 8, 64, 64), f)
    for c in range(8):
        b, j = c // 4, c % 4
        yo = R[c]["y_own"].reshape(4, 512, D)
        kvo = R[c]["kv_own"].reshape(4, 512, 4, 2, 64)
        for k in range(4):
            i = 4 * k + j
            y_prompt[b, 512 * i:512 * (i + 1)] = yo[k]
            kv_prompt[0, b, 512 * i:512 * (i + 1)] = kvo[k]
        if j == 3:
            win_prompt[0, b] = R[c]["win_out"].reshape(512, 2, 2, 64)
            ro = R[c]["ret_out"].reshape(2, 64, 4, 64)
            ret_prompt[0, b] = ro.transpose(2, 0, 1, 3).reshape(8, 64, 64)
    y_sample = np.zeros((32, 1, D), f)
    kv_sample = np.zeros((1, 32, 1, 4, 2, 64), f)
    win_sample = np.zeros((1, 32, 512, 2, 2, 64), f)
    ret_sample = np.zeros((1, 32, 8, 64, 64), f)
    for c in range(8):
        y_sample[4 * c:4 * c + 4, 0] = R[c]["y_s"][0:4]
        kv_sample[0, 4 * c:4 * c + 4, 0] = R[c]["kv_s"][0:4].reshape(4, 4, 2, 64)
        win_sample[0, 4 * c:4 * c + 4] = R[c]["win_s"].reshape(4, 512, 2, 2, 64)
        rs = R[c]["ret_s"].reshape(4, 2, 64, 4, 64)
        ret_sample[0, 4 * c:4 * c + 4] = rs.transpose(0, 3, 1, 2, 4).reshape(4, 8, 64, 64)
    return (y_prompt, y_sample, kv_prompt, kv_sample, win_prompt, win_sample, ret_prompt, ret_sample)
```
